# Optimizing a Trainium2 kernel written in Bass

```python
import math
import jax
import jax.numpy as jnp
from jax import lax
import numpy as np

D_MODEL = 1024
BATCH = 16
SEQ = 256
DEPTH = 4
DEC_BATCH = 8
DEC_SEQ = 2048
PAST_LEN = 512

GRID_W = 64
N_MIXERS = 3
N_LAYERS_A = (DEPTH + 2) // N_MIXERS
N_LAYERS_B = (DEPTH + 1) // N_MIXERS
N_LAYERS_C = DEPTH // N_MIXERS
D_FF = 4 * D_MODEL
N_MOD = 6
EPS = 1e-6
ROPE_THETA = 10000.0
Q_BLOCK = 128
H_A = 8
DK_A = 128
DV_A = 128
WK_A = H_A * DK_A
WV_A = H_A * DV_A
CONV_K = 3
CHUNK = 64
H_B = 8
Q_LORA = 384
KV_LORA = 256
NOPE_B = 128
ROPE_B = 64
V_B = 128
H_C = 8
KVH_C = 2
HD_C = 128

kernel_name = 'hybrid_gdn_mla_gqa_diffusion_step'


def rms_norm(x, g):
    xf = x.astype(jnp.float32)
    y = xf * lax.rsqrt(jnp.mean(xf * xf, axis=-1, keepdims=True) + EPS)
    return (y * g.astype(jnp.float32)).astype(x.dtype)


def l2_norm(x):
    xf = x.astype(jnp.float32)
    return (xf * lax.rsqrt(jnp.sum(xf * xf, axis=-1, keepdims=True) + EPS)).astype(x.dtype)


def adaln(cond, w_mod, b_mod):
    m = jax.nn.silu(cond) @ w_mod + b_mod
    return jnp.split(m[:, None, :], N_MOD, axis=-1)


def modulated_norm(x, g, shift, scale):
    return rms_norm(x, g) * (1 + scale) + shift


def sq_relu_mlp(h, w_in, w_out):
    return jnp.square(jax.nn.relu(h @ w_in)) @ w_out


def grid_positions(rows):
    row = jnp.repeat(jnp.arange(rows, dtype=jnp.float32), GRID_W)
    col = jnp.tile(jnp.arange(GRID_W, dtype=jnp.float32), rows)
    return row, col


def rope_1d(x, pos):
    half = x.shape[-1] // 2
    freqs = ROPE_THETA ** (-jnp.arange(half, dtype=jnp.float32) / half)
    ang = pos[:, None] * freqs[None, :]
    cos = jnp.cos(ang)[None, :, None, :]
    sin = jnp.sin(ang)[None, :, None, :]
    x1 = x[..., :half].astype(jnp.float32)
    x2 = x[..., half:].astype(jnp.float32)
    return jnp.concatenate([x1 * cos - x2 * sin, x1 * sin + x2 * cos], axis=-1).astype(x.dtype)


def axial_rope(x, row_pos, col_pos):
    half = x.shape[-1] // 2
    return jnp.concatenate([rope_1d(x[..., :half], row_pos), rope_1d(x[..., half:], col_pos)], axis=-1)


def block_attention(q, k, v):
    b, sq, h, dk = q.shape
    hk, dv = k.shape[2], v.shape[-1]
    grp = h // hk
    scale = dk ** -0.5
    qb = jnp.moveaxis(q.reshape(b, sq // Q_BLOCK, Q_BLOCK, hk, grp, dk), 1, 0)

    def one_block(q_blk):
        s = jnp.einsum('bqkgd,bskd->bkgqs', q_blk, k, preferred_element_type=jnp.float32) * scale
        p = jax.nn.softmax(s, axis=-1).astype(v.dtype)
        return jnp.einsum('bkgqs,bskd->bqkgd', p, v)

    o = lax.map(one_block, qb)
    return jnp.moveaxis(o, 0, 1).reshape(b, sq, h, dv)


def centred_depthwise_conv(x, w):
    pad = (CONV_K - 1) // 2
    return lax.conv_general_dilated(x, w[:, None, :].astype(x.dtype), window_strides=(1,),
                                    padding=[(pad, pad)], dimension_numbers=('NWC', 'WIO', 'NWC'),
                                    feature_group_count=x.shape[-1])


def gated_delta_chunked(q, k, v, g, beta, s0):
    f32 = jnp.float32
    b, n_tok, h, _ = q.shape
    dv = v.shape[-1]
    n = n_tok // CHUNK

    def chunks(t):
        t = t.astype(f32).reshape((b, n, CHUNK, h) + t.shape[3:])
        return jnp.moveaxis(t, 3, 1)

    qc, kc, vc, gc, bc = map(chunks, (q, k, v, g, beta))
    gcum = jnp.cumsum(gc, axis=-1)
    idx = jnp.arange(CHUNK)
    incl = idx[:, None] >= idx[None, :]
    strict = idx[:, None] > idx[None, :]
    decay = jnp.exp(jnp.where(incl, gcum[..., :, None] - gcum[..., None, :], -jnp.inf))
    kb = kc * bc[..., None]
    a_mat = jnp.where(strict, jnp.einsum('bhncd,bhnmd->bhncm', kb, kc) * decay, 0.0)
    lower = a_mat + jnp.eye(CHUNK, dtype=f32)
    rhs = jnp.concatenate([vc * bc[..., None], kb * jnp.exp(gcum)[..., None]], axis=-1)
    sol = lax.linalg.triangular_solve(lower, rhs, left_side=True, lower=True, unit_diagonal=True)
    u, w = sol[..., :dv], sol[..., dv:]
    qk = jnp.einsum('bhncd,bhnmd->bhncm', qc, kc) * decay
    q_dec = qc * jnp.exp(gcum)[..., None]
    k_dec = kc * jnp.exp(gcum[..., -1:] - gcum)[..., None]
    g_tot = jnp.exp(gcum[..., -1])
    xs = tuple(jnp.moveaxis(t, 2, 0) for t in (u, w, qk, q_dec, k_dec, g_tot))

    def step(state, xs_i):
        u_i, w_i, qk_i, qd_i, kd_i, gt_i = xs_i
        v_new = u_i - jnp.einsum('bhcd,bhde->bhce', w_i, state)
        o_i = jnp.einsum('bhcd,bhde->bhce', qd_i, state) + jnp.einsum('bhcm,bhme->bhce', qk_i, v_new)
        state = state * gt_i[..., None, None] + jnp.einsum('bhcd,bhce->bhde', kd_i, v_new)
        return state, o_i

    s_fin, o = lax.scan(step, s0.astype(f32), xs)
    o = jnp.transpose(o, (1, 0, 3, 2, 4)).reshape(b, n_tok, h, dv)
    return o.astype(v.dtype), s_fin


def gdn_mixer(h, p, s0_fwd, s0_bwd):
    w_in, conv_w, a_log, dt_bias, out_norm, w_out = p
    b, s, _ = h.shape
    proj = h @ w_in
    n_qkv = 2 * WK_A + WV_A
    qkv = jax.nn.silu(centred_depthwise_conv(proj[..., :n_qkv], conv_w))
    z = proj[..., n_qkv:n_qkv + WV_A]
    gb = proj[..., n_qkv + WV_A:].astype(jnp.float32).reshape(b, s, 2, 2, H_A)
    q = l2_norm(qkv[..., :WK_A].reshape(b, s, H_A, DK_A)) * (DK_A ** -0.5)
    k = l2_norm(qkv[..., WK_A:2 * WK_A].reshape(b, s, H_A, DK_A))
    v = qkv[..., 2 * WK_A:].reshape(b, s, H_A, DV_A)
    g = -jnp.exp(a_log.astype(jnp.float32)) * jax.nn.softplus(gb[:, :, 0] + dt_bias.astype(jnp.float32))
    beta = jax.nn.sigmoid(gb[:, :, 1])
    o_f, s_f = gated_delta_chunked(q, k, v, g[:, :, 0], beta[:, :, 0], s0_fwd)
    rev = lambda t: jnp.flip(t, axis=1)
    o_b, s_b = gated_delta_chunked(rev(q), rev(k), rev(v), rev(g[:, :, 1]), rev(beta[:, :, 1]), s0_bwd)
    o = rms_norm(o_f + rev(o_b), out_norm) * jax.nn.silu(z).reshape(b, s, H_A, DV_A)
    return o.reshape(b, s, WV_A) @ w_out, s_f, s_b


def mla_project(h, p):
    w_down, q_lat_g, kv_lat_g, w_uq, w_ukv, qn_nope, qn_rope, kn_nope, kn_rope, w_out = p
    b, s, _ = h.shape
    proj = h @ w_down
    cq = rms_norm(proj[..., :Q_LORA], q_lat_g)
    ckv = rms_norm(proj[..., Q_LORA:Q_LORA + KV_LORA], kv_lat_g)
    k_rope = rms_norm(proj[..., Q_LORA + KV_LORA:], kn_rope)
    q = (cq @ w_uq).reshape(b, s, H_B, NOPE_B + ROPE_B)
    return rms_norm(q[..., :NOPE_B], qn_nope), rms_norm(q[..., NOPE_B:], qn_rope), ckv, k_rope


def mla_keys_values(ckv, k_rope, p):
    w_ukv, kn_nope = p[4], p[7]
    b, s, _ = ckv.shape
    kv = (ckv @ w_ukv).reshape(b, s, H_B, NOPE_B + V_B)
    k_nope = rms_norm(kv[..., :NOPE_B], kn_nope)
    k_rope_h = jnp.broadcast_to(k_rope[:, :, None, :], (b, s, H_B, ROPE_B))
    return jnp.concatenate([k_nope, k_rope_h], axis=-1), kv[..., NOPE_B:]


def mla_context(h, p):
    b, s, _ = h.shape
    q_nope, q_rope, ckv, k_rope = mla_project(h, p)
    k, v = mla_keys_values(ckv, k_rope, p)
    o = block_attention(jnp.concatenate([q_nope, q_rope], axis=-1), k, v)
    return o.reshape(b, s, H_B * V_B) @ p[9], ckv, k_rope


def mla_latent(h, ckv_ctx, krope_ctx, row_pos, col_pos, p):
    b, s, _ = h.shape
    q_nope, q_rope, ckv, k_rope = mla_project(h, p)
    q_rope = axial_rope(q_rope, row_pos, col_pos)
    k_rope = axial_rope(k_rope[:, :, None, :], row_pos, col_pos)[:, :, 0]
    k_lat, v_lat = mla_keys_values(ckv, k_rope, p)
    k_ctx, v_ctx = mla_keys_values(ckv_ctx, krope_ctx, p)
    o = block_attention(jnp.concatenate([q_nope, q_rope], axis=-1),
                        jnp.concatenate([k_ctx, k_lat], axis=1), jnp.concatenate([v_ctx, v_lat], axis=1))
    return o.reshape(b, s, H_B * V_B) @ p[9]


def gqa_project(h, p):
    w_in, q_g, k_g, _ = p
    b, s, _ = h.shape
    proj = h @ w_in
    q = rms_norm(proj[..., :H_C * HD_C].reshape(b, s, H_C, HD_C), q_g)
    k = rms_norm(proj[..., H_C * HD_C:(H_C + KVH_C) * HD_C].reshape(b, s, KVH_C, HD_C), k_g)
    v = proj[..., (H_C + KVH_C) * HD_C:].reshape(b, s, KVH_C, HD_C)
    return q, k, v


def gqa_context(h, p):
    b, s, _ = h.shape
    q, k, v = gqa_project(h, p)
    o = block_attention(q, k, v)
    return o.reshape(b, s, H_C * HD_C) @ p[3], k, v


def gqa_latent(h, k_ctx, v_ctx, row_pos, col_pos, p):
    b, s, _ = h.shape
    q, k, v = gqa_project(h, p)
    q = axial_rope(q, row_pos, col_pos)
    k = axial_rope(k, row_pos, col_pos)
    o = block_attention(q, jnp.concatenate([k_ctx, k], axis=1), jnp.concatenate([v_ctx, v], axis=1))
    return o.reshape(b, s, H_C * HD_C) @ p[3]


def setup_inputs(seed: int = 0) -> dict:
    key = jax.random.key(seed)
    keys = iter(jax.random.split(key, 64))
    f32 = jnp.float32
    d = D_MODEL

    def normal(shape, scale):
        return scale * jax.random.normal(next(keys), shape, f32)

    def gain(shape):
        return 1.0 + 0.05 * jax.random.normal(next(keys), shape, f32)

    a_log = jnp.log(jax.random.uniform(next(keys), (N_LAYERS_A, 2, H_A), f32, 1.0, 16.0))
    dt = jnp.exp(jax.random.uniform(next(keys), (N_LAYERS_A, 2, H_A), f32, math.log(1e-3), math.log(1e-1)))
    dt_bias = dt + jnp.log(-jnp.expm1(-dt))
    return {
        'x_prompt': normal((BATCH, SEQ, d), 1.0),
        'x_sample': normal((DEC_BATCH, DEC_SEQ, d), 1.0),
        'state_gdn_fwd': normal((DEC_BATCH, N_LAYERS_A, H_A, DK_A, DV_A), 0.2),
        'state_gdn_bwd': normal((DEC_BATCH, N_LAYERS_A, H_A, DK_A, DV_A), 0.2),
        'cache_mla_ckv': normal((DEC_BATCH, N_LAYERS_B, PAST_LEN, KV_LORA), 1.0),
        'cache_mla_krope': normal((DEC_BATCH, N_LAYERS_B, PAST_LEN, ROPE_B), 1.0),
        'cache_gqa_k': normal((DEC_BATCH, N_LAYERS_C, PAST_LEN, KVH_C, HD_C), 1.0),
        'cache_gqa_v': normal((DEC_BATCH, N_LAYERS_C, PAST_LEN, KVH_C, HD_C), 1.0),
        'c': normal((DEC_BATCH, d), 1.0),
        'c_ctx': normal((d,), 1.0),
        'norm_mix': gain((DEPTH, d)),
        'norm_mlp': gain((DEPTH, d)),
        'w_mod': normal((DEPTH, d, N_MOD * d), 0.5 * d ** -0.5),
        'b_mod': normal((DEPTH, N_MOD * d), 0.02),
        'w_mlp_in': normal((DEPTH, d, D_FF), d ** -0.5),
        'w_mlp_out': normal((DEPTH, D_FF, d), D_FF ** -0.5),
        'gdn_w_in': normal((N_LAYERS_A, d, 2 * WK_A + 2 * WV_A + 4 * H_A), d ** -0.5),
        'gdn_conv': normal((N_LAYERS_A, CONV_K, 2 * WK_A + WV_A), CONV_K ** -0.5),
        'gdn_a_log': a_log,
        'gdn_dt_bias': dt_bias,
        'gdn_out_norm': gain((N_LAYERS_A, DV_A)),
        'gdn_w_out': normal((N_LAYERS_A, WV_A, d), WV_A ** -0.5),
        'mla_w_down': normal((N_LAYERS_B, d, Q_LORA + KV_LORA + ROPE_B), d ** -0.5),
        'mla_q_lat_norm': gain((N_LAYERS_B, Q_LORA)),
        'mla_kv_lat_norm': gain((N_LAYERS_B, KV_LORA)),
        'mla_w_uq': normal((N_LAYERS_B, Q_LORA, H_B * (NOPE_B + ROPE_B)), Q_LORA ** -0.5),
        'mla_w_ukv': normal((N_LAYERS_B, KV_LORA, H_B * (NOPE_B + V_B)), KV_LORA ** -0.5),
        'mla_qn_nope': gain((N_LAYERS_B, NOPE_B)),
        'mla_qn_rope': gain((N_LAYERS_B, ROPE_B)),
        'mla_kn_nope': gain((N_LAYERS_B, NOPE_B)),
        'mla_kn_rope': gain((N_LAYERS_B, ROPE_B)),
        'mla_w_out': normal((N_LAYERS_B, H_B * V_B, d), (H_B * V_B) ** -0.5),
        'gqa_w_in': normal((N_LAYERS_C, d, (H_C + 2 * KVH_C) * HD_C), d ** -0.5),
        'gqa_q_norm': gain((N_LAYERS_C, HD_C)),
        'gqa_k_norm': gain((N_LAYERS_C, HD_C)),
        'gqa_w_out': normal((N_LAYERS_C, H_C * HD_C, d), (H_C * HD_C) ** -0.5),
    }


def reference(x_prompt, x_sample, state_gdn_fwd, state_gdn_bwd, cache_mla_ckv, cache_mla_krope, cache_gqa_k,
              cache_gqa_v, c, c_ctx, norm_mix, norm_mlp, w_mod, b_mod, w_mlp_in, w_mlp_out, gdn_w_in, gdn_conv,
              gdn_a_log, gdn_dt_bias, gdn_out_norm, gdn_w_out, mla_w_down, mla_q_lat_norm, mla_kv_lat_norm,
              mla_w_uq, mla_w_ukv, mla_qn_nope, mla_qn_rope, mla_kn_nope, mla_kn_rope, mla_w_out, gqa_w_in,
              gqa_q_norm, gqa_k_norm, gqa_w_out):
    rows = x_sample.shape[1] // GRID_W
    row_pos, col_pos = grid_positions(rows)
    xp, xs = x_prompt, x_sample
    bp = x_prompt.shape[0]
    cond_ctx = c_ctx[None, :]
    gdn_f, gdn_b, mla_c, mla_r, gqa_k, gqa_v = [], [], [], [], [], []
    for i in range(DEPTH):
        kind, j = i % N_MIXERS, i // N_MIXERS
        mp = adaln(cond_ctx, w_mod[i], b_mod[i])
        ms = adaln(c, w_mod[i], b_mod[i])
        hp = modulated_norm(xp, norm_mix[i], mp[0], mp[1])
        hs = modulated_norm(xs, norm_mix[i], ms[0], ms[1])
        if kind == 0:
            p = (gdn_w_in[j], gdn_conv[j], gdn_a_log[j], gdn_dt_bias[j], gdn_out_norm[j], gdn_w_out[j])
            zero = jnp.zeros((bp, H_A, DK_A, DV_A), jnp.float32)
            op, s_f, s_b = gdn_mixer(hp, p, zero, zero)
            os_, _, _ = gdn_mixer(hs, p, state_gdn_fwd[:, j], state_gdn_bwd[:, j])
            gdn_f.append(s_f)
            gdn_b.append(s_b)
        elif kind == 1:
            p = (mla_w_down[j], mla_q_lat_norm[j], mla_kv_lat_norm[j], mla_w_uq[j], mla_w_ukv[j],
                 mla_qn_nope[j], mla_qn_rope[j], mla_kn_nope[j], mla_kn_rope[j], mla_w_out[j])
            op, ckv, kr = mla_context(hp, p)
            os_ = mla_latent(hs, cache_mla_ckv[:, j], cache_mla_krope[:, j], row_pos, col_pos, p)
            mla_c.append(ckv)
            mla_r.append(kr)
        else:
            p = (gqa_w_in[j], gqa_q_norm[j], gqa_k_norm[j], gqa_w_out[j])
            op, kc, vc = gqa_context(hp, p)
            os_ = gqa_latent(hs, cache_gqa_k[:, j], cache_gqa_v[:, j], row_pos, col_pos, p)
            gqa_k.append(kc)
            gqa_v.append(vc)
        xp = xp + mp[2] * op
        xs = xs + ms[2] * os_
        hp = modulated_norm(xp, norm_mlp[i], mp[3], mp[4])
        hs = modulated_norm(xs, norm_mlp[i], ms[3], ms[4])
        xp = xp + mp[5] * sq_relu_mlp(hp, w_mlp_in[i], w_mlp_out[i])
        xs = xs + ms[5] * sq_relu_mlp(hs, w_mlp_in[i], w_mlp_out[i])
    dt = x_prompt.dtype
    new_gdn_fwd = jnp.stack(gdn_f, axis=1).astype(dt)
    new_gdn_bwd = jnp.stack(gdn_b, axis=1).astype(dt)
    new_mla_ckv = jnp.stack(mla_c, axis=1).astype(dt)
    new_mla_krope = jnp.stack(mla_r, axis=1).astype(dt)
    new_gqa_k = jnp.stack(gqa_k, axis=1).astype(dt)
    new_gqa_v = jnp.stack(gqa_v, axis=1).astype(dt)
    return (xp, xs, new_gdn_fwd, new_gdn_bwd, new_mla_ckv, new_mla_krope, new_gqa_k, new_gqa_v)
```

```python
import contextlib
import math
import numpy as np
import concourse.bass as bass
import concourse.mybir as mybir
from concourse.bass_utils import run_bass_kernel_spmd

F32 = mybir.dt.float32
BF16 = mybir.dt.bfloat16
AF = mybir.ActivationFunctionType
ALU = mybir.AluOpType
AX = mybir.AxisListType

D = 1024
NKC = 8
T = 2560
NCORES = 8
EPS = 1e-6
TB = [(0, 512, 0), (512, 512, 1), (1024, 512, 1), (1536, 512, 1), (2048, 512, 1)]
KINDS_FULL = (0, 1, 2, 0)


class Buf:
    __slots__ = ("ap", "name", "last_write", "reads", "excl")

    def __init__(self, ap, name="", excl=False):
        self.ap = ap
        self.name = name
        self.last_write = None
        self.reads = []
        self.excl = excl

    def __getitem__(self, idx):
        return View(self, self.ap[idx])

    @property
    def v(self):
        return View(self, self.ap)


class View:
    __slots__ = ("buf", "ap")

    def __init__(self, buf, ap):
        self.buf = buf
        self.ap = ap

    def __getitem__(self, idx):
        return View(self.buf, self.ap[idx])

    def re(self, pat, **kw):
        return View(self.buf, self.ap.rearrange(pat, **kw))

    def bc(self, shape):
        return View(self.buf, self.ap.broadcast_to(shape))

    def un(self, axis):
        return View(self.buf, self.ap.unsqueeze(axis))


class Op:
    __slots__ = ("eng", "fn", "deps", "flag", "ticket", "dma", "dsem", "dval", "dprev", "idx")

    def __init__(self, eng, fn):
        self.eng = eng
        self.fn = fn
        self.deps = []
        self.flag = False
        self.ticket = 0
        self.dma = False
        self.dsem = None
        self.dval = 0
        self.dprev = None
        self.idx = 0


ENGS = ("pe", "act", "dve", "pool", "sp")
import os
DUMP = os.environ.get("KDUMP", "") == "1"


def compress(ops):
    best = {}
    for o in ops:
        key = ("d", o.dsem) if o.dma else ("e", o.eng)
        b = best.get(key)
        if b is None or o.idx > b.idx:
            best[key] = o
    return list(best.values())


class Sched:
    def __init__(self, nc, n_dma_sems=32):
        self.nc = nc
        self.ops = {e: [] for e in ENGS}
        self.n_dma_sems = n_dma_sems
        self.dma_count = 0
        self.dma_last = [None] * n_dma_sems
        self.dma_vals = [0] * n_dma_sems
        self.nops = 0

    def op(self, eng, fn, reads=(), writes=(), dma=False):
        o = Op(eng, fn)
        o.idx = self.nops
        self.nops += 1
        deps = {}
        for r in reads:
            b = r.buf if isinstance(r, View) else r
            if b.last_write is not None:
                deps[id(b.last_write)] = (b.last_write, True)
            if b.excl:
                for rd in b.reads:
                    if rd.eng != eng and id(rd) not in deps:
                        deps[id(rd)] = (rd, False)
        for w in writes:
            b = w.buf if isinstance(w, View) else w
            if b.last_write is not None and id(b.last_write) not in deps:
                deps[id(b.last_write)] = (b.last_write, False)
            for rd in b.reads:
                if id(rd) not in deps:
                    deps[id(rd)] = (rd, False)
        for d, strong in deps.values():
            if d is o:
                continue
            if d.eng == eng and not d.dma and not dma:
                if eng == "pe" or not strong:
                    continue
            o.deps.append(d)
            d.flag = True
        for w in writes:
            b = w.buf if isinstance(w, View) else w
            b.last_write = o
            b.reads = []
        for r in reads:
            b = r.buf if isinstance(r, View) else r
            if b.last_write is not o:
                b.reads.append(o)
                if len(b.reads) > 24:
                    b.reads = compress(b.reads)
        if dma:
            o.dma = True
            s = self.dma_count % self.n_dma_sems
            self.dma_count += 1
            o.dsem = s
            self.dma_vals[s] += 16
            o.dval = self.dma_vals[s]
            o.dprev = self.dma_last[s]
            self.dma_last[s] = o
        self.ops[eng].append(o)
        return o

    def mm(self, out, lhsT, rhs, start=True, stop=True):
        return self.op("pe", lambda e: e.matmul(out.ap, lhsT.ap, rhs.ap, start=start, stop=stop),
                       reads=[lhsT, rhs] + ([] if start else [out]), writes=[out])

    def transpose(self, out, in_, ident):
        return self.op("pe", lambda e: e.transpose(out.ap, in_.ap, ident.ap), reads=[in_, ident], writes=[out])

    def act(self, out, in_, func, bias=None, scale=None, accum=None):
        kw = {}
        reads = [in_]
        writes = [out]
        if bias is not None:
            if isinstance(bias, View):
                kw["bias"] = bias.ap
                reads.append(bias)
            else:
                kw["bias"] = bias
        if scale is not None:
            if isinstance(scale, View):
                kw["scale"] = scale.ap
                reads.append(scale)
            else:
                kw["scale"] = scale
        if accum is not None:
            kw["accum_out"] = accum.ap
            writes.append(accum)
        return self.op("act", lambda e: e.activation(out.ap, in_.ap, func, **kw), reads=reads, writes=writes)

    def tt(self, eng, out, a, b, op):
        return self.op(eng, lambda e: e.tensor_tensor(out.ap, a.ap, b.ap, op), reads=[a, b], writes=[out])

    def ts(self, eng, out, a, s1, op0, s2=None, op1=None):
        reads = [a]
        s1a = s1.ap if isinstance(s1, View) else s1
        s2a = s2.ap if isinstance(s2, View) else s2
        if isinstance(s1, View):
            reads.append(s1)
        if isinstance(s2, View):
            reads.append(s2)
        kw = {}
        if op1 is not None:
            kw["op1"] = op1
        return self.op(eng, lambda e: e.tensor_scalar(out.ap, a.ap, s1a, s2a, op0, **kw), reads=reads, writes=[out])

    def stt(self, eng, out, a, s, b, op0, op1):
        reads = [a, b]
        sa = s.ap if isinstance(s, View) else s
        if isinstance(s, View):
            reads.append(s)
        return self.op(eng, lambda e: e.scalar_tensor_tensor(out.ap, a.ap, sa, b.ap, op0, op1),
                       reads=reads, writes=[out])

    def copy(self, eng, out, in_):
        if eng == "act":
            return self.op(eng, lambda e: e.copy(out.ap, in_.ap), reads=[in_], writes=[out])
        return self.op(eng, lambda e: e.tensor_copy(out.ap, in_.ap), reads=[in_], writes=[out])

    def recip(self, out, in_):
        return self.op("dve", lambda e: e.reciprocal(out.ap, in_.ap), reads=[in_], writes=[out])

    def reduce(self, eng, out, in_, op=None):
        return self.op(eng, lambda e: e.tensor_reduce(out.ap, in_.ap, AX.X, op or ALU.add), reads=[in_], writes=[out])

    def memset(self, eng, out, val):
        return self.op(eng, lambda e: e.memset(out.ap, val), writes=[out])

    def dma(self, out, in_, eng="sp"):
        return self.op(eng, lambda e: e.dma_start(out=out.ap, in_=in_.ap), reads=[in_], writes=[out], dma=True)

    def emit(self):
        nc = self.nc
        for e in ENGS:
            c = 0
            for o in self.ops[e]:
                if o.dma:
                    continue
                if o.flag:
                    c += 1
                    o.ticket = c
        with contextlib.ExitStack() as st:
            esems = {e: st.enter_context(nc.semaphore("s_" + e)) for e in ENGS}
            dsems = [st.enter_context(nc.semaphore("d%d" % i)) for i in range(self.n_dma_sems)]
            block = st.enter_context(nc.Block())
            sched = self

            def make(ename):
                def body(eng):
                    waited = {}
                    for o in sched.ops[ename]:
                        ws = []
                        for d in o.deps:
                            if d.dma:
                                ws.append((("d", d.dsem), dsems[d.dsem], d.dval))
                            else:
                                ws.append((("e", d.eng), esems[d.eng], d.ticket))
                        if o.dma and o.dprev is not None:
                            d = o.dprev
                            ws.append((("d", d.dsem), dsems[d.dsem], d.dval))
                        for key, sem, val in ws:
                            if waited.get(key, 0) >= val:
                                continue
                            waited[key] = val
                            eng.wait_ge(sem, val)
                            if DUMP:
                                print("   ", ename, "wait", key, val)
                        ins = o.fn(eng)
                        if DUMP:
                            print(ename, o.idx, "dma" if o.dma else "", ("inc d%d->%d" % (o.dsem, o.dval)) if o.dma else ("inc e->%d" % o.ticket if o.flag else ""), str(ins)[:150])
                        if o.dma:
                            ins.then_inc(dsems[o.dsem], 16)
                        elif o.flag:
                            ins.then_inc(esems[ename], 1)
                    if ename == "sp":
                        for s in range(sched.n_dma_sems):
                            if sched.dma_vals[s] > 0 and waited.get(("d", s), 0) < sched.dma_vals[s]:
                                eng.wait_ge(dsems[s], sched.dma_vals[s])
                return body

            block.tensor(make("pe"))
            block.scalar(make("act"))
            block.vector(make("dve"))
            block.gpsimd(make("pool"))
            block.sync(make("sp"))


class Region:
    def __init__(self, ap_f32, nbytes):
        self.ap = ap_f32
        self.nbytes = nbytes
        self.live = []
        self.top = 0

    def reset(self):
        self.top = 0

    def alloc(self, shape, dt, name=""):
        esz = 2 if dt == BF16 else 4
        free = int(np.prod(shape[1:]))
        nb = (free * esz + 31) // 32 * 32
        s = self.top
        e = s + nb
        assert e <= self.nbytes, (name, e, self.nbytes)
        self.top = e
        ap = self.ap[0:shape[0], s // 4:e // 4]
        if dt == BF16:
            ap = ap.bitcast(BF16)
        ap = ap[:, 0:free]
        if len(shape) == 3:
            ap = ap.rearrange("p (a b) -> p a b", a=shape[1])
        elif len(shape) == 4:
            ap = ap.rearrange("p (a b c) -> p a b c", a=shape[1], b=shape[2])
        b = Buf(ap, name)
        inherit = []
        keep = []
        for (ps, pe, pb) in self.live:
            if ps < e and s < pe:
                inherit.extend(pb.reads)
                if pb.last_write is not None:
                    inherit.append(pb.last_write)
                if ps >= s and pe <= e:
                    continue
            keep.append((ps, pe, pb))
        keep.append((s, e, b))
        self.live = keep
        b.reads = compress(inherit)
        return b


def rope_tables(half, n_tok_rows):
    rows = 2048 // 64
    row = np.repeat(np.arange(rows, dtype=np.float32), 64)
    col = np.tile(np.arange(64, dtype=np.float32), rows)
    freqs = (10000.0 ** (-np.arange(half, dtype=np.float32) / half)).astype(np.float32)
    ang_r = row[None, :] * freqs[:, None]
    ang_c = col[None, :] * freqs[:, None]
    cos = np.concatenate([np.cos(ang_r), np.cos(ang_r), np.cos(ang_c), np.cos(ang_c)], 0).astype(np.float32)
    sin = np.concatenate([np.sin(ang_r), np.sin(ang_r), np.sin(ang_c), np.sin(ang_c)], 0).astype(np.float32)
    n = 4 * half
    P = np.zeros((n, n), np.float32)
    for blk in range(2):
        o = blk * 2 * half
        for d in range(half):
            P[o + d, o + d + half] = -1.0
            P[o + d + half, o + d] = 1.0
    return cos, sin, np.ascontiguousarray(P.T)


def build(kinds=KINDS_FULL, do_mlp=True):
    nc = bass.Bass("TRN2", target_bir_lowering=False)
    n_a = sum(1 for k in kinds if k == 0)
    n_b = sum(1 for k in kinds if k == 1)
    n_c = sum(1 for k in kinds if k == 2)
    depth = len(kinds)

    def din(name, shape):
        return Buf(nc.dram_tensor(name, list(shape), F32, kind="ExternalInput").ap(), name)

    def dout(name, shape):
        return Buf(nc.dram_tensor(name, list(shape), F32, kind="ExternalOutput").ap(), name)

    I = {}
    I["xp"] = din("xp", (512, D))
    I["xs"] = din("xs", (2048, D))
    I["cond"] = din("cond", (2, 8, 128))
    dd = max(depth, 1)
    I["norm_mix"] = din("norm_mix", (dd, 8, 128))
    I["norm_mlp"] = din("norm_mlp", (dd, 8, 128))
    I["w_mod"] = din("w_mod", (dd, D, 6 * D))
    I["b_mod"] = din("b_mod", (dd, 48, 128))
    I["w_mlp_in"] = din("w_mlp_in", (dd, D, 4 * D))
    I["w_mlp_out"] = din("w_mlp_out", (dd, 4 * D, D))
    I["ident"] = din("ident", (128, 128))
    I["ones"] = din("ones", (128, 128))
    if n_c:
        I["gqa_w_in"] = din("gqa_w_in", (n_c, D, 1536))
        I["gqa_q_norm"] = din("gqa_q_norm", (n_c, 128))
        I["gqa_k_norm"] = din("gqa_k_norm", (n_c, 128))
        I["gqa_w_out"] = din("gqa_w_out", (n_c, D, D))
        I["cache_gqa_k"] = din("cache_gqa_k", (n_c, 512, 256))
        I["cache_gqa_v"] = din("cache_gqa_v", (n_c, 512, 256))
        I["rope_c_cos"] = din("rope_c_cos", (128, 2048))
        I["rope_c_sin"] = din("rope_c_sin", (128, 2048))
        I["rope_c_p"] = din("rope_c_p", (128, 128))
    if n_b:
        I["mla_w_down"] = din("mla_w_down", (n_b, D, 704))
        I["mla_q_lat_norm"] = din("mla_q_lat_norm", (n_b, 3, 128))
        I["mla_kv_lat_norm"] = din("mla_kv_lat_norm", (n_b, 2, 128))
        I["mla_w_uq"] = din("mla_w_uq", (n_b, 384, 1536))
        I["mla_w_ukv"] = din("mla_w_ukv", (n_b, 256, 2048))
        I["mla_qn_nope"] = din("mla_qn_nope", (n_b, 128))
        I["mla_qn_rope"] = din("mla_qn_rope", (n_b, 64))
        I["mla_kn_nope"] = din("mla_kn_nope", (n_b, 128))
        I["mla_kn_rope"] = din("mla_kn_rope", (n_b, 64))
        I["mla_w_out"] = din("mla_w_out", (n_b, D, D))
        I["cache_mla_ckv"] = din("cache_mla_ckv", (n_b, 512, 256))
        I["cache_mla_krope"] = din("cache_mla_krope", (n_b, 512, 64))
        I["rope_b_cos"] = din("rope_b_cos", (64, 2048))
        I["rope_b_sin"] = din("rope_b_sin", (64, 2048))
        I["rope_b_p"] = din("rope_b_p", (64, 64))
    if n_a:
        I["gdn_w_in"] = din("gdn_w_in", (n_a, D, 4096))
        I["gdn_w_gb"] = din("gdn_w_gb", (n_a, D, 32))
        I["gdn_conv"] = din("gdn_conv", (n_a, 72, 128))
        I["gdn_a_log"] = din("gdn_a_log", (n_a, 16))
        I["gdn_dt_bias"] = din("gdn_dt_bias", (n_a, 16))
        I["gdn_out_norm"] = din("gdn_out_norm", (n_a, 128))
        I["gdn_w_out"] = din("gdn_w_out", (n_a, D, D))
        I["state_f"] = din("state_f", (n_a, 8, 128, 128))
        I["state_b"] = din("state_b", (n_a, 8, 128, 128))
        I["gdn_masks"] = din("gdn_masks", (8, 128, 128))

    O = {}
    O["yp"] = dout("yp", (512, D))
    O["ys"] = dout("ys", (2048, D))
    O["o_sf"] = dout("o_sf", (2, max(n_a, 1), 8, 128, 128))
    O["o_sb"] = dout("o_sb", (2, max(n_a, 1), 8, 128, 128))
    O["o_ckv"] = dout("o_ckv", (2, max(n_b, 1), 256, 256))
    O["o_kr"] = dout("o_kr", (2, max(n_b, 1), 256, 64))
    O["o_gk"] = dout("o_gk", (2, max(n_c, 1), 256, 256))
    O["o_gv"] = dout("o_gv", (2, max(n_c, 1), 256, 256))

    st = contextlib.ExitStack()
    with st:
        K = Sched(nc)

        def sb(name, shape, dt):
            return Buf(st.enter_context(nc.sbuf_tensor("sb_" + name, list(shape), dt)).ap(), name)

        XT = [[sb("x%d_%d" % (kc, b), (128, 512), F32) for b in range(5)] for kc in range(NKC)]
        HT = [[sb("h%d_%d" % (kc, b), (128, 512), BF16) for b in range(5)] for kc in range(NKC)]
        ident = sb("ident", (128, 128), F32)
        identb = sb("identb", (128, 128), BF16)
        onesf = sb("onesf", (128, 128), F32)
        onesb = sb("onesb", (128, 128), BF16)
        condT = sb("condT", (128, 2, 8), F32)
        modT = [sb("modT%d" % i, (128, 48, 2), F32) for i in range(2)]
        coefA = sb("coefA", (128, 2, 8, 2), F32)
        gnorm = sb("gnorm", (128, 2, 8), F32)
        epsv = sb("epsv", (128, 1), F32)
        WR_BYTES = 85 * 1024
        wr_t = sb("wr", (128, WR_BYTES // 4), F32)
        R = Region(wr_t.ap, WR_BYTES)
        PS = [Buf(st.enter_context(nc.psum_tensor("ps%d" % i, [128, 512], F32)).ap(), "ps%d" % i, excl=True) for i in range(8)]

        def psb(i, dt=F32):
            if dt == BF16:
                return View(PS[i], PS[i].ap.bitcast(BF16))
            return PS[i].v

        K.dma(ident.v, I["ident"].v)
        K.dma(onesf.v, I["ones"].v)
        K.dma(identb.v, I["ident"].v, eng="pool")
        K.dma(onesb.v, I["ones"].v, eng="pool")
        K.memset("dve", epsv.v, EPS)

        R.reset()
        xst = [R.alloc((128, D), F32, "xst%d" % i) for i in range(2)]
        import os
        NT_DBG = int(os.environ.get("KNT", "20"))
        for t in range(NT_DBG):
            src = I["xp"][t * 128:(t + 1) * 128, :] if t < 4 else I["xs"][(t - 4) * 128:(t - 3) * 128, :]
            s_ = xst[(t + int(os.environ.get("KSL", "0"))) % 2]
            if os.environ.get("KV", "") == "a":
                src = I["xp"][0:128, :]
            if os.environ.get("KV", "") == "b":
                s_ = xst[0]
            K.dma(s_.v, src, eng=os.environ.get("KQ", "sp"))
            if os.environ.get("KV", "") == "c" and t == 1:
                continue
            b, off = divmod(t * 128, 512)
            for g in range(2):
                pb = psb((t * 2 + g + int(os.environ.get("KOFF", "0"))) % int(os.environ.get("KNB", "4")))
                for q in range(4):
                    kc = g * 4 + q
                    K.transpose(pb[:, q * 128:(q + 1) * 128], s_[:, kc * 128:(kc + 1) * 128], ident.v)
                for q in range(4):
                    kc = g * 4 + q
                    eng = "dve" if q % 2 == 0 else "act"
                    if os.environ.get("KV", "") == "d":
                        eng = "dve"
                    if os.environ.get("KV", "") == "e":
                        eng = "act"
                    K.copy(eng, XT[kc][b][:, off:off + 128], pb[:, q * 128:(q + 1) * 128])

        import os
        if os.environ.get("KDBG", "") != "1":
            cst = R.alloc((16, 128), F32, "cst")
            K.dma(cst.v, I["cond"].v.re("c k p -> (c k) p"))
            K.transpose(psb(4)[:, 0:16], cst.v, ident[0:16, 0:16])
            K.act(condT.v.re("p c k -> p (c k)"), psb(4)[:, 0:16], AF.Silu)

        def adaln(i):
            mt = modT[i % 2]
            bst = R.alloc((48, 128), F32, "bst")
            bT = R.alloc((128, 48), F32, "bT")
            K.dma(bst.v, I["b_mod"][i])
            K.transpose(psb(4)[:, 0:48], bst.v, ident[0:48, 0:48])
            K.copy("dve", bT.v, psb(4)[:, 0:48])
            wst = [R.alloc((128, 8, 256), F32, "wmod%d" % q) for q in range(2)]
            for cb in range(24):
                w_ = wst[cb % 2]
                K.dma(w_.v, I["w_mod"][i].re("(k p) n -> p k n", p=128)[:, :, cb * 256:(cb + 1) * 256])
                pb = psb(5)
                for fl in range(2):
                    f = cb * 2 + fl
                    for kc in range(8):
                        K.mm(pb[:, f * 2:f * 2 + 2], w_[:, kc, fl * 128:(fl + 1) * 128], condT[:, :, kc],
                             start=(kc == 0), stop=(kc == 7))
                if cb % 2 == 1:
                    yield
            K.tt("dve", mt.v, psb(5)[:, 0:96].re("p (f c) -> p f c", c=2), bT.v.un(2).bc([128, 48, 2]), ALU.add)
            yield

        def run(gen):
            for _ in gen:
                pass

        def load_gains(i):
            gst = R.alloc((16, 128), F32, "gst")
            K.dma(gst[0:8, :], I["norm_mix"][i])
            K.dma(gst[8:16, :], I["norm_mlp"][i])
            K.transpose(psb(4)[:, 0:16], gst.v, ident[0:16, 0:16])
            K.copy("dve", gnorm.v.re("p w k -> p (w k)"), psb(4)[:, 0:16])

        def make_coef(i):
            mt = modT[i % 2]
            for w in range(2):
                sc = mt[:, (3 * w + 1) * 8:(3 * w + 2) * 8, :]
                K.ts("dve", coefA[:, w], sc, 1.0, ALU.add)
                K.tt("dve", coefA[:, w], coefA[:, w], gnorm[:, w].un(2).bc([128, 8, 2]), ALU.mult)

        def mod_norm(i, w):
            mt = modT[i % 2]
            sq = [R.alloc((128, 512), BF16, "sq%d" % q) for q in range(2)]
            rs = [R.alloc((128, 512), F32, "rs%d" % q) for q in range(2)]
            tmp = [R.alloc((128, 512), F32, "ntmp%d" % q) for q in range(2)]
            for b, (c0, n, cond) in enumerate(TB):
                pb = psb(b % 2)
                for kc in range(8):
                    s_ = sq[kc % 2]
                    if kc % 2 == 0:
                        K.act(s_.v, XT[kc][b].v, AF.Square)
                    else:
                        K.tt("pool", s_.v, XT[kc][b].v, XT[kc][b].v, ALU.mult)
                    K.mm(pb, onesb.v, s_.v, start=(kc == 0), stop=(kc == 7))
                r_ = rs[b % 2]
                K.act(r_.v, pb, AF.Sqrt, bias=epsv.v, scale=1.0 / D)
                K.recip(r_.v, r_.v)
                for kc in range(8):
                    t_ = tmp[kc % 2]
                    K.stt("dve", t_.v, XT[kc][b].v, coefA[:, w, kc, cond:cond + 1], r_.v, ALU.mult, ALU.mult)
                    K.act(HT[kc][b].v, t_.v, AF.Identity, bias=mt[:, 3 * w * 8 + kc, cond:cond + 1])

        def mlp(i, bg):
            mt = modT[i % 2]
            win = [R.alloc((128, 8, 512), BF16, "win%d" % q) for q in range(2)]
            wout = [R.alloc((128, 4, D), BF16, "wout%d" % q) for q in range(2)]
            uu = [R.alloc((128, 4, 512), BF16, "uu%d" % q) for q in range(2)]
            rr = [R.alloc((128, 512), BF16, "rr%d" % q) for q in range(2)]
            it = 0
            for fb in range(8):
                wi = win[fb % 2]
                wo = wout[fb % 2]
                K.dma(wi.v, I["w_mlp_in"][i].re("(k p) n -> p k n", p=128)[:, :, fb * 512:(fb + 1) * 512], eng="pool")
                K.dma(wo.v, I["w_mlp_out"][i][fb * 512:(fb + 1) * 512, :].re("(f p) n -> p f n", p=128), eng="pool")
                for b, (c0, n, cond) in enumerate(TB):
                    u_ = uu[it % 2]
                    for fc in range(4):
                        pb = psb(fc % 2)
                        for kc in range(8):
                            K.mm(pb, wi[:, kc, fc * 128:(fc + 1) * 128], HT[kc][b].v, start=(kc == 0), stop=(kc == 7))
                        r_ = rr[fc % 2]
                        K.act(r_.v, pb, AF.Relu)
                        K.tt("pool", u_[:, fc, :], r_.v, r_.v, ALU.mult)
                    for oc in range(8):
                        pb = psb(2 + oc % 2)
                        for fc in range(4):
                            K.mm(pb, wo[:, fc, oc * 128:(oc + 1) * 128], u_[:, fc, :], start=(fc == 0), stop=(fc == 3))
                        K.stt("dve", XT[oc][b].v, pb, mt[:, 40 + oc, cond:cond + 1], XT[oc][b].v, ALU.mult, ALU.add)
                    it += 1
                    if bg is not None:
                        next(bg, None)
            if bg is not None:
                run(bg)

        def out_accum(i, oT, wo_h):
            mt = modT[i % 2]
            for b, (c0, n, cond) in enumerate(TB):
                for oc in range(8):
                    pb = psb(6 + oc % 2)
                    K.mm(pb, wo_h[:, oc * 128:(oc + 1) * 128], oT[b])
                    K.stt("dve", XT[oc][b].v, pb, mt[:, 16 + oc, cond:cond + 1], XT[oc][b].v, ALU.mult, ALU.add)

        def fm_norm(dst, src_ps, npart, ncols, gain, nfeat, scr, rope=None, ones_v=None, eps_v=None):
            P = npart
            K.act(scr["sq"][0:P, 0:ncols], src_ps[0:P, 0:ncols], AF.Square)
            pd = scr["pd"]
            K.mm(pd[0:P, 0:ncols], onesb[0:P, 0:P], scr["sq"][0:P, 0:ncols])
            K.act(scr["rs"][0:P, 0:ncols], pd[0:P, 0:ncols], AF.Sqrt, bias=(eps_v or epsv)[0:P, :], scale=1.0 / nfeat)
            K.recip(scr["rs"][0:P, 0:ncols], scr["rs"][0:P, 0:ncols])
            if rope is None:
                K.stt("dve", dst, src_ps[0:P, 0:ncols], gain, scr["rs"][0:P, 0:ncols], ALU.mult, ALU.mult)
            else:
                cos, sin, permT = rope
                nb = scr["nb"]
                K.stt("dve", nb[0:P, 0:ncols], src_ps[0:P, 0:ncols], gain, scr["rs"][0:P, 0:ncols], ALU.mult, ALU.mult)
                K.mm(pd[0:P, 0:ncols], permT, nb[0:P, 0:ncols])
                K.tt("pool", scr["t1"][0:P, 0:ncols], nb[0:P, 0:ncols], cos, ALU.mult)
                K.tt("dve", scr["t2"][0:P, 0:ncols], pd[0:P, 0:ncols], sin, ALU.mult)
                K.tt("pool", dst, scr["t1"][0:P, 0:ncols], scr["t2"][0:P, 0:ncols], ALU.add)

        def norm_scr(light=False):
            if light:
                return {"sq": R.alloc((128, 512), BF16, "nsq"), "rs": R.alloc((128, 512), F32, "nrs"), "pd": psb(5)}
            return {"sq": R.alloc((128, 512), BF16, "nsq"), "rs": R.alloc((128, 512), F32, "nrs"),
                    "t1": R.alloc((128, 512), F32, "nt1"), "t2": R.alloc((128, 512), F32, "nt2"),
                    "nb": R.alloc((128, 512), BF16, "nnb"), "pd": psb(5)}

        QB = [(0, 256, [0, 1]), (256, 256, [2, 3])] + [(512 + 512 * q, 512, list(range(4, 24))) for q in range(4)]

        def attn_head(kparts, vtile, qparts, oT, scale, pbufs):
            for qi, (q0, nq, kts) in enumerate(QB):
                po = psb(3)
                pdn = psb(4)
                for ji, kt in enumerate(kts):
                    pss = psb(ji % 3)
                    for pi, (kT, P) in enumerate(kparts):
                        K.mm(pss[:, 0:nq], kT[0:P, kt * 128:(kt + 1) * 128], qparts[pi][0:P, q0:q0 + nq],
                             start=(pi == 0), stop=(pi == len(kparts) - 1))
                    pt = pbufs[ji % len(pbufs)]
                    K.act(pt[:, 0:nq], pss[:, 0:nq], AF.Exp, scale=scale)
                    K.mm(po[:, 0:nq], vtile(kt), pt[:, 0:nq], start=(ji == 0), stop=(ji == len(kts) - 1))
                    K.mm(pdn[:, 0:nq], onesb.v, pt[:, 0:nq], start=(ji == 0), stop=(ji == len(kts) - 1))
                rd = pbufs_rd[qi % 2]
                K.recip(rd[:, 0:nq], pdn[:, 0:nq])
                K.tt("dve", oT[:, q0:q0 + nq], po[:, 0:nq], rd[:, 0:nq], ALU.mult)

        pbufs_rd = [None, None]

        def gqa_layer(i, j):
            R.reset()
            kT = R.alloc((128, 2, 3072), BF16, "kT")
            V = R.alloc((128, 24, 256), BF16, "V")
            cos = R.alloc((128, 2048), BF16, "cos")
            sin = R.alloc((128, 2048), BF16, "sin")
            permT = R.alloc((128, 128), BF16, "permT")
            kg = R.alloc((128, 2), F32, "kg")
            kgb = R.alloc((128, 128), F32, "kgb")
            K.dma(cos.v, I["rope_c_cos"].v, eng="pool")
            K.dma(sin.v, I["rope_c_sin"].v, eng="pool")
            K.dma(permT.v, I["rope_c_p"].v, eng="pool")
            gst = R.alloc((2, 128), F32, "gst2")
            K.dma(gst[0:1, :], I["gqa_q_norm"][j:j + 1, :])
            K.dma(gst[1:2, :], I["gqa_k_norm"][j:j + 1, :])
            K.transpose(psb(4)[:, 0:2], gst.v, ident[0:2, 0:2])
            K.copy("dve", kg.v, psb(4)[:, 0:2])
            K.dma(kgb.v, View(I["gqa_k_norm"], I["gqa_k_norm"].ap[j].partition_broadcast(128)))
            top0 = R.top
            wkv = R.alloc((128, 8, 512), BF16, "wkv")
            K.dma(wkv.v, I["gqa_w_in"][j].re("(k p) n -> p k n", p=128)[:, :, 1024:1536], eng="pool")
            scr = norm_scr()
            for kvh in range(2):
                for b, (c0, n, cond) in enumerate(TB):
                    pb = psb(b % 2)
                    for kc in range(8):
                        K.mm(pb, wkv[:, kc, kvh * 128:(kvh + 1) * 128], HT[kc][b].v, start=(kc == 0), stop=(kc == 7))
                    rope = None if b == 0 else (cos[:, c0 - 512:c0], sin[:, c0 - 512:c0], permT.v)
                    fm_norm(kT[:, kvh, c0:c0 + 512], pb, 128, 512, kg[:, 1:2], 128, scr, rope)
            vst = R.alloc((128, 256), F32, "vst")
            kst = R.alloc((128, 256), F32, "kst")
            ssq = R.alloc((128, 2), F32, "ssq")
            ksq = R.alloc((128, 256), F32, "ksq")
            for t in range(20):
                b, off = divmod(t * 128, 512)
                pb = psb(2 + t % 2)
                ncol = 512 if t < 4 else 256
                for kc in range(8):
                    rhs = wkv[:, kc, 0:512] if t < 4 else wkv[:, kc, 256:512]
                    K.mm(pb[:, 0:ncol], HT[kc][b][:, off:off + 128], rhs, start=(kc == 0), stop=(kc == 7))
                if t < 4:
                    K.copy("act", V[:, t, :], pb[:, 256:512])
                    K.copy("dve", vst.v, pb[:, 256:512])
                    K.dma(O["o_gv"][t // 2, j, (t % 2) * 128:(t % 2 + 1) * 128, :], vst.v)
                    K.act(ksq.v, pb[:, 0:256], AF.Square)
                    K.reduce("dve", ssq.v, ksq.v.re("p (h d) -> p h d", h=2))
                    K.act(ssq.v, ssq.v, AF.Sqrt, bias=epsv.v, scale=1.0 / 128)
                    K.recip(ssq.v, ssq.v)
                    K.tt("dve", kst.v.re("p (h d) -> p h d", h=2), pb[:, 0:256].re("p (h d) -> p h d", h=2),
                         ssq.v.un(2).bc([128, 2, 128]), ALU.mult)
                    K.tt("pool", kst.v.re("p (h d) -> p h d", h=2), kst.v.re("p (h d) -> p h d", h=2),
                         kgb.v.un(1).bc([128, 2, 128]), ALU.mult)
                    K.dma(O["o_gk"][t // 2, j, (t % 2) * 128:(t % 2 + 1) * 128, :], kst.v)
                else:
                    K.copy("act", V[:, t, :], pb[:, 0:256])
            K.dma(V[:, 20:24, :], I["cache_gqa_v"][j].re("(t p) n -> p t n", p=128), eng="pool")
            cst_ = R.alloc((128, 4, 256), F32, "cst_")
            K.dma(cst_.v, I["cache_gqa_k"][j].re("(t p) n -> p t n", p=128))
            for t in range(4):
                for kvh in range(2):
                    pb = psb(t % 2)
                    K.transpose(pb[:, 0:128], cst_[:, t, kvh * 128:(kvh + 1) * 128], ident.v)
                    K.copy("act", kT[:, kvh, 2560 + t * 128:2560 + (t + 1) * 128], pb[:, 0:128])
            R.top = top0
            scr = norm_scr()
            wq = [R.alloc((128, 8, 128), BF16, "wq%d" % q) for q in range(2)]
            woh = [R.alloc((128, D), BF16, "woh%d" % q) for q in range(2)]
            qT = [R.alloc((128, T), BF16, "qT%d" % q) for q in range(2)]
            oT = [R.alloc((128, T), BF16, "oT%d" % q) for q in range(2)]
            pb_ = [R.alloc((128, 512), BF16, "pt%d" % q) for q in range(4)]
            pbufs_rd[0] = R.alloc((128, 512), F32, "rd0")
            pbufs_rd[1] = R.alloc((128, 512), F32, "rd1")
            for h in range(8):
                kvh = h // 4
                w_ = wq[h % 2]
                K.dma(w_.v, I["gqa_w_in"][j].re("(k p) n -> p k n", p=128)[:, :, h * 128:(h + 1) * 128], eng="pool")
                K.dma(woh[h % 2].v, I["gqa_w_out"][j][h * 128:(h + 1) * 128, :], eng="pool")
                q_ = qT[h % 2]
                for b, (c0, n, cond) in enumerate(TB):
                    pb = psb(6 + b % 2)
                    for kc in range(8):
                        K.mm(pb, w_[:, kc, :], HT[kc][b].v, start=(kc == 0), stop=(kc == 7))
                    rope = None if b == 0 else (cos[:, c0 - 512:c0], sin[:, c0 - 512:c0], permT.v)
                    fm_norm(q_[:, c0:c0 + 512], pb, 128, 512, kg[:, 0:1], 128, scr, rope)
                o_ = oT[h % 2]
                attn_head([(kT[:, kvh, :], 128)], lambda kt, kvh=kvh: V[:, kt, kvh * 128:(kvh + 1) * 128],
                          [q_.v], o_.v, 128 ** -0.5, pb_)
                out_accum(i, [o_[:, c0:c0 + 512] for (c0, n, cond) in TB], woh[h % 2].v)

        def mla_layer(i, j):
            R.reset()
            cqn = R.alloc((128, 3, T), BF16, "cqn")
            ckvn = R.alloc((128, 2, 3072), BF16, "ckvn")
            krT = R.alloc((64, 3072), BF16, "krT")
            cos = R.alloc((64, 2048), BF16, "cosb")
            sin = R.alloc((64, 2048), BF16, "sinb")
            permT = R.alloc((64, 64), BF16, "permTb")
            gl = R.alloc((128, 9), F32, "gl")
            K.dma(cos.v, I["rope_b_cos"].v, eng="pool")
            K.dma(sin.v, I["rope_b_sin"].v, eng="pool")
            K.dma(permT.v, I["rope_b_p"].v, eng="pool")
            gst = R.alloc((9, 128), F32, "gst9")
            K.memset("dve", gst.v, 0.0)
            K.dma(gst[0:3, :], I["mla_q_lat_norm"][j])
            K.dma(gst[3:5, :], I["mla_kv_lat_norm"][j])
            K.dma(gst[5:6, :], I["mla_qn_nope"][j:j + 1, :])
            K.dma(gst[6:7, :], I["mla_kn_nope"][j:j + 1, :])
            K.dma(gst[7:8, 0:64], I["mla_qn_rope"][j:j + 1, :])
            K.dma(gst[8:9, 0:64], I["mla_kn_rope"][j:j + 1, :])
            K.transpose(psb(4)[:, 0:9], gst.v, ident[0:9, 0:9])
            K.copy("dve", gl.v, psb(4)[:, 0:9])
            top0 = R.top
            wd = R.alloc((128, 8, 704), BF16, "wd")
            K.dma(wd.v, I["mla_w_down"][j].re("(k p) n -> p k n", p=128), eng="pool")
            scr = norm_scr()
            sqs = [R.alloc((128, 512), BF16, "msq%d" % q) for q in range(3)]
            rs = R.alloc((128, 512), F32, "mrs")
            f32o = R.alloc((128, 3, 512), F32, "f32o")
            groups = [(0, 3, 0, 384), (384, 2, 3, 256)]
            for b, (c0, n, cond) in enumerate(TB):
                for (col0, nch, gcol, nfeat) in groups:
                    pbs = [psb(q) for q in range(nch)]
                    for c in range(nch):
                        for kc in range(8):
                            K.mm(pbs[c], wd[:, kc, col0 + c * 128:col0 + (c + 1) * 128], HT[kc][b].v,
                                 start=(kc == 0), stop=(kc == 7))
                    pd = psb(5)
                    for c in range(nch):
                        K.act(sqs[c].v, pbs[c], AF.Square)
                        K.mm(pd, onesb.v, sqs[c].v, start=(c == 0), stop=(c == nch - 1))
                    K.act(rs.v, pd, AF.Sqrt, bias=epsv.v, scale=1.0 / nfeat)
                    K.recip(rs.v, rs.v)
                    for c in range(nch):
                        if nch == 3:
                            dst = cqn[:, c, c0:c0 + 512]
                        else:
                            dst = ckvn[:, c, c0:c0 + 512]
                        K.stt("dve", dst, pbs[c], gl[:, gcol + c:gcol + c + 1], rs.v, ALU.mult, ALU.mult)
                        if nch == 2 and b == 0:
                            K.stt("dve", f32o[:, c, :], pbs[c], gl[:, gcol + c:gcol + c + 1], rs.v, ALU.mult, ALU.mult)
                pb = psb(3)
                for kc in range(8):
                    K.mm(pb[0:64, :], wd[:, kc, 640:704], HT[kc][b].v, start=(kc == 0), stop=(kc == 7))
                rope = None if b == 0 else (cos[:, c0 - 512:c0], sin[:, c0 - 512:c0], permT.v)
                fm_norm(krT[:, c0:c0 + 512], pb, 64, 512, gl[0:64, 8:9], 64, scr, rope)
                if b == 0:
                    fm_norm(f32o[0:64, 2, :], pb, 64, 512, gl[0:64, 8:9], 64, scr, None)
            ost = R.alloc((128, 320), F32, "ost")
            for t in range(4):
                pb = psb(t % 2)
                for c in range(2):
                    K.transpose(pb[:, c * 128:(c + 1) * 128], f32o[:, c, t * 128:(t + 1) * 128], ident.v)
                K.transpose(pb[:, 256:320], f32o[0:64, 2, t * 128:(t + 1) * 128], ident[0:64, 0:64])
                K.copy("dve", ost.v, pb[:, 0:320])
                K.dma(O["o_ckv"][t // 2, j, (t % 2) * 128:(t % 2 + 1) * 128, :], ost[:, 0:256])
                K.dma(O["o_kr"][t // 2, j, (t % 2) * 128:(t % 2 + 1) * 128, :], ost[:, 256:320])
            cst_ = R.alloc((128, 4, 320), F32, "mcst")
            K.dma(cst_[:, :, 0:256], I["cache_mla_ckv"][j].re("(t p) n -> p t n", p=128))
            K.dma(cst_[:, :, 256:320], I["cache_mla_krope"][j].re("(t p) n -> p t n", p=128))
            for t in range(4):
                pb = psb(t % 2)
                for c in range(2):
                    K.transpose(pb[:, c * 128:(c + 1) * 128], cst_[:, t, c * 128:(c + 1) * 128], ident.v)
                K.transpose(pb[0:64, 256:384], cst_[:, t, 256:320], ident.v)
                for c in range(2):
                    K.copy("act", ckvn[:, c, 2560 + t * 128:2560 + (t + 1) * 128], pb[:, c * 128:(c + 1) * 128])
                K.copy("dve", krT[:, 2560 + t * 128:2560 + (t + 1) * 128], pb[0:64, 256:384])
            R.top = top0
            scr = norm_scr()
            wuq = [R.alloc((128, 3, 192), BF16, "wuq%d" % q) for q in range(1)]
            wukv = [R.alloc((128, 2, 256), BF16, "wukv%d" % q) for q in range(1)]
            woh = [R.alloc((128, D), BF16, "mwoh%d" % q) for q in range(1)]
            qn = [R.alloc((128, T), BF16, "qn%d" % q) for q in range(1)]
            qr = [R.alloc((64, T), BF16, "qr%d" % q) for q in range(1)]
            kn = [R.alloc((128, 3072), BF16, "kn%d" % q) for q in range(1)]
            Vh = [R.alloc((128, 24, 128), BF16, "Vh%d" % q) for q in range(1)]
            oT = qn
            pb_ = [R.alloc((128, 512), BF16, "mpt%d" % q) for q in range(3)]
            pbufs_rd[0] = R.alloc((128, 512), F32, "mrd0")
            pbufs_rd[1] = pbufs_rd[0]
            for h in range(8):
                p_ = 0
                K.dma(wuq[p_].v, I["mla_w_uq"][j].re("(k p) n -> p k n", p=128)[:, :, h * 192:(h + 1) * 192], eng="pool")
                K.dma(wukv[p_].v, I["mla_w_ukv"][j].re("(k p) n -> p k n", p=128)[:, :, h * 256:(h + 1) * 256], eng="pool")
                K.dma(woh[p_].v, I["mla_w_out"][j][h * 128:(h + 1) * 128, :], eng="pool")
                for cb in range(6):
                    c0 = cb * 512
                    pb = psb(6 + cb % 2)
                    for c in range(2):
                        K.mm(pb, wukv[p_][:, c, 0:128], ckvn[:, c, c0:c0 + 512], start=(c == 0), stop=(c == 1))
                    fm_norm(kn[p_][:, c0:c0 + 512], pb, 128, 512, gl[:, 6:7], 128, scr, None)
                for g in range(6):
                    pb = psb(6 + g % 2)
                    for q in range(4):
                        t = g * 4 + q
                        for c in range(2):
                            K.mm(pb[:, q * 128:(q + 1) * 128], ckvn[:, c, t * 128:(t + 1) * 128], wukv[p_][:, c, 128:256],
                                 start=(c == 0), stop=(c == 1))
                    K.copy("act", Vh[p_][:, g * 4:(g + 1) * 4, :], pb.re("p (q d) -> p q d", q=4))
                for b, (c0, n, cond) in enumerate(TB):
                    pb = psb(6 + b % 2)
                    for c in range(3):
                        K.mm(pb, wuq[p_][:, c, 0:128], cqn[:, c, c0:c0 + 512], start=(c == 0), stop=(c == 2))
                    fm_norm(qn[p_][:, c0:c0 + 512], pb, 128, 512, gl[:, 5:6], 128, scr, None)
                    pb2 = psb(7 - b % 2)
                    for c in range(3):
                        K.mm(pb2[0:64, :], wuq[p_][:, c, 128:192], cqn[:, c, c0:c0 + 512], start=(c == 0), stop=(c == 2))
                    rope = None if b == 0 else (cos[:, c0 - 512:c0], sin[:, c0 - 512:c0], permT.v)
                    fm_norm(qr[p_][:, c0:c0 + 512], pb2, 64, 512, gl[0:64, 7:8], 64, scr, rope)
                attn_head([(kn[p_].v, 128), (krT.v, 64)], lambda kt, p_=p_: Vh[p_][:, kt, :],
                          [qn[p_].v, qr[p_].v], oT[p_].v, 192 ** -0.5, pb_)
                out_accum(i, [oT[p_][:, c0:c0 + 512] for (c0, n, cond) in TB], woh[p_].v)


        def gdn_layer(i, j):
            R.reset()
            NEG = -30000.0
            msk = R.alloc((64, 6, 64), F32, "msk")
            K.dma(msk.v, I["gdn_masks"][0:6, 0:64, 0:64].re("m p n -> p m n"))
            U = [msk[:, 0, :], msk[:, 1, :]]
            NEGS = [msk[:, 2, :], msk[:, 4, :]]
            NEGI = [msk[:, 3, :], msk[:, 5, :]]
            I64 = ident[0:64, 0:64]
            eps2 = R.alloc((128, 1), F32, "eps2")
            K.memset("dve", eps2.v, EPS / 128.0)
            gq = R.alloc((128, 3), F32, "gq")
            K.memset("dve", gq[:, 0:1], 1.0 / 128.0)
            K.memset("dve", gq[:, 1:2], 128.0 ** -0.5)
            cst_ = R.alloc((73, 128), F32, "gcst")
            K.dma(cst_[0:72, :], I["gdn_conv"][j])
            K.dma(cst_[72:73, :], I["gdn_out_norm"][j:j + 1, :])
            cw = R.alloc((128, 73), F32, "cw")
            K.transpose(psb(4)[:, 0:73], cst_.v, ident[0:73, 0:73])
            K.copy("dve", cw.v, psb(4)[:, 0:73])
            alb = R.alloc((64, 16), F32, "alb")
            dtb = R.alloc((64, 16), F32, "dtb")
            K.dma(alb.v, View(I["gdn_a_log"], I["gdn_a_log"].ap[j].partition_broadcast(64)))
            K.dma(dtb.v, View(I["gdn_dt_bias"], I["gdn_dt_bias"].ap[j].partition_broadcast(64)))
            K.act(alb.v, alb.v, AF.Exp)
            K.ts("dve", alb.v, alb.v, -1.0, ALU.mult)
            gcum = R.alloc((64, 40, 16), F32, "gcum")
            eg = R.alloc((64, 40, 16), F32, "eg")
            ekd = R.alloc((64, 40, 16), F32, "ekd")
            sA = R.alloc((64, 40, 16), F32, "sA")
            beta = R.alloc((64, 40, 16), F32, "beta")
            egt = R.alloc((128, 40, 16), F32, "egt")
            top_g = R.top
            wgb = R.alloc((128, 8, 32), BF16, "wgb")
            K.dma(wgb.v, I["gdn_w_gb"][j].re("(k p) n -> p k n", p=128), eng="pool")
            G = R.alloc((64, 40, 32), F32, "G")
            g_ = R.alloc((64, 40, 16), F32, "g_")
            t1 = R.alloc((64, 40, 16), F32, "gt1")
            t2 = R.alloc((64, 40, 16), F32, "gt2")
            for grp in range(3):
                c_lo, c_hi = grp * 16, min(40, grp * 16 + 16)
                pb = psb(grp % 2)
                for c in range(c_lo, c_hi):
                    b, off = divmod(c * 64, 512)
                    for kc in range(8):
                        K.mm(pb[0:64, (c - c_lo) * 32:(c - c_lo + 1) * 32], HT[kc][b][:, off:off + 64], wgb[:, kc, :],
                             start=(kc == 0), stop=(kc == 7))
                K.copy("dve", G[:, c_lo:c_hi, :], pb[0:64, 0:(c_hi - c_lo) * 32].re("p (c n) -> p c n", n=32))

            def softplus(dst, x):
                K.act(t1.v, x, AF.Abs)
                K.act(t1.v, t1.v, AF.Exp, scale=-1.0)
                K.act(t1.v, t1.v, AF.Ln, bias=1.0)
                K.ts("dve", t2.v, x, 0.0, ALU.max)
                K.tt("dve", dst, t1.v, t2.v, ALU.add)
            K.tt("dve", g_.v, G[:, :, 0:16], dtb.v.un(1).bc([64, 40, 16]), ALU.add)
            softplus(g_.v, g_.v)
            K.tt("dve", g_.v, g_.v, alb.v.un(1).bc([64, 40, 16]), ALU.mult)
            K.ts("dve", sA.v, G[:, :, 16:32], -1.0, ALU.mult)
            softplus(sA.v, sA.v)
            K.ts("dve", sA.v, sA.v, -1.0, ALU.mult)
            K.act(beta.v, sA.v, AF.Exp)
            for d in range(2):
                pb = psb(2 + d)
                K.mm(pb[0:64, 0:320].re("p (c h) -> p c h", h=8), U[d], g_[:, :, d * 8:(d + 1) * 8])
                K.copy("dve", gcum[:, :, d * 8:(d + 1) * 8], pb[0:64, 0:320].re("p (c h) -> p c h", h=8))
            pbt = [psb(4), psb(5)]
            K.mm(pbt[0][:, 0:320], onesf[0:64, 0:128], g_[:, 0:20, :])
            K.mm(pbt[1][:, 0:320], onesf[0:64, 0:128], g_[:, 20:40, :])
            for q in range(2):
                K.act(egt[:, q * 20:(q + 1) * 20, :], pbt[q][:, 0:320].re("p (c h) -> p c h", h=16), AF.Exp)
                K.tt("dve", ekd[:, q * 20:(q + 1) * 20, :], pbt[q][0:64, 0:320].re("p (c h) -> p c h", h=16),
                     gcum[:, q * 20:(q + 1) * 20, :], ALU.subtract)
            K.act(ekd.v, ekd.v, AF.Exp)
            K.act(eg.v, gcum.v, AF.Exp)
            K.tt("dve", sA.v, sA.v, gcum.v, ALU.subtract)
            R.top = top_g
            qT = R.alloc((128, T), BF16, "gqT")
            kT = R.alloc((128, T), BF16, "gkT")
            k_c = R.alloc((64, 40, 128), BF16, "k_c")
            v_c = R.alloc((64, 40, 128), BF16, "v_c")
            oacc = R.alloc((128, T), F32, "oacc")
            S = R.alloc((128, 128), F32, "S")
            Sb = R.alloc((128, 128), BF16, "Sb")
            top_h = R.top
            SEQ = [(0, 4, 0), (4, 8, 1), (8, 40, 2)]
            for h in range(8):
                R.top = top_h
                praw = R.alloc((128, T), F32, "praw")
                cv = R.alloc((128, T), F32, "cv")
                win = R.alloc((128, 8, 128), BF16, "gwin")
                scr = norm_scr(light=True)
                vT = None
                for xi in range(3):
                    K.dma(win.v, I["gdn_w_in"][j].re("(k p) n -> p k n", p=128)[:, :, xi * 1024 + h * 128:xi * 1024 + (h + 1) * 128],
                          eng="pool")
                    for b, (c0, n, cond) in enumerate(TB):
                        pb = psb(b % 2)
                        for kc in range(8):
                            K.mm(pb, win[:, kc, :], HT[kc][b].v, start=(kc == 0), stop=(kc == 7))
                        K.copy("act" if b % 2 else "dve", praw[:, c0:c0 + 512], pb)
                    ch = xi * 8 + h
                    for (s0, s1) in [(0, 256), (256, 512), (512, 2560)]:
                        K.ts("dve", cv[:, s0:s1], praw[:, s0:s1], cw[:, 24 + ch:25 + ch], ALU.mult)
                        K.stt("dve", cv[:, s0 + 1:s1], praw[:, s0:s1 - 1], cw[:, ch:ch + 1], cv[:, s0 + 1:s1], ALU.mult, ALU.add)
                        K.stt("dve", cv[:, s0:s1 - 1], praw[:, s0 + 1:s1], cw[:, 48 + ch:49 + ch], cv[:, s0:s1 - 1], ALU.mult, ALU.add)
                    K.act(cv.v, cv.v, AF.Silu)
                    if xi < 2:
                        dstT = qT if xi == 0 else kT
                        for b, (c0, n, cond) in enumerate(TB):
                            fm_norm(dstT[:, c0:c0 + 512], cv[:, c0:c0 + 512], 128, 512, gq[:, xi:xi + 1], 128, scr, None, eps_v=eps2)
                    else:
                        sv_top = R.top
                        R.top = top_h
                        vT = R.alloc((128, T), BF16, "vT")
                        R.top = sv_top
                        K.copy("pool", vT.v, cv.v)
                for (srcT, dstc) in ((kT, k_c), (vT, v_c)):
                    for b in range(5):
                        pbb = psb(2 + b % 2, BF16)
                        for c in range(8):
                            K.transpose(pbb[0:64, c * 128:(c + 1) * 128], srcT[:, b * 512 + c * 64:b * 512 + (c + 1) * 64], identb.v)
                        K.copy("act" if b % 2 else "dve", dstc[:, b * 8:(b + 1) * 8, :], pbb[0:64, :].re("p (c d) -> p c d", d=128))
                R.top = top_h
                scrs = [R.alloc((64, 8, 64), F32, "gs%d" % q) for q in range(7)]
                kgn = R.alloc((64, 8, 128), BF16, "kgn")
                kdec = R.alloc((64, 8, 128), BF16, "kdec")
                erow = R.alloc((128, 512), F32, "erow")
                qdT = R.alloc((128, 8, 64), BF16, "qdT")
                QKt = R.alloc((64, 8, 64), BF16, "QKt")
                Tt = R.alloc((64, 8, 64), BF16, "Tt")
                WTn = R.alloc((128, 8, 64), BF16, "WTn")
                vnew = R.alloc((64, 128), BF16, "vnew")
                for d in range(2):
                    dh = d * 8 + h
                    border = [0, 1, 2, 3, 4] if d == 0 else [0, 4, 3, 2, 1]
                    for b in border:
                        cb0 = b * 8
                        gc_b = gcum[:, cb0:cb0 + 8, dh]
                        Dg, RBA, RBQ, EA, EQ, At, M = scrs
                        K.tt("pool", Dg.v, View(ident, I64.ap.unsqueeze(1).broadcast_to([64, 8, 64])),
                             gc_b.un(2).bc([64, 8, 64]), ALU.mult)
                        K.tt("dve", RBA.v, View(msk, NEGS[d].ap.unsqueeze(1).broadcast_to([64, 8, 64])),
                             sA[:, cb0:cb0 + 8, dh].un(2).bc([64, 8, 64]), ALU.add)
                        K.tt("pool", RBQ.v, View(msk, NEGI[d].ap.unsqueeze(1).broadcast_to([64, 8, 64])),
                             gc_b.un(2).bc([64, 8, 64]), ALU.subtract)
                        f2 = lambda v_: v_.re("p c i -> p (c i)")
                        pA, pQ, pE, pK = psb(0), psb(1), psb(2), psb(3)
                        K.mm(pA[0:64, :], onesf[0:64, 0:64], f2(Dg.v), start=True, stop=False)
                        K.mm(pA[0:64, :], I64, f2(RBA.v), start=False, stop=True)
                        K.mm(pQ[0:64, :], onesf[0:64, 0:64], f2(Dg.v), start=True, stop=False)
                        K.mm(pQ[0:64, :], I64, f2(RBQ.v), start=False, stop=True)
                        K.mm(pE, onesf[0:64, 0:128], f2(Dg.v))
                        K.act(f2(EA.v), pA[0:64, :], AF.Exp)
                        K.act(f2(EQ.v), pQ[0:64, :], AF.Exp)
                        K.act(erow.v, pE, AF.Exp)
                        K.tt("dve", qdT.v.re("p c i -> p (c i)"), qT[:, b * 512:(b + 1) * 512], erow.v, ALU.mult)
                        for c in range(8):
                            cs = slice(b * 512 + c * 64, b * 512 + (c + 1) * 64)
                            K.mm(pK[0:64, c * 64:(c + 1) * 64], kT[:, cs], kT[:, cs])
                        K.tt("dve", f2(At.v), pK[0:64, :], f2(EA.v), ALU.mult)
                        pK2 = psb(4)
                        for c in range(8):
                            cs = slice(b * 512 + c * 64, b * 512 + (c + 1) * 64)
                            K.mm(pK2[0:64, c * 64:(c + 1) * 64], kT[:, cs], qT[:, cs])
                        K.tt("dve", f2(QKt.v), pK2[0:64, :], f2(EQ.v), ALU.mult)
                        Xa = EQ
                        pX = psb(0)
                        for c in range(8):
                            K.transpose(pX[0:64, c * 64:(c + 1) * 64], At[:, c, :], I64)
                        K.copy("act", f2(Xa.v), pX[0:64, :])
                        K.tt("dve", M.v, View(ident, I64.ap.unsqueeze(1).broadcast_to([64, 8, 64])), At.v, ALU.subtract)
                        Pa, Pt = Xa, At
                        for lvl in range(1, 6):
                            p1, p2, p3 = psb(1), psb(2), psb(3)
                            for c in range(8):
                                K.mm(p1[0:64, c * 64:(c + 1) * 64], Pt[:, c, :], Pa[:, c, :])
                            if lvl < 5:
                                for c in range(8):
                                    K.mm(p2[0:64, c * 64:(c + 1) * 64], Pa[:, c, :], Pt[:, c, :])
                            Pa2 = Dg if lvl % 2 == 1 else RBQ
                            K.copy("act", f2(Pa2.v), p1[0:64, :])
                            if lvl < 5:
                                Pt2 = RBA if lvl % 2 == 1 else EA
                                K.copy("dve", f2(Pt2.v), p2[0:64, :])
                            for c in range(8):
                                K.mm(p3[0:64, c * 64:(c + 1) * 64], Pa2[:, c, :], M[:, c, :])
                            K.tt("dve", f2(M.v), p3[0:64, :], f2(M.v), ALU.add)
                            Pa = Pa2
                            if lvl < 5:
                                Pt = Pt2
                        K.copy("act", Tt.v, M.v)
                        K.tt("pool", kgn.v, k_c[:, cb0:cb0 + 8, :], eg[:, cb0:cb0 + 8, dh].un(2).bc([64, 8, 128]), ALU.mult)
                        K.ts("pool", kgn.v, kgn.v, -1.0, ALU.mult)
                        K.tt("pool", kdec.v, k_c[:, cb0:cb0 + 8, :], ekd[:, cb0:cb0 + 8, dh].un(2).bc([64, 8, 128]), ALU.mult)
                        pW = psb(5)
                        for c in range(8):
                            K.mm(pW[:, c * 64:(c + 1) * 64], kgn[:, c, :], Tt[:, c, :])
                        K.copy("act", WTn.v.re("p c i -> p (c i)"), pW)
                        pO = psb(6)
                        corder = list(range(8)) if d == 0 else list(range(7, -1, -1))
                        for c in corder:
                            cg = cb0 + c
                            seq = 0 if cg < 4 else (1 if cg < 8 else 2)
                            first = (cg == SEQ[seq][0]) if d == 0 else (cg == SEQ[seq][1] - 1)
                            last = (cg == SEQ[seq][1] - 1) if d == 0 else (cg == SEQ[seq][0])
                            if first:
                                if seq < 2:
                                    K.memset("dve", S.v, 0.0)
                                else:
                                    K.dma(S.v, (I["state_f"] if d == 0 else I["state_b"])[j, h])
                                K.copy("act", Sb.v, S.v)
                            pv = psb(7)
                            K.mm(pv[0:64, 0:128], Tt[:, c, :], v_c[:, cg, :], start=True, stop=False)
                            K.mm(pv[0:64, 0:128], WTn[:, c, :], Sb.v, start=False, stop=True)
                            K.ts("dve", vnew.v, pv[0:64, 0:128], beta[:, cg, dh:dh + 1], ALU.mult)
                            K.mm(pO[:, c * 64:(c + 1) * 64], Sb.v, qdT[:, c, :], start=True, stop=False)
                            K.mm(pO[:, c * 64:(c + 1) * 64], vnew.v, QKt[:, c, :], start=False, stop=True)
                            K.mm(pv[:, 128:256], kdec[:, c, :], vnew.v)
                            K.stt("dve", S.v, S.v, egt[:, cg, dh:dh + 1], pv[:, 128:256], ALU.mult, ALU.add)
                            if last and seq < 2:
                                K.dma((O["o_sf"] if d == 0 else O["o_sb"])[seq, j, h], S.v)
                            if not last:
                                K.copy("act", Sb.v, S.v)
                        if d == 0:
                            K.copy("act", oacc[:, b * 512:(b + 1) * 512], pO)
                        else:
                            K.tt("dve", oacc[:, b * 512:(b + 1) * 512], pO, oacc[:, b * 512:(b + 1) * 512], ALU.add)
                R.top = top_h
                scr = norm_scr(light=True)
                wz = R.alloc((128, 8, 128), BF16, "wz")
                woh = R.alloc((128, D), BF16, "gwoh")
                og = R.alloc((128, T), BF16, "og")
                zs = R.alloc((128, 512), F32, "zs")
                on = R.alloc((128, 512), F32, "on")
                K.dma(wz.v, I["gdn_w_in"][j].re("(k p) n -> p k n", p=128)[:, :, 3072 + h * 128:3072 + (h + 1) * 128], eng="pool")
                K.dma(woh.v, I["gdn_w_out"][j][h * 128:(h + 1) * 128, :], eng="pool")
                for b, (c0, n, cond) in enumerate(TB):
                    pb = psb(b % 2)
                    for kc in range(8):
                        K.mm(pb, wz[:, kc, :], HT[kc][b].v, start=(kc == 0), stop=(kc == 7))
                    K.act(zs.v, pb, AF.Silu)
                    fm_norm(on.v, oacc[:, c0:c0 + 512], 128, 512, cw[:, 72:73], 128, scr, None)
                    K.tt("dve", og[:, c0:c0 + 512], on.v, zs.v, ALU.mult)
                out_accum(i, [og[:, c0:c0 + 512] for (c0, n, cond) in TB], woh.v)

        if depth:
            load_gains(0)
            run(adaln(0))
        cnt = {0: 0, 1: 0, 2: 0}
        for i, kind in enumerate(kinds):
            j = cnt[kind]
            cnt[kind] += 1
            R.reset()
            make_coef(i)
            mod_norm(i, 0)
            if kind == 2:
                gqa_layer(i, j)
            elif kind == 1:
                mla_layer(i, j)
            elif kind == 0:
                gdn_layer(i, j)
            R.reset()
            mod_norm(i, 1)
            bg = None
            if i + 1 < depth:
                def bgf(i=i):
                    yield from adaln(i + 1)
                bg = bgf()
            if do_mlp:
                mlp(i, bg)
            elif bg is not None:
                run(bg)
            if i + 1 < depth:
                load_gains(i + 1)

        R.reset()
        ost = [R.alloc((128, D), F32, "xo%d" % q) for q in range(2)]
        for t in range(int(os.environ.get("KNT_OUT", str(NT_DBG)))):
            b, off = divmod(t * 128, 512)
            o_ = ost[(t + int(os.environ.get("KSL", "0"))) % 2]
            for g in range(2):
                pb = psb((t * 2 + g + int(os.environ.get("KOFF", "0"))) % int(os.environ.get("KNB", "4")))
                for q in range(4):
                    kc = g * 4 + q
                    K.transpose(pb[:, q * 128:(q + 1) * 128], XT[kc][b][:, off:off + 128], ident.v)
                K.copy("dve" if g == 0 else "act", o_[:, g * 512:(g + 1) * 512], pb)
            dst = O["yp"][t * 128:(t + 1) * 128, :] if t < 4 else O["ys"][(t - 4) * 128:(t - 3) * 128, :]
            K.dma(dst, o_.v)
        K.emit()
    return nc


def make_in_maps(inp, kinds=KINDS_FULL):
    f = lambda a: np.ascontiguousarray(np.asarray(a, dtype=np.float32))
    depth = max(len(kinds), 1)
    n_a = sum(1 for k in kinds if k == 0)
    n_b = sum(1 for k in kinds if k == 1)
    n_c = sum(1 for k in kinds if k == 2)
    shared = {
        "norm_mix": f(inp["norm_mix"])[:depth].reshape(depth, 8, 128),
        "norm_mlp": f(inp["norm_mlp"])[:depth].reshape(depth, 8, 128),
        "w_mod": f(inp["w_mod"])[:depth],
        "b_mod": f(inp["b_mod"])[:depth].reshape(depth, 48, 128),
        "w_mlp_in": f(inp["w_mlp_in"])[:depth],
        "w_mlp_out": f(inp["w_mlp_out"])[:depth],
        "ident": np.eye(128, dtype=np.float32),
        "ones": np.ones((128, 128), np.float32),
    }
    if n_c:
        cos, sin, pT = rope_tables(32, 2048)
        shared.update({
            "gqa_w_in": f(inp["gqa_w_in"])[:n_c], "gqa_q_norm": f(inp["gqa_q_norm"])[:n_c],
            "gqa_k_norm": f(inp["gqa_k_norm"])[:n_c], "gqa_w_out": f(inp["gqa_w_out"])[:n_c],
            "rope_c_cos": cos, "rope_c_sin": sin, "rope_c_p": pT})
    if n_b:
        cos, sin, pT = rope_tables(16, 2048)
        shared.update({
            "mla_w_down": f(inp["mla_w_down"])[:n_b],
            "mla_q_lat_norm": f(inp["mla_q_lat_norm"])[:n_b].reshape(n_b, 3, 128),
            "mla_kv_lat_norm": f(inp["mla_kv_lat_norm"])[:n_b].reshape(n_b, 2, 128),
            "mla_w_uq": f(inp["mla_w_uq"])[:n_b], "mla_w_ukv": f(inp["mla_w_ukv"])[:n_b],
            "mla_qn_nope": f(inp["mla_qn_nope"])[:n_b], "mla_qn_rope": f(inp["mla_qn_rope"])[:n_b],
            "mla_kn_nope": f(inp["mla_kn_nope"])[:n_b], "mla_kn_rope": f(inp["mla_kn_rope"])[:n_b],
            "mla_w_out": f(inp["mla_w_out"])[:n_b],
            "rope_b_cos": cos, "rope_b_sin": sin, "rope_b_p": pT})
    if n_a:
        shared.update(gdn_host_inputs(inp, n_a))
    maps = []
    xp = f(inp["x_prompt"])
    xs = f(inp["x_sample"])
    for c in range(NCORES):
        m = dict(shared)
        m["xp"] = xp[2 * c:2 * c + 2].reshape(512, D)
        m["xs"] = xs[c]
        m["cond"] = np.stack([f(inp["c_ctx"]).reshape(8, 128), f(inp["c"])[c].reshape(8, 128)], 0)
        if n_c:
            m["cache_gqa_k"] = f(inp["cache_gqa_k"])[c, :n_c].reshape(n_c, 512, 256)
            m["cache_gqa_v"] = f(inp["cache_gqa_v"])[c, :n_c].reshape(n_c, 512, 256)
        if n_b:
            m["cache_mla_ckv"] = f(inp["cache_mla_ckv"])[c, :n_b]
            m["cache_mla_krope"] = f(inp["cache_mla_krope"])[c, :n_b]
        if n_a:
            m["state_f"] = f(inp["state_gdn_fwd"])[c, :n_a]
            m["state_b"] = f(inp["state_gdn_bwd"])[c, :n_a]
        maps.append(m)
    return maps


def gdn_host_inputs(inp, n_a):
    f = lambda a: np.ascontiguousarray(np.asarray(a, dtype=np.float32))
    w = f(inp["gdn_w_in"])[:n_a]
    idx = np.arange(64)
    NEG = -30000.0
    m = np.zeros((8, 128, 128), np.float32)
    jj, ii = np.meshgrid(idx, idx, indexing="ij")
    m[0, :64, :64] = (jj <= ii)
    m[1, :64, :64] = (jj >= ii)
    m[2, :64, :64] = np.where(ii > jj, 0.0, NEG)
    m[3, :64, :64] = np.where(ii >= jj, 0.0, NEG)
    m[4, :64, :64] = np.where(ii < jj, 0.0, NEG)
    m[5, :64, :64] = np.where(ii <= jj, 0.0, NEG)
    return {
        "gdn_w_in": np.ascontiguousarray(w[:, :, :4096]),
        "gdn_w_gb": np.ascontiguousarray(w[:, :, 4096:4128]),
        "gdn_conv": f(inp["gdn_conv"])[:n_a].reshape(n_a, 72, 128),
        "gdn_a_log": f(inp["gdn_a_log"])[:n_a].reshape(n_a, 16),
        "gdn_dt_bias": f(inp["gdn_dt_bias"])[:n_a].reshape(n_a, 16),
        "gdn_out_norm": f(inp["gdn_out_norm"])[:n_a],
        "gdn_w_out": f(inp["gdn_w_out"])[:n_a],
        "gdn_masks": m,
    }


def assemble(results, kinds=KINDS_FULL):
    n_a = sum(1 for k in kinds if k == 0)
    n_b = sum(1 for k in kinds if k == 1)
    n_c = sum(1 for k in kinds if k == 2)
    yp = np.concatenate([r["yp"].reshape(2, 256, D) for r in results], 0)
    ys = np.stack([r["ys"] for r in results], 0)
    sf = np.concatenate([r["o_sf"][:, :n_a] for r in results], 0)
    sbw = np.concatenate([r["o_sb"][:, :n_a] for r in results], 0)
    ckv = np.concatenate([r["o_ckv"][:, :n_b] for r in results], 0)
    kr = np.concatenate([r["o_kr"][:, :n_b] for r in results], 0)
    gk = np.concatenate([r["o_gk"][:, :n_c].reshape(2, n_c, 256, 2, 128) for r in results], 0)
    gv = np.concatenate([r["o_gv"][:, :n_c].reshape(2, n_c, 256, 2, 128) for r in results], 0)
    return (yp, ys, sf, sbw, ckv, kr, gk, gv)


_NC_CACHE = {}


def kernel(**inputs):
    kinds = KINDS_FULL
    if kinds not in _NC_CACHE:
        _NC_CACHE[kinds] = build(kinds)
    nc = _NC_CACHE[kinds]
    maps = make_in_maps(inputs, kinds)
    res = run_bass_kernel_spmd(nc, maps, core_ids=list(range(NCORES)))
    return assemble(res.results, kinds)
```

```python
import contextlib
import math
import numpy as np
import concourse.bass as bass
import concourse.mybir as mybir
from concourse.bass_utils import run_bass_kernel_spmd

F32 = mybir.dt.float32
BF16 = mybir.dt.bfloat16
AF = mybir.ActivationFunctionType
ALU = mybir.AluOpType
AX = mybir.AxisListType

D = 1024
NKC = 8
T = 2560
NCORES = 8
EPS = 1e-6
TB = [(0, 512, 0), (512, 512, 1), (1024, 512, 1), (1536, 512, 1), (2048, 512, 1)]
KINDS_FULL = (0, 1, 2, 0)


class Buf:
    __slots__ = ("ap", "name", "last_write", "reads", "excl")

    def __init__(self, ap, name="", excl=False):
        self.ap = ap
        self.name = name
        self.last_write = None
        self.reads = []
        self.excl = excl

    def __getitem__(self, idx):
        return View(self, self.ap[idx])

    @property
    def v(self):
        return View(self, self.ap)


class View:
    __slots__ = ("buf", "ap")

    def __init__(self, buf, ap):
        self.buf = buf
        self.ap = ap

    def __getitem__(self, idx):
        return View(self.buf, self.ap[idx])

    def re(self, pat, **kw):
        return View(self.buf, self.ap.rearrange(pat, **kw))

    def bc(self, shape):
        return View(self.buf, self.ap.broadcast_to(shape))

    def un(self, axis):
        return View(self.buf, self.ap.unsqueeze(axis))


class Op:
    __slots__ = ("eng", "fn", "deps", "flag", "ticket", "dma", "dsem", "dval", "dprev", "idx")

    def __init__(self, eng, fn):
        self.eng = eng
        self.fn = fn
        self.deps = []
        self.flag = False
        self.ticket = 0
        self.dma = False
        self.dsem = None
        self.dval = 0
        self.dprev = None
        self.idx = 0


ENGS = ("pe", "act", "dve", "pool", "sp")
import os
DUMP = os.environ.get("KDUMP", "") == "1"


def compress(ops):
    best = {}
    for o in ops:
        key = ("d", o.dsem) if o.dma else ("e", o.eng)
        b = best.get(key)
        if b is None or o.idx > b.idx:
            best[key] = o
    return list(best.values())


class Sched:
    def __init__(self, nc, n_dma_sems=32):
        self.nc = nc
        self.ops = {e: [] for e in ENGS}
        self.n_dma_sems = n_dma_sems
        self.dma_count = 0
        self.dma_last = [None] * n_dma_sems
        self.dma_vals = [0] * n_dma_sems
        self.nops = 0

    def op(self, eng, fn, reads=(), writes=(), dma=False):
        o = Op(eng, fn)
        o.idx = self.nops
        self.nops += 1
        deps = {}
        for r in reads:
            b = r.buf if isinstance(r, View) else r
            if b.last_write is not None:
                deps[id(b.last_write)] = (b.last_write, True)
            if b.excl:
                for rd in b.reads:
                    if rd.eng != eng and id(rd) not in deps:
                        deps[id(rd)] = (rd, False)
        for w in writes:
            b = w.buf if isinstance(w, View) else w
            if b.last_write is not None and id(b.last_write) not in deps:
                deps[id(b.last_write)] = (b.last_write, False)
            for rd in b.reads:
                if id(rd) not in deps:
                    deps[id(rd)] = (rd, False)
        for d, strong in deps.values():
            if d is o:
                continue
            if d.eng == eng and not d.dma and not dma:
                if eng == "pe" or not strong:
                    continue
            o.deps.append(d)
            d.flag = True
        for w in writes:
            b = w.buf if isinstance(w, View) else w
            b.last_write = o
            b.reads = []
        for r in reads:
            b = r.buf if isinstance(r, View) else r
            if b.last_write is not o:
                b.reads.append(o)
                if len(b.reads) > 24:
                    b.reads = compress(b.reads)
        if dma:
            o.dma = True
            s = self.dma_count % self.n_dma_sems
            self.dma_count += 1
            o.dsem = s
            self.dma_vals[s] += 16
            o.dval = self.dma_vals[s]
            o.dprev = self.dma_last[s]
            self.dma_last[s] = o
        self.ops[eng].append(o)
        return o

    def mm(self, out, lhsT, rhs, start=True, stop=True):
        return self.op("pe", lambda e: e.matmul(out.ap, lhsT.ap, rhs.ap, start=start, stop=stop),
                       reads=[lhsT, rhs] + ([] if start else [out]), writes=[out])

    def transpose(self, out, in_, ident):
        return self.op("pe", lambda e: e.transpose(out.ap, in_.ap, ident.ap), reads=[in_, ident], writes=[out])

    def act(self, out, in_, func, bias=None, scale=None, accum=None):
        kw = {}
        reads = [in_]
        writes = [out]
        if bias is not None:
            if isinstance(bias, View):
                kw["bias"] = bias.ap
                reads.append(bias)
            else:
                kw["bias"] = bias
        if scale is not None:
            if isinstance(scale, View):
                kw["scale"] = scale.ap
                reads.append(scale)
            else:
                kw["scale"] = scale
        if accum is not None:
            kw["accum_out"] = accum.ap
            writes.append(accum)
        return self.op("act", lambda e: e.activation(out.ap, in_.ap, func, **kw), reads=reads, writes=writes)

    def tt(self, eng, out, a, b, op):
        return self.op(eng, lambda e: e.tensor_tensor(out.ap, a.ap, b.ap, op), reads=[a, b], writes=[out])

    def ts(self, eng, out, a, s1, op0, s2=None, op1=None):
        reads = [a]
        s1a = s1.ap if isinstance(s1, View) else s1
        s2a = s2.ap if isinstance(s2, View) else s2
        if isinstance(s1, View):
            reads.append(s1)
        if isinstance(s2, View):
            reads.append(s2)
        kw = {}
        if op1 is not None:
            kw["op1"] = op1
        return self.op(eng, lambda e: e.tensor_scalar(out.ap, a.ap, s1a, s2a, op0, **kw), reads=reads, writes=[out])

    def stt(self, eng, out, a, s, b, op0, op1):
        reads = [a, b]
        sa = s.ap if isinstance(s, View) else s
        if isinstance(s, View):
            reads.append(s)
        return self.op(eng, lambda e: e.scalar_tensor_tensor(out.ap, a.ap, sa, b.ap, op0, op1),
                       reads=reads, writes=[out])

    def copy(self, eng, out, in_):
        if eng == "act":
            return self.op(eng, lambda e: e.copy(out.ap, in_.ap), reads=[in_], writes=[out])
        return self.op(eng, lambda e: e.tensor_copy(out.ap, in_.ap), reads=[in_], writes=[out])

    def recip(self, out, in_):
        return self.op("dve", lambda e: e.reciprocal(out.ap, in_.ap), reads=[in_], writes=[out])

    def reduce(self, eng, out, in_, op=None):
        return self.op(eng, lambda e: e.tensor_reduce(out.ap, in_.ap, AX.X, op or ALU.add), reads=[in_], writes=[out])

    def memset(self, eng, out, val):
        return self.op(eng, lambda e: e.memset(out.ap, val), writes=[out])

    def dma(self, out, in_, eng="sp"):
        return self.op(eng, lambda e: e.dma_start(out=out.ap, in_=in_.ap), reads=[in_], writes=[out], dma=True)

    def emit(self):
        nc = self.nc
        for e in ENGS:
            c = 0
            for o in self.ops[e]:
                if o.dma:
                    continue
                if o.flag:
                    c += 1
                    o.ticket = c
        with contextlib.ExitStack() as st:
            esems = {e: st.enter_context(nc.semaphore("s_" + e)) for e in ENGS}
            dsems = [st.enter_context(nc.semaphore("d%d" % i)) for i in range(self.n_dma_sems)]
            block = st.enter_context(nc.Block())
            sched = self

            def make(ename):
                def body(eng):
                    waited = {}
                    for o in sched.ops[ename]:
                        ws = []
                        for d in o.deps:
                            if d.dma:
                                ws.append((("d", d.dsem), dsems[d.dsem], d.dval))
                            else:
                                ws.append((("e", d.eng), esems[d.eng], d.ticket))
                        if o.dma and o.dprev is not None:
                            d = o.dprev
                            ws.append((("d", d.dsem), dsems[d.dsem], d.dval))
                        for key, sem, val in ws:
                            if waited.get(key, 0) >= val:
                                continue
                            waited[key] = val
                            eng.wait_ge(sem, val)
                            if DUMP:
                                print("   ", ename, "wait", key, val)
                        ins = o.fn(eng)
                        if DUMP:
                            print(ename, o.idx, "dma" if o.dma else "", ("inc d%d->%d" % (o.dsem, o.dval)) if o.dma else ("inc e->%d" % o.ticket if o.flag else ""), str(ins)[:150])
                        if o.dma:
                            ins.then_inc(dsems[o.dsem], 16)
                        elif o.flag:
                            ins.then_inc(esems[ename], 1)
                    if ename == "sp":
                        for s in range(sched.n_dma_sems):
                            if sched.dma_vals[s] > 0 and waited.get(("d", s), 0) < sched.dma_vals[s]:
                                eng.wait_ge(dsems[s], sched.dma_vals[s])
                return body

            block.tensor(make("pe"))
            block.scalar(make("act"))
            block.vector(make("dve"))
            block.gpsimd(make("pool"))
            block.sync(make("sp"))


class Region:
    def __init__(self, ap_f32, nbytes):
        self.ap = ap_f32
        self.nbytes = nbytes
        self.live = []
        self.top = 0

    def reset(self):
        self.top = 0

    def alloc(self, shape, dt, name=""):
        esz = 2 if dt == BF16 else 4
        free = int(np.prod(shape[1:]))
        nb = (free * esz + 31) // 32 * 32
        s = self.top
        e = s + nb
        assert e <= self.nbytes, (name, e, self.nbytes)
        self.top = e
        ap = self.ap[0:shape[0], s // 4:e // 4]
        if dt == BF16:
            ap = ap.bitcast(BF16)
        ap = ap[:, 0:free]
        if len(shape) == 3:
            ap = ap.rearrange("p (a b) -> p a b", a=shape[1])
        elif len(shape) == 4:
            ap = ap.rearrange("p (a b c) -> p a b c", a=shape[1], b=shape[2])
        b = Buf(ap, name)
        inherit = []
        keep = []
        for (ps, pe, pb) in self.live:
            if ps < e and s < pe:
                inherit.extend(pb.reads)
                if pb.last_write is not None:
                    inherit.append(pb.last_write)
                if ps >= s and pe <= e:
                    continue
            keep.append((ps, pe, pb))
        keep.append((s, e, b))
        self.live = keep
        b.reads = compress(inherit)
        return b


def rope_tables(half, n_tok_rows):
    rows = 2048 // 64
    row = np.repeat(np.arange(rows, dtype=np.float32), 64)
    col = np.tile(np.arange(64, dtype=np.float32), rows)
    freqs = (10000.0 ** (-np.arange(half, dtype=np.float32) / half)).astype(np.float32)
    ang_r = row[None, :] * freqs[:, None]
    ang_c = col[None, :] * freqs[:, None]
    cos = np.concatenate([np.cos(ang_r), np.cos(ang_r), np.cos(ang_c), np.cos(ang_c)], 0).astype(np.float32)
    sin = np.concatenate([np.sin(ang_r), np.sin(ang_r), np.sin(ang_c), np.sin(ang_c)], 0).astype(np.float32)
    n = 4 * half
    P = np.zeros((n, n), np.float32)
    for blk in range(2):
        o = blk * 2 * half
        for d in range(half):
            P[o + d, o + d + half] = -1.0
            P[o + d + half, o + d] = 1.0
    return cos, sin, np.ascontiguousarray(P.T)


def build(kinds=KINDS_FULL, do_mlp=True):
    nc = bass.Bass("TRN2", target_bir_lowering=False)
    n_a = sum(1 for k in kinds if k == 0)
    n_b = sum(1 for k in kinds if k == 1)
    n_c = sum(1 for k in kinds if k == 2)
    depth = len(kinds)

    def din(name, shape):
        return Buf(nc.dram_tensor(name, list(shape), F32, kind="ExternalInput").ap(), name)

    def dout(name, shape):
        return Buf(nc.dram_tensor(name, list(shape), F32, kind="ExternalOutput").ap(), name)

    I = {}
    I["xp"] = din("xp", (512, D))
    I["xs"] = din("xs", (2048, D))
    I["cond"] = din("cond", (2, 8, 128))
    dd = max(depth, 1)
    I["norm_mix"] = din("norm_mix", (dd, 8, 128))
    I["norm_mlp"] = din("norm_mlp", (dd, 8, 128))
    I["w_mod"] = din("w_mod", (dd, D, 6 * D))
    I["b_mod"] = din("b_mod", (dd, 48, 128))
    I["w_mlp_in"] = din("w_mlp_in", (dd, D, 4 * D))
    I["w_mlp_out"] = din("w_mlp_out", (dd, 4 * D, D))
    I["ident"] = din("ident", (128, 128))
    I["ones"] = din("ones", (128, 128))
    if n_c:
        I["gqa_w_in"] = din("gqa_w_in", (n_c, D, 1536))
        I["gqa_q_norm"] = din("gqa_q_norm", (n_c, 128))
        I["gqa_k_norm"] = din("gqa_k_norm", (n_c, 128))
        I["gqa_w_out"] = din("gqa_w_out", (n_c, D, D))
        I["cache_gqa_k"] = din("cache_gqa_k", (n_c, 512, 256))
        I["cache_gqa_v"] = din("cache_gqa_v", (n_c, 512, 256))
        I["rope_c_cos"] = din("rope_c_cos", (128, 2048))
        I["rope_c_sin"] = din("rope_c_sin", (128, 2048))
        I["rope_c_p"] = din("rope_c_p", (128, 128))
    if n_b:
        I["mla_w_down"] = din("mla_w_down", (n_b, D, 704))
        I["mla_q_lat_norm"] = din("mla_q_lat_norm", (n_b, 3, 128))
        I["mla_kv_lat_norm"] = din("mla_kv_lat_norm", (n_b, 2, 128))
        I["mla_w_uq"] = din("mla_w_uq", (n_b, 384, 1536))
        I["mla_w_ukv"] = din("mla_w_ukv", (n_b, 256, 2048))
        I["mla_qn_nope"] = din("mla_qn_nope", (n_b, 128))
        I["mla_qn_rope"] = din("mla_qn_rope", (n_b, 64))
        I["mla_kn_nope"] = din("mla_kn_nope", (n_b, 128))
        I["mla_kn_rope"] = din("mla_kn_rope", (n_b, 64))
        I["mla_w_out"] = din("mla_w_out", (n_b, D, D))
        I["cache_mla_ckv"] = din("cache_mla_ckv", (n_b, 512, 256))
        I["cache_mla_krope"] = din("cache_mla_krope", (n_b, 512, 64))
        I["rope_b_cos"] = din("rope_b_cos", (64, 2048))
        I["rope_b_sin"] = din("rope_b_sin", (64, 2048))
        I["rope_b_p"] = din("rope_b_p", (64, 64))
    if n_a:
        I["gdn_w_in"] = din("gdn_w_in", (n_a, D, 4096))
        I["gdn_w_gb"] = din("gdn_w_gb", (n_a, D, 32))
        I["gdn_conv"] = din("gdn_conv", (n_a, 72, 128))
        I["gdn_a_log"] = din("gdn_a_log", (n_a, 16))
        I["gdn_dt_bias"] = din("gdn_dt_bias", (n_a, 16))
        I["gdn_out_norm"] = din("gdn_out_norm", (n_a, 128))
        I["gdn_w_out"] = din("gdn_w_out", (n_a, D, D))
        I["state_f"] = din("state_f", (n_a, 8, 128, 128))
        I["state_b"] = din("state_b", (n_a, 8, 128, 128))
        I["gdn_masks"] = din("gdn_masks", (8, 128, 128))

    O = {}
    O["yp"] = dout("yp", (512, D))
    O["ys"] = dout("ys", (2048, D))
    O["o_sf"] = dout("o_sf", (2, max(n_a, 1), 8, 128, 128))
    O["o_sb"] = dout("o_sb", (2, max(n_a, 1), 8, 128, 128))
    O["o_ckv"] = dout("o_ckv", (2, max(n_b, 1), 256, 256))
    O["o_kr"] = dout("o_kr", (2, max(n_b, 1), 256, 64))
    O["o_gk"] = dout("o_gk", (2, max(n_c, 1), 256, 256))
    O["o_gv"] = dout("o_gv", (2, max(n_c, 1), 256, 256))

    st = contextlib.ExitStack()
    with st:
        K = Sched(nc)

        def sb(name, shape, dt):
            return Buf(st.enter_context(nc.sbuf_tensor("sb_" + name, list(shape), dt)).ap(), name)

        XT = [[sb("x%d_%d" % (kc, b), (128, 512), F32) for b in range(5)] for kc in range(NKC)]
        HT = [[sb("h%d_%d" % (kc, b), (128, 512), BF16) for b in range(5)] for kc in range(NKC)]
        ident = sb("ident", (128, 128), F32)
        identb = sb("identb", (128, 128), BF16)
        onesf = sb("onesf", (128, 128), F32)
        onesb = sb("onesb", (128, 128), BF16)
        condT = sb("condT", (128, 2, 8), F32)
        modT = [sb("modT%d" % i, (128, 48, 2), F32) for i in range(2)]
        coefA = sb("coefA", (128, 2, 8, 2), F32)
        gnorm = sb("gnorm", (128, 2, 8), F32)
        epsv = sb("epsv", (128, 1), F32)
        WR_BYTES = 85 * 1024
        wr_t = sb("wr", (128, WR_BYTES // 4), F32)
        R = Region(wr_t.ap, WR_BYTES)
        PS = [Buf(st.enter_context(nc.psum_tensor("ps%d" % i, [128, 512], F32)).ap(), "ps%d" % i, excl=True) for i in range(8)]

        def psb(i, dt=F32):
            if dt == BF16:
                return View(PS[i], PS[i].ap.bitcast(BF16))
            return PS[i].v

        K.dma(ident.v, I["ident"].v)
        K.dma(onesf.v, I["ones"].v)
        K.dma(identb.v, I["ident"].v, eng="pool")
        K.dma(onesb.v, I["ones"].v, eng="pool")
        K.memset("dve", epsv.v, EPS)

        R.reset()
        xst = [R.alloc((128, D), F32, "xst%d" % i) for i in range(2)]
        import os
        NT_DBG = int(os.environ.get("KNT", "20"))
        for t in range(NT_DBG):
            src = I["xp"][t * 128:(t + 1) * 128, :] if t < 4 else I["xs"][(t - 4) * 128:(t - 3) * 128, :]
            s_ = xst[(t + int(os.environ.get("KSL", "0"))) % 2]
            if os.environ.get("KV", "") == "a":
                src = I["xp"][0:128, :]
            if os.environ.get("KV", "") == "b":
                s_ = xst[0]
            K.dma(s_.v, src, eng=os.environ.get("KQ", "sp"))
            if os.environ.get("KV", "") == "c" and t == 1:
                continue
            b, off = divmod(t * 128, 512)
            for g in range(2):
                pb = psb((t * 2 + g + int(os.environ.get("KOFF", "0"))) % int(os.environ.get("KNB", "4")))
                for q in range(4):
                    kc = g * 4 + q
                    K.transpose(pb[:, q * 128:(q + 1) * 128], s_[:, kc * 128:(kc + 1) * 128], ident.v)
                for q in range(4):
                    kc = g * 4 + q
                    eng = "dve" if q % 2 == 0 else "act"
                    if os.environ.get("KV", "") == "d":
                        eng = "dve"
                    if os.environ.get("KV", "") == "e":
                        eng = "act"
                    K.copy(eng, XT[kc][b][:, off:off + 128], pb[:, q * 128:(q + 1) * 128])

        import os
        if os.environ.get("KDBG", "") != "1":
            cst = R.alloc((16, 128), F32, "cst")
            K.dma(cst.v, I["cond"].v.re("c k p -> (c k) p"))
            K.transpose(psb(4)[:, 0:16], cst.v, ident[0:16, 0:16])
            K.act(condT.v.re("p c k -> p (c k)"), psb(4)[:, 0:16], AF.Silu)

        def adaln(i):
            mt = modT[i % 2]
            bst = R.alloc((48, 128), F32, "bst")
            bT = R.alloc((128, 48), F32, "bT")
            K.dma(bst.v, I["b_mod"][i])
            K.transpose(psb(4)[:, 0:48], bst.v, ident[0:48, 0:48])
            K.copy("dve", bT.v, psb(4)[:, 0:48])
            wst = [R.alloc((128, 8, 256), F32, "wmod%d" % q) for q in range(2)]
            for cb in range(24):
                w_ = wst[cb % 2]
                K.dma(w_.v, I["w_mod"][i].re("(k p) n -> p k n", p=128)[:, :, cb * 256:(cb + 1) * 256])
                pb = psb(5)
                for fl in range(2):
                    f = cb * 2 + fl
                    for kc in range(8):
                        K.mm(pb[:, f * 2:f * 2 + 2], w_[:, kc, fl * 128:(fl + 1) * 128], condT[:, :, kc],
                             start=(kc == 0), stop=(kc == 7))
                if cb % 2 == 1:
                    yield
            K.tt("dve", mt.v, psb(5)[:, 0:96].re("p (f c) -> p f c", c=2), bT.v.un(2).bc([128, 48, 2]), ALU.add)
            yield

        def run(gen):
            for _ in gen:
                pass

        def load_gains(i):
            gst = R.alloc((16, 128), F32, "gst")
            K.dma(gst[0:8, :], I["norm_mix"][i])
            K.dma(gst[8:16, :], I["norm_mlp"][i])
            K.transpose(psb(4)[:, 0:16], gst.v, ident[0:16, 0:16])
            K.copy("dve", gnorm.v.re("p w k -> p (w k)"), psb(4)[:, 0:16])

        def make_coef(i):
            mt = modT[i % 2]
            for w in range(2):
                sc = mt[:, (3 * w + 1) * 8:(3 * w + 2) * 8, :]
                K.ts("dve", coefA[:, w], sc, 1.0, ALU.add)
                K.tt("dve", coefA[:, w], coefA[:, w], gnorm[:, w].un(2).bc([128, 8, 2]), ALU.mult)

        def mod_norm(i, w):
            mt = modT[i % 2]
            sq = [R.alloc((128, 512), BF16, "sq%d" % q) for q in range(2)]
            rs = [R.alloc((128, 512), F32, "rs%d" % q) for q in range(2)]
            tmp = [R.alloc((128, 512), F32, "ntmp%d" % q) for q in range(2)]
            for b, (c0, n, cond) in enumerate(TB):
                pb = psb(b % 2)
                for kc in range(8):
                    s_ = sq[kc % 2]
                    if kc % 2 == 0:
                        K.act(s_.v, XT[kc][b].v, AF.Square)
                    else:
                        K.tt("pool", s_.v, XT[kc][b].v, XT[kc][b].v, ALU.mult)
                    K.mm(pb, onesb.v, s_.v, start=(kc == 0), stop=(kc == 7))
                r_ = rs[b % 2]
                K.act(r_.v, pb, AF.Sqrt, bias=epsv.v, scale=1.0 / D)
                K.recip(r_.v, r_.v)
                for kc in range(8):
                    t_ = tmp[kc % 2]
                    K.stt("dve", t_.v, XT[kc][b].v, coefA[:, w, kc, cond:cond + 1], r_.v, ALU.mult, ALU.mult)
                    K.act(HT[kc][b].v, t_.v, AF.Identity, bias=mt[:, 3 * w * 8 + kc, cond:cond + 1])

        def mlp(i, bg):
            mt = modT[i % 2]
            win = [R.alloc((128, 8, 512), BF16, "win%d" % q) for q in range(2)]
            wout = [R.alloc((128, 4, D), BF16, "wout%d" % q) for q in range(2)]
            uu = [R.alloc((128, 4, 512), BF16, "uu%d" % q) for q in range(2)]
            rr = [R.alloc((128, 512), BF16, "rr%d" % q) for q in range(2)]
            it = 0
            for fb in range(8):
                wi = win[fb % 2]
                wo = wout[fb % 2]
                K.dma(wi.v, I["w_mlp_in"][i].re("(k p) n -> p k n", p=128)[:, :, fb * 512:(fb + 1) * 512], eng="pool")
                K.dma(wo.v, I["w_mlp_out"][i][fb * 512:(fb + 1) * 512, :].re("(f p) n -> p f n", p=128), eng="pool")
                for b, (c0, n, cond) in enumerate(TB):
                    u_ = uu[it % 2]
                    for fc in range(4):
                        pb = psb(fc % 2)
                        for kc in range(8):
                            K.mm(pb, wi[:, kc, fc * 128:(fc + 1) * 128], HT[kc][b].v, start=(kc == 0), stop=(kc == 7))
                        r_ = rr[fc % 2]
                        K.act(r_.v, pb, AF.Relu)
                        K.tt("pool", u_[:, fc, :], r_.v, r_.v, ALU.mult)
                    for oc in range(8):
                        pb = psb(2 + oc % 2)
                        for fc in range(4):
                            K.mm(pb, wo[:, fc, oc * 128:(oc + 1) * 128], u_[:, fc, :], start=(fc == 0), stop=(fc == 3))
                        K.stt("dve", XT[oc][b].v, pb, mt[:, 40 + oc, cond:cond + 1], XT[oc][b].v, ALU.mult, ALU.add)
                    it += 1
                    if bg is not None:
                        next(bg, None)
            if bg is not None:
                run(bg)

        def out_accum(i, oT, wo_h):
            mt = modT[i % 2]
            for b, (c0, n, cond) in enumerate(TB):
                for oc in range(8):
                    pb = psb(6 + oc % 2)
                    K.mm(pb, wo_h[:, oc * 128:(oc + 1) * 128], oT[b])
                    K.stt("dve", XT[oc][b].v, pb, mt[:, 16 + oc, cond:cond + 1], XT[oc][b].v, ALU.mult, ALU.add)

        def fm_norm(dst, src_ps, npart, ncols, gain, nfeat, scr, rope=None, ones_v=None, eps_v=None):
            P = npart
            K.act(scr["sq"][0:P, 0:ncols], src_ps[0:P, 0:ncols], AF.Square)
            pd = scr["pd"]
            K.mm(pd[0:P, 0:ncols], onesb[0:P, 0:P], scr["sq"][0:P, 0:ncols])
            K.act(scr["rs"][0:P, 0:ncols], pd[0:P, 0:ncols], AF.Sqrt, bias=(eps_v or epsv)[0:P, :], scale=1.0 / nfeat)
            K.recip(scr["rs"][0:P, 0:ncols], scr["rs"][0:P, 0:ncols])
            if rope is None:
                K.stt("dve", dst, src_ps[0:P, 0:ncols], gain, scr["rs"][0:P, 0:ncols], ALU.mult, ALU.mult)
            else:
                cos, sin, permT = rope
                nb = scr["nb"]
                K.stt("dve", nb[0:P, 0:ncols], src_ps[0:P, 0:ncols], gain, scr["rs"][0:P, 0:ncols], ALU.mult, ALU.mult)
                K.mm(pd[0:P, 0:ncols], permT, nb[0:P, 0:ncols])
                K.tt("pool", scr["t1"][0:P, 0:ncols], nb[0:P, 0:ncols], cos, ALU.mult)
                K.tt("dve", scr["t2"][0:P, 0:ncols], pd[0:P, 0:ncols], sin, ALU.mult)
                K.tt("pool", dst, scr["t1"][0:P, 0:ncols], scr["t2"][0:P, 0:ncols], ALU.add)

        def norm_scr(light=False):
            if light:
                return {"sq": R.alloc((128, 512), BF16, "nsq"), "rs": R.alloc((128, 512), F32, "nrs"), "pd": psb(5)}
            return {"sq": R.alloc((128, 512), BF16, "nsq"), "rs": R.alloc((128, 512), F32, "nrs"),
                    "t1": R.alloc((128, 512), F32, "nt1"), "t2": R.alloc((128, 512), F32, "nt2"),
                    "nb": R.alloc((128, 512), BF16, "nnb"), "pd": psb(5)}

        QB = [(0, 256, [0, 1]), (256, 256, [2, 3])] + [(512 + 512 * q, 512, list(range(4, 24))) for q in range(4)]

        def attn_head(kparts, vtile, qparts, oT, scale, pbufs):
            steps = []
            for qi, (q0, nq, kts) in enumerate(QB):
                for ji, kt in enumerate(kts):
                    steps.append((qi, q0, nq, ji, kt, len(kts)))
            pts = {}

            def issue_s(n):
                qi, q0, nq, ji, kt, nk = steps[n]
                pss = psb(n % 3)
                for pi, (kT, P) in enumerate(kparts):
                    K.mm(pss[:, 0:nq], kT[0:P, kt * 128:(kt + 1) * 128], qparts[pi][0:P, q0:q0 + nq],
                         start=(pi == 0), stop=(pi == len(kparts) - 1))
                pt = pbufs[n % len(pbufs)]
                K.act(pt[:, 0:nq], pss[:, 0:nq], AF.Exp, scale=scale)
                pts[n] = pt

            def issue_pv(n):
                qi, q0, nq, ji, kt, nk = steps[n]
                po, pdn = (psb(3), psb(4)) if qi % 2 == 0 else (psb(5), psb(6))
                pt = pts.pop(n)
                K.mm(po[:, 0:nq], vtile(kt), pt[:, 0:nq], start=(ji == 0), stop=(ji == nk - 1))
                K.mm(pdn[:, 0:nq], onesb.v, pt[:, 0:nq], start=(ji == 0), stop=(ji == nk - 1))
                if ji == nk - 1:
                    rd = pbufs_rd[qi % 2]
                    K.recip(rd[:, 0:nq], pdn[:, 0:nq])
                    K.tt("dve", oT[:, q0:q0 + nq], po[:, 0:nq], rd[:, 0:nq], ALU.mult)
            SK = 2
            for n in range(len(steps) + SK):
                if n < len(steps):
                    issue_s(n)
                if n - SK >= 0:
                    issue_pv(n - SK)

        pbufs_rd = [None, None]

        def gqa_layer(i, j):
            R.reset()
            kT = R.alloc((128, 2, 3072), BF16, "kT")
            V = R.alloc((128, 24, 256), BF16, "V")
            cos = R.alloc((128, 2048), BF16, "cos")
            sin = R.alloc((128, 2048), BF16, "sin")
            permT = R.alloc((128, 128), BF16, "permT")
            kg = R.alloc((128, 2), F32, "kg")
            kgb = R.alloc((128, 128), F32, "kgb")
            K.dma(cos.v, I["rope_c_cos"].v, eng="pool")
            K.dma(sin.v, I["rope_c_sin"].v, eng="pool")
            K.dma(permT.v, I["rope_c_p"].v, eng="pool")
            gst = R.alloc((2, 128), F32, "gst2")
            K.dma(gst[0:1, :], I["gqa_q_norm"][j:j + 1, :])
            K.dma(gst[1:2, :], I["gqa_k_norm"][j:j + 1, :])
            K.transpose(psb(4)[:, 0:2], gst.v, ident[0:2, 0:2])
            K.copy("dve", kg.v, psb(4)[:, 0:2])
            K.dma(kgb.v, View(I["gqa_k_norm"], I["gqa_k_norm"].ap[j].partition_broadcast(128)))
            top0 = R.top
            wkv = R.alloc((128, 8, 512), BF16, "wkv")
            K.dma(wkv.v, I["gqa_w_in"][j].re("(k p) n -> p k n", p=128)[:, :, 1024:1536], eng="pool")
            scr = norm_scr()
            for kvh in range(2):
                for b, (c0, n, cond) in enumerate(TB):
                    pb = psb(b % 2)
                    for kc in range(8):
                        K.mm(pb, wkv[:, kc, kvh * 128:(kvh + 1) * 128], HT[kc][b].v, start=(kc == 0), stop=(kc == 7))
                    rope = None if b == 0 else (cos[:, c0 - 512:c0], sin[:, c0 - 512:c0], permT.v)
                    fm_norm(kT[:, kvh, c0:c0 + 512], pb, 128, 512, kg[:, 1:2], 128, scr, rope)
            vst = R.alloc((128, 256), F32, "vst")
            kst = R.alloc((128, 256), F32, "kst")
            ssq = R.alloc((128, 2), F32, "ssq")
            ksq = R.alloc((128, 256), F32, "ksq")
            for t in range(20):
                b, off = divmod(t * 128, 512)
                pb = psb(2 + t % 2)
                ncol = 512 if t < 4 else 256
                for kc in range(8):
                    rhs = wkv[:, kc, 0:512] if t < 4 else wkv[:, kc, 256:512]
                    K.mm(pb[:, 0:ncol], HT[kc][b][:, off:off + 128], rhs, start=(kc == 0), stop=(kc == 7))
                if t < 4:
                    K.copy("act", V[:, t, :], pb[:, 256:512])
                    K.copy("dve", vst.v, pb[:, 256:512])
                    K.dma(O["o_gv"][t // 2, j, (t % 2) * 128:(t % 2 + 1) * 128, :], vst.v)
                    K.act(ksq.v, pb[:, 0:256], AF.Square)
                    K.reduce("dve", ssq.v, ksq.v.re("p (h d) -> p h d", h=2))
                    K.act(ssq.v, ssq.v, AF.Sqrt, bias=epsv.v, scale=1.0 / 128)
                    K.recip(ssq.v, ssq.v)
                    K.tt("dve", kst.v.re("p (h d) -> p h d", h=2), pb[:, 0:256].re("p (h d) -> p h d", h=2),
                         ssq.v.un(2).bc([128, 2, 128]), ALU.mult)
                    K.tt("pool", kst.v.re("p (h d) -> p h d", h=2), kst.v.re("p (h d) -> p h d", h=2),
                         kgb.v.un(1).bc([128, 2, 128]), ALU.mult)
                    K.dma(O["o_gk"][t // 2, j, (t % 2) * 128:(t % 2 + 1) * 128, :], kst.v)
                else:
                    K.copy("act", V[:, t, :], pb[:, 0:256])
            K.dma(V[:, 20:24, :], I["cache_gqa_v"][j].re("(t p) n -> p t n", p=128), eng="pool")
            cst_ = R.alloc((128, 4, 256), F32, "cst_")
            K.dma(cst_.v, I["cache_gqa_k"][j].re("(t p) n -> p t n", p=128))
            for t in range(4):
                for kvh in range(2):
                    pb = psb(t % 2)
                    K.transpose(pb[:, 0:128], cst_[:, t, kvh * 128:(kvh + 1) * 128], ident.v)
                    K.copy("act", kT[:, kvh, 2560 + t * 128:2560 + (t + 1) * 128], pb[:, 0:128])
            R.top = top0
            scr = norm_scr()
            wq = [R.alloc((128, 8, 128), BF16, "wq%d" % q) for q in range(2)]
            woh = [R.alloc((128, D), BF16, "woh%d" % q) for q in range(2)]
            qT = [R.alloc((128, T), BF16, "qT%d" % q) for q in range(2)]
            oT = [R.alloc((128, T), BF16, "oT%d" % q) for q in range(2)]
            pb_ = [R.alloc((128, 512), BF16, "pt%d" % q) for q in range(4)]
            pbufs_rd[0] = R.alloc((128, 512), F32, "rd0")
            pbufs_rd[1] = R.alloc((128, 512), F32, "rd1")
            for h in range(8):
                kvh = h // 4
                w_ = wq[h % 2]
                K.dma(w_.v, I["gqa_w_in"][j].re("(k p) n -> p k n", p=128)[:, :, h * 128:(h + 1) * 128], eng="pool")
                K.dma(woh[h % 2].v, I["gqa_w_out"][j][h * 128:(h + 1) * 128, :], eng="pool")
                q_ = qT[h % 2]
                for b, (c0, n, cond) in enumerate(TB):
                    pb = psb(6 + b % 2)
                    for kc in range(8):
                        K.mm(pb, w_[:, kc, :], HT[kc][b].v, start=(kc == 0), stop=(kc == 7))
                    rope = None if b == 0 else (cos[:, c0 - 512:c0], sin[:, c0 - 512:c0], permT.v)
                    fm_norm(q_[:, c0:c0 + 512], pb, 128, 512, kg[:, 0:1], 128, scr, rope)
                o_ = oT[h % 2]
                attn_head([(kT[:, kvh, :], 128)], lambda kt, kvh=kvh: V[:, kt, kvh * 128:(kvh + 1) * 128],
                          [q_.v], o_.v, 128 ** -0.5, pb_)
                out_accum(i, [o_[:, c0:c0 + 512] for (c0, n, cond) in TB], woh[h % 2].v)

        def mla_layer(i, j):
            R.reset()
            cqn = R.alloc((128, 3, T), BF16, "cqn")
            ckvn = R.alloc((128, 2, 3072), BF16, "ckvn")
            krT = R.alloc((64, 3072), BF16, "krT")
            cos = R.alloc((64, 2048), BF16, "cosb")
            sin = R.alloc((64, 2048), BF16, "sinb")
            permT = R.alloc((64, 64), BF16, "permTb")
            gl = R.alloc((128, 9), F32, "gl")
            K.dma(cos.v, I["rope_b_cos"].v, eng="pool")
            K.dma(sin.v, I["rope_b_sin"].v, eng="pool")
            K.dma(permT.v, I["rope_b_p"].v, eng="pool")
            gst = R.alloc((9, 128), F32, "gst9")
            K.memset("dve", gst.v, 0.0)
            K.dma(gst[0:3, :], I["mla_q_lat_norm"][j])
            K.dma(gst[3:5, :], I["mla_kv_lat_norm"][j])
            K.dma(gst[5:6, :], I["mla_qn_nope"][j:j + 1, :])
            K.dma(gst[6:7, :], I["mla_kn_nope"][j:j + 1, :])
            K.dma(gst[7:8, 0:64], I["mla_qn_rope"][j:j + 1, :])
            K.dma(gst[8:9, 0:64], I["mla_kn_rope"][j:j + 1, :])
            K.transpose(psb(4)[:, 0:9], gst.v, ident[0:9, 0:9])
            K.copy("dve", gl.v, psb(4)[:, 0:9])
            top0 = R.top
            wd = R.alloc((128, 8, 704), BF16, "wd")
            K.dma(wd.v, I["mla_w_down"][j].re("(k p) n -> p k n", p=128), eng="pool")
            scr = norm_scr()
            sqs = [R.alloc((128, 512), BF16, "msq%d" % q) for q in range(3)]
            rs = R.alloc((128, 512), F32, "mrs")
            f32o = R.alloc((128, 3, 512), F32, "f32o")
            groups = [(0, 3, 0, 384), (384, 2, 3, 256)]
            for b, (c0, n, cond) in enumerate(TB):
                for (col0, nch, gcol, nfeat) in groups:
                    pbs = [psb(q) for q in range(nch)]
                    for c in range(nch):
                        for kc in range(8):
                            K.mm(pbs[c], wd[:, kc, col0 + c * 128:col0 + (c + 1) * 128], HT[kc][b].v,
                                 start=(kc == 0), stop=(kc == 7))
                    pd = psb(5)
                    for c in range(nch):
                        K.act(sqs[c].v, pbs[c], AF.Square)
                        K.mm(pd, onesb.v, sqs[c].v, start=(c == 0), stop=(c == nch - 1))
                    K.act(rs.v, pd, AF.Sqrt, bias=epsv.v, scale=1.0 / nfeat)
                    K.recip(rs.v, rs.v)
                    for c in range(nch):
                        if nch == 3:
                            dst = cqn[:, c, c0:c0 + 512]
                        else:
                            dst = ckvn[:, c, c0:c0 + 512]
                        K.stt("dve", dst, pbs[c], gl[:, gcol + c:gcol + c + 1], rs.v, ALU.mult, ALU.mult)
                        if nch == 2 and b == 0:
                            K.stt("dve", f32o[:, c, :], pbs[c], gl[:, gcol + c:gcol + c + 1], rs.v, ALU.mult, ALU.mult)
                pb = psb(3)
                for kc in range(8):
                    K.mm(pb[0:64, :], wd[:, kc, 640:704], HT[kc][b].v, start=(kc == 0), stop=(kc == 7))
                rope = None if b == 0 else (cos[:, c0 - 512:c0], sin[:, c0 - 512:c0], permT.v)
                fm_norm(krT[:, c0:c0 + 512], pb, 64, 512, gl[0:64, 8:9], 64, scr, rope)
                if b == 0:
                    fm_norm(f32o[0:64, 2, :], pb, 64, 512, gl[0:64, 8:9], 64, scr, None)
            ost = R.alloc((128, 320), F32, "ost")
            for t in range(4):
                pb = psb(t % 2)
                for c in range(2):
                    K.transpose(pb[:, c * 128:(c + 1) * 128], f32o[:, c, t * 128:(t + 1) * 128], ident.v)
                K.transpose(pb[:, 256:320], f32o[0:64, 2, t * 128:(t + 1) * 128], ident[0:64, 0:64])
                K.copy("dve", ost.v, pb[:, 0:320])
                K.dma(O["o_ckv"][t // 2, j, (t % 2) * 128:(t % 2 + 1) * 128, :], ost[:, 0:256])
                K.dma(O["o_kr"][t // 2, j, (t % 2) * 128:(t % 2 + 1) * 128, :], ost[:, 256:320])
            cst_ = R.alloc((128, 4, 320), F32, "mcst")
            K.dma(cst_[:, :, 0:256], I["cache_mla_ckv"][j].re("(t p) n -> p t n", p=128))
            K.dma(cst_[:, :, 256:320], I["cache_mla_krope"][j].re("(t p) n -> p t n", p=128))
            for t in range(4):
                pb = psb(t % 2)
                for c in range(2):
                    K.transpose(pb[:, c * 128:(c + 1) * 128], cst_[:, t, c * 128:(c + 1) * 128], ident.v)
                K.transpose(pb[0:64, 256:384], cst_[:, t, 256:320], ident.v)
                for c in range(2):
                    K.copy("act", ckvn[:, c, 2560 + t * 128:2560 + (t + 1) * 128], pb[:, c * 128:(c + 1) * 128])
                K.copy("dve", krT[:, 2560 + t * 128:2560 + (t + 1) * 128], pb[0:64, 256:384])
            R.top = top0
            scr = norm_scr()
            wuq = [R.alloc((128, 3, 192), BF16, "wuq%d" % q) for q in range(1)]
            wukv = [R.alloc((128, 2, 256), BF16, "wukv%d" % q) for q in range(1)]
            woh = [R.alloc((128, D), BF16, "mwoh%d" % q) for q in range(1)]
            qn = [R.alloc((128, T), BF16, "qn%d" % q) for q in range(1)]
            qr = [R.alloc((64, T), BF16, "qr%d" % q) for q in range(1)]
            kn = [R.alloc((128, 3072), BF16, "kn%d" % q) for q in range(1)]
            Vh = [R.alloc((128, 24, 128), BF16, "Vh%d" % q) for q in range(1)]
            oT = qn
            pb_ = [R.alloc((128, 512), BF16, "mpt%d" % q) for q in range(3)]
            pbufs_rd[0] = R.alloc((128, 512), F32, "mrd0")
            pbufs_rd[1] = pbufs_rd[0]
            for h in range(8):
                p_ = 0
                K.dma(wuq[p_].v, I["mla_w_uq"][j].re("(k p) n -> p k n", p=128)[:, :, h * 192:(h + 1) * 192], eng="pool")
                K.dma(wukv[p_].v, I["mla_w_ukv"][j].re("(k p) n -> p k n", p=128)[:, :, h * 256:(h + 1) * 256], eng="pool")
                K.dma(woh[p_].v, I["mla_w_out"][j][h * 128:(h + 1) * 128, :], eng="pool")
                for cb in range(6):
                    c0 = cb * 512
                    pb = psb(6 + cb % 2)
                    for c in range(2):
                        K.mm(pb, wukv[p_][:, c, 0:128], ckvn[:, c, c0:c0 + 512], start=(c == 0), stop=(c == 1))
                    fm_norm(kn[p_][:, c0:c0 + 512], pb, 128, 512, gl[:, 6:7], 128, scr, None)
                for g in range(6):
                    pb = psb(6 + g % 2)
                    for q in range(4):
                        t = g * 4 + q
                        for c in range(2):
                            K.mm(pb[:, q * 128:(q + 1) * 128], ckvn[:, c, t * 128:(t + 1) * 128], wukv[p_][:, c, 128:256],
                                 start=(c == 0), stop=(c == 1))
                    K.copy("act", Vh[p_][:, g * 4:(g + 1) * 4, :], pb.re("p (q d) -> p q d", q=4))
                for b, (c0, n, cond) in enumerate(TB):
                    pb = psb(6 + b % 2)
                    for c in range(3):
                        K.mm(pb, wuq[p_][:, c, 0:128], cqn[:, c, c0:c0 + 512], start=(c == 0), stop=(c == 2))
                    fm_norm(qn[p_][:, c0:c0 + 512], pb, 128, 512, gl[:, 5:6], 128, scr, None)
                    pb2 = psb(7 - b % 2)
                    for c in range(3):
                        K.mm(pb2[0:64, :], wuq[p_][:, c, 128:192], cqn[:, c, c0:c0 + 512], start=(c == 0), stop=(c == 2))
                    rope = None if b == 0 else (cos[:, c0 - 512:c0], sin[:, c0 - 512:c0], permT.v)
                    fm_norm(qr[p_][:, c0:c0 + 512], pb2, 64, 512, gl[0:64, 7:8], 64, scr, rope)
                attn_head([(kn[p_].v, 128), (krT.v, 64)], lambda kt, p_=p_: Vh[p_][:, kt, :],
                          [qn[p_].v, qr[p_].v], oT[p_].v, 192 ** -0.5, pb_)
                out_accum(i, [oT[p_][:, c0:c0 + 512] for (c0, n, cond) in TB], woh[p_].v)


        def gdn_layer(i, j):
            R.reset()
            NEG = -30000.0
            msk = R.alloc((64, 6, 64), F32, "msk")
            K.dma(msk.v, I["gdn_masks"][0:6, 0:64, 0:64].re("m p n -> p m n"))
            U = [msk[:, 0, :], msk[:, 1, :]]
            NEGS = [msk[:, 2, :], msk[:, 4, :]]
            NEGI = [msk[:, 3, :], msk[:, 5, :]]
            I64 = ident[0:64, 0:64]
            eps2 = R.alloc((128, 1), F32, "eps2")
            K.memset("dve", eps2.v, EPS / 128.0)
            gq = R.alloc((128, 3), F32, "gq")
            K.memset("dve", gq[:, 0:1], 1.0 / 128.0)
            K.memset("dve", gq[:, 1:2], 128.0 ** -0.5)
            cst_ = R.alloc((73, 128), F32, "gcst")
            K.dma(cst_[0:72, :], I["gdn_conv"][j])
            K.dma(cst_[72:73, :], I["gdn_out_norm"][j:j + 1, :])
            cw = R.alloc((128, 73), F32, "cw")
            K.transpose(psb(4)[:, 0:73], cst_.v, ident[0:73, 0:73])
            K.copy("dve", cw.v, psb(4)[:, 0:73])
            alb = R.alloc((64, 16), F32, "alb")
            dtb = R.alloc((64, 16), F32, "dtb")
            K.dma(alb.v, View(I["gdn_a_log"], I["gdn_a_log"].ap[j].partition_broadcast(64)))
            K.dma(dtb.v, View(I["gdn_dt_bias"], I["gdn_dt_bias"].ap[j].partition_broadcast(64)))
            K.act(alb.v, alb.v, AF.Exp)
            K.ts("dve", alb.v, alb.v, -1.0, ALU.mult)
            gcum = R.alloc((64, 40, 16), F32, "gcum")
            eg = R.alloc((64, 40, 16), F32, "eg")
            ekd = R.alloc((64, 40, 16), F32, "ekd")
            sA = R.alloc((64, 40, 16), F32, "sA")
            beta = R.alloc((64, 40, 16), F32, "beta")
            egt = R.alloc((128, 40, 16), F32, "egt")
            top_g = R.top
            wgb = R.alloc((128, 8, 32), BF16, "wgb")
            K.dma(wgb.v, I["gdn_w_gb"][j].re("(k p) n -> p k n", p=128), eng="pool")
            G = R.alloc((64, 40, 32), F32, "G")
            g_ = R.alloc((64, 40, 16), F32, "g_")
            t1 = R.alloc((64, 40, 16), F32, "gt1")
            t2 = R.alloc((64, 40, 16), F32, "gt2")
            for grp in range(3):
                c_lo, c_hi = grp * 16, min(40, grp * 16 + 16)
                pb = psb(grp % 2)
                for c in range(c_lo, c_hi):
                    b, off = divmod(c * 64, 512)
                    for kc in range(8):
                        K.mm(pb[0:64, (c - c_lo) * 32:(c - c_lo + 1) * 32], HT[kc][b][:, off:off + 64], wgb[:, kc, :],
                             start=(kc == 0), stop=(kc == 7))
                K.copy("dve", G[:, c_lo:c_hi, :], pb[0:64, 0:(c_hi - c_lo) * 32].re("p (c n) -> p c n", n=32))

            def softplus(dst, x):
                K.act(t1.v, x, AF.Abs)
                K.act(t1.v, t1.v, AF.Exp, scale=-1.0)
                K.act(t1.v, t1.v, AF.Ln, bias=1.0)
                K.ts("dve", t2.v, x, 0.0, ALU.max)
                K.tt("dve", dst, t1.v, t2.v, ALU.add)
            K.tt("dve", g_.v, G[:, :, 0:16], dtb.v.un(1).bc([64, 40, 16]), ALU.add)
            softplus(g_.v, g_.v)
            K.tt("dve", g_.v, g_.v, alb.v.un(1).bc([64, 40, 16]), ALU.mult)
            K.ts("dve", sA.v, G[:, :, 16:32], -1.0, ALU.mult)
            softplus(sA.v, sA.v)
            K.ts("dve", sA.v, sA.v, -1.0, ALU.mult)
            K.act(beta.v, sA.v, AF.Exp)
            for d in range(2):
                pb = psb(2 + d)
                K.mm(pb[0:64, 0:320].re("p (c h) -> p c h", h=8), U[d], g_[:, :, d * 8:(d + 1) * 8])
                K.copy("dve", gcum[:, :, d * 8:(d + 1) * 8], pb[0:64, 0:320].re("p (c h) -> p c h", h=8))
            pbt = [psb(4), psb(5)]
            K.mm(pbt[0][:, 0:320], onesf[0:64, 0:128], g_[:, 0:20, :])
            K.mm(pbt[1][:, 0:320], onesf[0:64, 0:128], g_[:, 20:40, :])
            for q in range(2):
                K.act(egt[:, q * 20:(q + 1) * 20, :], pbt[q][:, 0:320].re("p (c h) -> p c h", h=16), AF.Exp)
                K.tt("dve", ekd[:, q * 20:(q + 1) * 20, :], pbt[q][0:64, 0:320].re("p (c h) -> p c h", h=16),
                     gcum[:, q * 20:(q + 1) * 20, :], ALU.subtract)
            K.act(ekd.v, ekd.v, AF.Exp)
            K.act(eg.v, gcum.v, AF.Exp)
            K.tt("dve", sA.v, sA.v, gcum.v, ALU.subtract)
            R.top = top_g
            qT = R.alloc((128, T), BF16, "gqT")
            kT = R.alloc((128, T), BF16, "gkT")
            k_c = R.alloc((64, 40, 128), BF16, "k_c")
            v_c = R.alloc((64, 40, 128), BF16, "v_c")
            oacc = R.alloc((128, T), F32, "oacc")
            S = R.alloc((128, 128), F32, "S")
            Sb = R.alloc((128, 128), BF16, "Sb")
            top_h = R.top
            SEQ = [(0, 4, 0), (4, 8, 1), (8, 40, 2)]
            for h in range(8):
                R.top = top_h
                praw = R.alloc((128, T), F32, "praw")
                cv = R.alloc((128, T), F32, "cv")
                win = R.alloc((128, 8, 128), BF16, "gwin")
                scr = norm_scr(light=True)
                vT = None
                for xi in range(3):
                    K.dma(win.v, I["gdn_w_in"][j].re("(k p) n -> p k n", p=128)[:, :, xi * 1024 + h * 128:xi * 1024 + (h + 1) * 128],
                          eng="pool")
                    for b, (c0, n, cond) in enumerate(TB):
                        pb = psb(b % 2)
                        for kc in range(8):
                            K.mm(pb, win[:, kc, :], HT[kc][b].v, start=(kc == 0), stop=(kc == 7))
                        K.copy("act" if b % 2 else "dve", praw[:, c0:c0 + 512], pb)
                    ch = xi * 8 + h
                    for (s0, s1) in [(0, 256), (256, 512), (512, 2560)]:
                        K.ts("dve", cv[:, s0:s1], praw[:, s0:s1], cw[:, 24 + ch:25 + ch], ALU.mult)
                        K.stt("dve", cv[:, s0 + 1:s1], praw[:, s0:s1 - 1], cw[:, ch:ch + 1], cv[:, s0 + 1:s1], ALU.mult, ALU.add)
                        K.stt("dve", cv[:, s0:s1 - 1], praw[:, s0 + 1:s1], cw[:, 48 + ch:49 + ch], cv[:, s0:s1 - 1], ALU.mult, ALU.add)
                    K.act(cv.v, cv.v, AF.Silu)
                    if xi < 2:
                        dstT = qT if xi == 0 else kT
                        for b, (c0, n, cond) in enumerate(TB):
                            fm_norm(dstT[:, c0:c0 + 512], cv[:, c0:c0 + 512], 128, 512, gq[:, xi:xi + 1], 128, scr, None, eps_v=eps2)
                    else:
                        sv_top = R.top
                        R.top = top_h
                        vT = R.alloc((128, T), BF16, "vT")
                        R.top = sv_top
                        K.copy("pool", vT.v, cv.v)
                for (srcT, dstc) in ((kT, k_c), (vT, v_c)):
                    for b in range(5):
                        pbb = psb(2 + b % 2, BF16)
                        for c in range(8):
                            K.transpose(pbb[0:64, c * 128:(c + 1) * 128], srcT[:, b * 512 + c * 64:b * 512 + (c + 1) * 64], identb.v)
                        K.copy("act" if b % 2 else "dve", dstc[:, b * 8:(b + 1) * 8, :], pbb[0:64, :].re("p (c d) -> p c d", d=128))
                R.top = top_h
                scrs = [R.alloc((64, 8, 64), F32, "gs%d" % q) for q in range(7)]
                kgn = R.alloc((64, 8, 128), BF16, "kgn")
                kdec = R.alloc((64, 8, 128), BF16, "kdec")
                erow = R.alloc((128, 512), F32, "erow")
                qdT = R.alloc((128, 8, 64), BF16, "qdT")
                QKt = R.alloc((64, 8, 64), BF16, "QKt")
                Tt = R.alloc((64, 8, 64), BF16, "Tt")
                WTn = R.alloc((128, 8, 64), BF16, "WTn")
                vnew = R.alloc((64, 128), BF16, "vnew")
                for d in range(2):
                    dh = d * 8 + h
                    border = [0, 1, 2, 3, 4] if d == 0 else [0, 4, 3, 2, 1]
                    for b in border:
                        cb0 = b * 8
                        gc_b = gcum[:, cb0:cb0 + 8, dh]
                        Dg, RBA, RBQ, EA, EQ, At, M = scrs
                        K.tt("pool", Dg.v, View(ident, I64.ap.unsqueeze(1).broadcast_to([64, 8, 64])),
                             gc_b.un(2).bc([64, 8, 64]), ALU.mult)
                        K.tt("dve", RBA.v, View(msk, NEGS[d].ap.unsqueeze(1).broadcast_to([64, 8, 64])),
                             sA[:, cb0:cb0 + 8, dh].un(2).bc([64, 8, 64]), ALU.add)
                        K.tt("pool", RBQ.v, View(msk, NEGI[d].ap.unsqueeze(1).broadcast_to([64, 8, 64])),
                             gc_b.un(2).bc([64, 8, 64]), ALU.subtract)
                        f2 = lambda v_: v_.re("p c i -> p (c i)")
                        pA, pQ, pE, pK = psb(0), psb(1), psb(2), psb(3)
                        K.mm(pA[0:64, :], onesf[0:64, 0:64], f2(Dg.v), start=True, stop=False)
                        K.mm(pA[0:64, :], I64, f2(RBA.v), start=False, stop=True)
                        K.mm(pQ[0:64, :], onesf[0:64, 0:64], f2(Dg.v), start=True, stop=False)
                        K.mm(pQ[0:64, :], I64, f2(RBQ.v), start=False, stop=True)
                        K.mm(pE, onesf[0:64, 0:128], f2(Dg.v))
                        K.act(f2(EA.v), pA[0:64, :], AF.Exp)
                        K.act(f2(EQ.v), pQ[0:64, :], AF.Exp)
                        K.act(erow.v, pE, AF.Exp)
                        K.tt("dve", qdT.v.re("p c i -> p (c i)"), qT[:, b * 512:(b + 1) * 512], erow.v, ALU.mult)
                        for c in range(8):
                            cs = slice(b * 512 + c * 64, b * 512 + (c + 1) * 64)
                            K.mm(pK[0:64, c * 64:(c + 1) * 64], kT[:, cs], kT[:, cs])
                        K.tt("dve", f2(At.v), pK[0:64, :], f2(EA.v), ALU.mult)
                        pK2 = psb(4)
                        for c in range(8):
                            cs = slice(b * 512 + c * 64, b * 512 + (c + 1) * 64)
                            K.mm(pK2[0:64, c * 64:(c + 1) * 64], kT[:, cs], qT[:, cs])
                        K.tt("dve", f2(QKt.v), pK2[0:64, :], f2(EQ.v), ALU.mult)
                        Xa = EQ
                        pX = psb(0)
                        for c in range(8):
                            K.transpose(pX[0:64, c * 64:(c + 1) * 64], At[:, c, :], I64)
                        K.copy("act", f2(Xa.v), pX[0:64, :])
                        K.tt("dve", M.v, View(ident, I64.ap.unsqueeze(1).broadcast_to([64, 8, 64])), At.v, ALU.subtract)
                        Pa, Pt = Xa, At
                        for lvl in range(1, 6):
                            p1, p2, p3 = psb(1), psb(2), psb(3)
                            for c in range(8):
                                K.mm(p1[0:64, c * 64:(c + 1) * 64], Pt[:, c, :], Pa[:, c, :])
                            if lvl < 5:
                                for c in range(8):
                                    K.mm(p2[0:64, c * 64:(c + 1) * 64], Pa[:, c, :], Pt[:, c, :])
                            Pa2 = Dg if lvl % 2 == 1 else RBQ
                            K.copy("act", f2(Pa2.v), p1[0:64, :])
                            if lvl < 5:
                                Pt2 = RBA if lvl % 2 == 1 else EA
                                K.copy("dve", f2(Pt2.v), p2[0:64, :])
                            for c in range(8):
                                K.mm(p3[0:64, c * 64:(c + 1) * 64], Pa2[:, c, :], M[:, c, :])
                            K.tt("dve", f2(M.v), p3[0:64, :], f2(M.v), ALU.add)
                            Pa = Pa2
                            if lvl < 5:
                                Pt = Pt2
                        K.copy("act", Tt.v, M.v)
                        K.tt("pool", kgn.v, k_c[:, cb0:cb0 + 8, :], eg[:, cb0:cb0 + 8, dh].un(2).bc([64, 8, 128]), ALU.mult)
                        K.ts("pool", kgn.v, kgn.v, -1.0, ALU.mult)
                        K.tt("pool", kdec.v, k_c[:, cb0:cb0 + 8, :], ekd[:, cb0:cb0 + 8, dh].un(2).bc([64, 8, 128]), ALU.mult)
                        pW = psb(5)
                        for c in range(8):
                            K.mm(pW[:, c * 64:(c + 1) * 64], kgn[:, c, :], Tt[:, c, :])
                        K.copy("act", WTn.v.re("p c i -> p (c i)"), pW)
                        pO = psb(6)
                        corder = list(range(8)) if d == 0 else list(range(7, -1, -1))
                        for c in corder:
                            cg = cb0 + c
                            seq = 0 if cg < 4 else (1 if cg < 8 else 2)
                            first = (cg == SEQ[seq][0]) if d == 0 else (cg == SEQ[seq][1] - 1)
                            last = (cg == SEQ[seq][1] - 1) if d == 0 else (cg == SEQ[seq][0])
                            if first:
                                if seq < 2:
                                    K.memset("dve", S.v, 0.0)
                                else:
                                    K.dma(S.v, (I["state_f"] if d == 0 else I["state_b"])[j, h])
                                K.copy("act", Sb.v, S.v)
                            pv = psb(7)
                            K.mm(pv[0:64, 0:128], Tt[:, c, :], v_c[:, cg, :], start=True, stop=False)
                            K.mm(pv[0:64, 0:128], WTn[:, c, :], Sb.v, start=False, stop=True)
                            K.ts("dve", vnew.v, pv[0:64, 0:128], beta[:, cg, dh:dh + 1], ALU.mult)
                            K.mm(pO[:, c * 64:(c + 1) * 64], Sb.v, qdT[:, c, :], start=True, stop=False)
                            K.mm(pO[:, c * 64:(c + 1) * 64], vnew.v, QKt[:, c, :], start=False, stop=True)
                            K.mm(pv[:, 128:256], kdec[:, c, :], vnew.v)
                            K.stt("dve", S.v, S.v, egt[:, cg, dh:dh + 1], pv[:, 128:256], ALU.mult, ALU.add)
                            if last and seq < 2:
                                K.dma((O["o_sf"] if d == 0 else O["o_sb"])[seq, j, h], S.v)
                            if not last:
                                K.copy("act", Sb.v, S.v)
                        if d == 0:
                            K.copy("act", oacc[:, b * 512:(b + 1) * 512], pO)
                        else:
                            K.tt("dve", oacc[:, b * 512:(b + 1) * 512], pO, oacc[:, b * 512:(b + 1) * 512], ALU.add)
                R.top = top_h
                scr = norm_scr(light=True)
                wz = R.alloc((128, 8, 128), BF16, "wz")
                woh = R.alloc((128, D), BF16, "gwoh")
                og = R.alloc((128, T), BF16, "og")
                zs = R.alloc((128, 512), F32, "zs")
                on = R.alloc((128, 512), F32, "on")
                K.dma(wz.v, I["gdn_w_in"][j].re("(k p) n -> p k n", p=128)[:, :, 3072 + h * 128:3072 + (h + 1) * 128], eng="pool")
                K.dma(woh.v, I["gdn_w_out"][j][h * 128:(h + 1) * 128, :], eng="pool")
                for b, (c0, n, cond) in enumerate(TB):
                    pb = psb(b % 2)
                    for kc in range(8):
                        K.mm(pb, wz[:, kc, :], HT[kc][b].v, start=(kc == 0), stop=(kc == 7))
                    K.act(zs.v, pb, AF.Silu)
                    fm_norm(on.v, oacc[:, c0:c0 + 512], 128, 512, cw[:, 72:73], 128, scr, None)
                    K.tt("dve", og[:, c0:c0 + 512], on.v, zs.v, ALU.mult)
                out_accum(i, [og[:, c0:c0 + 512] for (c0, n, cond) in TB], woh.v)

        if depth:
            load_gains(0)
            run(adaln(0))
        cnt = {0: 0, 1: 0, 2: 0}
        for i, kind in enumerate(kinds):
            j = cnt[kind]
            cnt[kind] += 1
            R.reset()
            make_coef(i)
            mod_norm(i, 0)
            if kind == 2:
                gqa_layer(i, j)
            elif kind == 1:
                mla_layer(i, j)
            elif kind == 0:
                gdn_layer(i, j)
            R.reset()
            mod_norm(i, 1)
            bg = None
            if i + 1 < depth:
                def bgf(i=i):
                    yield from adaln(i + 1)
                bg = bgf()
            if do_mlp:
                mlp(i, bg)
            elif bg is not None:
                run(bg)
            if i + 1 < depth:
                load_gains(i + 1)

        R.reset()
        ost = [R.alloc((128, D), F32, "xo%d" % q) for q in range(2)]
        for t in range(int(os.environ.get("KNT_OUT", str(NT_DBG)))):
            b, off = divmod(t * 128, 512)
            o_ = ost[(t + int(os.environ.get("KSL", "0"))) % 2]
            for g in range(2):
                pb = psb((t * 2 + g + int(os.environ.get("KOFF", "0"))) % int(os.environ.get("KNB", "4")))
                for q in range(4):
                    kc = g * 4 + q
                    K.transpose(pb[:, q * 128:(q + 1) * 128], XT[kc][b][:, off:off + 128], ident.v)
                K.copy("dve" if g == 0 else "act", o_[:, g * 512:(g + 1) * 512], pb)
            dst = O["yp"][t * 128:(t + 1) * 128, :] if t < 4 else O["ys"][(t - 4) * 128:(t - 3) * 128, :]
            K.dma(dst, o_.v)
        K.emit()
    return nc


def make_in_maps(inp, kinds=KINDS_FULL):
    f = lambda a: np.ascontiguousarray(np.asarray(a, dtype=np.float32))
    depth = max(len(kinds), 1)
    n_a = sum(1 for k in kinds if k == 0)
    n_b = sum(1 for k in kinds if k == 1)
    n_c = sum(1 for k in kinds if k == 2)
    shared = {
        "norm_mix": f(inp["norm_mix"])[:depth].reshape(depth, 8, 128),
        "norm_mlp": f(inp["norm_mlp"])[:depth].reshape(depth, 8, 128),
        "w_mod": f(inp["w_mod"])[:depth],
        "b_mod": f(inp["b_mod"])[:depth].reshape(depth, 48, 128),
        "w_mlp_in": f(inp["w_mlp_in"])[:depth],
        "w_mlp_out": f(inp["w_mlp_out"])[:depth],
        "ident": np.eye(128, dtype=np.float32),
        "ones": np.ones((128, 128), np.float32),
    }
    if n_c:
        cos, sin, pT = rope_tables(32, 2048)
        shared.update({
            "gqa_w_in": f(inp["gqa_w_in"])[:n_c], "gqa_q_norm": f(inp["gqa_q_norm"])[:n_c],
            "gqa_k_norm": f(inp["gqa_k_norm"])[:n_c], "gqa_w_out": f(inp["gqa_w_out"])[:n_c],
            "rope_c_cos": cos, "rope_c_sin": sin, "rope_c_p": pT})
    if n_b:
        cos, sin, pT = rope_tables(16, 2048)
        shared.update({
            "mla_w_down": f(inp["mla_w_down"])[:n_b],
            "mla_q_lat_norm": f(inp["mla_q_lat_norm"])[:n_b].reshape(n_b, 3, 128),
            "mla_kv_lat_norm": f(inp["mla_kv_lat_norm"])[:n_b].reshape(n_b, 2, 128),
            "mla_w_uq": f(inp["mla_w_uq"])[:n_b], "mla_w_ukv": f(inp["mla_w_ukv"])[:n_b],
            "mla_qn_nope": f(inp["mla_qn_nope"])[:n_b], "mla_qn_rope": f(inp["mla_qn_rope"])[:n_b],
            "mla_kn_nope": f(inp["mla_kn_nope"])[:n_b], "mla_kn_rope": f(inp["mla_kn_rope"])[:n_b],
            "mla_w_out": f(inp["mla_w_out"])[:n_b],
            "rope_b_cos": cos, "rope_b_sin": sin, "rope_b_p": pT})
    if n_a:
        shared.update(gdn_host_inputs(inp, n_a))
    maps = []
    xp = f(inp["x_prompt"])
    xs = f(inp["x_sample"])
    for c in range(NCORES):
        m = dict(shared)
        m["xp"] = xp[2 * c:2 * c + 2].reshape(512, D)
        m["xs"] = xs[c]
        m["cond"] = np.stack([f(inp["c_ctx"]).reshape(8, 128), f(inp["c"])[c].reshape(8, 128)], 0)
        if n_c:
            m["cache_gqa_k"] = f(inp["cache_gqa_k"])[c, :n_c].reshape(n_c, 512, 256)
            m["cache_gqa_v"] = f(inp["cache_gqa_v"])[c, :n_c].reshape(n_c, 512, 256)
        if n_b:
            m["cache_mla_ckv"] = f(inp["cache_mla_ckv"])[c, :n_b]
            m["cache_mla_krope"] = f(inp["cache_mla_krope"])[c, :n_b]
        if n_a:
            m["state_f"] = f(inp["state_gdn_fwd"])[c, :n_a]
            m["state_b"] = f(inp["state_gdn_bwd"])[c, :n_a]
        maps.append(m)
    return maps


def gdn_host_inputs(inp, n_a):
    f = lambda a: np.ascontiguousarray(np.asarray(a, dtype=np.float32))
    w = f(inp["gdn_w_in"])[:n_a]
    idx = np.arange(64)
    NEG = -30000.0
    m = np.zeros((8, 128, 128), np.float32)
    jj, ii = np.meshgrid(idx, idx, indexing="ij")
    m[0, :64, :64] = (jj <= ii)
    m[1, :64, :64] = (jj >= ii)
    m[2, :64, :64] = np.where(ii > jj, 0.0, NEG)
    m[3, :64, :64] = np.where(ii >= jj, 0.0, NEG)
    m[4, :64, :64] = np.where(ii < jj, 0.0, NEG)
    m[5, :64, :64] = np.where(ii <= jj, 0.0, NEG)
    return {
        "gdn_w_in": np.ascontiguousarray(w[:, :, :4096]),
        "gdn_w_gb": np.ascontiguousarray(w[:, :, 4096:4128]),
        "gdn_conv": f(inp["gdn_conv"])[:n_a].reshape(n_a, 72, 128),
        "gdn_a_log": f(inp["gdn_a_log"])[:n_a].reshape(n_a, 16),
        "gdn_dt_bias": f(inp["gdn_dt_bias"])[:n_a].reshape(n_a, 16),
        "gdn_out_norm": f(inp["gdn_out_norm"])[:n_a],
        "gdn_w_out": f(inp["gdn_w_out"])[:n_a],
        "gdn_masks": m,
    }


def assemble(results, kinds=KINDS_FULL):
    n_a = sum(1 for k in kinds if k == 0)
    n_b = sum(1 for k in kinds if k == 1)
    n_c = sum(1 for k in kinds if k == 2)
    yp = np.concatenate([r["yp"].reshape(2, 256, D) for r in results], 0)
    ys = np.stack([r["ys"] for r in results], 0)
    sf = np.concatenate([r["o_sf"][:, :n_a] for r in results], 0)
    sbw = np.concatenate([r["o_sb"][:, :n_a] for r in results], 0)
    ckv = np.concatenate([r["o_ckv"][:, :n_b] for r in results], 0)
    kr = np.concatenate([r["o_kr"][:, :n_b] for r in results], 0)
    gk = np.concatenate([r["o_gk"][:, :n_c].reshape(2, n_c, 256, 2, 128) for r in results], 0)
    gv = np.concatenate([r["o_gv"][:, :n_c].reshape(2, n_c, 256, 2, 128) for r in results], 0)
    return (yp, ys, sf, sbw, ckv, kr, gk, gv)


_NC_CACHE = {}


def kernel(**inputs):
    kinds = KINDS_FULL
    if kinds not in _NC_CACHE:
        _NC_CACHE[kinds] = build(kinds)
    nc = _NC_CACHE[kinds]
    maps = make_in_maps(inputs, kinds)
    res = run_bass_kernel_spmd(nc, maps, core_ids=list(range(NCORES)))
    return assemble(res.results, kinds)
```

```python
import contextlib
import math
import numpy as np
import concourse.bass as bass
import concourse.mybir as mybir
from concourse.bass_utils import run_bass_kernel_spmd

F32 = mybir.dt.float32
BF16 = mybir.dt.bfloat16
AF = mybir.ActivationFunctionType
ALU = mybir.AluOpType
AX = mybir.AxisListType

D = 1024
NKC = 8
T = 2560
NCORES = 8
EPS = 1e-6
TB = [(0, 512, 0), (512, 512, 1), (1024, 512, 1), (1536, 512, 1), (2048, 512, 1)]
KINDS_FULL = (0, 1, 2, 0)


class Buf:
    __slots__ = ("ap", "name", "last_write", "reads", "excl")

    def __init__(self, ap, name="", excl=False):
        self.ap = ap
        self.name = name
        self.last_write = None
        self.reads = []
        self.excl = excl

    def __getitem__(self, idx):
        return View(self, self.ap[idx])

    @property
    def v(self):
        return View(self, self.ap)


class View:
    __slots__ = ("buf", "ap")

    def __init__(self, buf, ap):
        self.buf = buf
        self.ap = ap

    def __getitem__(self, idx):
        return View(self.buf, self.ap[idx])

    def re(self, pat, **kw):
        return View(self.buf, self.ap.rearrange(pat, **kw))

    def bc(self, shape):
        return View(self.buf, self.ap.broadcast_to(shape))

    def un(self, axis):
        return View(self.buf, self.ap.unsqueeze(axis))


class Op:
    __slots__ = ("eng", "fn", "deps", "flag", "ticket", "dma", "dsem", "dval", "dprev", "idx")

    def __init__(self, eng, fn):
        self.eng = eng
        self.fn = fn
        self.deps = []
        self.flag = False
        self.ticket = 0
        self.dma = False
        self.dsem = None
        self.dval = 0
        self.dprev = None
        self.idx = 0


ENGS = ("pe", "act", "dve", "pool", "sp")
import os
DUMP = os.environ.get("KDUMP", "") == "1"


def compress(ops):
    best = {}
    for o in ops:
        key = ("d", o.dsem) if o.dma else ("e", o.eng)
        b = best.get(key)
        if b is None or o.idx > b.idx:
            best[key] = o
    return list(best.values())


class Sched:
    def __init__(self, nc, n_dma_sems=32):
        self.nc = nc
        self.ops = {e: [] for e in ENGS}
        self.n_dma_sems = n_dma_sems
        self.dma_count = 0
        self.dma_last = [None] * n_dma_sems
        self.dma_vals = [0] * n_dma_sems
        self.nops = 0

    def op(self, eng, fn, reads=(), writes=(), dma=False):
        o = Op(eng, fn)
        o.idx = self.nops
        self.nops += 1
        deps = {}
        for r in reads:
            b = r.buf if isinstance(r, View) else r
            if b.last_write is not None:
                deps[id(b.last_write)] = (b.last_write, True)
            if b.excl:
                for rd in b.reads:
                    if rd.eng != eng and id(rd) not in deps:
                        deps[id(rd)] = (rd, False)
        for w in writes:
            b = w.buf if isinstance(w, View) else w
            if b.last_write is not None and id(b.last_write) not in deps:
                deps[id(b.last_write)] = (b.last_write, False)
            for rd in b.reads:
                if id(rd) not in deps:
                    deps[id(rd)] = (rd, False)
        for d, strong in deps.values():
            if d is o:
                continue
            if d.eng == eng and not d.dma and not dma:
                if eng == "pe" or not strong:
                    continue
            o.deps.append(d)
            d.flag = True
        for w in writes:
            b = w.buf if isinstance(w, View) else w
            b.last_write = o
            b.reads = []
        for r in reads:
            b = r.buf if isinstance(r, View) else r
            if b.last_write is not o:
                b.reads.append(o)
                if len(b.reads) > 24:
                    b.reads = compress(b.reads)
        if dma:
            o.dma = True
            s = self.dma_count % self.n_dma_sems
            self.dma_count += 1
            o.dsem = s
            self.dma_vals[s] += 16
            o.dval = self.dma_vals[s]
            o.dprev = self.dma_last[s]
            self.dma_last[s] = o
        self.ops[eng].append(o)
        return o

    def mm(self, out, lhsT, rhs, start=True, stop=True):
        return self.op("pe", lambda e: e.matmul(out.ap, lhsT.ap, rhs.ap, start=start, stop=stop),
                       reads=[lhsT, rhs] + ([] if start else [out]), writes=[out])

    def transpose(self, out, in_, ident):
        return self.op("pe", lambda e: e.transpose(out.ap, in_.ap, ident.ap), reads=[in_, ident], writes=[out])

    def act(self, out, in_, func, bias=None, scale=None, accum=None):
        kw = {}
        reads = [in_]
        writes = [out]
        if bias is not None:
            if isinstance(bias, View):
                kw["bias"] = bias.ap
                reads.append(bias)
            else:
                kw["bias"] = bias
        if scale is not None:
            if isinstance(scale, View):
                kw["scale"] = scale.ap
                reads.append(scale)
            else:
                kw["scale"] = scale
        if accum is not None:
            kw["accum_out"] = accum.ap
            writes.append(accum)
        return self.op("act", lambda e: e.activation(out.ap, in_.ap, func, **kw), reads=reads, writes=writes)

    def tt(self, eng, out, a, b, op):
        return self.op(eng, lambda e: e.tensor_tensor(out.ap, a.ap, b.ap, op), reads=[a, b], writes=[out])

    def ts(self, eng, out, a, s1, op0, s2=None, op1=None):
        reads = [a]
        s1a = s1.ap if isinstance(s1, View) else s1
        s2a = s2.ap if isinstance(s2, View) else s2
        if isinstance(s1, View):
            reads.append(s1)
        if isinstance(s2, View):
            reads.append(s2)
        kw = {}
        if op1 is not None:
            kw["op1"] = op1
        return self.op(eng, lambda e: e.tensor_scalar(out.ap, a.ap, s1a, s2a, op0, **kw), reads=reads, writes=[out])

    def stt(self, eng, out, a, s, b, op0, op1):
        reads = [a, b]
        sa = s.ap if isinstance(s, View) else s
        if isinstance(s, View):
            reads.append(s)
        return self.op(eng, lambda e: e.scalar_tensor_tensor(out.ap, a.ap, sa, b.ap, op0, op1),
                       reads=reads, writes=[out])

    def copy(self, eng, out, in_):
        if eng == "act":
            return self.op(eng, lambda e: e.copy(out.ap, in_.ap), reads=[in_], writes=[out])
        return self.op(eng, lambda e: e.tensor_copy(out.ap, in_.ap), reads=[in_], writes=[out])

    def recip(self, out, in_):
        return self.op("dve", lambda e: e.reciprocal(out.ap, in_.ap), reads=[in_], writes=[out])

    def reduce(self, eng, out, in_, op=None):
        return self.op(eng, lambda e: e.tensor_reduce(out.ap, in_.ap, AX.X, op or ALU.add), reads=[in_], writes=[out])

    def memset(self, eng, out, val):
        return self.op(eng, lambda e: e.memset(out.ap, val), writes=[out])

    def dma(self, out, in_, eng="sp"):
        return self.op(eng, lambda e: e.dma_start(out=out.ap, in_=in_.ap), reads=[in_], writes=[out], dma=True)

    def emit(self):
        nc = self.nc
        for e in ENGS:
            c = 0
            for o in self.ops[e]:
                if o.dma:
                    continue
                if o.flag:
                    c += 1
                    o.ticket = c
        with contextlib.ExitStack() as st:
            esems = {e: st.enter_context(nc.semaphore("s_" + e)) for e in ENGS}
            dsems = [st.enter_context(nc.semaphore("d%d" % i)) for i in range(self.n_dma_sems)]
            block = st.enter_context(nc.Block())
            sched = self

            def make(ename):
                def body(eng):
                    waited = {}
                    for o in sched.ops[ename]:
                        ws = []
                        for d in o.deps:
                            if d.dma:
                                ws.append((("d", d.dsem), dsems[d.dsem], d.dval))
                            else:
                                ws.append((("e", d.eng), esems[d.eng], d.ticket))
                        if o.dma and o.dprev is not None:
                            d = o.dprev
                            ws.append((("d", d.dsem), dsems[d.dsem], d.dval))
                        for key, sem, val in ws:
                            if waited.get(key, 0) >= val:
                                continue
                            waited[key] = val
                            eng.wait_ge(sem, val)
                            if DUMP:
                                print("   ", ename, "wait", key, val)
                        ins = o.fn(eng)
                        if DUMP:
                            print(ename, o.idx, "dma" if o.dma else "", ("inc d%d->%d" % (o.dsem, o.dval)) if o.dma else ("inc e->%d" % o.ticket if o.flag else ""), str(ins)[:150])
                        if o.dma:
                            ins.then_inc(dsems[o.dsem], 16)
                        elif o.flag:
                            ins.then_inc(esems[ename], 1)
                    if ename == "sp":
                        for s in range(sched.n_dma_sems):
                            if sched.dma_vals[s] > 0 and waited.get(("d", s), 0) < sched.dma_vals[s]:
                                eng.wait_ge(dsems[s], sched.dma_vals[s])
                return body

            block.tensor(make("pe"))
            block.scalar(make("act"))
            block.vector(make("dve"))
            block.gpsimd(make("pool"))
            block.sync(make("sp"))


class Region:
    def __init__(self, ap_f32, nbytes):
        self.ap = ap_f32
        self.nbytes = nbytes
        self.live = []
        self.top = 0

    def reset(self):
        self.top = 0

    def alloc(self, shape, dt, name=""):
        esz = 2 if dt == BF16 else 4
        free = int(np.prod(shape[1:]))
        nb = (free * esz + 31) // 32 * 32
        s = self.top
        e = s + nb
        assert e <= self.nbytes, (name, e, self.nbytes)
        self.top = e
        ap = self.ap[0:shape[0], s // 4:e // 4]
        if dt == BF16:
            ap = ap.bitcast(BF16)
        ap = ap[:, 0:free]
        if len(shape) == 3:
            ap = ap.rearrange("p (a b) -> p a b", a=shape[1])
        elif len(shape) == 4:
            ap = ap.rearrange("p (a b c) -> p a b c", a=shape[1], b=shape[2])
        b = Buf(ap, name)
        inherit = []
        keep = []
        for (ps, pe, pb) in self.live:
            if ps < e and s < pe:
                inherit.extend(pb.reads)
                if pb.last_write is not None:
                    inherit.append(pb.last_write)
                if ps >= s and pe <= e:
                    continue
            keep.append((ps, pe, pb))
        keep.append((s, e, b))
        self.live = keep
        b.reads = compress(inherit)
        return b


def rope_tables(half, n_tok_rows):
    rows = 2048 // 64
    row = np.repeat(np.arange(rows, dtype=np.float32), 64)
    col = np.tile(np.arange(64, dtype=np.float32), rows)
    freqs = (10000.0 ** (-np.arange(half, dtype=np.float32) / half)).astype(np.float32)
    ang_r = row[None, :] * freqs[:, None]
    ang_c = col[None, :] * freqs[:, None]
    cos = np.concatenate([np.cos(ang_r), np.cos(ang_r), np.cos(ang_c), np.cos(ang_c)], 0).astype(np.float32)
    sin = np.concatenate([np.sin(ang_r), np.sin(ang_r), np.sin(ang_c), np.sin(ang_c)], 0).astype(np.float32)
    n = 4 * half
    P = np.zeros((n, n), np.float32)
    for blk in range(2):
        o = blk * 2 * half
        for d in range(half):
            P[o + d, o + d + half] = -1.0
            P[o + d + half, o + d] = 1.0
    return cos, sin, np.ascontiguousarray(P.T)


def build(kinds=KINDS_FULL, do_mlp=True):
    nc = bass.Bass("TRN2", target_bir_lowering=False)
    n_a = sum(1 for k in kinds if k == 0)
    n_b = sum(1 for k in kinds if k == 1)
    n_c = sum(1 for k in kinds if k == 2)
    depth = len(kinds)

    def din(name, shape):
        return Buf(nc.dram_tensor(name, list(shape), F32, kind="ExternalInput").ap(), name)

    def dout(name, shape):
        return Buf(nc.dram_tensor(name, list(shape), F32, kind="ExternalOutput").ap(), name)

    I = {}
    I["xp"] = din("xp", (512, D))
    I["xs"] = din("xs", (2048, D))
    I["cond"] = din("cond", (2, 8, 128))
    dd = max(depth, 1)
    I["norm_mix"] = din("norm_mix", (dd, 8, 128))
    I["norm_mlp"] = din("norm_mlp", (dd, 8, 128))
    I["w_mod"] = din("w_mod", (dd, D, 6 * D))
    I["b_mod"] = din("b_mod", (dd, 48, 128))
    I["w_mlp_in"] = din("w_mlp_in", (dd, D, 4 * D))
    I["w_mlp_out"] = din("w_mlp_out", (dd, 4 * D, D))
    I["ident"] = din("ident", (128, 128))
    I["ones"] = din("ones", (128, 128))
    if n_c:
        I["gqa_w_in"] = din("gqa_w_in", (n_c, D, 1536))
        I["gqa_q_norm"] = din("gqa_q_norm", (n_c, 128))
        I["gqa_k_norm"] = din("gqa_k_norm", (n_c, 128))
        I["gqa_w_out"] = din("gqa_w_out", (n_c, D, D))
        I["cache_gqa_k"] = din("cache_gqa_k", (n_c, 512, 256))
        I["cache_gqa_v"] = din("cache_gqa_v", (n_c, 512, 256))
        I["rope_c_cos"] = din("rope_c_cos", (128, 2048))
        I["rope_c_sin"] = din("rope_c_sin", (128, 2048))
        I["rope_c_p"] = din("rope_c_p", (128, 128))
    if n_b:
        I["mla_w_down"] = din("mla_w_down", (n_b, D, 704))
        I["mla_q_lat_norm"] = din("mla_q_lat_norm", (n_b, 3, 128))
        I["mla_kv_lat_norm"] = din("mla_kv_lat_norm", (n_b, 2, 128))
        I["mla_w_uq"] = din("mla_w_uq", (n_b, 384, 1536))
        I["mla_w_ukv"] = din("mla_w_ukv", (n_b, 256, 2048))
        I["mla_qn_nope"] = din("mla_qn_nope", (n_b, 128))
        I["mla_qn_rope"] = din("mla_qn_rope", (n_b, 64))
        I["mla_kn_nope"] = din("mla_kn_nope", (n_b, 128))
        I["mla_kn_rope"] = din("mla_kn_rope", (n_b, 64))
        I["mla_w_out"] = din("mla_w_out", (n_b, D, D))
        I["cache_mla_ckv"] = din("cache_mla_ckv", (n_b, 512, 256))
        I["cache_mla_krope"] = din("cache_mla_krope", (n_b, 512, 64))
        I["rope_b_cos"] = din("rope_b_cos", (64, 2048))
        I["rope_b_sin"] = din("rope_b_sin", (64, 2048))
        I["rope_b_p"] = din("rope_b_p", (64, 64))
    if n_a:
        I["gdn_w_in"] = din("gdn_w_in", (n_a, D, 4096))
        I["gdn_w_gb"] = din("gdn_w_gb", (n_a, D, 32))
        I["gdn_conv"] = din("gdn_conv", (n_a, 72, 128))
        I["gdn_a_log"] = din("gdn_a_log", (n_a, 16))
        I["gdn_dt_bias"] = din("gdn_dt_bias", (n_a, 16))
        I["gdn_out_norm"] = din("gdn_out_norm", (n_a, 128))
        I["gdn_w_out"] = din("gdn_w_out", (n_a, D, D))
        I["state_f"] = din("state_f", (n_a, 8, 128, 128))
        I["state_b"] = din("state_b", (n_a, 8, 128, 128))
        I["gdn_masks"] = din("gdn_masks", (8, 128, 128))

    O = {}
    O["yp"] = dout("yp", (512, D))
    O["ys"] = dout("ys", (2048, D))
    O["o_sf"] = dout("o_sf", (2, max(n_a, 1), 8, 128, 128))
    O["o_sb"] = dout("o_sb", (2, max(n_a, 1), 8, 128, 128))
    O["o_ckv"] = dout("o_ckv", (2, max(n_b, 1), 256, 256))
    O["o_kr"] = dout("o_kr", (2, max(n_b, 1), 256, 64))
    O["o_gk"] = dout("o_gk", (2, max(n_c, 1), 256, 256))
    O["o_gv"] = dout("o_gv", (2, max(n_c, 1), 256, 256))

    st = contextlib.ExitStack()
    with st:
        K = Sched(nc)

        def sb(name, shape, dt):
            return Buf(st.enter_context(nc.sbuf_tensor("sb_" + name, list(shape), dt)).ap(), name)

        XT = [[sb("x%d_%d" % (kc, b), (128, 512), F32) for b in range(5)] for kc in range(NKC)]
        HT = [[sb("h%d_%d" % (kc, b), (128, 512), BF16) for b in range(5)] for kc in range(NKC)]
        ident = sb("ident", (128, 128), F32)
        identb = sb("identb", (128, 128), BF16)
        onesf = sb("onesf", (128, 128), F32)
        onesb = sb("onesb", (128, 128), BF16)
        condT = sb("condT", (128, 2, 8), F32)
        modT = [sb("modT%d" % i, (128, 48, 2), F32) for i in range(2)]
        coefA = sb("coefA", (128, 2, 8, 2), F32)
        gnorm = sb("gnorm", (128, 2, 8), F32)
        epsv = sb("epsv", (128, 1), F32)
        WR_BYTES = 85 * 1024
        wr_t = sb("wr", (128, WR_BYTES // 4), F32)
        R = Region(wr_t.ap, WR_BYTES)
        PS = [Buf(st.enter_context(nc.psum_tensor("ps%d" % i, [128, 512], F32)).ap(), "ps%d" % i, excl=True) for i in range(8)]

        def psb(i, dt=F32):
            if dt == BF16:
                return View(PS[i], PS[i].ap.bitcast(BF16))
            return PS[i].v

        K.dma(ident.v, I["ident"].v)
        K.dma(onesf.v, I["ones"].v)
        K.dma(identb.v, I["ident"].v, eng="pool")
        K.dma(onesb.v, I["ones"].v, eng="pool")
        K.memset("dve", epsv.v, EPS)

        R.reset()
        xst = [R.alloc((128, D), F32, "xst%d" % i) for i in range(2)]
        import os
        NT_DBG = int(os.environ.get("KNT", "20"))
        for t in range(NT_DBG):
            src = I["xp"][t * 128:(t + 1) * 128, :] if t < 4 else I["xs"][(t - 4) * 128:(t - 3) * 128, :]
            s_ = xst[(t + int(os.environ.get("KSL", "0"))) % 2]
            if os.environ.get("KV", "") == "a":
                src = I["xp"][0:128, :]
            if os.environ.get("KV", "") == "b":
                s_ = xst[0]
            K.dma(s_.v, src, eng=os.environ.get("KQ", "sp"))
            if os.environ.get("KV", "") == "c" and t == 1:
                continue
            b, off = divmod(t * 128, 512)
            for g in range(2):
                pb = psb((t * 2 + g + int(os.environ.get("KOFF", "0"))) % int(os.environ.get("KNB", "4")))
                for q in range(4):
                    kc = g * 4 + q
                    K.transpose(pb[:, q * 128:(q + 1) * 128], s_[:, kc * 128:(kc + 1) * 128], ident.v)
                for q in range(4):
                    kc = g * 4 + q
                    eng = "dve" if q % 2 == 0 else "act"
                    if os.environ.get("KV", "") == "d":
                        eng = "dve"
                    if os.environ.get("KV", "") == "e":
                        eng = "act"
                    K.copy(eng, XT[kc][b][:, off:off + 128], pb[:, q * 128:(q + 1) * 128])

        import os
        if os.environ.get("KDBG", "") != "1":
            cst = R.alloc((16, 128), F32, "cst")
            K.dma(cst.v, I["cond"].v.re("c k p -> (c k) p"))
            K.transpose(psb(4)[:, 0:16], cst.v, ident[0:16, 0:16])
            K.act(condT.v.re("p c k -> p (c k)"), psb(4)[:, 0:16], AF.Silu)

        def adaln(i):
            mt = modT[i % 2]
            bst = R.alloc((48, 128), F32, "bst")
            bT = R.alloc((128, 48), F32, "bT")
            K.dma(bst.v, I["b_mod"][i])
            K.transpose(psb(4)[:, 0:48], bst.v, ident[0:48, 0:48])
            K.copy("dve", bT.v, psb(4)[:, 0:48])
            wst = [R.alloc((128, 8, 256), F32, "wmod%d" % q) for q in range(2)]
            for cb in range(24):
                w_ = wst[cb % 2]
                K.dma(w_.v, I["w_mod"][i].re("(k p) n -> p k n", p=128)[:, :, cb * 256:(cb + 1) * 256])
                pb = psb(5)
                for fl in range(2):
                    f = cb * 2 + fl
                    for kc in range(8):
                        K.mm(pb[:, f * 2:f * 2 + 2], w_[:, kc, fl * 128:(fl + 1) * 128], condT[:, :, kc],
                             start=(kc == 0), stop=(kc == 7))
                if cb % 2 == 1:
                    yield
            K.tt("dve", mt.v, psb(5)[:, 0:96].re("p (f c) -> p f c", c=2), bT.v.un(2).bc([128, 48, 2]), ALU.add)
            yield

        def run(gen):
            for _ in gen:
                pass

        def load_gains(i):
            gst = R.alloc((16, 128), F32, "gst")
            K.dma(gst[0:8, :], I["norm_mix"][i])
            K.dma(gst[8:16, :], I["norm_mlp"][i])
            K.transpose(psb(4)[:, 0:16], gst.v, ident[0:16, 0:16])
            K.copy("dve", gnorm.v.re("p w k -> p (w k)"), psb(4)[:, 0:16])

        def make_coef(i):
            mt = modT[i % 2]
            for w in range(2):
                sc = mt[:, (3 * w + 1) * 8:(3 * w + 2) * 8, :]
                K.ts("dve", coefA[:, w], sc, 1.0, ALU.add)
                K.tt("dve", coefA[:, w], coefA[:, w], gnorm[:, w].un(2).bc([128, 8, 2]), ALU.mult)

        def mod_norm(i, w):
            mt = modT[i % 2]
            sq = [R.alloc((128, 512), BF16, "sq%d" % q) for q in range(2)]
            rs = [R.alloc((128, 512), F32, "rs%d" % q) for q in range(2)]
            tmp = [R.alloc((128, 512), F32, "ntmp%d" % q) for q in range(2)]
            for b, (c0, n, cond) in enumerate(TB):
                pb = psb(b % 2)
                for kc in range(8):
                    s_ = sq[kc % 2]
                    if kc % 2 == 0:
                        K.act(s_.v, XT[kc][b].v, AF.Square)
                    else:
                        K.tt("pool", s_.v, XT[kc][b].v, XT[kc][b].v, ALU.mult)
                    K.mm(pb, onesb.v, s_.v, start=(kc == 0), stop=(kc == 7))
                r_ = rs[b % 2]
                K.act(r_.v, pb, AF.Ln, bias=epsv.v, scale=1.0 / D)
                K.act(r_.v, r_.v, AF.Exp, scale=-0.5)
                for kc in range(8):
                    t_ = tmp[kc % 2]
                    K.stt("dve", t_.v, XT[kc][b].v, coefA[:, w, kc, cond:cond + 1], r_.v, ALU.mult, ALU.mult)
                    K.act(HT[kc][b].v, t_.v, AF.Identity, bias=mt[:, 3 * w * 8 + kc, cond:cond + 1])

        def mlp(i, bg):
            mt = modT[i % 2]
            win = [R.alloc((128, 8, 512), BF16, "win%d" % q) for q in range(2)]
            wout = [R.alloc((128, 4, D), BF16, "wout%d" % q) for q in range(2)]
            uu = [R.alloc((128, 4, 512), BF16, "uu%d" % q) for q in range(2)]
            rr = [R.alloc((128, 512), BF16, "rr%d" % q) for q in range(2)]
            it = 0
            for fb in range(8):
                wi = win[fb % 2]
                wo = wout[fb % 2]
                K.dma(wi.v, I["w_mlp_in"][i].re("(k p) n -> p k n", p=128)[:, :, fb * 512:(fb + 1) * 512], eng="pool")
                K.dma(wo.v, I["w_mlp_out"][i][fb * 512:(fb + 1) * 512, :].re("(f p) n -> p f n", p=128), eng="pool")
                for b, (c0, n, cond) in enumerate(TB):
                    u_ = uu[it % 2]
                    for fc in range(4):
                        pb = psb(fc % 2)
                        for kc in range(8):
                            K.mm(pb, wi[:, kc, fc * 128:(fc + 1) * 128], HT[kc][b].v, start=(kc == 0), stop=(kc == 7))
                        r_ = rr[fc % 2]
                        K.act(r_.v, pb, AF.Relu)
                        K.tt("pool", u_[:, fc, :], r_.v, r_.v, ALU.mult)
                    for oc in range(8):
                        pb = psb(2 + oc % 2)
                        for fc in range(4):
                            K.mm(pb, wo[:, fc, oc * 128:(oc + 1) * 128], u_[:, fc, :], start=(fc == 0), stop=(fc == 3))
                        K.stt("dve", XT[oc][b].v, pb, mt[:, 40 + oc, cond:cond + 1], XT[oc][b].v, ALU.mult, ALU.add)
                    it += 1
                    if bg is not None:
                        next(bg, None)
            if bg is not None:
                run(bg)

        def out_accum(i, oT, wo_h):
            mt = modT[i % 2]
            for b, (c0, n, cond) in enumerate(TB):
                for oc in range(8):
                    pb = psb(6 + oc % 2)
                    K.mm(pb, wo_h[:, oc * 128:(oc + 1) * 128], oT[b])
                    K.stt("dve", XT[oc][b].v, pb, mt[:, 16 + oc, cond:cond + 1], XT[oc][b].v, ALU.mult, ALU.add)

        def fm_norm(dst, src_ps, npart, ncols, gain, nfeat, scr, rope=None, ones_v=None, eps_v=None):
            P = npart
            K.act(scr["sq"][0:P, 0:ncols], src_ps[0:P, 0:ncols], AF.Square)
            pd = scr["pd"]
            K.mm(pd[0:P, 0:ncols], onesb[0:P, 0:P], scr["sq"][0:P, 0:ncols])
            K.act(scr["rs"][0:P, 0:ncols], pd[0:P, 0:ncols], AF.Ln, bias=(eps_v or epsv)[0:P, :], scale=1.0 / nfeat)
            K.act(scr["rs"][0:P, 0:ncols], scr["rs"][0:P, 0:ncols], AF.Exp, scale=-0.5)
            if rope is None:
                K.stt("dve", dst, src_ps[0:P, 0:ncols], gain, scr["rs"][0:P, 0:ncols], ALU.mult, ALU.mult)
            else:
                cos, sin, permT = rope
                nb = scr["nb"]
                K.stt("dve", nb[0:P, 0:ncols], src_ps[0:P, 0:ncols], gain, scr["rs"][0:P, 0:ncols], ALU.mult, ALU.mult)
                K.mm(pd[0:P, 0:ncols], permT, nb[0:P, 0:ncols])
                K.tt("pool", scr["t1"][0:P, 0:ncols], nb[0:P, 0:ncols], cos, ALU.mult)
                K.tt("dve", scr["t2"][0:P, 0:ncols], pd[0:P, 0:ncols], sin, ALU.mult)
                K.tt("pool", dst, scr["t1"][0:P, 0:ncols], scr["t2"][0:P, 0:ncols], ALU.add)

        def norm_scr(light=False):
            if light:
                return {"sq": R.alloc((128, 512), BF16, "nsq"), "rs": R.alloc((128, 512), F32, "nrs"), "pd": psb(5)}
            return {"sq": R.alloc((128, 512), BF16, "nsq"), "rs": R.alloc((128, 512), F32, "nrs"),
                    "t1": R.alloc((128, 512), F32, "nt1"), "t2": R.alloc((128, 512), F32, "nt2"),
                    "nb": R.alloc((128, 512), BF16, "nnb"), "pd": psb(5)}

        QB = [(0, 256, [0, 1]), (256, 256, [2, 3])] + [(512 + 512 * q, 512, list(range(4, 24))) for q in range(4)]

        def attn_head(kparts, vtile, qparts, oT, scale, pbufs):
            steps = []
            for qi, (q0, nq, kts) in enumerate(QB):
                for ji, kt in enumerate(kts):
                    steps.append((qi, q0, nq, ji, kt, len(kts)))
            pts = {}

            def issue_s(n):
                qi, q0, nq, ji, kt, nk = steps[n]
                pss = psb(n % 3)
                for pi, (kT, P) in enumerate(kparts):
                    K.mm(pss[:, 0:nq], kT[0:P, kt * 128:(kt + 1) * 128], qparts[pi][0:P, q0:q0 + nq],
                         start=(pi == 0), stop=(pi == len(kparts) - 1))
                pt = pbufs[n % len(pbufs)]
                K.act(pt[:, 0:nq], pss[:, 0:nq], AF.Exp, scale=scale)
                pts[n] = pt

            def issue_pv(n):
                qi, q0, nq, ji, kt, nk = steps[n]
                po, pdn = (psb(3), psb(4)) if qi % 2 == 0 else (psb(5), psb(6))
                pt = pts.pop(n)
                K.mm(po[:, 0:nq], vtile(kt), pt[:, 0:nq], start=(ji == 0), stop=(ji == nk - 1))
                K.mm(pdn[:, 0:nq], onesb.v, pt[:, 0:nq], start=(ji == 0), stop=(ji == nk - 1))
                if ji == nk - 1:
                    rd = pbufs_rd[qi % 2]
                    K.act(rd[:, 0:nq], pdn[:, 0:nq], AF.Ln)
                    K.act(rd[:, 0:nq], rd[:, 0:nq], AF.Exp, scale=-1.0)
                    K.tt("dve", oT[:, q0:q0 + nq], po[:, 0:nq], rd[:, 0:nq], ALU.mult)
            SK = 2
            for n in range(len(steps) + SK):
                if n < len(steps):
                    issue_s(n)
                if n - SK >= 0:
                    issue_pv(n - SK)

        pbufs_rd = [None, None]

        def gqa_layer(i, j):
            R.reset()
            kT = R.alloc((128, 2, 3072), BF16, "kT")
            V = R.alloc((128, 24, 256), BF16, "V")
            cos = R.alloc((128, 2048), BF16, "cos")
            sin = R.alloc((128, 2048), BF16, "sin")
            permT = R.alloc((128, 128), BF16, "permT")
            kg = R.alloc((128, 2), F32, "kg")
            kgb = R.alloc((128, 128), F32, "kgb")
            K.dma(cos.v, I["rope_c_cos"].v, eng="pool")
            K.dma(sin.v, I["rope_c_sin"].v, eng="pool")
            K.dma(permT.v, I["rope_c_p"].v, eng="pool")
            gst = R.alloc((2, 128), F32, "gst2")
            K.dma(gst[0:1, :], I["gqa_q_norm"][j:j + 1, :])
            K.dma(gst[1:2, :], I["gqa_k_norm"][j:j + 1, :])
            K.transpose(psb(4)[:, 0:2], gst.v, ident[0:2, 0:2])
            K.copy("dve", kg.v, psb(4)[:, 0:2])
            K.dma(kgb.v, View(I["gqa_k_norm"], I["gqa_k_norm"].ap[j].partition_broadcast(128)))
            top0 = R.top
            wkv = R.alloc((128, 8, 512), BF16, "wkv")
            K.dma(wkv.v, I["gqa_w_in"][j].re("(k p) n -> p k n", p=128)[:, :, 1024:1536], eng="pool")
            scr = norm_scr()
            for kvh in range(2):
                for b, (c0, n, cond) in enumerate(TB):
                    pb = psb(b % 2)
                    for kc in range(8):
                        K.mm(pb, wkv[:, kc, kvh * 128:(kvh + 1) * 128], HT[kc][b].v, start=(kc == 0), stop=(kc == 7))
                    rope = None if b == 0 else (cos[:, c0 - 512:c0], sin[:, c0 - 512:c0], permT.v)
                    fm_norm(kT[:, kvh, c0:c0 + 512], pb, 128, 512, kg[:, 1:2], 128, scr, rope)
            vst = R.alloc((128, 256), F32, "vst")
            kst = R.alloc((128, 256), F32, "kst")
            ssq = R.alloc((128, 2), F32, "ssq")
            ksq = R.alloc((128, 256), F32, "ksq")
            for t in range(20):
                b, off = divmod(t * 128, 512)
                pb = psb(2 + t % 2)
                ncol = 512 if t < 4 else 256
                for kc in range(8):
                    rhs = wkv[:, kc, 0:512] if t < 4 else wkv[:, kc, 256:512]
                    K.mm(pb[:, 0:ncol], HT[kc][b][:, off:off + 128], rhs, start=(kc == 0), stop=(kc == 7))
                if t < 4:
                    K.copy("act", V[:, t, :], pb[:, 256:512])
                    K.copy("dve", vst.v, pb[:, 256:512])
                    K.dma(O["o_gv"][t // 2, j, (t % 2) * 128:(t % 2 + 1) * 128, :], vst.v)
                    K.act(ksq.v, pb[:, 0:256], AF.Square)
                    K.reduce("dve", ssq.v, ksq.v.re("p (h d) -> p h d", h=2))
                    K.act(ssq.v, ssq.v, AF.Ln, bias=epsv.v, scale=1.0 / 128)
                    K.act(ssq.v, ssq.v, AF.Exp, scale=-0.5)
                    K.tt("dve", kst.v.re("p (h d) -> p h d", h=2), pb[:, 0:256].re("p (h d) -> p h d", h=2),
                         ssq.v.un(2).bc([128, 2, 128]), ALU.mult)
                    K.tt("pool", kst.v.re("p (h d) -> p h d", h=2), kst.v.re("p (h d) -> p h d", h=2),
                         kgb.v.un(1).bc([128, 2, 128]), ALU.mult)
                    K.dma(O["o_gk"][t // 2, j, (t % 2) * 128:(t % 2 + 1) * 128, :], kst.v)
                else:
                    K.copy("act", V[:, t, :], pb[:, 0:256])
            K.dma(V[:, 20:24, :], I["cache_gqa_v"][j].re("(t p) n -> p t n", p=128), eng="pool")
            cst_ = R.alloc((128, 4, 256), F32, "cst_")
            K.dma(cst_.v, I["cache_gqa_k"][j].re("(t p) n -> p t n", p=128))
            for t in range(4):
                for kvh in range(2):
                    pb = psb(t % 2)
                    K.transpose(pb[:, 0:128], cst_[:, t, kvh * 128:(kvh + 1) * 128], ident.v)
                    K.copy("act", kT[:, kvh, 2560 + t * 128:2560 + (t + 1) * 128], pb[:, 0:128])
            R.top = top0
            scr = norm_scr()
            wq = [R.alloc((128, 8, 128), BF16, "wq%d" % q) for q in range(2)]
            woh = [R.alloc((128, D), BF16, "woh%d" % q) for q in range(2)]
            qT = [R.alloc((128, T), BF16, "qT%d" % q) for q in range(2)]
            oT = [R.alloc((128, T), BF16, "oT%d" % q) for q in range(2)]
            pb_ = [R.alloc((128, 512), BF16, "pt%d" % q) for q in range(4)]
            pbufs_rd[0] = R.alloc((128, 512), F32, "rd0")
            pbufs_rd[1] = R.alloc((128, 512), F32, "rd1")
            for h in range(8):
                kvh = h // 4
                w_ = wq[h % 2]
                K.dma(w_.v, I["gqa_w_in"][j].re("(k p) n -> p k n", p=128)[:, :, h * 128:(h + 1) * 128], eng="pool")
                K.dma(woh[h % 2].v, I["gqa_w_out"][j][h * 128:(h + 1) * 128, :], eng="pool")
                q_ = qT[h % 2]
                for b, (c0, n, cond) in enumerate(TB):
                    pb = psb(6 + b % 2)
                    for kc in range(8):
                        K.mm(pb, w_[:, kc, :], HT[kc][b].v, start=(kc == 0), stop=(kc == 7))
                    rope = None if b == 0 else (cos[:, c0 - 512:c0], sin[:, c0 - 512:c0], permT.v)
                    fm_norm(q_[:, c0:c0 + 512], pb, 128, 512, kg[:, 0:1], 128, scr, rope)
                o_ = oT[h % 2]
                attn_head([(kT[:, kvh, :], 128)], lambda kt, kvh=kvh: V[:, kt, kvh * 128:(kvh + 1) * 128],
                          [q_.v], o_.v, 128 ** -0.5, pb_)
                out_accum(i, [o_[:, c0:c0 + 512] for (c0, n, cond) in TB], woh[h % 2].v)

        def mla_layer(i, j):
            R.reset()
            cqn = R.alloc((128, 3, T), BF16, "cqn")
            ckvn = R.alloc((128, 2, 3072), BF16, "ckvn")
            krT = R.alloc((64, 3072), BF16, "krT")
            cos = R.alloc((64, 2048), BF16, "cosb")
            sin = R.alloc((64, 2048), BF16, "sinb")
            permT = R.alloc((64, 64), BF16, "permTb")
            gl = R.alloc((128, 9), F32, "gl")
            K.dma(cos.v, I["rope_b_cos"].v, eng="pool")
            K.dma(sin.v, I["rope_b_sin"].v, eng="pool")
            K.dma(permT.v, I["rope_b_p"].v, eng="pool")
            gst = R.alloc((9, 128), F32, "gst9")
            K.memset("dve", gst.v, 0.0)
            K.dma(gst[0:3, :], I["mla_q_lat_norm"][j])
            K.dma(gst[3:5, :], I["mla_kv_lat_norm"][j])
            K.dma(gst[5:6, :], I["mla_qn_nope"][j:j + 1, :])
            K.dma(gst[6:7, :], I["mla_kn_nope"][j:j + 1, :])
            K.dma(gst[7:8, 0:64], I["mla_qn_rope"][j:j + 1, :])
            K.dma(gst[8:9, 0:64], I["mla_kn_rope"][j:j + 1, :])
            K.transpose(psb(4)[:, 0:9], gst.v, ident[0:9, 0:9])
            K.copy("dve", gl.v, psb(4)[:, 0:9])
            top0 = R.top
            wd = R.alloc((128, 8, 704), BF16, "wd")
            K.dma(wd.v, I["mla_w_down"][j].re("(k p) n -> p k n", p=128), eng="pool")
            scr = norm_scr()
            sqs = [R.alloc((128, 512), BF16, "msq%d" % q) for q in range(3)]
            rs = R.alloc((128, 512), F32, "mrs")
            f32o = R.alloc((128, 3, 512), F32, "f32o")
            groups = [(0, 3, 0, 384), (384, 2, 3, 256)]
            for b, (c0, n, cond) in enumerate(TB):
                for (col0, nch, gcol, nfeat) in groups:
                    pbs = [psb(q) for q in range(nch)]
                    for c in range(nch):
                        for kc in range(8):
                            K.mm(pbs[c], wd[:, kc, col0 + c * 128:col0 + (c + 1) * 128], HT[kc][b].v,
                                 start=(kc == 0), stop=(kc == 7))
                    pd = psb(5)
                    for c in range(nch):
                        K.act(sqs[c].v, pbs[c], AF.Square)
                        K.mm(pd, onesb.v, sqs[c].v, start=(c == 0), stop=(c == nch - 1))
                    K.act(rs.v, pd, AF.Ln, bias=epsv.v, scale=1.0 / nfeat)
                    K.act(rs.v, rs.v, AF.Exp, scale=-0.5)
                    for c in range(nch):
                        if nch == 3:
                            dst = cqn[:, c, c0:c0 + 512]
                        else:
                            dst = ckvn[:, c, c0:c0 + 512]
                        K.stt("dve", dst, pbs[c], gl[:, gcol + c:gcol + c + 1], rs.v, ALU.mult, ALU.mult)
                        if nch == 2 and b == 0:
                            K.stt("dve", f32o[:, c, :], pbs[c], gl[:, gcol + c:gcol + c + 1], rs.v, ALU.mult, ALU.mult)
                pb = psb(3)
                for kc in range(8):
                    K.mm(pb[0:64, :], wd[:, kc, 640:704], HT[kc][b].v, start=(kc == 0), stop=(kc == 7))
                rope = None if b == 0 else (cos[:, c0 - 512:c0], sin[:, c0 - 512:c0], permT.v)
                fm_norm(krT[:, c0:c0 + 512], pb, 64, 512, gl[0:64, 8:9], 64, scr, rope)
                if b == 0:
                    fm_norm(f32o[0:64, 2, :], pb, 64, 512, gl[0:64, 8:9], 64, scr, None)
            ost = R.alloc((128, 320), F32, "ost")
            for t in range(4):
                pb = psb(t % 2)
                for c in range(2):
                    K.transpose(pb[:, c * 128:(c + 1) * 128], f32o[:, c, t * 128:(t + 1) * 128], ident.v)
                K.transpose(pb[:, 256:320], f32o[0:64, 2, t * 128:(t + 1) * 128], ident[0:64, 0:64])
                K.copy("dve", ost.v, pb[:, 0:320])
                K.dma(O["o_ckv"][t // 2, j, (t % 2) * 128:(t % 2 + 1) * 128, :], ost[:, 0:256])
                K.dma(O["o_kr"][t // 2, j, (t % 2) * 128:(t % 2 + 1) * 128, :], ost[:, 256:320])
            cst_ = R.alloc((128, 4, 320), F32, "mcst")
            K.dma(cst_[:, :, 0:256], I["cache_mla_ckv"][j].re("(t p) n -> p t n", p=128))
            K.dma(cst_[:, :, 256:320], I["cache_mla_krope"][j].re("(t p) n -> p t n", p=128))
            for t in range(4):
                pb = psb(t % 2)
                for c in range(2):
                    K.transpose(pb[:, c * 128:(c + 1) * 128], cst_[:, t, c * 128:(c + 1) * 128], ident.v)
                K.transpose(pb[0:64, 256:384], cst_[:, t, 256:320], ident.v)
                for c in range(2):
                    K.copy("act", ckvn[:, c, 2560 + t * 128:2560 + (t + 1) * 128], pb[:, c * 128:(c + 1) * 128])
                K.copy("dve", krT[:, 2560 + t * 128:2560 + (t + 1) * 128], pb[0:64, 256:384])
            R.top = top0
            scr = norm_scr()
            wuq = [R.alloc((128, 3, 192), BF16, "wuq%d" % q) for q in range(1)]
            wukv = [R.alloc((128, 2, 256), BF16, "wukv%d" % q) for q in range(1)]
            woh = [R.alloc((128, D), BF16, "mwoh%d" % q) for q in range(1)]
            qn = [R.alloc((128, T), BF16, "qn%d" % q) for q in range(1)]
            qr = [R.alloc((64, T), BF16, "qr%d" % q) for q in range(1)]
            kn = [R.alloc((128, 3072), BF16, "kn%d" % q) for q in range(1)]
            Vh = [R.alloc((128, 24, 128), BF16, "Vh%d" % q) for q in range(1)]
            oT = qn
            pb_ = [R.alloc((128, 512), BF16, "mpt%d" % q) for q in range(3)]
            pbufs_rd[0] = R.alloc((128, 512), F32, "mrd0")
            pbufs_rd[1] = pbufs_rd[0]
            for h in range(8):
                p_ = 0
                K.dma(wuq[p_].v, I["mla_w_uq"][j].re("(k p) n -> p k n", p=128)[:, :, h * 192:(h + 1) * 192], eng="pool")
                K.dma(wukv[p_].v, I["mla_w_ukv"][j].re("(k p) n -> p k n", p=128)[:, :, h * 256:(h + 1) * 256], eng="pool")
                K.dma(woh[p_].v, I["mla_w_out"][j][h * 128:(h + 1) * 128, :], eng="pool")
                for cb in range(6):
                    c0 = cb * 512
                    pb = psb(6 + cb % 2)
                    for c in range(2):
                        K.mm(pb, wukv[p_][:, c, 0:128], ckvn[:, c, c0:c0 + 512], start=(c == 0), stop=(c == 1))
                    fm_norm(kn[p_][:, c0:c0 + 512], pb, 128, 512, gl[:, 6:7], 128, scr, None)
                for g in range(6):
                    pb = psb(6 + g % 2)
                    for q in range(4):
                        t = g * 4 + q
                        for c in range(2):
                            K.mm(pb[:, q * 128:(q + 1) * 128], ckvn[:, c, t * 128:(t + 1) * 128], wukv[p_][:, c, 128:256],
                                 start=(c == 0), stop=(c == 1))
                    K.copy("act", Vh[p_][:, g * 4:(g + 1) * 4, :], pb.re("p (q d) -> p q d", q=4))
                for b, (c0, n, cond) in enumerate(TB):
                    pb = psb(6 + b % 2)
                    for c in range(3):
                        K.mm(pb, wuq[p_][:, c, 0:128], cqn[:, c, c0:c0 + 512], start=(c == 0), stop=(c == 2))
                    fm_norm(qn[p_][:, c0:c0 + 512], pb, 128, 512, gl[:, 5:6], 128, scr, None)
                    pb2 = psb(7 - b % 2)
                    for c in range(3):
                        K.mm(pb2[0:64, :], wuq[p_][:, c, 128:192], cqn[:, c, c0:c0 + 512], start=(c == 0), stop=(c == 2))
                    rope = None if b == 0 else (cos[:, c0 - 512:c0], sin[:, c0 - 512:c0], permT.v)
                    fm_norm(qr[p_][:, c0:c0 + 512], pb2, 64, 512, gl[0:64, 7:8], 64, scr, rope)
                attn_head([(kn[p_].v, 128), (krT.v, 64)], lambda kt, p_=p_: Vh[p_][:, kt, :],
                          [qn[p_].v, qr[p_].v], oT[p_].v, 192 ** -0.5, pb_)
                out_accum(i, [oT[p_][:, c0:c0 + 512] for (c0, n, cond) in TB], woh[p_].v)


        def gdn_layer(i, j):
            R.reset()
            NEG = -30000.0
            msk = R.alloc((64, 6, 64), F32, "msk")
            K.dma(msk.v, I["gdn_masks"][0:6, 0:64, 0:64].re("m p n -> p m n"))
            U = [msk[:, 0, :], msk[:, 1, :]]
            NEGS = [msk[:, 2, :], msk[:, 4, :]]
            NEGI = [msk[:, 3, :], msk[:, 5, :]]
            I64 = ident[0:64, 0:64]
            eps2 = R.alloc((128, 1), F32, "eps2")
            K.memset("dve", eps2.v, EPS / 128.0)
            gq = R.alloc((128, 3), F32, "gq")
            K.memset("dve", gq[:, 0:1], 1.0 / 128.0)
            K.memset("dve", gq[:, 1:2], 128.0 ** -0.5)
            cst_ = R.alloc((73, 128), F32, "gcst")
            K.dma(cst_[0:72, :], I["gdn_conv"][j])
            K.dma(cst_[72:73, :], I["gdn_out_norm"][j:j + 1, :])
            cw = R.alloc((128, 73), F32, "cw")
            K.transpose(psb(4)[:, 0:73], cst_.v, ident[0:73, 0:73])
            K.copy("dve", cw.v, psb(4)[:, 0:73])
            alb = R.alloc((64, 16), F32, "alb")
            dtb = R.alloc((64, 16), F32, "dtb")
            K.dma(alb.v, View(I["gdn_a_log"], I["gdn_a_log"].ap[j].partition_broadcast(64)))
            K.dma(dtb.v, View(I["gdn_dt_bias"], I["gdn_dt_bias"].ap[j].partition_broadcast(64)))
            K.act(alb.v, alb.v, AF.Exp)
            K.ts("dve", alb.v, alb.v, -1.0, ALU.mult)
            gcum = R.alloc((64, 40, 16), F32, "gcum")
            eg = R.alloc((64, 40, 16), F32, "eg")
            ekd = R.alloc((64, 40, 16), F32, "ekd")
            sA = R.alloc((64, 40, 16), F32, "sA")
            beta = R.alloc((64, 40, 16), F32, "beta")
            egt = R.alloc((128, 40, 16), F32, "egt")
            top_g = R.top
            wgb = R.alloc((128, 8, 32), BF16, "wgb")
            K.dma(wgb.v, I["gdn_w_gb"][j].re("(k p) n -> p k n", p=128), eng="pool")
            G = R.alloc((64, 40, 32), F32, "G")
            g_ = R.alloc((64, 40, 16), F32, "g_")
            t1 = R.alloc((64, 40, 16), F32, "gt1")
            t2 = R.alloc((64, 40, 16), F32, "gt2")
            for grp in range(3):
                c_lo, c_hi = grp * 16, min(40, grp * 16 + 16)
                pb = psb(grp % 2)
                for c in range(c_lo, c_hi):
                    b, off = divmod(c * 64, 512)
                    for kc in range(8):
                        K.mm(pb[0:64, (c - c_lo) * 32:(c - c_lo + 1) * 32], HT[kc][b][:, off:off + 64], wgb[:, kc, :],
                             start=(kc == 0), stop=(kc == 7))
                K.copy("dve", G[:, c_lo:c_hi, :], pb[0:64, 0:(c_hi - c_lo) * 32].re("p (c n) -> p c n", n=32))

            def softplus(dst, x):
                K.act(t1.v, x, AF.Abs)
                K.act(t1.v, t1.v, AF.Exp, scale=-1.0)
                K.act(t1.v, t1.v, AF.Ln, bias=1.0)
                K.ts("dve", t2.v, x, 0.0, ALU.max)
                K.tt("dve", dst, t1.v, t2.v, ALU.add)
            K.tt("dve", g_.v, G[:, :, 0:16], dtb.v.un(1).bc([64, 40, 16]), ALU.add)
            softplus(g_.v, g_.v)
            K.tt("dve", g_.v, g_.v, alb.v.un(1).bc([64, 40, 16]), ALU.mult)
            K.ts("dve", sA.v, G[:, :, 16:32], -1.0, ALU.mult)
            softplus(sA.v, sA.v)
            K.ts("dve", sA.v, sA.v, -1.0, ALU.mult)
            K.act(beta.v, sA.v, AF.Exp)
            for d in range(2):
                pb = psb(2 + d)
                K.mm(pb[0:64, 0:320].re("p (c h) -> p c h", h=8), U[d], g_[:, :, d * 8:(d + 1) * 8])
                K.copy("dve", gcum[:, :, d * 8:(d + 1) * 8], pb[0:64, 0:320].re("p (c h) -> p c h", h=8))
            pbt = [psb(4), psb(5)]
            K.mm(pbt[0][:, 0:320], onesf[0:64, 0:128], g_[:, 0:20, :])
            K.mm(pbt[1][:, 0:320], onesf[0:64, 0:128], g_[:, 20:40, :])
            for q in range(2):
                K.act(egt[:, q * 20:(q + 1) * 20, :], pbt[q][:, 0:320].re("p (c h) -> p c h", h=16), AF.Exp)
                K.tt("dve", ekd[:, q * 20:(q + 1) * 20, :], pbt[q][0:64, 0:320].re("p (c h) -> p c h", h=16),
                     gcum[:, q * 20:(q + 1) * 20, :], ALU.subtract)
            K.act(ekd.v, ekd.v, AF.Exp)
            K.act(eg.v, gcum.v, AF.Exp)
            K.ts("dve", eg.v, eg.v, -1.0, ALU.mult)
            K.tt("dve", sA.v, sA.v, gcum.v, ALU.subtract)
            R.top = top_g
            qT = R.alloc((128, T), BF16, "gqT")
            kT = R.alloc((128, T), BF16, "gkT")
            k_c = R.alloc((64, 40, 128), BF16, "k_c")
            v_c = R.alloc((64, 40, 128), BF16, "v_c")
            oacc = R.alloc((128, T), BF16, "oacc")
            S = R.alloc((128, 128), F32, "S")
            Sb = R.alloc((128, 128), BF16, "Sb")
            top_h = R.top
            SEQ = [(0, 4, 0), (4, 8, 1), (8, 40, 2)]
            for h in range(8):
                R.top = top_h
                praw = R.alloc((128, T), F32, "praw")
                cv = R.alloc((128, T), F32, "cv")
                win = R.alloc((128, 8, 128), BF16, "gwin")
                scr = norm_scr(light=True)
                vT = None
                for xi in range(3):
                    K.dma(win.v, I["gdn_w_in"][j].re("(k p) n -> p k n", p=128)[:, :, xi * 1024 + h * 128:xi * 1024 + (h + 1) * 128],
                          eng="pool")
                    for b, (c0, n, cond) in enumerate(TB):
                        pb = psb(b % 2)
                        for kc in range(8):
                            K.mm(pb, win[:, kc, :], HT[kc][b].v, start=(kc == 0), stop=(kc == 7))
                        K.copy("act" if b % 2 else "dve", praw[:, c0:c0 + 512], pb)
                    ch = xi * 8 + h
                    for (s0, s1) in [(0, 256), (256, 512), (512, 2560)]:
                        K.ts("dve", cv[:, s0:s1], praw[:, s0:s1], cw[:, 24 + ch:25 + ch], ALU.mult)
                        K.stt("dve", cv[:, s0 + 1:s1], praw[:, s0:s1 - 1], cw[:, ch:ch + 1], cv[:, s0 + 1:s1], ALU.mult, ALU.add)
                        K.stt("dve", cv[:, s0:s1 - 1], praw[:, s0 + 1:s1], cw[:, 48 + ch:49 + ch], cv[:, s0:s1 - 1], ALU.mult, ALU.add)
                    K.act(cv.v, cv.v, AF.Silu)
                    if xi < 2:
                        dstT = qT if xi == 0 else kT
                        for b, (c0, n, cond) in enumerate(TB):
                            fm_norm(dstT[:, c0:c0 + 512], cv[:, c0:c0 + 512], 128, 512, gq[:, xi:xi + 1], 128, scr, None, eps_v=eps2)
                    else:
                        sv_top = R.top
                        R.top = top_h
                        vT = R.alloc((128, T), BF16, "vT")
                        R.top = sv_top
                        K.copy("pool", vT.v, cv.v)
                for (srcT, dstc) in ((kT, k_c), (vT, v_c)):
                    for b in range(5):
                        pbb = psb(2 + b % 2, BF16)
                        for c in range(8):
                            K.transpose(pbb[0:64, c * 128:(c + 1) * 128], srcT[:, b * 512 + c * 64:b * 512 + (c + 1) * 64], identb.v)
                        K.copy("act" if b % 2 else "dve", dstc[:, b * 8:(b + 1) * 8, :], pbb[0:64, :].re("p (c d) -> p c d", d=128))
                R.top = top_h
                scrs = [R.alloc((64, 8, 64), F32, "gs%d" % q) for q in range(7)]
                kgn = R.alloc((64, 8, 128), BF16, "kgn")
                erow = R.alloc((128, 512), BF16, "erow")
                vnew = R.alloc((64, 128), BF16, "vnew")
                psets = []
                for q in range(2):
                    psets.append({"kdec": R.alloc((64, 8, 128), BF16, "kdec%d" % q),
                                  "qdT": R.alloc((128, 8, 64), BF16, "qdT%d" % q),
                                  "QKt": R.alloc((64, 8, 64), BF16, "QKt%d" % q),
                                  "Tt": R.alloc((64, 8, 64), BF16, "Tt%d" % q),
                                  "WTn": R.alloc((128, 8, 64), BF16, "WTn%d" % q)})
                f2 = lambda v_: v_.re("p c i -> p (c i)")
                I64b = View(ident, I64.ap.unsqueeze(1).broadcast_to([64, 8, 64]))

                def intra(d, b, ps_):
                    dh = d * 8 + h
                    cb0 = b * 8
                    kdec, qdT, QKt, Tt, WTn = ps_["kdec"], ps_["qdT"], ps_["QKt"], ps_["Tt"], ps_["WTn"]
                    gc_b = gcum[:, cb0:cb0 + 8, dh]
                    Dg, RBA, RBQ, EA, EQ, At, M = scrs
                    K.tt("pool", Dg.v, I64b, gc_b.un(2).bc([64, 8, 64]), ALU.mult)
                    K.tt("dve", RBA.v, View(msk, NEGS[d].ap.unsqueeze(1).broadcast_to([64, 8, 64])),
                         sA[:, cb0:cb0 + 8, dh].un(2).bc([64, 8, 64]), ALU.add)
                    K.tt("pool", RBQ.v, View(msk, NEGI[d].ap.unsqueeze(1).broadcast_to([64, 8, 64])),
                         gc_b.un(2).bc([64, 8, 64]), ALU.subtract)
                    K.tt("pool", kgn.v, k_c[:, cb0:cb0 + 8, :], eg[:, cb0:cb0 + 8, dh].un(2).bc([64, 8, 128]), ALU.mult)
                    K.tt("pool", kdec.v, k_c[:, cb0:cb0 + 8, :], ekd[:, cb0:cb0 + 8, dh].un(2).bc([64, 8, 128]), ALU.mult)
                    yield
                    pA, pQ, pE, pK = psb(0), psb(1), psb(2), psb(3)
                    K.mm(pA[0:64, :], onesf[0:64, 0:64], f2(Dg.v), start=True, stop=False)
                    K.mm(pA[0:64, :], I64, f2(RBA.v), start=False, stop=True)
                    K.mm(pQ[0:64, :], onesf[0:64, 0:64], f2(Dg.v), start=True, stop=False)
                    K.mm(pQ[0:64, :], I64, f2(RBQ.v), start=False, stop=True)
                    K.mm(pE, onesf[0:64, 0:128], f2(Dg.v))
                    for c in range(8):
                        cs = slice(b * 512 + c * 64, b * 512 + (c + 1) * 64)
                        K.mm(pK[0:64, c * 64:(c + 1) * 64], kT[:, cs], kT[:, cs])
                    pK2 = psb(4)
                    for c in range(8):
                        cs = slice(b * 512 + c * 64, b * 512 + (c + 1) * 64)
                        K.mm(pK2[0:64, c * 64:(c + 1) * 64], kT[:, cs], qT[:, cs])
                    K.act(f2(EA.v), pA[0:64, :], AF.Exp)
                    K.act(f2(EQ.v), pQ[0:64, :], AF.Exp)
                    K.act(erow.v, pE, AF.Exp)
                    yield
                    K.tt("dve", f2(At.v), pK[0:64, :], f2(EA.v), ALU.mult)
                    K.tt("dve", f2(QKt.v), pK2[0:64, :], f2(EQ.v), ALU.mult)
                    K.tt("pool", qdT.v.re("p c i -> p (c i)"), qT[:, b * 512:(b + 1) * 512], erow.v, ALU.mult)
                    Xa = EQ
                    pX = psb(0)
                    for c in range(8):
                        K.transpose(pX[0:64, c * 64:(c + 1) * 64], At[:, c, :], I64)
                    K.copy("act", f2(Xa.v), pX[0:64, :])
                    K.tt("dve", M.v, I64b, At.v, ALU.subtract)
                    yield
                    Pa, Pt = Xa, At
                    for lvl in range(1, 6):
                        p1, p2, p3 = psb(1), psb(2), psb(3)
                        for c in range(8):
                            K.mm(p1[0:64, c * 64:(c + 1) * 64], Pt[:, c, :], Pa[:, c, :])
                        if lvl < 5:
                            for c in range(8):
                                K.mm(p2[0:64, c * 64:(c + 1) * 64], Pa[:, c, :], Pt[:, c, :])
                        Pa2 = Dg if lvl % 2 == 1 else RBQ
                        K.copy("act", f2(Pa2.v), p1[0:64, :])
                        if lvl < 5:
                            Pt2 = RBA if lvl % 2 == 1 else EA
                            K.copy("dve", f2(Pt2.v), p2[0:64, :])
                        yield
                        for c in range(8):
                            K.mm(p3[0:64, c * 64:(c + 1) * 64], Pa2[:, c, :], M[:, c, :])
                        K.tt("dve", f2(M.v), p3[0:64, :], f2(M.v), ALU.add)
                        Pa = Pa2
                        if lvl < 5:
                            Pt = Pt2
                        yield
                    K.copy("act", Tt.v, M.v)
                    pW = psb(5)
                    for c in range(8):
                        K.mm(pW[:, c * 64:(c + 1) * 64], kgn[:, c, :], Tt[:, c, :])
                    K.copy("act", WTn.v.re("p c i -> p (c i)"), pW)
                    yield

                def scan(d, b, ps_):
                    dh = d * 8 + h
                    cb0 = b * 8
                    kdec, qdT, QKt, Tt, WTn = ps_["kdec"], ps_["qdT"], ps_["QKt"], ps_["Tt"], ps_["WTn"]
                    pO = psb(6)
                    corder = list(range(8)) if d == 0 else list(range(7, -1, -1))
                    for c in corder:
                        cg = cb0 + c
                        seq = 0 if cg < 4 else (1 if cg < 8 else 2)
                        first = (cg == SEQ[seq][0]) if d == 0 else (cg == SEQ[seq][1] - 1)
                        last = (cg == SEQ[seq][1] - 1) if d == 0 else (cg == SEQ[seq][0])
                        if first:
                            if seq < 2:
                                K.memset("dve", S.v, 0.0)
                            else:
                                K.dma(S.v, (I["state_f"] if d == 0 else I["state_b"])[j, h])
                            K.copy("act", Sb.v, S.v)
                        pv = psb(7)
                        K.mm(pv[0:64, 0:128], Tt[:, c, :], v_c[:, cg, :], start=True, stop=False)
                        K.mm(pv[0:64, 0:128], WTn[:, c, :], Sb.v, start=False, stop=True)
                        K.ts("dve", vnew.v, pv[0:64, 0:128], beta[:, cg, dh:dh + 1], ALU.mult)
                        K.mm(pO[:, c * 64:(c + 1) * 64], Sb.v, qdT[:, c, :], start=True, stop=False)
                        K.mm(pO[:, c * 64:(c + 1) * 64], vnew.v, QKt[:, c, :], start=False, stop=True)
                        K.mm(pv[:, 128:256], kdec[:, c, :], vnew.v)
                        K.stt("dve", S.v, S.v, egt[:, cg, dh:dh + 1], pv[:, 128:256], ALU.mult, ALU.add)
                        if last and seq < 2:
                            K.dma((O["o_sf"] if d == 0 else O["o_sb"])[seq, j, h], S.v)
                        if not last:
                            K.copy("act", Sb.v, S.v)
                        yield
                    if d == 0:
                        K.copy("act", oacc[:, b * 512:(b + 1) * 512], pO)
                    else:
                        K.tt("dve", oacc[:, b * 512:(b + 1) * 512], pO, oacc[:, b * 512:(b + 1) * 512], ALU.add)
                    yield

                units = [(0, bb) for bb in [0, 1, 2, 3, 4]] + [(1, bb) for bb in [0, 4, 3, 2, 1]]
                prev = None
                for n in range(len(units) + 1):
                    gens = []
                    if prev is not None:
                        gens.append(scan(prev[0], prev[1], psets[(n - 1) % 2]))
                    if n < len(units):
                        gens.append(intra(units[n][0], units[n][1], psets[n % 2]))
                        prev = units[n]
                    else:
                        prev = None
                    while gens:
                        for g_ in list(gens):
                            try:
                                next(g_)
                            except StopIteration:
                                gens.remove(g_)
                R.top = top_h
                scr = norm_scr(light=True)
                wz = R.alloc((128, 8, 128), BF16, "wz")
                woh = R.alloc((128, D), BF16, "gwoh")
                og = R.alloc((128, T), BF16, "og")
                zs = R.alloc((128, 512), F32, "zs")
                on = R.alloc((128, 512), F32, "on")
                K.dma(wz.v, I["gdn_w_in"][j].re("(k p) n -> p k n", p=128)[:, :, 3072 + h * 128:3072 + (h + 1) * 128], eng="pool")
                K.dma(woh.v, I["gdn_w_out"][j][h * 128:(h + 1) * 128, :], eng="pool")
                for b, (c0, n, cond) in enumerate(TB):
                    pb = psb(b % 2)
                    for kc in range(8):
                        K.mm(pb, wz[:, kc, :], HT[kc][b].v, start=(kc == 0), stop=(kc == 7))
                    K.act(zs.v, pb, AF.Silu)
                    fm_norm(on.v, oacc[:, c0:c0 + 512], 128, 512, cw[:, 72:73], 128, scr, None)
                    K.tt("dve", og[:, c0:c0 + 512], on.v, zs.v, ALU.mult)
                out_accum(i, [og[:, c0:c0 + 512] for (c0, n, cond) in TB], woh.v)

        if depth:
            load_gains(0)
            run(adaln(0))
        cnt = {0: 0, 1: 0, 2: 0}
        for i, kind in enumerate(kinds):
            j = cnt[kind]
            cnt[kind] += 1
            R.reset()
            make_coef(i)
            mod_norm(i, 0)
            if kind == 2:
                gqa_layer(i, j)
            elif kind == 1:
                mla_layer(i, j)
            elif kind == 0:
                gdn_layer(i, j)
            R.reset()
            mod_norm(i, 1)
            bg = None
            if i + 1 < depth:
                def bgf(i=i):
                    yield from adaln(i + 1)
                bg = bgf()
            if do_mlp:
                mlp(i, bg)
            elif bg is not None:
                run(bg)
            if i + 1 < depth:
                load_gains(i + 1)

        R.reset()
        ost = [R.alloc((128, D), F32, "xo%d" % q) for q in range(2)]
        for t in range(int(os.environ.get("KNT_OUT", str(NT_DBG)))):
            b, off = divmod(t * 128, 512)
            o_ = ost[(t + int(os.environ.get("KSL", "0"))) % 2]
            for g in range(2):
                pb = psb((t * 2 + g + int(os.environ.get("KOFF", "0"))) % int(os.environ.get("KNB", "4")))
                for q in range(4):
                    kc = g * 4 + q
                    K.transpose(pb[:, q * 128:(q + 1) * 128], XT[kc][b][:, off:off + 128], ident.v)
                K.copy("dve" if g == 0 else "act", o_[:, g * 512:(g + 1) * 512], pb)
            dst = O["yp"][t * 128:(t + 1) * 128, :] if t < 4 else O["ys"][(t - 4) * 128:(t - 3) * 128, :]
            K.dma(dst, o_.v)
        K.emit()
    return nc


def make_in_maps(inp, kinds=KINDS_FULL):
    f = lambda a: np.ascontiguousarray(np.asarray(a, dtype=np.float32))
    depth = max(len(kinds), 1)
    n_a = sum(1 for k in kinds if k == 0)
    n_b = sum(1 for k in kinds if k == 1)
    n_c = sum(1 for k in kinds if k == 2)
    shared = {
        "norm_mix": f(inp["norm_mix"])[:depth].reshape(depth, 8, 128),
        "norm_mlp": f(inp["norm_mlp"])[:depth].reshape(depth, 8, 128),
        "w_mod": f(inp["w_mod"])[:depth],
        "b_mod": f(inp["b_mod"])[:depth].reshape(depth, 48, 128),
        "w_mlp_in": f(inp["w_mlp_in"])[:depth],
        "w_mlp_out": f(inp["w_mlp_out"])[:depth],
        "ident": np.eye(128, dtype=np.float32),
        "ones": np.ones((128, 128), np.float32),
    }
    if n_c:
        cos, sin, pT = rope_tables(32, 2048)
        shared.update({
            "gqa_w_in": f(inp["gqa_w_in"])[:n_c], "gqa_q_norm": f(inp["gqa_q_norm"])[:n_c],
            "gqa_k_norm": f(inp["gqa_k_norm"])[:n_c], "gqa_w_out": f(inp["gqa_w_out"])[:n_c],
            "rope_c_cos": cos, "rope_c_sin": sin, "rope_c_p": pT})
    if n_b:
        cos, sin, pT = rope_tables(16, 2048)
        shared.update({
            "mla_w_down": f(inp["mla_w_down"])[:n_b],
            "mla_q_lat_norm": f(inp["mla_q_lat_norm"])[:n_b].reshape(n_b, 3, 128),
            "mla_kv_lat_norm": f(inp["mla_kv_lat_norm"])[:n_b].reshape(n_b, 2, 128),
            "mla_w_uq": f(inp["mla_w_uq"])[:n_b], "mla_w_ukv": f(inp["mla_w_ukv"])[:n_b],
            "mla_qn_nope": f(inp["mla_qn_nope"])[:n_b], "mla_qn_rope": f(inp["mla_qn_rope"])[:n_b],
            "mla_kn_nope": f(inp["mla_kn_nope"])[:n_b], "mla_kn_rope": f(inp["mla_kn_rope"])[:n_b],
            "mla_w_out": f(inp["mla_w_out"])[:n_b],
            "rope_b_cos": cos, "rope_b_sin": sin, "rope_b_p": pT})
    if n_a:
        shared.update(gdn_host_inputs(inp, n_a))
    maps = []
    xp = f(inp["x_prompt"])
    xs = f(inp["x_sample"])
    for c in range(NCORES):
        m = dict(shared)
        m["xp"] = xp[2 * c:2 * c + 2].reshape(512, D)
        m["xs"] = xs[c]
        m["cond"] = np.stack([f(inp["c_ctx"]).reshape(8, 128), f(inp["c"])[c].reshape(8, 128)], 0)
        if n_c:
            m["cache_gqa_k"] = f(inp["cache_gqa_k"])[c, :n_c].reshape(n_c, 512, 256)
            m["cache_gqa_v"] = f(inp["cache_gqa_v"])[c, :n_c].reshape(n_c, 512, 256)
        if n_b:
            m["cache_mla_ckv"] = f(inp["cache_mla_ckv"])[c, :n_b]
            m["cache_mla_krope"] = f(inp["cache_mla_krope"])[c, :n_b]
        if n_a:
            m["state_f"] = f(inp["state_gdn_fwd"])[c, :n_a]
            m["state_b"] = f(inp["state_gdn_bwd"])[c, :n_a]
        maps.append(m)
    return maps


def gdn_host_inputs(inp, n_a):
    f = lambda a: np.ascontiguousarray(np.asarray(a, dtype=np.float32))
    w = f(inp["gdn_w_in"])[:n_a]
    idx = np.arange(64)
    NEG = -30000.0
    m = np.zeros((8, 128, 128), np.float32)
    jj, ii = np.meshgrid(idx, idx, indexing="ij")
    m[0, :64, :64] = (jj <= ii)
    m[1, :64, :64] = (jj >= ii)
    m[2, :64, :64] = np.where(ii > jj, 0.0, NEG)
    m[3, :64, :64] = np.where(ii >= jj, 0.0, NEG)
    m[4, :64, :64] = np.where(ii < jj, 0.0, NEG)
    m[5, :64, :64] = np.where(ii <= jj, 0.0, NEG)
    return {
        "gdn_w_in": np.ascontiguousarray(w[:, :, :4096]),
        "gdn_w_gb": np.ascontiguousarray(w[:, :, 4096:4128]),
        "gdn_conv": f(inp["gdn_conv"])[:n_a].reshape(n_a, 72, 128),
        "gdn_a_log": f(inp["gdn_a_log"])[:n_a].reshape(n_a, 16),
        "gdn_dt_bias": f(inp["gdn_dt_bias"])[:n_a].reshape(n_a, 16),
        "gdn_out_norm": f(inp["gdn_out_norm"])[:n_a],
        "gdn_w_out": f(inp["gdn_w_out"])[:n_a],
        "gdn_masks": m,
    }


def assemble(results, kinds=KINDS_FULL):
    n_a = sum(1 for k in kinds if k == 0)
    n_b = sum(1 for k in kinds if k == 1)
    n_c = sum(1 for k in kinds if k == 2)
    yp = np.concatenate([r["yp"].reshape(2, 256, D) for r in results], 0)
    ys = np.stack([r["ys"] for r in results], 0)
    sf = np.concatenate([r["o_sf"][:, :n_a] for r in results], 0)
    sbw = np.concatenate([r["o_sb"][:, :n_a] for r in results], 0)
    ckv = np.concatenate([r["o_ckv"][:, :n_b] for r in results], 0)
    kr = np.concatenate([r["o_kr"][:, :n_b] for r in results], 0)
    gk = np.concatenate([r["o_gk"][:, :n_c].reshape(2, n_c, 256, 2, 128) for r in results], 0)
    gv = np.concatenate([r["o_gv"][:, :n_c].reshape(2, n_c, 256, 2, 128) for r in results], 0)
    return (yp, ys, sf, sbw, ckv, kr, gk, gv)


_NC_CACHE = {}


def kernel(**inputs):
    kinds = KINDS_FULL
    if kinds not in _NC_CACHE:
        _NC_CACHE[kinds] = build(kinds)
    nc = _NC_CACHE[kinds]
    maps = make_in_maps(inputs, kinds)
    res = run_bass_kernel_spmd(nc, maps, core_ids=list(range(NCORES)))
    return assemble(res.results, kinds)
```

```python
import contextlib
import math
import numpy as np
import concourse.bass as bass
import concourse.mybir as mybir
from concourse.bass_utils import run_bass_kernel_spmd

F32 = mybir.dt.float32
BF16 = mybir.dt.bfloat16
AF = mybir.ActivationFunctionType
ALU = mybir.AluOpType
AX = mybir.AxisListType

D = 1024
NKC = 8
T = 2560
NCORES = 8
EPS = 1e-6
TB = [(0, 512, 0), (512, 512, 1), (1024, 512, 1), (1536, 512, 1), (2048, 512, 1)]
KINDS_FULL = (0, 1, 2, 0)


class Buf:
    __slots__ = ("ap", "name", "last_write", "reads", "excl")

    def __init__(self, ap, name="", excl=False):
        self.ap = ap
        self.name = name
        self.last_write = None
        self.reads = []
        self.excl = excl

    def __getitem__(self, idx):
        return View(self, self.ap[idx])

    @property
    def v(self):
        return View(self, self.ap)


class View:
    __slots__ = ("buf", "ap")

    def __init__(self, buf, ap):
        self.buf = buf
        self.ap = ap

    def __getitem__(self, idx):
        return View(self.buf, self.ap[idx])

    def re(self, pat, **kw):
        return View(self.buf, self.ap.rearrange(pat, **kw))

    def bc(self, shape):
        return View(self.buf, self.ap.broadcast_to(shape))

    def un(self, axis):
        return View(self.buf, self.ap.unsqueeze(axis))


class Op:
    __slots__ = ("eng", "fn", "deps", "flag", "ticket", "dma", "dsem", "dval", "dprev", "idx")

    def __init__(self, eng, fn):
        self.eng = eng
        self.fn = fn
        self.deps = []
        self.flag = False
        self.ticket = 0
        self.dma = False
        self.dsem = None
        self.dval = 0
        self.dprev = None
        self.idx = 0


ENGS = ("pe", "act", "dve", "pool", "sp")
import os
DUMP = os.environ.get("KDUMP", "") == "1"


def compress(ops):
    best = {}
    for o in ops:
        key = ("d", o.dsem) if o.dma else ("e", o.eng)
        b = best.get(key)
        if b is None or o.idx > b.idx:
            best[key] = o
    return list(best.values())


class Sched:
    def __init__(self, nc, n_dma_sems=32):
        self.nc = nc
        self.ops = {e: [] for e in ENGS}
        self.n_dma_sems = n_dma_sems
        self.dma_count = 0
        self.dma_cnt_q = {}
        self.dma_last = [None] * n_dma_sems
        self.dma_vals = [0] * n_dma_sems
        self.nops = 0

    def op(self, eng, fn, reads=(), writes=(), dma=False):
        o = Op(eng, fn)
        o.idx = self.nops
        self.nops += 1
        deps = {}
        for r in reads:
            b = r.buf if isinstance(r, View) else r
            if b.last_write is not None:
                deps[id(b.last_write)] = (b.last_write, True)
            if b.excl:
                for rd in b.reads:
                    if rd.eng != eng and id(rd) not in deps:
                        deps[id(rd)] = (rd, False)
        for w in writes:
            b = w.buf if isinstance(w, View) else w
            if b.last_write is not None and id(b.last_write) not in deps:
                deps[id(b.last_write)] = (b.last_write, False)
            for rd in b.reads:
                if id(rd) not in deps:
                    deps[id(rd)] = (rd, False)
        for d, strong in deps.values():
            if d is o:
                continue
            if d.eng == eng and not d.dma and not dma:
                if eng == "pe" or not strong:
                    continue
            o.deps.append(d)
            d.flag = True
        for w in writes:
            b = w.buf if isinstance(w, View) else w
            b.last_write = o
            b.reads = []
        for r in reads:
            b = r.buf if isinstance(r, View) else r
            if b.last_write is not o:
                b.reads.append(o)
                if len(b.reads) > 24:
                    b.reads = compress(b.reads)
        if dma:
            o.dma = True
            half = self.n_dma_sems // 2
            cnt = self.dma_cnt_q.get(eng, 0)
            self.dma_cnt_q[eng] = cnt + 1
            s = (cnt % half) + (half if eng == "pool" else 0)
            self.dma_count += 1
            o.dsem = s
            self.dma_vals[s] += 16
            o.dval = self.dma_vals[s]
            o.dprev = self.dma_last[s]
            self.dma_last[s] = o
        self.ops[eng].append(o)
        return o

    def mm(self, out, lhsT, rhs, start=True, stop=True):
        return self.op("pe", lambda e: e.matmul(out.ap, lhsT.ap, rhs.ap, start=start, stop=stop),
                       reads=[lhsT, rhs] + ([] if start else [out]), writes=[out])

    def transpose(self, out, in_, ident):
        return self.op("pe", lambda e: e.transpose(out.ap, in_.ap, ident.ap), reads=[in_, ident], writes=[out])

    def act(self, out, in_, func, bias=None, scale=None, accum=None):
        kw = {}
        reads = [in_]
        writes = [out]
        if bias is not None:
            if isinstance(bias, View):
                kw["bias"] = bias.ap
                reads.append(bias)
            else:
                kw["bias"] = bias
        if scale is not None:
            if isinstance(scale, View):
                kw["scale"] = scale.ap
                reads.append(scale)
            else:
                kw["scale"] = scale
        if accum is not None:
            kw["accum_out"] = accum.ap
            writes.append(accum)
        return self.op("act", lambda e: e.activation(out.ap, in_.ap, func, **kw), reads=reads, writes=writes)

    def tt(self, eng, out, a, b, op):
        return self.op(eng, lambda e: e.tensor_tensor(out.ap, a.ap, b.ap, op), reads=[a, b], writes=[out])

    def ts(self, eng, out, a, s1, op0, s2=None, op1=None):
        reads = [a]
        s1a = s1.ap if isinstance(s1, View) else s1
        s2a = s2.ap if isinstance(s2, View) else s2
        if isinstance(s1, View):
            reads.append(s1)
        if isinstance(s2, View):
            reads.append(s2)
        kw = {}
        if op1 is not None:
            kw["op1"] = op1
        return self.op(eng, lambda e: e.tensor_scalar(out.ap, a.ap, s1a, s2a, op0, **kw), reads=reads, writes=[out])

    def stt(self, eng, out, a, s, b, op0, op1):
        reads = [a, b]
        sa = s.ap if isinstance(s, View) else s
        if isinstance(s, View):
            reads.append(s)
        return self.op(eng, lambda e: e.scalar_tensor_tensor(out.ap, a.ap, sa, b.ap, op0, op1),
                       reads=reads, writes=[out])

    def copy(self, eng, out, in_):
        if eng == "act":
            return self.op(eng, lambda e: e.copy(out.ap, in_.ap), reads=[in_], writes=[out])
        return self.op(eng, lambda e: e.tensor_copy(out.ap, in_.ap), reads=[in_], writes=[out])

    def recip(self, out, in_):
        return self.op("dve", lambda e: e.reciprocal(out.ap, in_.ap), reads=[in_], writes=[out])

    def reduce(self, eng, out, in_, op=None):
        return self.op(eng, lambda e: e.tensor_reduce(out.ap, in_.ap, AX.X, op or ALU.add), reads=[in_], writes=[out])

    def memset(self, eng, out, val):
        return self.op(eng, lambda e: e.memset(out.ap, val), writes=[out])

    def dma(self, out, in_, eng="sp"):
        return self.op(eng, lambda e: e.dma_start(out=out.ap, in_=in_.ap), reads=[in_], writes=[out], dma=True)

    def emit(self):
        nc = self.nc
        for e in ENGS:
            c = 0
            for o in self.ops[e]:
                if o.dma:
                    continue
                if o.flag:
                    c += 1
                    o.ticket = c
        with contextlib.ExitStack() as st:
            esems = {e: st.enter_context(nc.semaphore("s_" + e)) for e in ENGS}
            dsems = [st.enter_context(nc.semaphore("d%d" % i)) for i in range(self.n_dma_sems)]
            block = st.enter_context(nc.Block())
            sched = self

            def make(ename):
                def body(eng):
                    waited = {}
                    for o in sched.ops[ename]:
                        ws = []
                        for d in o.deps:
                            if d.dma:
                                ws.append((("d", d.dsem), dsems[d.dsem], d.dval))
                            else:
                                ws.append((("e", d.eng), esems[d.eng], d.ticket))
                        if o.dma and o.dprev is not None:
                            d = o.dprev
                            ws.append((("d", d.dsem), dsems[d.dsem], d.dval))
                        for key, sem, val in ws:
                            if waited.get(key, 0) >= val:
                                continue
                            waited[key] = val
                            eng.wait_ge(sem, val)
                            if DUMP:
                                print("   ", ename, "wait", key, val)
                        ins = o.fn(eng)
                        if DUMP:
                            print(ename, o.idx, "dma" if o.dma else "", ("inc d%d->%d" % (o.dsem, o.dval)) if o.dma else ("inc e->%d" % o.ticket if o.flag else ""), str(ins)[:150])
                        if o.dma:
                            ins.then_inc(dsems[o.dsem], 16)
                        elif o.flag:
                            ins.then_inc(esems[ename], 1)
                    if ename == "sp":
                        for s in range(sched.n_dma_sems):
                            if sched.dma_vals[s] > 0 and waited.get(("d", s), 0) < sched.dma_vals[s]:
                                eng.wait_ge(dsems[s], sched.dma_vals[s])
                return body

            block.tensor(make("pe"))
            block.scalar(make("act"))
            block.vector(make("dve"))
            block.gpsimd(make("pool"))
            block.sync(make("sp"))


class Region:
    def __init__(self, ap_f32, nbytes):
        self.ap = ap_f32
        self.nbytes = nbytes
        self.live = []
        self.top = 0

    def reset(self):
        self.top = 0

    def alloc(self, shape, dt, name=""):
        esz = 2 if dt == BF16 else 4
        free = int(np.prod(shape[1:]))
        nb = (free * esz + 31) // 32 * 32
        s = self.top
        e = s + nb
        assert e <= self.nbytes, (name, e, self.nbytes)
        self.top = e
        ap = self.ap[0:shape[0], s // 4:e // 4]
        if dt == BF16:
            ap = ap.bitcast(BF16)
        ap = ap[:, 0:free]
        if len(shape) == 3:
            ap = ap.rearrange("p (a b) -> p a b", a=shape[1])
        elif len(shape) == 4:
            ap = ap.rearrange("p (a b c) -> p a b c", a=shape[1], b=shape[2])
        b = Buf(ap, name)
        inherit = []
        keep = []
        for (ps, pe, pb) in self.live:
            if ps < e and s < pe:
                inherit.extend(pb.reads)
                if pb.last_write is not None:
                    inherit.append(pb.last_write)
                if ps >= s and pe <= e:
                    continue
            keep.append((ps, pe, pb))
        keep.append((s, e, b))
        self.live = keep
        b.reads = compress(inherit)
        return b


def rope_tables(half, n_tok_rows):
    rows = 2048 // 64
    row = np.repeat(np.arange(rows, dtype=np.float32), 64)
    col = np.tile(np.arange(64, dtype=np.float32), rows)
    freqs = (10000.0 ** (-np.arange(half, dtype=np.float32) / half)).astype(np.float32)
    ang_r = row[None, :] * freqs[:, None]
    ang_c = col[None, :] * freqs[:, None]
    cos = np.concatenate([np.cos(ang_r), np.cos(ang_r), np.cos(ang_c), np.cos(ang_c)], 0).astype(np.float32)
    sin = np.concatenate([np.sin(ang_r), np.sin(ang_r), np.sin(ang_c), np.sin(ang_c)], 0).astype(np.float32)
    n = 4 * half
    P = np.zeros((n, n), np.float32)
    for blk in range(2):
        o = blk * 2 * half
        for d in range(half):
            P[o + d, o + d + half] = -1.0
            P[o + d + half, o + d] = 1.0
    return cos, sin, np.ascontiguousarray(P.T)


def build(kinds=KINDS_FULL, do_mlp=True):
    nc = bass.Bass("TRN2", target_bir_lowering=False)
    n_a = sum(1 for k in kinds if k == 0)
    n_b = sum(1 for k in kinds if k == 1)
    n_c = sum(1 for k in kinds if k == 2)
    depth = len(kinds)

    def din(name, shape):
        return Buf(nc.dram_tensor(name, list(shape), F32, kind="ExternalInput").ap(), name)

    def dout(name, shape):
        return Buf(nc.dram_tensor(name, list(shape), F32, kind="ExternalOutput").ap(), name)

    I = {}
    I["xp"] = din("xp", (512, D))
    I["xs"] = din("xs", (2048, D))
    I["cond"] = din("cond", (2, 8, 128))
    dd = max(depth, 1)
    I["norm_mix"] = din("norm_mix", (dd, 8, 128))
    I["norm_mlp"] = din("norm_mlp", (dd, 8, 128))
    I["w_mod"] = din("w_mod", (dd, D, 6 * D))
    I["b_mod"] = din("b_mod", (dd, 48, 128))
    I["w_mlp_in"] = din("w_mlp_in", (dd, D, 4 * D))
    I["w_mlp_out"] = din("w_mlp_out", (dd, 4 * D, D))
    I["ident"] = din("ident", (128, 128))
    I["ones"] = din("ones", (128, 128))
    if n_c:
        I["gqa_w_in"] = din("gqa_w_in", (n_c, D, 1536))
        I["gqa_q_norm"] = din("gqa_q_norm", (n_c, 128))
        I["gqa_k_norm"] = din("gqa_k_norm", (n_c, 128))
        I["gqa_w_out"] = din("gqa_w_out", (n_c, D, D))
        I["cache_gqa_k"] = din("cache_gqa_k", (n_c, 512, 256))
        I["cache_gqa_v"] = din("cache_gqa_v", (n_c, 512, 256))
        I["rope_c_cos"] = din("rope_c_cos", (128, 2048))
        I["rope_c_sin"] = din("rope_c_sin", (128, 2048))
        I["rope_c_p"] = din("rope_c_p", (128, 128))
    if n_b:
        I["mla_w_down"] = din("mla_w_down", (n_b, D, 704))
        I["mla_q_lat_norm"] = din("mla_q_lat_norm", (n_b, 3, 128))
        I["mla_kv_lat_norm"] = din("mla_kv_lat_norm", (n_b, 2, 128))
        I["mla_w_uq"] = din("mla_w_uq", (n_b, 384, 1536))
        I["mla_w_ukv"] = din("mla_w_ukv", (n_b, 256, 2048))
        I["mla_qn_nope"] = din("mla_qn_nope", (n_b, 128))
        I["mla_qn_rope"] = din("mla_qn_rope", (n_b, 64))
        I["mla_kn_nope"] = din("mla_kn_nope", (n_b, 128))
        I["mla_kn_rope"] = din("mla_kn_rope", (n_b, 64))
        I["mla_w_out"] = din("mla_w_out", (n_b, D, D))
        I["cache_mla_ckv"] = din("cache_mla_ckv", (n_b, 512, 256))
        I["cache_mla_krope"] = din("cache_mla_krope", (n_b, 512, 64))
        I["rope_b_cos"] = din("rope_b_cos", (64, 2048))
        I["rope_b_sin"] = din("rope_b_sin", (64, 2048))
        I["rope_b_p"] = din("rope_b_p", (64, 64))
    if n_a:
        I["gdn_w_in"] = din("gdn_w_in", (n_a, D, 4096))
        I["gdn_w_gb"] = din("gdn_w_gb", (n_a, D, 32))
        I["gdn_conv"] = din("gdn_conv", (n_a, 72, 128))
        I["gdn_a_log"] = din("gdn_a_log", (n_a, 16))
        I["gdn_dt_bias"] = din("gdn_dt_bias", (n_a, 16))
        I["gdn_out_norm"] = din("gdn_out_norm", (n_a, 128))
        I["gdn_w_out"] = din("gdn_w_out", (n_a, D, D))
        I["state_f"] = din("state_f", (n_a, 8, 128, 128))
        I["state_b"] = din("state_b", (n_a, 8, 128, 128))
        I["gdn_masks"] = din("gdn_masks", (8, 128, 128))

    O = {}
    O["yp"] = dout("yp", (512, D))
    O["ys"] = dout("ys", (2048, D))
    O["o_sf"] = dout("o_sf", (2, max(n_a, 1), 8, 128, 128))
    O["o_sb"] = dout("o_sb", (2, max(n_a, 1), 8, 128, 128))
    O["o_ckv"] = dout("o_ckv", (2, max(n_b, 1), 256, 256))
    O["o_kr"] = dout("o_kr", (2, max(n_b, 1), 256, 64))
    O["o_gk"] = dout("o_gk", (2, max(n_c, 1), 256, 256))
    O["o_gv"] = dout("o_gv", (2, max(n_c, 1), 256, 256))

    st = contextlib.ExitStack()
    with st:
        K = Sched(nc)

        def sb(name, shape, dt):
            return Buf(st.enter_context(nc.sbuf_tensor("sb_" + name, list(shape), dt)).ap(), name)

        XT = [[sb("x%d_%d" % (kc, b), (128, 512), F32) for b in range(5)] for kc in range(NKC)]
        HT = [[sb("h%d_%d" % (kc, b), (128, 512), BF16) for b in range(5)] for kc in range(NKC)]
        ident = sb("ident", (128, 128), F32)
        identb = sb("identb", (128, 128), BF16)
        onesf = sb("onesf", (128, 128), F32)
        onesb = sb("onesb", (128, 128), BF16)
        condT = sb("condT", (128, 2, 8), F32)
        modT = [sb("modT%d" % i, (128, 48, 2), F32) for i in range(2)]
        coefA = sb("coefA", (128, 2, 8, 2), F32)
        gnorm = sb("gnorm", (128, 2, 8), F32)
        epsv = sb("epsv", (128, 1), F32)
        WR_BYTES = 85 * 1024
        wr_t = sb("wr", (128, WR_BYTES // 4), F32)
        R = Region(wr_t.ap, WR_BYTES)
        PS = [Buf(st.enter_context(nc.psum_tensor("ps%d" % i, [128, 512], F32)).ap(), "ps%d" % i, excl=True) for i in range(8)]

        def psb(i, dt=F32):
            if dt == BF16:
                return View(PS[i], PS[i].ap.bitcast(BF16))
            return PS[i].v

        K.dma(ident.v, I["ident"].v)
        K.dma(onesf.v, I["ones"].v)
        K.dma(identb.v, I["ident"].v, eng="pool")
        K.dma(onesb.v, I["ones"].v, eng="pool")
        K.memset("dve", epsv.v, EPS)

        R.reset()
        xst = [R.alloc((128, D), F32, "xst%d" % i) for i in range(2)]
        import os
        NT_DBG = int(os.environ.get("KNT", "20"))
        for t in range(NT_DBG):
            src = I["xp"][t * 128:(t + 1) * 128, :] if t < 4 else I["xs"][(t - 4) * 128:(t - 3) * 128, :]
            s_ = xst[(t + int(os.environ.get("KSL", "0"))) % 2]
            if os.environ.get("KV", "") == "a":
                src = I["xp"][0:128, :]
            if os.environ.get("KV", "") == "b":
                s_ = xst[0]
            K.dma(s_.v, src, eng=os.environ.get("KQ", "sp"))
            if os.environ.get("KV", "") == "c" and t == 1:
                continue
            b, off = divmod(t * 128, 512)
            for g in range(2):
                pb = psb((t * 2 + g + int(os.environ.get("KOFF", "0"))) % int(os.environ.get("KNB", "4")))
                for q in range(4):
                    kc = g * 4 + q
                    K.transpose(pb[:, q * 128:(q + 1) * 128], s_[:, kc * 128:(kc + 1) * 128], ident.v)
                for q in range(4):
                    kc = g * 4 + q
                    eng = "dve" if q % 2 == 0 else "act"
                    if os.environ.get("KV", "") == "d":
                        eng = "dve"
                    if os.environ.get("KV", "") == "e":
                        eng = "act"
                    K.copy(eng, XT[kc][b][:, off:off + 128], pb[:, q * 128:(q + 1) * 128])

        import os
        if os.environ.get("KDBG", "") != "1":
            cst = R.alloc((16, 128), F32, "cst")
            K.dma(cst.v, I["cond"].v.re("c k p -> (c k) p"))
            K.transpose(psb(4)[:, 0:16], cst.v, ident[0:16, 0:16])
            K.act(condT.v.re("p c k -> p (c k)"), psb(4)[:, 0:16], AF.Silu)

        def adaln(i):
            mt = modT[i % 2]
            bst = R.alloc((48, 128), F32, "bst")
            bT = R.alloc((128, 48), F32, "bT")
            K.dma(bst.v, I["b_mod"][i])
            K.transpose(psb(4)[:, 0:48], bst.v, ident[0:48, 0:48])
            K.copy("dve", bT.v, psb(4)[:, 0:48])
            wst = [R.alloc((128, 8, 256), F32, "wmod%d" % q) for q in range(2)]
            for cb in range(24):
                w_ = wst[cb % 2]
                K.dma(w_.v, I["w_mod"][i].re("(k p) n -> p k n", p=128)[:, :, cb * 256:(cb + 1) * 256])
                pb = psb(5)
                for fl in range(2):
                    f = cb * 2 + fl
                    for kc in range(8):
                        K.mm(pb[:, f * 2:f * 2 + 2], w_[:, kc, fl * 128:(fl + 1) * 128], condT[:, :, kc],
                             start=(kc == 0), stop=(kc == 7))
                if cb % 2 == 1:
                    yield
            K.tt("dve", mt.v, psb(5)[:, 0:96].re("p (f c) -> p f c", c=2), bT.v.un(2).bc([128, 48, 2]), ALU.add)
            yield

        def run(gen):
            for _ in gen:
                pass

        def load_gains(i):
            gst = R.alloc((16, 128), F32, "gst")
            K.dma(gst[0:8, :], I["norm_mix"][i])
            K.dma(gst[8:16, :], I["norm_mlp"][i])
            K.transpose(psb(4)[:, 0:16], gst.v, ident[0:16, 0:16])
            K.copy("dve", gnorm.v.re("p w k -> p (w k)"), psb(4)[:, 0:16])

        def make_coef(i):
            mt = modT[i % 2]
            for w in range(2):
                sc = mt[:, (3 * w + 1) * 8:(3 * w + 2) * 8, :]
                K.ts("dve", coefA[:, w], sc, 1.0, ALU.add)
                K.tt("dve", coefA[:, w], coefA[:, w], gnorm[:, w].un(2).bc([128, 8, 2]), ALU.mult)

        def mod_norm(i, w):
            mt = modT[i % 2]
            sq = [R.alloc((128, 512), BF16, "sq%d" % q) for q in range(2)]
            rs = [R.alloc((128, 512), F32, "rs%d" % q) for q in range(2)]
            tmp = [R.alloc((128, 512), F32, "ntmp%d" % q) for q in range(2)]
            for b, (c0, n, cond) in enumerate(TB):
                pb = psb(b % 2)
                for kc in range(8):
                    s_ = sq[kc % 2]
                    if kc % 2 == 0:
                        K.act(s_.v, XT[kc][b].v, AF.Square)
                    else:
                        K.tt("pool", s_.v, XT[kc][b].v, XT[kc][b].v, ALU.mult)
                    K.mm(pb, onesb.v, s_.v, start=(kc == 0), stop=(kc == 7))
                r_ = rs[b % 2]
                K.act(r_.v, pb, AF.Ln, bias=epsv.v, scale=1.0 / D)
                K.act(r_.v, r_.v, AF.Exp, scale=-0.5)
                for kc in range(8):
                    t_ = tmp[kc % 2]
                    K.stt("dve", t_.v, XT[kc][b].v, coefA[:, w, kc, cond:cond + 1], r_.v, ALU.mult, ALU.mult)
                    K.act(HT[kc][b].v, t_.v, AF.Identity, bias=mt[:, 3 * w * 8 + kc, cond:cond + 1])

        def mlp(i, bg):
            mt = modT[i % 2]
            win = [R.alloc((128, 8, 512), BF16, "win%d" % q) for q in range(2)]
            wout = [R.alloc((128, 4, D), BF16, "wout%d" % q) for q in range(2)]
            uu = [R.alloc((128, 4, 512), BF16, "uu%d" % q) for q in range(2)]
            rr = [R.alloc((128, 512), BF16, "rr%d" % q) for q in range(2)]
            it = 0
            for fb in range(8):
                wi = win[fb % 2]
                wo = wout[fb % 2]
                K.dma(wi.v, I["w_mlp_in"][i].re("(k p) n -> p k n", p=128)[:, :, fb * 512:(fb + 1) * 512], eng="pool")
                K.dma(wo.v, I["w_mlp_out"][i][fb * 512:(fb + 1) * 512, :].re("(f p) n -> p f n", p=128), eng="pool")
                for b, (c0, n, cond) in enumerate(TB):
                    u_ = uu[it % 2]
                    for fc in range(4):
                        pb = psb(fc % 2)
                        for kc in range(8):
                            K.mm(pb, wi[:, kc, fc * 128:(fc + 1) * 128], HT[kc][b].v, start=(kc == 0), stop=(kc == 7))
                        r_ = rr[fc % 2]
                        K.act(r_.v, pb, AF.Relu)
                        K.tt("pool", u_[:, fc, :], r_.v, r_.v, ALU.mult)
                    for oc in range(8):
                        pb = psb(2 + oc % 2)
                        for fc in range(4):
                            K.mm(pb, wo[:, fc, oc * 128:(oc + 1) * 128], u_[:, fc, :], start=(fc == 0), stop=(fc == 3))
                        K.stt("dve", XT[oc][b].v, pb, mt[:, 40 + oc, cond:cond + 1], XT[oc][b].v, ALU.mult, ALU.add)
                    it += 1
                    if bg is not None:
                        next(bg, None)
            if bg is not None:
                run(bg)

        def out_accum(i, oT, wo_h):
            mt = modT[i % 2]
            for b, (c0, n, cond) in enumerate(TB):
                for oc in range(8):
                    pb = psb(6 + oc % 2)
                    K.mm(pb, wo_h[:, oc * 128:(oc + 1) * 128], oT[b])
                    K.stt("dve", XT[oc][b].v, pb, mt[:, 16 + oc, cond:cond + 1], XT[oc][b].v, ALU.mult, ALU.add)

        def fm_norm(dst, src_ps, npart, ncols, gain, nfeat, scr, rope=None, ones_v=None, eps_v=None):
            P = npart
            K.act(scr["sq"][0:P, 0:ncols], src_ps[0:P, 0:ncols], AF.Square)
            pd = scr["pd"]
            K.mm(pd[0:P, 0:ncols], onesb[0:P, 0:P], scr["sq"][0:P, 0:ncols])
            K.act(scr["rs"][0:P, 0:ncols], pd[0:P, 0:ncols], AF.Ln, bias=(eps_v or epsv)[0:P, :], scale=1.0 / nfeat)
            K.act(scr["rs"][0:P, 0:ncols], scr["rs"][0:P, 0:ncols], AF.Exp, scale=-0.5)
            if rope is None:
                K.stt("dve", dst, src_ps[0:P, 0:ncols], gain, scr["rs"][0:P, 0:ncols], ALU.mult, ALU.mult)
            else:
                cos, sin, permT = rope
                nb = scr["nb"]
                K.stt("dve", nb[0:P, 0:ncols], src_ps[0:P, 0:ncols], gain, scr["rs"][0:P, 0:ncols], ALU.mult, ALU.mult)
                K.mm(pd[0:P, 0:ncols], permT, nb[0:P, 0:ncols])
                K.tt("pool", scr["t1"][0:P, 0:ncols], nb[0:P, 0:ncols], cos, ALU.mult)
                K.tt("dve", scr["t2"][0:P, 0:ncols], pd[0:P, 0:ncols], sin, ALU.mult)
                K.tt("pool", dst, scr["t1"][0:P, 0:ncols], scr["t2"][0:P, 0:ncols], ALU.add)

        def norm_scr(light=False):
            if light:
                return {"sq": R.alloc((128, 512), BF16, "nsq"), "rs": R.alloc((128, 512), F32, "nrs"), "pd": psb(5)}
            return {"sq": R.alloc((128, 512), BF16, "nsq"), "rs": R.alloc((128, 512), F32, "nrs"),
                    "t1": R.alloc((128, 512), F32, "nt1"), "t2": R.alloc((128, 512), F32, "nt2"),
                    "nb": R.alloc((128, 512), BF16, "nnb"), "pd": psb(5)}

        QB = [(0, 256, [0, 1]), (256, 256, [2, 3])] + [(512 + 512 * q, 512, list(range(4, 24))) for q in range(4)]

        def attn_head(kparts, vtile, qparts, oT, scale, pbufs):
            steps = []
            for qi, (q0, nq, kts) in enumerate(QB):
                for ji, kt in enumerate(kts):
                    steps.append((qi, q0, nq, ji, kt, len(kts)))
            pts = {}

            def issue_s(n):
                qi, q0, nq, ji, kt, nk = steps[n]
                pss = psb(n % 3)
                for pi, (kT, P) in enumerate(kparts):
                    K.mm(pss[:, 0:nq], kT[0:P, kt * 128:(kt + 1) * 128], qparts[pi][0:P, q0:q0 + nq],
                         start=(pi == 0), stop=(pi == len(kparts) - 1))
                pt = pbufs[n % len(pbufs)]
                K.act(pt[:, 0:nq], pss[:, 0:nq], AF.Exp, scale=scale)
                pts[n] = pt

            def issue_pv(n):
                qi, q0, nq, ji, kt, nk = steps[n]
                po, pdn = (psb(3), psb(4)) if qi % 2 == 0 else (psb(5), psb(6))
                pt = pts.pop(n)
                K.mm(po[:, 0:nq], vtile(kt), pt[:, 0:nq], start=(ji == 0), stop=(ji == nk - 1))
                K.mm(pdn[:, 0:nq], onesb.v, pt[:, 0:nq], start=(ji == 0), stop=(ji == nk - 1))
                if ji == nk - 1:
                    rd = pbufs_rd[qi % 2]
                    K.act(rd[:, 0:nq], pdn[:, 0:nq], AF.Ln)
                    K.act(rd[:, 0:nq], rd[:, 0:nq], AF.Exp, scale=-1.0)
                    K.tt("dve", oT[:, q0:q0 + nq], po[:, 0:nq], rd[:, 0:nq], ALU.mult)
            SK = 2
            for n in range(len(steps) + SK):
                if n < len(steps):
                    issue_s(n)
                if n - SK >= 0:
                    issue_pv(n - SK)

        pbufs_rd = [None, None]

        def gqa_layer(i, j):
            R.reset()
            kT = R.alloc((128, 2, 3072), BF16, "kT")
            V = R.alloc((128, 24, 256), BF16, "V")
            cos = R.alloc((128, 2048), BF16, "cos")
            sin = R.alloc((128, 2048), BF16, "sin")
            permT = R.alloc((128, 128), BF16, "permT")
            kg = R.alloc((128, 2), F32, "kg")
            kgb = R.alloc((128, 128), F32, "kgb")
            K.dma(cos.v, I["rope_c_cos"].v, eng="pool")
            K.dma(sin.v, I["rope_c_sin"].v, eng="pool")
            K.dma(permT.v, I["rope_c_p"].v, eng="pool")
            gst = R.alloc((2, 128), F32, "gst2")
            K.dma(gst[0:1, :], I["gqa_q_norm"][j:j + 1, :])
            K.dma(gst[1:2, :], I["gqa_k_norm"][j:j + 1, :])
            K.transpose(psb(4)[:, 0:2], gst.v, ident[0:2, 0:2])
            K.copy("dve", kg.v, psb(4)[:, 0:2])
            K.dma(kgb.v, View(I["gqa_k_norm"], I["gqa_k_norm"].ap[j].partition_broadcast(128)))
            top0 = R.top
            wkv = R.alloc((128, 8, 512), BF16, "wkv")
            K.dma(wkv.v, I["gqa_w_in"][j].re("(k p) n -> p k n", p=128)[:, :, 1024:1536], eng="pool")
            scr = norm_scr()
            for kvh in range(2):
                for b, (c0, n, cond) in enumerate(TB):
                    pb = psb(b % 2)
                    for kc in range(8):
                        K.mm(pb, wkv[:, kc, kvh * 128:(kvh + 1) * 128], HT[kc][b].v, start=(kc == 0), stop=(kc == 7))
                    rope = None if b == 0 else (cos[:, c0 - 512:c0], sin[:, c0 - 512:c0], permT.v)
                    fm_norm(kT[:, kvh, c0:c0 + 512], pb, 128, 512, kg[:, 1:2], 128, scr, rope)
            vst = R.alloc((128, 256), F32, "vst")
            kst = R.alloc((128, 256), F32, "kst")
            ssq = R.alloc((128, 2), F32, "ssq")
            ksq = R.alloc((128, 256), F32, "ksq")
            for t in range(20):
                b, off = divmod(t * 128, 512)
                pb = psb(2 + t % 2)
                ncol = 512 if t < 4 else 256
                for kc in range(8):
                    rhs = wkv[:, kc, 0:512] if t < 4 else wkv[:, kc, 256:512]
                    K.mm(pb[:, 0:ncol], HT[kc][b][:, off:off + 128], rhs, start=(kc == 0), stop=(kc == 7))
                if t < 4:
                    K.copy("act", V[:, t, :], pb[:, 256:512])
                    K.copy("dve", vst.v, pb[:, 256:512])
                    K.dma(O["o_gv"][t // 2, j, (t % 2) * 128:(t % 2 + 1) * 128, :], vst.v)
                    K.act(ksq.v, pb[:, 0:256], AF.Square)
                    K.reduce("dve", ssq.v, ksq.v.re("p (h d) -> p h d", h=2))
                    K.act(ssq.v, ssq.v, AF.Ln, bias=epsv.v, scale=1.0 / 128)
                    K.act(ssq.v, ssq.v, AF.Exp, scale=-0.5)
                    K.tt("dve", kst.v.re("p (h d) -> p h d", h=2), pb[:, 0:256].re("p (h d) -> p h d", h=2),
                         ssq.v.un(2).bc([128, 2, 128]), ALU.mult)
                    K.tt("pool", kst.v.re("p (h d) -> p h d", h=2), kst.v.re("p (h d) -> p h d", h=2),
                         kgb.v.un(1).bc([128, 2, 128]), ALU.mult)
                    K.dma(O["o_gk"][t // 2, j, (t % 2) * 128:(t % 2 + 1) * 128, :], kst.v)
                else:
                    K.copy("act", V[:, t, :], pb[:, 0:256])
            K.dma(V[:, 20:24, :], I["cache_gqa_v"][j].re("(t p) n -> p t n", p=128), eng="pool")
            cst_ = R.alloc((128, 4, 256), F32, "cst_")
            K.dma(cst_.v, I["cache_gqa_k"][j].re("(t p) n -> p t n", p=128))
            for t in range(4):
                for kvh in range(2):
                    pb = psb(t % 2)
                    K.transpose(pb[:, 0:128], cst_[:, t, kvh * 128:(kvh + 1) * 128], ident.v)
                    K.copy("act", kT[:, kvh, 2560 + t * 128:2560 + (t + 1) * 128], pb[:, 0:128])
            R.top = top0
            scr = norm_scr()
            wq = [R.alloc((128, 8, 128), BF16, "wq%d" % q) for q in range(2)]
            woh = [R.alloc((128, D), BF16, "woh%d" % q) for q in range(2)]
            qT = [R.alloc((128, T), BF16, "qT%d" % q) for q in range(2)]
            oT = [R.alloc((128, T), BF16, "oT%d" % q) for q in range(2)]
            pb_ = [R.alloc((128, 512), BF16, "pt%d" % q) for q in range(4)]
            pbufs_rd[0] = R.alloc((128, 512), F32, "rd0")
            pbufs_rd[1] = R.alloc((128, 512), F32, "rd1")
            for h in range(8):
                kvh = h // 4
                w_ = wq[h % 2]
                K.dma(w_.v, I["gqa_w_in"][j].re("(k p) n -> p k n", p=128)[:, :, h * 128:(h + 1) * 128], eng="pool")
                K.dma(woh[h % 2].v, I["gqa_w_out"][j][h * 128:(h + 1) * 128, :], eng="pool")
                q_ = qT[h % 2]
                for b, (c0, n, cond) in enumerate(TB):
                    pb = psb(6 + b % 2)
                    for kc in range(8):
                        K.mm(pb, w_[:, kc, :], HT[kc][b].v, start=(kc == 0), stop=(kc == 7))
                    rope = None if b == 0 else (cos[:, c0 - 512:c0], sin[:, c0 - 512:c0], permT.v)
                    fm_norm(q_[:, c0:c0 + 512], pb, 128, 512, kg[:, 0:1], 128, scr, rope)
                o_ = oT[h % 2]
                attn_head([(kT[:, kvh, :], 128)], lambda kt, kvh=kvh: V[:, kt, kvh * 128:(kvh + 1) * 128],
                          [q_.v], o_.v, 128 ** -0.5, pb_)
                out_accum(i, [o_[:, c0:c0 + 512] for (c0, n, cond) in TB], woh[h % 2].v)

        def mla_layer(i, j):
            R.reset()
            cqn = R.alloc((128, 3, T), BF16, "cqn")
            ckvn = R.alloc((128, 2, 3072), BF16, "ckvn")
            krT = R.alloc((64, 3072), BF16, "krT")
            cos = R.alloc((64, 2048), BF16, "cosb")
            sin = R.alloc((64, 2048), BF16, "sinb")
            permT = R.alloc((64, 64), BF16, "permTb")
            gl = R.alloc((128, 9), F32, "gl")
            K.dma(cos.v, I["rope_b_cos"].v, eng="pool")
            K.dma(sin.v, I["rope_b_sin"].v, eng="pool")
            K.dma(permT.v, I["rope_b_p"].v, eng="pool")
            gst = R.alloc((9, 128), F32, "gst9")
            K.memset("dve", gst.v, 0.0)
            K.dma(gst[0:3, :], I["mla_q_lat_norm"][j])
            K.dma(gst[3:5, :], I["mla_kv_lat_norm"][j])
            K.dma(gst[5:6, :], I["mla_qn_nope"][j:j + 1, :])
            K.dma(gst[6:7, :], I["mla_kn_nope"][j:j + 1, :])
            K.dma(gst[7:8, 0:64], I["mla_qn_rope"][j:j + 1, :])
            K.dma(gst[8:9, 0:64], I["mla_kn_rope"][j:j + 1, :])
            K.transpose(psb(4)[:, 0:9], gst.v, ident[0:9, 0:9])
            K.copy("dve", gl.v, psb(4)[:, 0:9])
            top0 = R.top
            wd = R.alloc((128, 8, 704), BF16, "wd")
            K.dma(wd.v, I["mla_w_down"][j].re("(k p) n -> p k n", p=128), eng="pool")
            scr = norm_scr()
            sqs = [R.alloc((128, 512), BF16, "msq%d" % q) for q in range(3)]
            rs = R.alloc((128, 512), F32, "mrs")
            f32o = R.alloc((128, 3, 512), F32, "f32o")
            groups = [(0, 3, 0, 384), (384, 2, 3, 256)]
            for b, (c0, n, cond) in enumerate(TB):
                for (col0, nch, gcol, nfeat) in groups:
                    pbs = [psb(q) for q in range(nch)]
                    for c in range(nch):
                        for kc in range(8):
                            K.mm(pbs[c], wd[:, kc, col0 + c * 128:col0 + (c + 1) * 128], HT[kc][b].v,
                                 start=(kc == 0), stop=(kc == 7))
                    pd = psb(5)
                    for c in range(nch):
                        K.act(sqs[c].v, pbs[c], AF.Square)
                        K.mm(pd, onesb.v, sqs[c].v, start=(c == 0), stop=(c == nch - 1))
                    K.act(rs.v, pd, AF.Ln, bias=epsv.v, scale=1.0 / nfeat)
                    K.act(rs.v, rs.v, AF.Exp, scale=-0.5)
                    for c in range(nch):
                        if nch == 3:
                            dst = cqn[:, c, c0:c0 + 512]
                        else:
                            dst = ckvn[:, c, c0:c0 + 512]
                        K.stt("dve", dst, pbs[c], gl[:, gcol + c:gcol + c + 1], rs.v, ALU.mult, ALU.mult)
                        if nch == 2 and b == 0:
                            K.stt("dve", f32o[:, c, :], pbs[c], gl[:, gcol + c:gcol + c + 1], rs.v, ALU.mult, ALU.mult)
                pb = psb(3)
                for kc in range(8):
                    K.mm(pb[0:64, :], wd[:, kc, 640:704], HT[kc][b].v, start=(kc == 0), stop=(kc == 7))
                rope = None if b == 0 else (cos[:, c0 - 512:c0], sin[:, c0 - 512:c0], permT.v)
                fm_norm(krT[:, c0:c0 + 512], pb, 64, 512, gl[0:64, 8:9], 64, scr, rope)
                if b == 0:
                    fm_norm(f32o[0:64, 2, :], pb, 64, 512, gl[0:64, 8:9], 64, scr, None)
            ost = R.alloc((128, 320), F32, "ost")
            for t in range(4):
                pb = psb(t % 2)
                for c in range(2):
                    K.transpose(pb[:, c * 128:(c + 1) * 128], f32o[:, c, t * 128:(t + 1) * 128], ident.v)
                K.transpose(pb[:, 256:320], f32o[0:64, 2, t * 128:(t + 1) * 128], ident[0:64, 0:64])
                K.copy("dve", ost.v, pb[:, 0:320])
                K.dma(O["o_ckv"][t // 2, j, (t % 2) * 128:(t % 2 + 1) * 128, :], ost[:, 0:256])
                K.dma(O["o_kr"][t // 2, j, (t % 2) * 128:(t % 2 + 1) * 128, :], ost[:, 256:320])
            cst_ = R.alloc((128, 4, 320), F32, "mcst")
            K.dma(cst_[:, :, 0:256], I["cache_mla_ckv"][j].re("(t p) n -> p t n", p=128))
            K.dma(cst_[:, :, 256:320], I["cache_mla_krope"][j].re("(t p) n -> p t n", p=128))
            for t in range(4):
                pb = psb(t % 2)
                for c in range(2):
                    K.transpose(pb[:, c * 128:(c + 1) * 128], cst_[:, t, c * 128:(c + 1) * 128], ident.v)
                K.transpose(pb[0:64, 256:384], cst_[:, t, 256:320], ident.v)
                for c in range(2):
                    K.copy("act", ckvn[:, c, 2560 + t * 128:2560 + (t + 1) * 128], pb[:, c * 128:(c + 1) * 128])
                K.copy("dve", krT[:, 2560 + t * 128:2560 + (t + 1) * 128], pb[0:64, 256:384])
            R.top = top0
            scr = norm_scr()
            wuq = [R.alloc((128, 3, 192), BF16, "wuq%d" % q) for q in range(1)]
            wukv = [R.alloc((128, 2, 256), BF16, "wukv%d" % q) for q in range(1)]
            woh = [R.alloc((128, D), BF16, "mwoh%d" % q) for q in range(1)]
            qn = [R.alloc((128, T), BF16, "qn%d" % q) for q in range(1)]
            qr = [R.alloc((64, T), BF16, "qr%d" % q) for q in range(1)]
            kn = [R.alloc((128, 3072), BF16, "kn%d" % q) for q in range(1)]
            Vh = [R.alloc((128, 24, 128), BF16, "Vh%d" % q) for q in range(1)]
            oT = qn
            pb_ = [R.alloc((128, 512), BF16, "mpt%d" % q) for q in range(3)]
            pbufs_rd[0] = R.alloc((128, 512), F32, "mrd0")
            pbufs_rd[1] = pbufs_rd[0]
            for h in range(8):
                p_ = 0
                K.dma(wuq[p_].v, I["mla_w_uq"][j].re("(k p) n -> p k n", p=128)[:, :, h * 192:(h + 1) * 192], eng="pool")
                K.dma(wukv[p_].v, I["mla_w_ukv"][j].re("(k p) n -> p k n", p=128)[:, :, h * 256:(h + 1) * 256], eng="pool")
                K.dma(woh[p_].v, I["mla_w_out"][j][h * 128:(h + 1) * 128, :], eng="pool")
                for cb in range(6):
                    c0 = cb * 512
                    pb = psb(6 + cb % 2)
                    for c in range(2):
                        K.mm(pb, wukv[p_][:, c, 0:128], ckvn[:, c, c0:c0 + 512], start=(c == 0), stop=(c == 1))
                    fm_norm(kn[p_][:, c0:c0 + 512], pb, 128, 512, gl[:, 6:7], 128, scr, None)
                for g in range(6):
                    pb = psb(6 + g % 2)
                    for q in range(4):
                        t = g * 4 + q
                        for c in range(2):
                            K.mm(pb[:, q * 128:(q + 1) * 128], ckvn[:, c, t * 128:(t + 1) * 128], wukv[p_][:, c, 128:256],
                                 start=(c == 0), stop=(c == 1))
                    K.copy("act", Vh[p_][:, g * 4:(g + 1) * 4, :], pb.re("p (q d) -> p q d", q=4))
                for b, (c0, n, cond) in enumerate(TB):
                    pb = psb(6 + b % 2)
                    for c in range(3):
                        K.mm(pb, wuq[p_][:, c, 0:128], cqn[:, c, c0:c0 + 512], start=(c == 0), stop=(c == 2))
                    fm_norm(qn[p_][:, c0:c0 + 512], pb, 128, 512, gl[:, 5:6], 128, scr, None)
                    pb2 = psb(7 - b % 2)
                    for c in range(3):
                        K.mm(pb2[0:64, :], wuq[p_][:, c, 128:192], cqn[:, c, c0:c0 + 512], start=(c == 0), stop=(c == 2))
                    rope = None if b == 0 else (cos[:, c0 - 512:c0], sin[:, c0 - 512:c0], permT.v)
                    fm_norm(qr[p_][:, c0:c0 + 512], pb2, 64, 512, gl[0:64, 7:8], 64, scr, rope)
                attn_head([(kn[p_].v, 128), (krT.v, 64)], lambda kt, p_=p_: Vh[p_][:, kt, :],
                          [qn[p_].v, qr[p_].v], oT[p_].v, 192 ** -0.5, pb_)
                out_accum(i, [oT[p_][:, c0:c0 + 512] for (c0, n, cond) in TB], woh[p_].v)


        def gdn_layer(i, j):
            R.reset()
            NEG = -30000.0
            msk = R.alloc((64, 6, 64), F32, "msk")
            K.dma(msk.v, I["gdn_masks"][0:6, 0:64, 0:64].re("m p n -> p m n"))
            U = [msk[:, 0, :], msk[:, 1, :]]
            NEGS = [msk[:, 2, :], msk[:, 4, :]]
            NEGI = [msk[:, 3, :], msk[:, 5, :]]
            I64 = ident[0:64, 0:64]
            eps2 = R.alloc((128, 1), F32, "eps2")
            K.memset("dve", eps2.v, EPS / 128.0)
            gq = R.alloc((128, 3), F32, "gq")
            K.memset("dve", gq[:, 0:1], 1.0 / 128.0)
            K.memset("dve", gq[:, 1:2], 128.0 ** -0.5)
            cst_ = R.alloc((73, 128), F32, "gcst")
            K.dma(cst_[0:72, :], I["gdn_conv"][j])
            K.dma(cst_[72:73, :], I["gdn_out_norm"][j:j + 1, :])
            cw = R.alloc((128, 73), F32, "cw")
            K.transpose(psb(4)[:, 0:73], cst_.v, ident[0:73, 0:73])
            K.copy("dve", cw.v, psb(4)[:, 0:73])
            alb = R.alloc((64, 16), F32, "alb")
            dtb = R.alloc((64, 16), F32, "dtb")
            K.dma(alb.v, View(I["gdn_a_log"], I["gdn_a_log"].ap[j].partition_broadcast(64)))
            K.dma(dtb.v, View(I["gdn_dt_bias"], I["gdn_dt_bias"].ap[j].partition_broadcast(64)))
            K.act(alb.v, alb.v, AF.Exp)
            K.ts("dve", alb.v, alb.v, -1.0, ALU.mult)
            gcum = R.alloc((64, 40, 16), F32, "gcum")
            sA = R.alloc((64, 40, 16), F32, "sA")
            eg2 = R.alloc((128, 40, 16), F32, "eg2")
            ekd2 = R.alloc((128, 40, 16), F32, "ekd2")
            beta2 = R.alloc((128, 40, 16), F32, "beta2")
            egt = R.alloc((128, 40, 16), F32, "egt")
            Idd = R.alloc((64, 128), F32, "Idd")
            Idup = R.alloc((128, 64), F32, "Idup")
            K.copy("dve", Idd[:, 0:64], ident[0:64, 0:64])
            K.copy("dve", Idd[:, 64:128], ident[0:64, 0:64])
            K.copy("dve", Idup[0:64, :], ident[0:64, 0:64])
            K.copy("dve", Idup[64:128, :], ident[64:128, 64:128])
            top_g = R.top
            eg = R.alloc((64, 40, 16), F32, "eg")
            ekd = R.alloc((64, 40, 16), F32, "ekd")
            beta = R.alloc((64, 40, 16), F32, "beta")
            wgb = R.alloc((128, 8, 32), BF16, "wgb")
            K.dma(wgb.v, I["gdn_w_gb"][j].re("(k p) n -> p k n", p=128), eng="pool")
            G = R.alloc((64, 40, 32), F32, "G")
            g_ = R.alloc((64, 40, 16), F32, "g_")
            t1 = R.alloc((64, 40, 16), F32, "gt1")
            t2 = R.alloc((64, 40, 16), F32, "gt2")
            for grp in range(3):
                c_lo, c_hi = grp * 16, min(40, grp * 16 + 16)
                pb = psb(grp % 2)
                for c in range(c_lo, c_hi):
                    b, off = divmod(c * 64, 512)
                    for kc in range(8):
                        K.mm(pb[0:64, (c - c_lo) * 32:(c - c_lo + 1) * 32], HT[kc][b][:, off:off + 64], wgb[:, kc, :],
                             start=(kc == 0), stop=(kc == 7))
                K.copy("dve", G[:, c_lo:c_hi, :], pb[0:64, 0:(c_hi - c_lo) * 32].re("p (c n) -> p c n", n=32))

            def softplus(dst, x):
                K.act(t1.v, x, AF.Abs)
                K.act(t1.v, t1.v, AF.Exp, scale=-1.0)
                K.act(t1.v, t1.v, AF.Ln, bias=1.0)
                K.ts("dve", t2.v, x, 0.0, ALU.max)
                K.tt("dve", dst, t1.v, t2.v, ALU.add)
            K.tt("dve", g_.v, G[:, :, 0:16], dtb.v.un(1).bc([64, 40, 16]), ALU.add)
            softplus(g_.v, g_.v)
            K.tt("dve", g_.v, g_.v, alb.v.un(1).bc([64, 40, 16]), ALU.mult)
            K.ts("dve", sA.v, G[:, :, 16:32], -1.0, ALU.mult)
            softplus(sA.v, sA.v)
            K.ts("dve", sA.v, sA.v, -1.0, ALU.mult)
            K.act(beta.v, sA.v, AF.Exp)
            for d in range(2):
                pb = psb(2 + d)
                K.mm(pb[0:64, 0:320].re("p (c h) -> p c h", h=8), U[d], g_[:, :, d * 8:(d + 1) * 8])
                K.copy("dve", gcum[:, :, d * 8:(d + 1) * 8], pb[0:64, 0:320].re("p (c h) -> p c h", h=8))
            pbt = [psb(4), psb(5)]
            K.mm(pbt[0][:, 0:320], onesf[0:64, 0:128], g_[:, 0:20, :])
            K.mm(pbt[1][:, 0:320], onesf[0:64, 0:128], g_[:, 20:40, :])
            for q in range(2):
                K.act(egt[:, q * 20:(q + 1) * 20, :], pbt[q][:, 0:320].re("p (c h) -> p c h", h=16), AF.Exp)
                K.tt("dve", ekd[:, q * 20:(q + 1) * 20, :], pbt[q][0:64, 0:320].re("p (c h) -> p c h", h=16),
                     gcum[:, q * 20:(q + 1) * 20, :], ALU.subtract)
            K.act(ekd.v, ekd.v, AF.Exp)
            K.act(eg.v, gcum.v, AF.Exp)
            K.ts("dve", eg.v, eg.v, -1.0, ALU.mult)
            K.tt("dve", sA.v, sA.v, gcum.v, ALU.subtract)
            for (src_, dst_) in ((eg, eg2), (ekd, ekd2), (beta, beta2)):
                for q in range(2):
                    pbq = psb(q)
                    K.mm(pbq[:, 0:320], Idd.v, src_[:, q * 20:(q + 1) * 20, :])
                    K.copy("act" if q else "dve", dst_[:, q * 20:(q + 1) * 20, :], pbq[:, 0:320].re("p (c h) -> p c h", h=16))
            R.top = top_g
            qT = R.alloc((128, T), BF16, "gqT")
            kT = R.alloc((128, T), BF16, "gkT")
            k_c = R.alloc((128, 5, 4, 128), BF16, "k_c")
            v_c = R.alloc((128, 5, 4, 128), BF16, "v_c")
            oacc = R.alloc((128, T), BF16, "oacc")
            S = R.alloc((128, 128), F32, "S")
            Sb = R.alloc((128, 128), BF16, "Sb")
            top_h = R.top
            SEQ = [(0, 4, 0), (4, 8, 1), (8, 40, 2)]
            for h in range(8):
                R.top = top_h
                praw = R.alloc((128, T), F32, "praw")
                cv = R.alloc((128, T), F32, "cv")
                win = R.alloc((128, 8, 128), BF16, "gwin")
                scr = norm_scr(light=True)
                vT = None
                for xi in range(3):
                    K.dma(win.v, I["gdn_w_in"][j].re("(k p) n -> p k n", p=128)[:, :, xi * 1024 + h * 128:xi * 1024 + (h + 1) * 128],
                          eng="pool")
                    for b, (c0, n, cond) in enumerate(TB):
                        pb = psb(b % 2)
                        for kc in range(8):
                            K.mm(pb, win[:, kc, :], HT[kc][b].v, start=(kc == 0), stop=(kc == 7))
                        K.copy("act" if b % 2 else "dve", praw[:, c0:c0 + 512], pb)
                    ch = xi * 8 + h
                    for (s0, s1) in [(0, 256), (256, 512), (512, 2560)]:
                        K.ts("dve", cv[:, s0:s1], praw[:, s0:s1], cw[:, 24 + ch:25 + ch], ALU.mult)
                        K.stt("dve", cv[:, s0 + 1:s1], praw[:, s0:s1 - 1], cw[:, ch:ch + 1], cv[:, s0 + 1:s1], ALU.mult, ALU.add)
                        K.stt("dve", cv[:, s0:s1 - 1], praw[:, s0 + 1:s1], cw[:, 48 + ch:49 + ch], cv[:, s0:s1 - 1], ALU.mult, ALU.add)
                    K.act(cv.v, cv.v, AF.Silu)
                    if xi < 2:
                        dstT = qT if xi == 0 else kT
                        for b, (c0, n, cond) in enumerate(TB):
                            fm_norm(dstT[:, c0:c0 + 512], cv[:, c0:c0 + 512], 128, 512, gq[:, xi:xi + 1], 128, scr, None, eps_v=eps2)
                    else:
                        sv_top = R.top
                        R.top = top_h
                        vT = R.alloc((128, T), BF16, "vT")
                        R.top = sv_top
                        K.copy("pool", vT.v, cv.v)
                for (srcT, dstc) in ((kT, k_c), (vT, v_c)):
                    for b in range(5):
                        pbf = psb(2 + b % 2)
                        for c in range(8):
                            h0, pr = 64 * (c // 4), c % 4
                            K.mm(pbf[h0:h0 + 64, pr * 128:(pr + 1) * 128], srcT[:, b * 512 + c * 64:b * 512 + (c + 1) * 64], identb.v)
                        K.copy("act" if b % 2 else "dve", dstc[:, b, :, :], pbf[:, 0:512].re("p (c d) -> p c d", d=128))
                R.top = top_h
                scrs = [R.alloc((64, 8, 64), F32, "gs%d" % q) for q in range(3)]
                st_ = [R.alloc((128, 4, 64), F32, "gt%d" % q) for q in range(10)]
                kgn = R.alloc((128, 4, 128), BF16, "kgn")
                erow = R.alloc((128, 512), BF16, "erow")
                vnew = R.alloc((128, 128), BF16, "vnew")
                psets = []
                for q in range(2):
                    psets.append({"kdec": R.alloc((128, 4, 128), BF16, "kdec%d" % q),
                                  "qdT": R.alloc((128, 8, 64), BF16, "qdT%d" % q),
                                  "QKt": R.alloc((128, 4, 64), BF16, "QKt%d" % q),
                                  "Tt": R.alloc((128, 4, 64), BF16, "Tt%d" % q),
                                  "WTn": R.alloc((128, 8, 64), BF16, "WTn%d" % q)})
                f2 = lambda v_: v_.re("p c i -> p (c i)")
                I64b = View(ident, I64.ap.unsqueeze(1).broadcast_to([64, 8, 64]))
                Idupb = View(Idup, Idup.ap.unsqueeze(1).broadcast_to([128, 4, 64]))
                HP = [(64 * (c // 4), c % 4) for c in range(8)]
                CORD = [0, 4, 1, 5, 2, 6, 3, 7]

                def intra(d, b, ps_):
                    dh = d * 8 + h
                    cb0 = b * 8
                    kdec, qdT, QKt, Tt, WTn = ps_["kdec"], ps_["qdT"], ps_["QKt"], ps_["Tt"], ps_["WTn"]
                    gc_b = gcum[:, cb0:cb0 + 8, dh]
                    Dg, RBA, RBQ = scrs
                    EA, EQ, At, Xa, M0_, M1_, PA0, PA1, PT0, PT1 = st_
                    K.tt("pool", Dg.v, I64b, gc_b.un(2).bc([64, 8, 64]), ALU.mult)
                    K.tt("dve", RBA.v, View(msk, NEGS[d].ap.unsqueeze(1).broadcast_to([64, 8, 64])),
                         sA[:, cb0:cb0 + 8, dh].un(2).bc([64, 8, 64]), ALU.add)
                    K.tt("pool", RBQ.v, View(msk, NEGI[d].ap.unsqueeze(1).broadcast_to([64, 8, 64])),
                         gc_b.un(2).bc([64, 8, 64]), ALU.subtract)
                    for hf in range(2):
                        r0 = 64 * hf
                        cc = cb0 + 4 * hf
                        K.tt("pool", kgn[r0:r0 + 64], k_c[r0:r0 + 64, b], eg2[r0:r0 + 64, cc:cc + 4, dh].un(2).bc([64, 4, 128]), ALU.mult)
                        K.tt("pool", kdec[r0:r0 + 64], k_c[r0:r0 + 64, b], ekd2[r0:r0 + 64, cc:cc + 4, dh].un(2).bc([64, 4, 128]), ALU.mult)
                    yield
                    pA, pQ, pE, pK = psb(0), psb(1), psb(2), psb(3)
                    for hf in range(2):
                        r0 = 64 * hf
                        K.mm(pA[r0:r0 + 64, 0:256], onesf[0:64, 0:64], f2(Dg[:, 4 * hf:4 * hf + 4, :]), start=True, stop=False)
                        K.mm(pA[r0:r0 + 64, 0:256], I64, f2(RBA[:, 4 * hf:4 * hf + 4, :]), start=False, stop=True)
                    for hf in range(2):
                        r0 = 64 * hf
                        K.mm(pQ[r0:r0 + 64, 0:256], onesf[0:64, 0:64], f2(Dg[:, 4 * hf:4 * hf + 4, :]), start=True, stop=False)
                        K.mm(pQ[r0:r0 + 64, 0:256], I64, f2(RBQ[:, 4 * hf:4 * hf + 4, :]), start=False, stop=True)
                    K.mm(pE, onesf[0:64, 0:128], f2(Dg.v))
                    for c in CORD:
                        r0, pr = HP[c]
                        cs = slice(b * 512 + c * 64, b * 512 + (c + 1) * 64)
                        K.mm(pK[r0:r0 + 64, pr * 64:(pr + 1) * 64], kT[:, cs], kT[:, cs])
                    pK2 = psb(4)
                    for c in CORD:
                        r0, pr = HP[c]
                        cs = slice(b * 512 + c * 64, b * 512 + (c + 1) * 64)
                        K.mm(pK2[r0:r0 + 64, pr * 64:(pr + 1) * 64], kT[:, cs], qT[:, cs])
                    K.act(f2(EA.v), pA[:, 0:256], AF.Exp)
                    K.act(f2(EQ.v), pQ[:, 0:256], AF.Exp)
                    K.act(erow.v, pE, AF.Exp)
                    yield
                    K.tt("dve", f2(At.v), pK[:, 0:256], f2(EA.v), ALU.mult)
                    K.tt("dve", f2(QKt.v), pK2[:, 0:256], f2(EQ.v), ALU.mult)
                    K.tt("pool", qdT.v.re("p c i -> p (c i)"), qT[:, b * 512:(b + 1) * 512], erow.v, ALU.mult)
                    pX = (psb(0), psb(1))
                    for c in CORD:
                        r0, pr = HP[c]
                        K.mm(pX[r0 // 64][r0:r0 + 64, pr * 64:(pr + 1) * 64], At[r0:r0 + 64, pr, :], ident[r0:r0 + 64, r0:r0 + 64])
                    K.copy("act", f2(Xa[0:64]), pX[0][0:64, 0:256])
                    K.copy("act", f2(Xa[64:128]), pX[1][64:128, 0:256])
                    M = M0_
                    K.tt("dve", M.v, Idupb, At.v, ALU.subtract)
                    yield
                    Pa, Pt = Xa, At
                    for lvl in range(1, 6):
                        p1, p2, p3 = (psb(1), psb(2)), (psb(3), psb(4)), (psb(0), psb(5))
                        for c in CORD:
                            r0, pr = HP[c]
                            K.mm(p1[r0 // 64][r0:r0 + 64, pr * 64:(pr + 1) * 64], Pt[r0:r0 + 64, pr, :], Pa[r0:r0 + 64, pr, :])
                        if lvl < 5:
                            for c in CORD:
                                r0, pr = HP[c]
                                K.mm(p2[r0 // 64][r0:r0 + 64, pr * 64:(pr + 1) * 64], Pa[r0:r0 + 64, pr, :], Pt[r0:r0 + 64, pr, :])
                        Pa2 = PA0 if lvl % 2 == 1 else PA1
                        for hf in range(2):
                            K.copy("act", f2(Pa2[64 * hf:64 * hf + 64]), p1[hf][64 * hf:64 * hf + 64, 0:256])
                        if lvl < 5:
                            Pt2 = PT0 if lvl % 2 == 1 else PT1
                            for hf in range(2):
                                K.copy("dve", f2(Pt2[64 * hf:64 * hf + 64]), p2[hf][64 * hf:64 * hf + 64, 0:256])
                        yield
                        for c in CORD:
                            r0, pr = HP[c]
                            K.mm(p3[r0 // 64][r0:r0 + 64, pr * 64:(pr + 1) * 64], Pa2[r0:r0 + 64, pr, :], M[r0:r0 + 64, pr, :])
                        if lvl < 5:
                            Mn = M1_ if M is M0_ else M0_
                            for hf in range(2):
                                K.tt("dve", f2(Mn[64 * hf:64 * hf + 64]), p3[hf][64 * hf:64 * hf + 64, 0:256], f2(M[64 * hf:64 * hf + 64]), ALU.add)
                            M = Mn
                        else:
                            for hf in range(2):
                                K.tt("dve", f2(Tt[64 * hf:64 * hf + 64]), p3[hf][64 * hf:64 * hf + 64, 0:256], f2(M[64 * hf:64 * hf + 64]), ALU.add)
                        Pa = Pa2
                        if lvl < 5:
                            Pt = Pt2
                        yield
                    pW = (psb(1), psb(2))
                    for c in CORD:
                        r0, pr = HP[c]
                        K.mm(pW[r0 // 64][:, pr * 64:(pr + 1) * 64], kgn[r0:r0 + 64, pr, :], Tt[r0:r0 + 64, pr, :])
                    for hf in range(2):
                        K.copy("act", WTn[:, 4 * hf:4 * hf + 4, :].re("p c i -> p (c i)"), pW[hf][:, 0:256])
                    yield

                def scan(d, b, ps_):
                    dh = d * 8 + h
                    cb0 = b * 8
                    kdec, qdT, QKt, Tt, WTn = ps_["kdec"], ps_["qdT"], ps_["QKt"], ps_["Tt"], ps_["WTn"]
                    pO = psb(6)
                    corder = list(range(8)) if d == 0 else list(range(7, -1, -1))
                    for c in corder:
                        cg = cb0 + c
                        r0, pr = HP[c]
                        seq = 0 if cg < 4 else (1 if cg < 8 else 2)
                        first = (cg == SEQ[seq][0]) if d == 0 else (cg == SEQ[seq][1] - 1)
                        last = (cg == SEQ[seq][1] - 1) if d == 0 else (cg == SEQ[seq][0])
                        if first:
                            if seq < 2:
                                K.memset("dve", S.v, 0.0)
                            else:
                                K.dma(S.v, (I["state_f"] if d == 0 else I["state_b"])[j, h])
                            K.copy("act", Sb.v, S.v)
                        pv = psb(7)
                        K.mm(pv[r0:r0 + 64, 0:128], Tt[r0:r0 + 64, pr, :], v_c[r0:r0 + 64, b, pr, :], start=True, stop=False)
                        K.mm(pv[r0:r0 + 64, 0:128], WTn[:, c, :], Sb.v, start=False, stop=True)
                        K.ts("dve", vnew[r0:r0 + 64, :], pv[r0:r0 + 64, 0:128], beta2[r0:r0 + 64, cg, dh:dh + 1], ALU.mult)
                        K.mm(pO[:, c * 64:(c + 1) * 64], Sb.v, qdT[:, c, :], start=True, stop=False)
                        K.mm(pO[:, c * 64:(c + 1) * 64], vnew[r0:r0 + 64, :], QKt[r0:r0 + 64, pr, :], start=False, stop=True)
                        K.mm(pv[:, 128:256], kdec[r0:r0 + 64, pr, :], vnew[r0:r0 + 64, :])
                        K.stt("dve", S.v, S.v, egt[:, cg, dh:dh + 1], pv[:, 128:256], ALU.mult, ALU.add)
                        if last and seq < 2:
                            K.dma((O["o_sf"] if d == 0 else O["o_sb"])[seq, j, h], S.v)
                        if not last:
                            K.copy("act", Sb.v, S.v)
                        yield
                    if d == 0:
                        K.copy("act", oacc[:, b * 512:(b + 1) * 512], pO)
                    else:
                        K.tt("dve", oacc[:, b * 512:(b + 1) * 512], pO, oacc[:, b * 512:(b + 1) * 512], ALU.add)
                    yield

                units = [(0, bb) for bb in [0, 1, 2, 3, 4]] + [(1, bb) for bb in [0, 4, 3, 2, 1]]
                prev = None
                for n in range(len(units) + 1):
                    gens = []
                    if prev is not None:
                        gens.append(scan(prev[0], prev[1], psets[(n - 1) % 2]))
                    if n < len(units):
                        gens.append(intra(units[n][0], units[n][1], psets[n % 2]))
                        prev = units[n]
                    else:
                        prev = None
                    while gens:
                        for g_ in list(gens):
                            try:
                                next(g_)
                            except StopIteration:
                                gens.remove(g_)
                R.top = top_h
                scr = norm_scr(light=True)
                wz = R.alloc((128, 8, 128), BF16, "wz")
                woh = R.alloc((128, D), BF16, "gwoh")
                og = R.alloc((128, T), BF16, "og")
                zs = R.alloc((128, 512), F32, "zs")
                on = R.alloc((128, 512), F32, "on")
                K.dma(wz.v, I["gdn_w_in"][j].re("(k p) n -> p k n", p=128)[:, :, 3072 + h * 128:3072 + (h + 1) * 128], eng="pool")
                K.dma(woh.v, I["gdn_w_out"][j][h * 128:(h + 1) * 128, :], eng="pool")
                for b, (c0, n, cond) in enumerate(TB):
                    pb = psb(b % 2)
                    for kc in range(8):
                        K.mm(pb, wz[:, kc, :], HT[kc][b].v, start=(kc == 0), stop=(kc == 7))
                    K.act(zs.v, pb, AF.Silu)
                    fm_norm(on.v, oacc[:, c0:c0 + 512], 128, 512, cw[:, 72:73], 128, scr, None)
                    K.tt("dve", og[:, c0:c0 + 512], on.v, zs.v, ALU.mult)
                out_accum(i, [og[:, c0:c0 + 512] for (c0, n, cond) in TB], woh.v)

        if depth:
            load_gains(0)
            run(adaln(0))
        cnt = {0: 0, 1: 0, 2: 0}
        for i, kind in enumerate(kinds):
            j = cnt[kind]
            cnt[kind] += 1
            R.reset()
            make_coef(i)
            mod_norm(i, 0)
            if kind == 2:
                gqa_layer(i, j)
            elif kind == 1:
                mla_layer(i, j)
            elif kind == 0:
                gdn_layer(i, j)
            R.reset()
            mod_norm(i, 1)
            bg = None
            if i + 1 < depth:
                def bgf(i=i):
                    yield from adaln(i + 1)
                bg = bgf()
            if do_mlp:
                mlp(i, bg)
            elif bg is not None:
                run(bg)
            if i + 1 < depth:
                load_gains(i + 1)

        R.reset()
        ost = [R.alloc((128, D), F32, "xo%d" % q) for q in range(2)]
        for t in range(int(os.environ.get("KNT_OUT", str(NT_DBG)))):
            b, off = divmod(t * 128, 512)
            o_ = ost[(t + int(os.environ.get("KSL", "0"))) % 2]
            for g in range(2):
                pb = psb((t * 2 + g + int(os.environ.get("KOFF", "0"))) % int(os.environ.get("KNB", "4")))
                for q in range(4):
                    kc = g * 4 + q
                    K.transpose(pb[:, q * 128:(q + 1) * 128], XT[kc][b][:, off:off + 128], ident.v)
                K.copy("dve" if g == 0 else "act", o_[:, g * 512:(g + 1) * 512], pb)
            dst = O["yp"][t * 128:(t + 1) * 128, :] if t < 4 else O["ys"][(t - 4) * 128:(t - 3) * 128, :]
            K.dma(dst, o_.v)
        K.emit()
    return nc


def make_in_maps(inp, kinds=KINDS_FULL):
    f = lambda a: np.ascontiguousarray(np.asarray(a, dtype=np.float32))
    depth = max(len(kinds), 1)
    n_a = sum(1 for k in kinds if k == 0)
    n_b = sum(1 for k in kinds if k == 1)
    n_c = sum(1 for k in kinds if k == 2)
    shared = {
        "norm_mix": f(inp["norm_mix"])[:depth].reshape(depth, 8, 128),
        "norm_mlp": f(inp["norm_mlp"])[:depth].reshape(depth, 8, 128),
        "w_mod": f(inp["w_mod"])[:depth],
        "b_mod": f(inp["b_mod"])[:depth].reshape(depth, 48, 128),
        "w_mlp_in": f(inp["w_mlp_in"])[:depth],
        "w_mlp_out": f(inp["w_mlp_out"])[:depth],
        "ident": np.eye(128, dtype=np.float32),
        "ones": np.ones((128, 128), np.float32),
    }
    if n_c:
        cos, sin, pT = rope_tables(32, 2048)
        shared.update({
            "gqa_w_in": f(inp["gqa_w_in"])[:n_c], "gqa_q_norm": f(inp["gqa_q_norm"])[:n_c],
            "gqa_k_norm": f(inp["gqa_k_norm"])[:n_c], "gqa_w_out": f(inp["gqa_w_out"])[:n_c],
            "rope_c_cos": cos, "rope_c_sin": sin, "rope_c_p": pT})
    if n_b:
        cos, sin, pT = rope_tables(16, 2048)
        shared.update({
            "mla_w_down": f(inp["mla_w_down"])[:n_b],
            "mla_q_lat_norm": f(inp["mla_q_lat_norm"])[:n_b].reshape(n_b, 3, 128),
            "mla_kv_lat_norm": f(inp["mla_kv_lat_norm"])[:n_b].reshape(n_b, 2, 128),
            "mla_w_uq": f(inp["mla_w_uq"])[:n_b], "mla_w_ukv": f(inp["mla_w_ukv"])[:n_b],
            "mla_qn_nope": f(inp["mla_qn_nope"])[:n_b], "mla_qn_rope": f(inp["mla_qn_rope"])[:n_b],
            "mla_kn_nope": f(inp["mla_kn_nope"])[:n_b], "mla_kn_rope": f(inp["mla_kn_rope"])[:n_b],
            "mla_w_out": f(inp["mla_w_out"])[:n_b],
            "rope_b_cos": cos, "rope_b_sin": sin, "rope_b_p": pT})
    if n_a:
        shared.update(gdn_host_inputs(inp, n_a))
    maps = []
    xp = f(inp["x_prompt"])
    xs = f(inp["x_sample"])
    for c in range(NCORES):
        m = dict(shared)
        m["xp"] = xp[2 * c:2 * c + 2].reshape(512, D)
        m["xs"] = xs[c]
        m["cond"] = np.stack([f(inp["c_ctx"]).reshape(8, 128), f(inp["c"])[c].reshape(8, 128)], 0)
        if n_c:
            m["cache_gqa_k"] = f(inp["cache_gqa_k"])[c, :n_c].reshape(n_c, 512, 256)
            m["cache_gqa_v"] = f(inp["cache_gqa_v"])[c, :n_c].reshape(n_c, 512, 256)
        if n_b:
            m["cache_mla_ckv"] = f(inp["cache_mla_ckv"])[c, :n_b]
            m["cache_mla_krope"] = f(inp["cache_mla_krope"])[c, :n_b]
        if n_a:
            m["state_f"] = f(inp["state_gdn_fwd"])[c, :n_a]
            m["state_b"] = f(inp["state_gdn_bwd"])[c, :n_a]
        maps.append(m)
    return maps


def gdn_host_inputs(inp, n_a):
    f = lambda a: np.ascontiguousarray(np.asarray(a, dtype=np.float32))
    w = f(inp["gdn_w_in"])[:n_a]
    idx = np.arange(64)
    NEG = -30000.0
    m = np.zeros((8, 128, 128), np.float32)
    jj, ii = np.meshgrid(idx, idx, indexing="ij")
    m[0, :64, :64] = (jj <= ii)
    m[1, :64, :64] = (jj >= ii)
    m[2, :64, :64] = np.where(ii > jj, 0.0, NEG)
    m[3, :64, :64] = np.where(ii >= jj, 0.0, NEG)
    m[4, :64, :64] = np.where(ii < jj, 0.0, NEG)
    m[5, :64, :64] = np.where(ii <= jj, 0.0, NEG)
    return {
        "gdn_w_in": np.ascontiguousarray(w[:, :, :4096]),
        "gdn_w_gb": np.ascontiguousarray(w[:, :, 4096:4128]),
        "gdn_conv": f(inp["gdn_conv"])[:n_a].reshape(n_a, 72, 128),
        "gdn_a_log": f(inp["gdn_a_log"])[:n_a].reshape(n_a, 16),
        "gdn_dt_bias": f(inp["gdn_dt_bias"])[:n_a].reshape(n_a, 16),
        "gdn_out_norm": f(inp["gdn_out_norm"])[:n_a],
        "gdn_w_out": f(inp["gdn_w_out"])[:n_a],
        "gdn_masks": m,
    }


def assemble(results, kinds=KINDS_FULL):
    n_a = sum(1 for k in kinds if k == 0)
    n_b = sum(1 for k in kinds if k == 1)
    n_c = sum(1 for k in kinds if k == 2)
    yp = np.concatenate([r["yp"].reshape(2, 256, D) for r in results], 0)
    ys = np.stack([r["ys"] for r in results], 0)
    sf = np.concatenate([r["o_sf"][:, :n_a] for r in results], 0)
    sbw = np.concatenate([r["o_sb"][:, :n_a] for r in results], 0)
    ckv = np.concatenate([r["o_ckv"][:, :n_b] for r in results], 0)
    kr = np.concatenate([r["o_kr"][:, :n_b] for r in results], 0)
    gk = np.concatenate([r["o_gk"][:, :n_c].reshape(2, n_c, 256, 2, 128) for r in results], 0)
    gv = np.concatenate([r["o_gv"][:, :n_c].reshape(2, n_c, 256, 2, 128) for r in results], 0)
    return (yp, ys, sf, sbw, ckv, kr, gk, gv)


_NC_CACHE = {}


def kernel(**inputs):
    kinds = KINDS_FULL
    if kinds not in _NC_CACHE:
        _NC_CACHE[kinds] = build(kinds)
    nc = _NC_CACHE[kinds]
    maps = make_in_maps(inputs, kinds)
    res = run_bass_kernel_spmd(nc, maps, core_ids=list(range(NCORES)))
    return assemble(res.results, kinds)
```

```python
import contextlib
import math
import numpy as np
import concourse.bass as bass
import concourse.mybir as mybir
from concourse.bass_utils import run_bass_kernel_spmd

F32 = mybir.dt.float32
BF16 = mybir.dt.bfloat16
AF = mybir.ActivationFunctionType
ALU = mybir.AluOpType
AX = mybir.AxisListType

D = 1024
NKC = 8
T = 2560
NCORES = 8
EPS = 1e-6
TB = [(0, 512, 0), (512, 512, 1), (1024, 512, 1), (1536, 512, 1), (2048, 512, 1)]
KINDS_FULL = (0, 1, 2, 0)


class Buf:
    __slots__ = ("ap", "name", "last_write", "reads", "excl")

    def __init__(self, ap, name="", excl=False):
        self.ap = ap
        self.name = name
        self.last_write = None
        self.reads = []
        self.excl = excl

    def __getitem__(self, idx):
        return View(self, self.ap[idx])

    @property
    def v(self):
        return View(self, self.ap)


class View:
    __slots__ = ("buf", "ap")

    def __init__(self, buf, ap):
        self.buf = buf
        self.ap = ap

    def __getitem__(self, idx):
        return View(self.buf, self.ap[idx])

    def re(self, pat, **kw):
        return View(self.buf, self.ap.rearrange(pat, **kw))

    def bc(self, shape):
        return View(self.buf, self.ap.broadcast_to(shape))

    def un(self, axis):
        return View(self.buf, self.ap.unsqueeze(axis))


class Op:
    __slots__ = ("eng", "fn", "deps", "flag", "ticket", "dma", "dsem", "dval", "dprev", "idx")

    def __init__(self, eng, fn):
        self.eng = eng
        self.fn = fn
        self.deps = []
        self.flag = False
        self.ticket = 0
        self.dma = False
        self.dsem = None
        self.dval = 0
        self.dprev = None
        self.idx = 0


ENGS = ("pe", "act", "dve", "pool", "sp")
import os
DUMP = os.environ.get("KDUMP", "") == "1"


def compress(ops):
    best = {}
    for o in ops:
        key = ("d", o.dsem) if o.dma else ("e", o.eng)
        b = best.get(key)
        if b is None or o.idx > b.idx:
            best[key] = o
    return list(best.values())


class Sched:
    def __init__(self, nc, n_dma_sems=32):
        self.nc = nc
        self.ops = {e: [] for e in ENGS}
        self.n_dma_sems = n_dma_sems
        self.dma_count = 0
        self.dma_cnt_q = {}
        self.dma_last = [None] * n_dma_sems
        self.dma_vals = [0] * n_dma_sems
        self.nops = 0

    def op(self, eng, fn, reads=(), writes=(), dma=False):
        o = Op(eng, fn)
        o.idx = self.nops
        self.nops += 1
        deps = {}
        for r in reads:
            b = r.buf if isinstance(r, View) else r
            if b.last_write is not None:
                deps[id(b.last_write)] = (b.last_write, True)
            if b.excl:
                for rd in b.reads:
                    if rd.eng != eng and id(rd) not in deps:
                        deps[id(rd)] = (rd, False)
        for w in writes:
            b = w.buf if isinstance(w, View) else w
            if b.last_write is not None and id(b.last_write) not in deps:
                deps[id(b.last_write)] = (b.last_write, False)
            for rd in b.reads:
                if id(rd) not in deps:
                    deps[id(rd)] = (rd, False)
        for d, strong in deps.values():
            if d is o:
                continue
            if d.eng == eng and not d.dma and not dma:
                if eng == "pe" or not strong:
                    continue
            o.deps.append(d)
            d.flag = True
        for w in writes:
            b = w.buf if isinstance(w, View) else w
            b.last_write = o
            b.reads = []
        for r in reads:
            b = r.buf if isinstance(r, View) else r
            if b.last_write is not o:
                b.reads.append(o)
                if len(b.reads) > 24:
                    b.reads = compress(b.reads)
        if dma:
            o.dma = True
            half = self.n_dma_sems // 2
            cnt = self.dma_cnt_q.get(eng, 0)
            self.dma_cnt_q[eng] = cnt + 1
            s = (cnt % half) + (half if eng == "pool" else 0)
            self.dma_count += 1
            o.dsem = s
            self.dma_vals[s] += 16
            o.dval = self.dma_vals[s]
            o.dprev = self.dma_last[s]
            self.dma_last[s] = o
        self.ops[eng].append(o)
        return o

    def mm(self, out, lhsT, rhs, start=True, stop=True):
        return self.op("pe", lambda e: e.matmul(out.ap, lhsT.ap, rhs.ap, start=start, stop=stop),
                       reads=[lhsT, rhs] + ([] if start else [out]), writes=[out])

    def transpose(self, out, in_, ident):
        return self.op("pe", lambda e: e.transpose(out.ap, in_.ap, ident.ap), reads=[in_, ident], writes=[out])

    def act(self, out, in_, func, bias=None, scale=None, accum=None):
        kw = {}
        reads = [in_]
        writes = [out]
        if bias is not None:
            if isinstance(bias, View):
                kw["bias"] = bias.ap
                reads.append(bias)
            else:
                kw["bias"] = bias
        if scale is not None:
            if isinstance(scale, View):
                kw["scale"] = scale.ap
                reads.append(scale)
            else:
                kw["scale"] = scale
        if accum is not None:
            kw["accum_out"] = accum.ap
            writes.append(accum)
        return self.op("act", lambda e: e.activation(out.ap, in_.ap, func, **kw), reads=reads, writes=writes)

    def tt(self, eng, out, a, b, op):
        return self.op(eng, lambda e: e.tensor_tensor(out.ap, a.ap, b.ap, op), reads=[a, b], writes=[out])

    def ts(self, eng, out, a, s1, op0, s2=None, op1=None):
        reads = [a]
        s1a = s1.ap if isinstance(s1, View) else s1
        s2a = s2.ap if isinstance(s2, View) else s2
        if isinstance(s1, View):
            reads.append(s1)
        if isinstance(s2, View):
            reads.append(s2)
        kw = {}
        if op1 is not None:
            kw["op1"] = op1
        return self.op(eng, lambda e: e.tensor_scalar(out.ap, a.ap, s1a, s2a, op0, **kw), reads=reads, writes=[out])

    def stt(self, eng, out, a, s, b, op0, op1):
        reads = [a, b]
        sa = s.ap if isinstance(s, View) else s
        if isinstance(s, View):
            reads.append(s)
        return self.op(eng, lambda e: e.scalar_tensor_tensor(out.ap, a.ap, sa, b.ap, op0, op1),
                       reads=reads, writes=[out])

    def copy(self, eng, out, in_):
        if eng == "act":
            return self.op(eng, lambda e: e.copy(out.ap, in_.ap), reads=[in_], writes=[out])
        return self.op(eng, lambda e: e.tensor_copy(out.ap, in_.ap), reads=[in_], writes=[out])

    def recip(self, out, in_):
        return self.op("dve", lambda e: e.reciprocal(out.ap, in_.ap), reads=[in_], writes=[out])

    def reduce(self, eng, out, in_, op=None):
        return self.op(eng, lambda e: e.tensor_reduce(out.ap, in_.ap, AX.X, op or ALU.add), reads=[in_], writes=[out])

    def memset(self, eng, out, val):
        return self.op(eng, lambda e: e.memset(out.ap, val), writes=[out])

    def dma(self, out, in_, eng="sp"):
        return self.op(eng, lambda e: e.dma_start(out=out.ap, in_=in_.ap), reads=[in_], writes=[out], dma=True)

    def emit(self):
        nc = self.nc
        for e in ENGS:
            c = 0
            for o in self.ops[e]:
                if o.dma:
                    continue
                if o.flag:
                    c += 1
                    o.ticket = c
        with contextlib.ExitStack() as st:
            esems = {e: st.enter_context(nc.semaphore("s_" + e)) for e in ENGS}
            dsems = [st.enter_context(nc.semaphore("d%d" % i)) for i in range(self.n_dma_sems)]
            block = st.enter_context(nc.Block())
            sched = self

            def make(ename):
                def body(eng):
                    waited = {}
                    for o in sched.ops[ename]:
                        ws = []
                        for d in o.deps:
                            if d.dma:
                                ws.append((("d", d.dsem), dsems[d.dsem], d.dval))
                            else:
                                ws.append((("e", d.eng), esems[d.eng], d.ticket))
                        if o.dma and o.dprev is not None:
                            d = o.dprev
                            ws.append((("d", d.dsem), dsems[d.dsem], d.dval))
                        for key, sem, val in ws:
                            if waited.get(key, 0) >= val:
                                continue
                            waited[key] = val
                            eng.wait_ge(sem, val)
                            if DUMP:
                                print("   ", ename, "wait", key, val)
                        ins = o.fn(eng)
                        if DUMP:
                            print(ename, o.idx, "dma" if o.dma else "", ("inc d%d->%d" % (o.dsem, o.dval)) if o.dma else ("inc e->%d" % o.ticket if o.flag else ""), str(ins)[:150])
                        if o.dma:
                            ins.then_inc(dsems[o.dsem], 16)
                        elif o.flag:
                            ins.then_inc(esems[ename], 1)
                    if ename == "sp":
                        for s in range(sched.n_dma_sems):
                            if sched.dma_vals[s] > 0 and waited.get(("d", s), 0) < sched.dma_vals[s]:
                                eng.wait_ge(dsems[s], sched.dma_vals[s])
                return body

            block.tensor(make("pe"))
            block.scalar(make("act"))
            block.vector(make("dve"))
            block.gpsimd(make("pool"))
            block.sync(make("sp"))


class Region:
    def __init__(self, ap_f32, nbytes):
        self.ap = ap_f32
        self.nbytes = nbytes
        self.live = []
        self.top = 0

    def reset(self):
        self.top = 0

    def alloc(self, shape, dt, name=""):
        esz = 2 if dt == BF16 else 4
        free = int(np.prod(shape[1:]))
        nb = (free * esz + 31) // 32 * 32
        s = self.top
        e = s + nb
        assert e <= self.nbytes, (name, e, self.nbytes)
        self.top = e
        ap = self.ap[0:shape[0], s // 4:e // 4]
        if dt == BF16:
            ap = ap.bitcast(BF16)
        ap = ap[:, 0:free]
        if len(shape) == 3:
            ap = ap.rearrange("p (a b) -> p a b", a=shape[1])
        elif len(shape) == 4:
            ap = ap.rearrange("p (a b c) -> p a b c", a=shape[1], b=shape[2])
        b = Buf(ap, name)
        inherit = []
        keep = []
        for (ps, pe, pb) in self.live:
            if ps < e and s < pe:
                inherit.extend(pb.reads)
                if pb.last_write is not None:
                    inherit.append(pb.last_write)
                if ps >= s and pe <= e:
                    continue
            keep.append((ps, pe, pb))
        keep.append((s, e, b))
        self.live = keep
        b.reads = compress(inherit)
        return b


def rope_tables(half, n_tok_rows):
    rows = 2048 // 64
    row = np.repeat(np.arange(rows, dtype=np.float32), 64)
    col = np.tile(np.arange(64, dtype=np.float32), rows)
    freqs = (10000.0 ** (-np.arange(half, dtype=np.float32) / half)).astype(np.float32)
    ang_r = row[None, :] * freqs[:, None]
    ang_c = col[None, :] * freqs[:, None]
    cos = np.concatenate([np.cos(ang_r), np.cos(ang_r), np.cos(ang_c), np.cos(ang_c)], 0).astype(np.float32)
    sin = np.concatenate([np.sin(ang_r), np.sin(ang_r), np.sin(ang_c), np.sin(ang_c)], 0).astype(np.float32)
    n = 4 * half
    P = np.zeros((n, n), np.float32)
    for blk in range(2):
        o = blk * 2 * half
        for d in range(half):
            P[o + d, o + d + half] = -1.0
            P[o + d + half, o + d] = 1.0
    return cos, sin, np.ascontiguousarray(P.T)


def build(kinds=KINDS_FULL, do_mlp=True):
    nc = bass.Bass("TRN2", target_bir_lowering=False)
    n_a = sum(1 for k in kinds if k == 0)
    n_b = sum(1 for k in kinds if k == 1)
    n_c = sum(1 for k in kinds if k == 2)
    depth = len(kinds)

    def din(name, shape):
        return Buf(nc.dram_tensor(name, list(shape), F32, kind="ExternalInput").ap(), name)

    def dout(name, shape):
        return Buf(nc.dram_tensor(name, list(shape), F32, kind="ExternalOutput").ap(), name)

    I = {}
    I["xp"] = din("xp", (512, D))
    I["xs"] = din("xs", (2048, D))
    I["cond"] = din("cond", (2, 8, 128))
    dd = max(depth, 1)
    I["norm_mix"] = din("norm_mix", (dd, 8, 128))
    I["norm_mlp"] = din("norm_mlp", (dd, 8, 128))
    I["w_mod"] = din("w_mod", (dd, D, 6 * D))
    I["b_mod"] = din("b_mod", (dd, 48, 128))
    I["w_mlp_in"] = din("w_mlp_in", (dd, D, 4 * D))
    I["w_mlp_out"] = din("w_mlp_out", (dd, 4 * D, D))
    I["ident"] = din("ident", (128, 128))
    I["ones"] = din("ones", (128, 128))
    if n_c:
        I["gqa_w_in"] = din("gqa_w_in", (n_c, D, 1536))
        I["gqa_q_norm"] = din("gqa_q_norm", (n_c, 128))
        I["gqa_k_norm"] = din("gqa_k_norm", (n_c, 128))
        I["gqa_w_out"] = din("gqa_w_out", (n_c, D, D))
        I["cache_gqa_k"] = din("cache_gqa_k", (n_c, 512, 256))
        I["cache_gqa_v"] = din("cache_gqa_v", (n_c, 512, 256))
        I["rope_c_cos"] = din("rope_c_cos", (128, 2048))
        I["rope_c_sin"] = din("rope_c_sin", (128, 2048))
        I["rope_c_p"] = din("rope_c_p", (128, 128))
    if n_b:
        I["mla_w_down"] = din("mla_w_down", (n_b, D, 704))
        I["mla_q_lat_norm"] = din("mla_q_lat_norm", (n_b, 3, 128))
        I["mla_kv_lat_norm"] = din("mla_kv_lat_norm", (n_b, 2, 128))
        I["mla_w_uq"] = din("mla_w_uq", (n_b, 384, 1536))
        I["mla_w_ukv"] = din("mla_w_ukv", (n_b, 256, 2048))
        I["mla_qn_nope"] = din("mla_qn_nope", (n_b, 128))
        I["mla_qn_rope"] = din("mla_qn_rope", (n_b, 64))
        I["mla_kn_nope"] = din("mla_kn_nope", (n_b, 128))
        I["mla_kn_rope"] = din("mla_kn_rope", (n_b, 64))
        I["mla_w_out"] = din("mla_w_out", (n_b, D, D))
        I["cache_mla_ckv"] = din("cache_mla_ckv", (n_b, 512, 256))
        I["cache_mla_krope"] = din("cache_mla_krope", (n_b, 512, 64))
        I["rope_b_cos"] = din("rope_b_cos", (64, 2048))
        I["rope_b_sin"] = din("rope_b_sin", (64, 2048))
        I["rope_b_p"] = din("rope_b_p", (64, 64))
    if n_a:
        I["gdn_w_in"] = din("gdn_w_in", (n_a, D, 4096))
        I["gdn_w_gb"] = din("gdn_w_gb", (n_a, D, 32))
        I["gdn_conv"] = din("gdn_conv", (n_a, 72, 128))
        I["gdn_a_log"] = din("gdn_a_log", (n_a, 16))
        I["gdn_dt_bias"] = din("gdn_dt_bias", (n_a, 16))
        I["gdn_out_norm"] = din("gdn_out_norm", (n_a, 128))
        I["gdn_w_out"] = din("gdn_w_out", (n_a, D, D))
        I["state_f"] = din("state_f", (n_a, 8, 128, 128))
        I["state_b"] = din("state_b", (n_a, 8, 128, 128))
        I["gdn_masks"] = din("gdn_masks", (8, 128, 128))

    O = {}
    O["yp"] = dout("yp", (512, D))
    O["ys"] = dout("ys", (2048, D))
    O["o_sf"] = dout("o_sf", (2, max(n_a, 1), 8, 128, 128))
    O["o_sb"] = dout("o_sb", (2, max(n_a, 1), 8, 128, 128))
    O["o_ckv"] = dout("o_ckv", (2, max(n_b, 1), 256, 256))
    O["o_kr"] = dout("o_kr", (2, max(n_b, 1), 256, 64))
    O["o_gk"] = dout("o_gk", (2, max(n_c, 1), 256, 256))
    O["o_gv"] = dout("o_gv", (2, max(n_c, 1), 256, 256))

    st = contextlib.ExitStack()
    with st:
        K = Sched(nc)

        def sb(name, shape, dt):
            return Buf(st.enter_context(nc.sbuf_tensor("sb_" + name, list(shape), dt)).ap(), name)

        XT = [[sb("x%d_%d" % (kc, b), (128, 512), F32) for b in range(5)] for kc in range(NKC)]
        HT = [[sb("h%d_%d" % (kc, b), (128, 512), BF16) for b in range(5)] for kc in range(NKC)]
        ident = sb("ident", (128, 128), F32)
        identb = sb("identb", (128, 128), BF16)
        onesf = sb("onesf", (128, 128), F32)
        onesb = sb("onesb", (128, 128), BF16)
        condT = sb("condT", (128, 2, 8), F32)
        modT = [sb("modT%d" % i, (128, 48, 2), F32) for i in range(2)]
        coefA = sb("coefA", (128, 2, 8, 2), F32)
        gnorm = sb("gnorm", (128, 2, 8), F32)
        epsv = sb("epsv", (128, 1), F32)
        WR_BYTES = 85 * 1024
        wr_t = sb("wr", (128, WR_BYTES // 4), F32)
        R = Region(wr_t.ap, WR_BYTES)
        PS = [Buf(st.enter_context(nc.psum_tensor("ps%d" % i, [128, 512], F32)).ap(), "ps%d" % i, excl=True) for i in range(8)]

        def psb(i, dt=F32):
            if dt == BF16:
                return View(PS[i], PS[i].ap.bitcast(BF16))
            return PS[i].v

        K.dma(ident.v, I["ident"].v)
        K.dma(onesf.v, I["ones"].v)
        K.dma(identb.v, I["ident"].v, eng="pool")
        K.dma(onesb.v, I["ones"].v, eng="pool")
        K.memset("dve", epsv.v, EPS)

        R.reset()
        xst = [R.alloc((128, D), F32, "xst%d" % i) for i in range(2)]
        import os
        NT_DBG = int(os.environ.get("KNT", "20"))
        for t in range(NT_DBG):
            src = I["xp"][t * 128:(t + 1) * 128, :] if t < 4 else I["xs"][(t - 4) * 128:(t - 3) * 128, :]
            s_ = xst[(t + int(os.environ.get("KSL", "0"))) % 2]
            if os.environ.get("KV", "") == "a":
                src = I["xp"][0:128, :]
            if os.environ.get("KV", "") == "b":
                s_ = xst[0]
            K.dma(s_.v, src, eng=os.environ.get("KQ", "sp"))
            if os.environ.get("KV", "") == "c" and t == 1:
                continue
            b, off = divmod(t * 128, 512)
            for g in range(2):
                pb = psb((t * 2 + g + int(os.environ.get("KOFF", "0"))) % int(os.environ.get("KNB", "4")))
                for q in range(4):
                    kc = g * 4 + q
                    K.transpose(pb[:, q * 128:(q + 1) * 128], s_[:, kc * 128:(kc + 1) * 128], ident.v)
                for q in range(4):
                    kc = g * 4 + q
                    eng = "dve" if q % 2 == 0 else "act"
                    if os.environ.get("KV", "") == "d":
                        eng = "dve"
                    if os.environ.get("KV", "") == "e":
                        eng = "act"
                    K.copy(eng, XT[kc][b][:, off:off + 128], pb[:, q * 128:(q + 1) * 128])

        import os
        if os.environ.get("KDBG", "") != "1":
            cst = R.alloc((16, 128), F32, "cst")
            K.dma(cst.v, I["cond"].v.re("c k p -> (c k) p"))
            K.transpose(psb(4)[:, 0:16], cst.v, ident[0:16, 0:16])
            K.act(condT.v.re("p c k -> p (c k)"), psb(4)[:, 0:16], AF.Silu)

        def adaln(i):
            mt = modT[i % 2]
            bst = R.alloc((48, 128), F32, "bst")
            bT = R.alloc((128, 48), F32, "bT")
            K.dma(bst.v, I["b_mod"][i])
            K.transpose(psb(4)[:, 0:48], bst.v, ident[0:48, 0:48])
            K.copy("dve", bT.v, psb(4)[:, 0:48])
            wst = [R.alloc((128, 8, 256), F32, "wmod%d" % q) for q in range(2)]
            mrow = [R.alloc((2, 256), F32, "mrow%d" % q) for q in range(2)]
            for cb in range(24):
                w_ = wst[cb % 2]
                K.dma(w_.v, I["w_mod"][i].re("(k p) n -> p k n", p=128)[:, :, cb * 256:(cb + 1) * 256])
                pr_ = psb(4)
                for kc in range(8):
                    K.mm(pr_[0:2, 0:256], condT[:, :, kc], w_[:, kc, :], start=(kc == 0), stop=(kc == 7))
                m_ = mrow[cb % 2]
                K.copy("act", m_.v, pr_[0:2, 0:256])
                pb = psb(5)
                for fl in range(2):
                    f = cb * 2 + fl
                    K.transpose(pb[:, f * 2:f * 2 + 2], m_[:, fl * 128:(fl + 1) * 128], ident[0:2, 0:2])
                if cb % 2 == 1:
                    yield
            K.tt("dve", mt.v, psb(5)[:, 0:96].re("p (f c) -> p f c", c=2), bT.v.un(2).bc([128, 48, 2]), ALU.add)
            yield

        def run(gen):
            for _ in gen:
                pass

        def load_gains(i):
            gst = R.alloc((16, 128), F32, "gst")
            K.dma(gst[0:8, :], I["norm_mix"][i])
            K.dma(gst[8:16, :], I["norm_mlp"][i])
            K.transpose(psb(4)[:, 0:16], gst.v, ident[0:16, 0:16])
            K.copy("dve", gnorm.v.re("p w k -> p (w k)"), psb(4)[:, 0:16])

        def make_coef(i):
            mt = modT[i % 2]
            for w in range(2):
                sc = mt[:, (3 * w + 1) * 8:(3 * w + 2) * 8, :]
                K.ts("dve", coefA[:, w], sc, 1.0, ALU.add)
                K.tt("dve", coefA[:, w], coefA[:, w], gnorm[:, w].un(2).bc([128, 8, 2]), ALU.mult)

        def mod_norm(i, w):
            mt = modT[i % 2]
            sq = [R.alloc((128, 512), BF16, "sq%d" % q) for q in range(2)]
            rs = [R.alloc((128, 512), F32, "rs%d" % q) for q in range(2)]
            tmp = [R.alloc((128, 512), F32, "ntmp%d" % q) for q in range(2)]
            for b, (c0, n, cond) in enumerate(TB):
                pb = psb(b % 2)
                for kc in range(8):
                    s_ = sq[kc % 2]
                    if kc % 2 == 0:
                        K.act(s_.v, XT[kc][b].v, AF.Square)
                    else:
                        K.tt("pool", s_.v, XT[kc][b].v, XT[kc][b].v, ALU.mult)
                    K.mm(pb, onesb.v, s_.v, start=(kc == 0), stop=(kc == 7))
                r_ = rs[b % 2]
                K.act(r_.v, pb, AF.Ln, bias=epsv.v, scale=1.0 / D)
                K.act(r_.v, r_.v, AF.Exp, scale=-0.5)
                for kc in range(8):
                    t_ = tmp[kc % 2]
                    K.stt("dve", t_.v, XT[kc][b].v, coefA[:, w, kc, cond:cond + 1], r_.v, ALU.mult, ALU.mult)
                    K.act(HT[kc][b].v, t_.v, AF.Identity, bias=mt[:, 3 * w * 8 + kc, cond:cond + 1])

        def mlp(i, bg):
            mt = modT[i % 2]
            win = [R.alloc((128, 8, 512), BF16, "win%d" % q) for q in range(2)]
            wout = [R.alloc((128, 4, D), BF16, "wout%d" % q) for q in range(2)]
            uu = [R.alloc((128, 4, 512), BF16, "uu%d" % q) for q in range(2)]
            rr = [R.alloc((128, 512), BF16, "rr%d" % q) for q in range(2)]
            iters = [(fb, b) for fb in range(8) for b in range(5)]
            wts = {}

            def u_phase(n):
                fb, b = iters[n]
                if b == 0:
                    wi = win[fb % 2]
                    wo = wout[fb % 2]
                    K.dma(wi.v, I["w_mlp_in"][i].re("(k p) n -> p k n", p=128)[:, :, fb * 512:(fb + 1) * 512], eng="pool")
                    K.dma(wo.v, I["w_mlp_out"][i][fb * 512:(fb + 1) * 512, :].re("(f p) n -> p f n", p=128), eng="pool")
                wi = win[fb % 2]
                u_ = uu[n % 2]
                for fc in range(4):
                    pb = psb(fc % 2)
                    for kc in range(8):
                        K.mm(pb, wi[:, kc, fc * 128:(fc + 1) * 128], HT[kc][b].v, start=(kc == 0), stop=(kc == 7))
                    r_ = rr[fc % 2]
                    K.act(r_.v, pb, AF.Relu)
                    K.tt("pool", u_[:, fc, :], r_.v, r_.v, ALU.mult)

            def y_phase(n):
                fb, b = iters[n]
                cond = TB[b][2]
                wo = wout[fb % 2]
                u_ = uu[n % 2]
                for oc in range(8):
                    pb = psb(2 + oc % 2)
                    for fc in range(4):
                        K.mm(pb, wo[:, fc, oc * 128:(oc + 1) * 128], u_[:, fc, :], start=(fc == 0), stop=(fc == 3))
                    K.stt("dve", XT[oc][b].v, pb, mt[:, 40 + oc, cond:cond + 1], XT[oc][b].v, ALU.mult, ALU.add)
                if bg is not None:
                    next(bg, None)
            for n in range(len(iters) + 1):
                if n < len(iters):
                    u_phase(n)
                if n >= 1:
                    y_phase(n - 1)
            if bg is not None:
                run(bg)

        def out_accum(i, oT, wo_h):
            mt = modT[i % 2]
            for b, (c0, n, cond) in enumerate(TB):
                for oc in range(8):
                    pb = psb(6 + oc % 2)
                    K.mm(pb, wo_h[:, oc * 128:(oc + 1) * 128], oT[b])
                    K.stt("dve", XT[oc][b].v, pb, mt[:, 16 + oc, cond:cond + 1], XT[oc][b].v, ALU.mult, ALU.add)

        def fm_norm(dst, src_ps, npart, ncols, gain, nfeat, scr, rope=None, ones_v=None, eps_v=None):
            P = npart
            nrm_ctr[0] += 1
            alt = nrm_ctr[0] % 2
            if alt and "sq2" in scr:
                scr = dict(scr, sq=scr["sq2"], rs=scr["rs2"], pd=scr["pd2"])
            K.act(scr["sq"][0:P, 0:ncols], src_ps[0:P, 0:ncols], AF.Square)
            pd = scr["pd"]
            K.mm(pd[0:P, 0:ncols], onesb[0:P, 0:P], scr["sq"][0:P, 0:ncols])
            K.act(scr["rs"][0:P, 0:ncols], pd[0:P, 0:ncols], AF.Ln, bias=(eps_v or epsv)[0:P, :], scale=1.0 / nfeat)
            K.act(scr["rs"][0:P, 0:ncols], scr["rs"][0:P, 0:ncols], AF.Exp, scale=-0.5)
            if rope is None:
                K.stt("dve", dst, src_ps[0:P, 0:ncols], gain, scr["rs"][0:P, 0:ncols], ALU.mult, ALU.mult)
            else:
                cos, sin, permT = rope
                nb = scr["nb"]
                K.stt("dve", nb[0:P, 0:ncols], src_ps[0:P, 0:ncols], gain, scr["rs"][0:P, 0:ncols], ALU.mult, ALU.mult)
                K.mm(pd[0:P, 0:ncols], permT, nb[0:P, 0:ncols])
                K.tt("pool", scr["t1"][0:P, 0:ncols], nb[0:P, 0:ncols], cos, ALU.mult)
                K.tt("dve", scr["t2"][0:P, 0:ncols], pd[0:P, 0:ncols], sin, ALU.mult)
                K.tt("pool", dst, scr["t1"][0:P, 0:ncols], scr["t2"][0:P, 0:ncols], ALU.add)

        nrm_ctr = [0]

        def norm_scr(light=False, dbl=True):
            if light:
                d_ = {"sq": R.alloc((128, 512), BF16, "nsq"), "rs": R.alloc((128, 512), F32, "nrs"), "pd": psb(5)}
                if dbl:
                    d_.update({"sq2": R.alloc((128, 512), BF16, "nsq2"), "rs2": R.alloc((128, 512), F32, "nrs2"), "pd2": psb(4)})
                return d_
            if dbl:
                return {"sq": R.alloc((128, 512), BF16, "nsq"), "rs": R.alloc((128, 512), F32, "nrs"),
                        "sq2": R.alloc((128, 512), BF16, "nsq2"), "rs2": R.alloc((128, 512), F32, "nrs2"), "pd2": psb(4),
                        "t1": R.alloc((128, 512), F32, "nt1"), "t2": R.alloc((128, 512), F32, "nt2"),
                        "nb": R.alloc((128, 512), BF16, "nnb"), "pd": psb(5)}
            return {"sq": R.alloc((128, 512), BF16, "nsq"), "rs": R.alloc((128, 512), F32, "nrs"),
                    "t1": R.alloc((128, 512), F32, "nt1"), "t2": R.alloc((128, 512), F32, "nt2"),
                    "nb": R.alloc((128, 512), BF16, "nnb"), "pd": psb(5)}

        QB = [(0, 256, [0, 1]), (256, 256, [2, 3])] + [(512 + 512 * q, 512, list(range(4, 24))) for q in range(4)]

        def attn_head(kparts, vtile, qparts, oT, scale, pbufs):
            steps = []
            for qi, (q0, nq, kts) in enumerate(QB):
                for ji, kt in enumerate(kts):
                    steps.append((qi, q0, nq, ji, kt, len(kts)))
            pts = {}

            def issue_s(n):
                qi, q0, nq, ji, kt, nk = steps[n]
                pss = psb(n % 3)
                for pi, (kT, P) in enumerate(kparts):
                    K.mm(pss[:, 0:nq], kT[0:P, kt * 128:(kt + 1) * 128], qparts[pi][0:P, q0:q0 + nq],
                         start=(pi == 0), stop=(pi == len(kparts) - 1))
                pt = pbufs[n % len(pbufs)]
                K.act(pt[:, 0:nq], pss[:, 0:nq], AF.Exp, scale=scale)
                pts[n] = pt

            def issue_pv(n):
                qi, q0, nq, ji, kt, nk = steps[n]
                po, pdn = (psb(3), psb(4)) if qi % 2 == 0 else (psb(5), psb(6))
                pt = pts.pop(n)
                K.mm(po[:, 0:nq], vtile(kt), pt[:, 0:nq], start=(ji == 0), stop=(ji == nk - 1))
                K.mm(pdn[:, 0:nq], onesb.v, pt[:, 0:nq], start=(ji == 0), stop=(ji == nk - 1))
                if ji == nk - 1:
                    rd = pbufs_rd[qi % 2]
                    K.act(rd[:, 0:nq], pdn[:, 0:nq], AF.Ln)
                    K.act(rd[:, 0:nq], rd[:, 0:nq], AF.Exp, scale=-1.0)
                    K.tt("dve", oT[:, q0:q0 + nq], po[:, 0:nq], rd[:, 0:nq], ALU.mult)
            SK = 2
            for n in range(len(steps) + SK):
                if n < len(steps):
                    issue_s(n)
                if n - SK >= 0:
                    issue_pv(n - SK)

        pbufs_rd = [None, None]

        def gqa_layer(i, j):
            R.reset()
            kT = R.alloc((128, 2, 3072), BF16, "kT")
            V = R.alloc((128, 24, 256), BF16, "V")
            cos = R.alloc((128, 2048), BF16, "cos")
            sin = R.alloc((128, 2048), BF16, "sin")
            permT = R.alloc((128, 128), BF16, "permT")
            kg = R.alloc((128, 2), F32, "kg")
            kgb = R.alloc((128, 128), F32, "kgb")
            K.dma(cos.v, I["rope_c_cos"].v, eng="pool")
            K.dma(sin.v, I["rope_c_sin"].v, eng="pool")
            K.dma(permT.v, I["rope_c_p"].v, eng="pool")
            gst = R.alloc((2, 128), F32, "gst2")
            K.dma(gst[0:1, :], I["gqa_q_norm"][j:j + 1, :])
            K.dma(gst[1:2, :], I["gqa_k_norm"][j:j + 1, :])
            K.transpose(psb(4)[:, 0:2], gst.v, ident[0:2, 0:2])
            K.copy("dve", kg.v, psb(4)[:, 0:2])
            K.dma(kgb.v, View(I["gqa_k_norm"], I["gqa_k_norm"].ap[j].partition_broadcast(128)))
            top0 = R.top
            wkv = R.alloc((128, 8, 512), BF16, "wkv")
            K.dma(wkv.v, I["gqa_w_in"][j].re("(k p) n -> p k n", p=128)[:, :, 1024:1536], eng="pool")
            scr = norm_scr()
            for kvh in range(2):
                for b, (c0, n, cond) in enumerate(TB):
                    pb = psb(b % 2)
                    for kc in range(8):
                        K.mm(pb, wkv[:, kc, kvh * 128:(kvh + 1) * 128], HT[kc][b].v, start=(kc == 0), stop=(kc == 7))
                    rope = None if b == 0 else (cos[:, c0 - 512:c0], sin[:, c0 - 512:c0], permT.v)
                    fm_norm(kT[:, kvh, c0:c0 + 512], pb, 128, 512, kg[:, 1:2], 128, scr, rope)
            vst = R.alloc((128, 256), F32, "vst")
            kst = R.alloc((128, 256), F32, "kst")
            ssq = R.alloc((128, 2), F32, "ssq")
            ksq = R.alloc((128, 256), F32, "ksq")
            for t in range(20):
                b, off = divmod(t * 128, 512)
                pb = psb(2 + t % 2)
                ncol = 512 if t < 4 else 256
                for kc in range(8):
                    rhs = wkv[:, kc, 0:512] if t < 4 else wkv[:, kc, 256:512]
                    K.mm(pb[:, 0:ncol], HT[kc][b][:, off:off + 128], rhs, start=(kc == 0), stop=(kc == 7))
                if t < 4:
                    K.copy("act", V[:, t, :], pb[:, 256:512])
                    K.copy("dve", vst.v, pb[:, 256:512])
                    K.dma(O["o_gv"][t // 2, j, (t % 2) * 128:(t % 2 + 1) * 128, :], vst.v)
                    K.act(ksq.v, pb[:, 0:256], AF.Square)
                    K.reduce("dve", ssq.v, ksq.v.re("p (h d) -> p h d", h=2))
                    K.act(ssq.v, ssq.v, AF.Ln, bias=epsv.v, scale=1.0 / 128)
                    K.act(ssq.v, ssq.v, AF.Exp, scale=-0.5)
                    K.tt("dve", kst.v.re("p (h d) -> p h d", h=2), pb[:, 0:256].re("p (h d) -> p h d", h=2),
                         ssq.v.un(2).bc([128, 2, 128]), ALU.mult)
                    K.tt("pool", kst.v.re("p (h d) -> p h d", h=2), kst.v.re("p (h d) -> p h d", h=2),
                         kgb.v.un(1).bc([128, 2, 128]), ALU.mult)
                    K.dma(O["o_gk"][t // 2, j, (t % 2) * 128:(t % 2 + 1) * 128, :], kst.v)
                else:
                    K.copy("act", V[:, t, :], pb[:, 0:256])
            K.dma(V[:, 20:24, :], I["cache_gqa_v"][j].re("(t p) n -> p t n", p=128), eng="pool")
            cst_ = R.alloc((128, 4, 256), F32, "cst_")
            K.dma(cst_.v, I["cache_gqa_k"][j].re("(t p) n -> p t n", p=128))
            for t in range(4):
                for kvh in range(2):
                    pb = psb(t % 2)
                    K.transpose(pb[:, 0:128], cst_[:, t, kvh * 128:(kvh + 1) * 128], ident.v)
                    K.copy("act", kT[:, kvh, 2560 + t * 128:2560 + (t + 1) * 128], pb[:, 0:128])
            R.top = top0
            scr = norm_scr()
            wq = [R.alloc((128, 8, 128), BF16, "wq%d" % q) for q in range(2)]
            woh = [R.alloc((128, D), BF16, "woh%d" % q) for q in range(2)]
            qT = [R.alloc((128, T), BF16, "qT%d" % q) for q in range(2)]
            oT = [R.alloc((128, T), BF16, "oT%d" % q) for q in range(2)]
            pb_ = [R.alloc((128, 512), BF16, "pt%d" % q) for q in range(4)]
            pbufs_rd[0] = R.alloc((128, 512), F32, "rd0")
            pbufs_rd[1] = R.alloc((128, 512), F32, "rd1")
            for h in range(8):
                kvh = h // 4
                w_ = wq[h % 2]
                K.dma(w_.v, I["gqa_w_in"][j].re("(k p) n -> p k n", p=128)[:, :, h * 128:(h + 1) * 128], eng="pool")
                K.dma(woh[h % 2].v, I["gqa_w_out"][j][h * 128:(h + 1) * 128, :], eng="pool")
                q_ = qT[h % 2]
                for b, (c0, n, cond) in enumerate(TB):
                    pb = psb(6 + b % 2)
                    for kc in range(8):
                        K.mm(pb, w_[:, kc, :], HT[kc][b].v, start=(kc == 0), stop=(kc == 7))
                    rope = None if b == 0 else (cos[:, c0 - 512:c0], sin[:, c0 - 512:c0], permT.v)
                    fm_norm(q_[:, c0:c0 + 512], pb, 128, 512, kg[:, 0:1], 128, scr, rope)
                o_ = oT[h % 2]
                attn_head([(kT[:, kvh, :], 128)], lambda kt, kvh=kvh: V[:, kt, kvh * 128:(kvh + 1) * 128],
                          [q_.v], o_.v, 128 ** -0.5, pb_)
                out_accum(i, [o_[:, c0:c0 + 512] for (c0, n, cond) in TB], woh[h % 2].v)

        def mla_layer(i, j):
            R.reset()
            cqn = R.alloc((128, 3, T), BF16, "cqn")
            ckvn = R.alloc((128, 2, 3072), BF16, "ckvn")
            krT = R.alloc((64, 3072), BF16, "krT")
            cos = R.alloc((64, 2048), BF16, "cosb")
            sin = R.alloc((64, 2048), BF16, "sinb")
            permT = R.alloc((64, 64), BF16, "permTb")
            gl = R.alloc((128, 9), F32, "gl")
            K.dma(cos.v, I["rope_b_cos"].v, eng="pool")
            K.dma(sin.v, I["rope_b_sin"].v, eng="pool")
            K.dma(permT.v, I["rope_b_p"].v, eng="pool")
            gst = R.alloc((9, 128), F32, "gst9")
            K.memset("dve", gst.v, 0.0)
            K.dma(gst[0:3, :], I["mla_q_lat_norm"][j])
            K.dma(gst[3:5, :], I["mla_kv_lat_norm"][j])
            K.dma(gst[5:6, :], I["mla_qn_nope"][j:j + 1, :])
            K.dma(gst[6:7, :], I["mla_kn_nope"][j:j + 1, :])
            K.dma(gst[7:8, 0:64], I["mla_qn_rope"][j:j + 1, :])
            K.dma(gst[8:9, 0:64], I["mla_kn_rope"][j:j + 1, :])
            K.transpose(psb(4)[:, 0:9], gst.v, ident[0:9, 0:9])
            K.copy("dve", gl.v, psb(4)[:, 0:9])
            top0 = R.top
            wd = R.alloc((128, 8, 704), BF16, "wd")
            K.dma(wd.v, I["mla_w_down"][j].re("(k p) n -> p k n", p=128), eng="pool")
            scr = norm_scr()
            sqs = [R.alloc((128, 512), BF16, "msq%d" % q) for q in range(3)]
            rs = R.alloc((128, 512), F32, "mrs")
            f32o = R.alloc((128, 3, 512), F32, "f32o")
            groups = [(0, 3, 0, 384), (384, 2, 3, 256)]
            for b, (c0, n, cond) in enumerate(TB):
                for (col0, nch, gcol, nfeat) in groups:
                    pbs = [psb(q) for q in range(nch)]
                    for c in range(nch):
                        for kc in range(8):
                            K.mm(pbs[c], wd[:, kc, col0 + c * 128:col0 + (c + 1) * 128], HT[kc][b].v,
                                 start=(kc == 0), stop=(kc == 7))
                    pd = psb(5)
                    for c in range(nch):
                        K.act(sqs[c].v, pbs[c], AF.Square)
                        K.mm(pd, onesb.v, sqs[c].v, start=(c == 0), stop=(c == nch - 1))
                    K.act(rs.v, pd, AF.Ln, bias=epsv.v, scale=1.0 / nfeat)
                    K.act(rs.v, rs.v, AF.Exp, scale=-0.5)
                    for c in range(nch):
                        if nch == 3:
                            dst = cqn[:, c, c0:c0 + 512]
                        else:
                            dst = ckvn[:, c, c0:c0 + 512]
                        K.stt("dve", dst, pbs[c], gl[:, gcol + c:gcol + c + 1], rs.v, ALU.mult, ALU.mult)
                        if nch == 2 and b == 0:
                            K.stt("dve", f32o[:, c, :], pbs[c], gl[:, gcol + c:gcol + c + 1], rs.v, ALU.mult, ALU.mult)
                pb = psb(3)
                for kc in range(8):
                    K.mm(pb[0:64, :], wd[:, kc, 640:704], HT[kc][b].v, start=(kc == 0), stop=(kc == 7))
                rope = None if b == 0 else (cos[:, c0 - 512:c0], sin[:, c0 - 512:c0], permT.v)
                fm_norm(krT[:, c0:c0 + 512], pb, 64, 512, gl[0:64, 8:9], 64, scr, rope)
                if b == 0:
                    fm_norm(f32o[0:64, 2, :], pb, 64, 512, gl[0:64, 8:9], 64, scr, None)
            ost = R.alloc((128, 320), F32, "ost")
            for t in range(4):
                pb = psb(t % 2)
                for c in range(2):
                    K.transpose(pb[:, c * 128:(c + 1) * 128], f32o[:, c, t * 128:(t + 1) * 128], ident.v)
                K.transpose(pb[:, 256:320], f32o[0:64, 2, t * 128:(t + 1) * 128], ident[0:64, 0:64])
                K.copy("dve", ost.v, pb[:, 0:320])
                K.dma(O["o_ckv"][t // 2, j, (t % 2) * 128:(t % 2 + 1) * 128, :], ost[:, 0:256])
                K.dma(O["o_kr"][t // 2, j, (t % 2) * 128:(t % 2 + 1) * 128, :], ost[:, 256:320])
            cst_ = R.alloc((128, 4, 320), F32, "mcst")
            K.dma(cst_[:, :, 0:256], I["cache_mla_ckv"][j].re("(t p) n -> p t n", p=128))
            K.dma(cst_[:, :, 256:320], I["cache_mla_krope"][j].re("(t p) n -> p t n", p=128))
            for t in range(4):
                pb = psb(t % 2)
                for c in range(2):
                    K.transpose(pb[:, c * 128:(c + 1) * 128], cst_[:, t, c * 128:(c + 1) * 128], ident.v)
                K.transpose(pb[0:64, 256:384], cst_[:, t, 256:320], ident.v)
                for c in range(2):
                    K.copy("act", ckvn[:, c, 2560 + t * 128:2560 + (t + 1) * 128], pb[:, c * 128:(c + 1) * 128])
                K.copy("dve", krT[:, 2560 + t * 128:2560 + (t + 1) * 128], pb[0:64, 256:384])
            R.top = top0
            scr = norm_scr()
            wuq = [R.alloc((128, 3, 192), BF16, "wuq%d" % q) for q in range(1)]
            wukv = [R.alloc((128, 2, 256), BF16, "wukv%d" % q) for q in range(1)]
            woh = [R.alloc((128, D), BF16, "mwoh%d" % q) for q in range(1)]
            qn = [R.alloc((128, T), BF16, "qn%d" % q) for q in range(1)]
            qr = [R.alloc((64, T), BF16, "qr%d" % q) for q in range(1)]
            kn = [R.alloc((128, 3072), BF16, "kn%d" % q) for q in range(1)]
            Vh = [R.alloc((128, 24, 128), BF16, "Vh%d" % q) for q in range(1)]
            oT = qn
            pb_ = [R.alloc((128, 512), BF16, "mpt%d" % q) for q in range(3)]
            pbufs_rd[0] = R.alloc((128, 512), F32, "mrd0")
            pbufs_rd[1] = pbufs_rd[0]
            for h in range(8):
                p_ = 0
                K.dma(wuq[p_].v, I["mla_w_uq"][j].re("(k p) n -> p k n", p=128)[:, :, h * 192:(h + 1) * 192], eng="pool")
                K.dma(wukv[p_].v, I["mla_w_ukv"][j].re("(k p) n -> p k n", p=128)[:, :, h * 256:(h + 1) * 256], eng="pool")
                K.dma(woh[p_].v, I["mla_w_out"][j][h * 128:(h + 1) * 128, :], eng="pool")
                for cb in range(6):
                    c0 = cb * 512
                    pb = psb(6 + cb % 2)
                    for c in range(2):
                        K.mm(pb, wukv[p_][:, c, 0:128], ckvn[:, c, c0:c0 + 512], start=(c == 0), stop=(c == 1))
                    fm_norm(kn[p_][:, c0:c0 + 512], pb, 128, 512, gl[:, 6:7], 128, scr, None)
                for g in range(6):
                    pb = psb(6 + g % 2)
                    for q in range(4):
                        t = g * 4 + q
                        for c in range(2):
                            K.mm(pb[:, q * 128:(q + 1) * 128], ckvn[:, c, t * 128:(t + 1) * 128], wukv[p_][:, c, 128:256],
                                 start=(c == 0), stop=(c == 1))
                    K.copy("act", Vh[p_][:, g * 4:(g + 1) * 4, :], pb.re("p (q d) -> p q d", q=4))
                for b, (c0, n, cond) in enumerate(TB):
                    pb = psb(6 + b % 2)
                    for c in range(3):
                        K.mm(pb, wuq[p_][:, c, 0:128], cqn[:, c, c0:c0 + 512], start=(c == 0), stop=(c == 2))
                    fm_norm(qn[p_][:, c0:c0 + 512], pb, 128, 512, gl[:, 5:6], 128, scr, None)
                    pb2 = psb(7 - b % 2)
                    for c in range(3):
                        K.mm(pb2[0:64, :], wuq[p_][:, c, 128:192], cqn[:, c, c0:c0 + 512], start=(c == 0), stop=(c == 2))
                    rope = None if b == 0 else (cos[:, c0 - 512:c0], sin[:, c0 - 512:c0], permT.v)
                    fm_norm(qr[p_][:, c0:c0 + 512], pb2, 64, 512, gl[0:64, 7:8], 64, scr, rope)
                attn_head([(kn[p_].v, 128), (krT.v, 64)], lambda kt, p_=p_: Vh[p_][:, kt, :],
                          [qn[p_].v, qr[p_].v], oT[p_].v, 192 ** -0.5, pb_)
                out_accum(i, [oT[p_][:, c0:c0 + 512] for (c0, n, cond) in TB], woh[p_].v)


        def gdn_layer(i, j):
            R.reset()
            NEG = -30000.0
            msk = R.alloc((64, 6, 64), F32, "msk")
            K.dma(msk.v, I["gdn_masks"][0:6, 0:64, 0:64].re("m p n -> p m n"))
            U = [msk[:, 0, :], msk[:, 1, :]]
            NEGS = [msk[:, 2, :], msk[:, 4, :]]
            NEGI = [msk[:, 3, :], msk[:, 5, :]]
            I64 = ident[0:64, 0:64]
            eps2 = R.alloc((128, 1), F32, "eps2")
            K.memset("dve", eps2.v, EPS / 128.0)
            gq = R.alloc((128, 3), F32, "gq")
            K.memset("dve", gq[:, 0:1], 1.0 / 128.0)
            K.memset("dve", gq[:, 1:2], 128.0 ** -0.5)
            cst_ = R.alloc((73, 128), F32, "gcst")
            K.dma(cst_[0:72, :], I["gdn_conv"][j])
            K.dma(cst_[72:73, :], I["gdn_out_norm"][j:j + 1, :])
            cw = R.alloc((128, 73), F32, "cw")
            K.transpose(psb(4)[:, 0:73], cst_.v, ident[0:73, 0:73])
            K.copy("dve", cw.v, psb(4)[:, 0:73])
            alb = R.alloc((64, 16), F32, "alb")
            dtb = R.alloc((64, 16), F32, "dtb")
            K.dma(alb.v, View(I["gdn_a_log"], I["gdn_a_log"].ap[j].partition_broadcast(64)))
            K.dma(dtb.v, View(I["gdn_dt_bias"], I["gdn_dt_bias"].ap[j].partition_broadcast(64)))
            K.act(alb.v, alb.v, AF.Exp)
            K.ts("dve", alb.v, alb.v, -1.0, ALU.mult)
            gcum = R.alloc((64, 40, 16), F32, "gcum")
            sA = R.alloc((64, 40, 16), F32, "sA")
            eg2 = R.alloc((128, 40, 16), F32, "eg2")
            ekd2 = R.alloc((128, 40, 16), F32, "ekd2")
            beta2 = R.alloc((128, 40, 16), F32, "beta2")
            egt = R.alloc((128, 40, 16), F32, "egt")
            Idd = R.alloc((64, 128), F32, "Idd")
            Idup = R.alloc((128, 64), F32, "Idup")
            K.copy("dve", Idd[:, 0:64], ident[0:64, 0:64])
            K.copy("dve", Idd[:, 64:128], ident[0:64, 0:64])
            K.copy("dve", Idup[0:64, :], ident[0:64, 0:64])
            K.copy("dve", Idup[64:128, :], ident[64:128, 64:128])
            top_g = R.top
            eg = R.alloc((64, 40, 16), F32, "eg")
            ekd = R.alloc((64, 40, 16), F32, "ekd")
            beta = R.alloc((64, 40, 16), F32, "beta")
            wgb = R.alloc((128, 8, 32), BF16, "wgb")
            K.dma(wgb.v, I["gdn_w_gb"][j].re("(k p) n -> p k n", p=128), eng="pool")
            G = R.alloc((64, 40, 32), F32, "G")
            g_ = R.alloc((64, 40, 16), F32, "g_")
            t1 = R.alloc((64, 40, 16), F32, "gt1")
            t2 = R.alloc((64, 40, 16), F32, "gt2")
            gbT = R.alloc((32, T), F32, "gbT")
            for b, (c0, n, cond) in enumerate(TB):
                pb = psb(b % 2)
                for kc in range(8):
                    K.mm(pb[0:32, :], wgb[:, kc, :], HT[kc][b].v, start=(kc == 0), stop=(kc == 7))
                K.copy("act" if b % 2 else "dve", gbT[:, c0:c0 + 512], pb[0:32, :])
            for grp in range(3):
                c_lo, c_hi = grp * 16, min(40, grp * 16 + 16)
                pb = psb(2 + grp % 2)
                for c in range(c_lo, c_hi):
                    K.transpose(pb[0:64, (c - c_lo) * 32:(c - c_lo + 1) * 32], gbT[:, c * 64:(c + 1) * 64], ident[0:32, 0:32])
                K.copy("dve", G[:, c_lo:c_hi, :], pb[0:64, 0:(c_hi - c_lo) * 32].re("p (c n) -> p c n", n=32))

            def softplus(dst, x):
                K.act(t1.v, x, AF.Abs)
                K.act(t1.v, t1.v, AF.Exp, scale=-1.0)
                K.act(t1.v, t1.v, AF.Ln, bias=1.0)
                K.ts("dve", t2.v, x, 0.0, ALU.max)
                K.tt("dve", dst, t1.v, t2.v, ALU.add)
            K.tt("dve", g_.v, G[:, :, 0:16], dtb.v.un(1).bc([64, 40, 16]), ALU.add)
            softplus(g_.v, g_.v)
            K.tt("dve", g_.v, g_.v, alb.v.un(1).bc([64, 40, 16]), ALU.mult)
            K.ts("dve", sA.v, G[:, :, 16:32], -1.0, ALU.mult)
            softplus(sA.v, sA.v)
            K.ts("dve", sA.v, sA.v, -1.0, ALU.mult)
            K.act(beta.v, sA.v, AF.Exp)
            for d in range(2):
                pb = psb(2 + d)
                K.mm(pb[0:64, 0:320].re("p (c h) -> p c h", h=8), U[d], g_[:, :, d * 8:(d + 1) * 8])
                K.copy("dve", gcum[:, :, d * 8:(d + 1) * 8], pb[0:64, 0:320].re("p (c h) -> p c h", h=8))
            pbt = [psb(4), psb(5)]
            K.mm(pbt[0][:, 0:320], onesf[0:64, 0:128], g_[:, 0:20, :])
            K.mm(pbt[1][:, 0:320], onesf[0:64, 0:128], g_[:, 20:40, :])
            for q in range(2):
                K.act(egt[:, q * 20:(q + 1) * 20, :], pbt[q][:, 0:320].re("p (c h) -> p c h", h=16), AF.Exp)
                K.tt("dve", ekd[:, q * 20:(q + 1) * 20, :], pbt[q][0:64, 0:320].re("p (c h) -> p c h", h=16),
                     gcum[:, q * 20:(q + 1) * 20, :], ALU.subtract)
            K.act(ekd.v, ekd.v, AF.Exp)
            K.act(eg.v, gcum.v, AF.Exp)
            K.ts("dve", eg.v, eg.v, -1.0, ALU.mult)
            K.tt("dve", sA.v, sA.v, gcum.v, ALU.subtract)
            for (src_, dst_) in ((eg, eg2), (ekd, ekd2), (beta, beta2)):
                for q in range(2):
                    pbq = psb(q)
                    K.mm(pbq[:, 0:320], Idd.v, src_[:, q * 20:(q + 1) * 20, :])
                    K.copy("act" if q else "dve", dst_[:, q * 20:(q + 1) * 20, :], pbq[:, 0:320].re("p (c h) -> p c h", h=16))
            R.top = top_g
            qT = R.alloc((128, T), BF16, "gqT")
            kT = R.alloc((128, T), BF16, "gkT")
            k_c = R.alloc((128, 5, 4, 128), BF16, "k_c")
            v_c = R.alloc((128, 5, 4, 128), BF16, "v_c")
            oaccs = [R.alloc((128, 512), BF16, "oacc%d" % q) for q in range(5)]
            S = R.alloc((128, 128), F32, "S")
            Sb = R.alloc((128, 128), BF16, "Sb")
            top_h = R.top
            SEQ = [(0, 4, 0), (4, 8, 1), (8, 40, 2)]
            for h in range(8):
                R.top = top_h
                praw = R.alloc((128, T), F32, "praw")
                cv = R.alloc((128, T), F32, "cv")
                win = R.alloc((128, 8, 128), BF16, "gwin")
                scr = norm_scr(light=True)
                vT = None
                for xi in range(3):
                    K.dma(win.v, I["gdn_w_in"][j].re("(k p) n -> p k n", p=128)[:, :, xi * 1024 + h * 128:xi * 1024 + (h + 1) * 128],
                          eng="pool")
                    for b, (c0, n, cond) in enumerate(TB):
                        pb = psb(b % 2)
                        for kc in range(8):
                            K.mm(pb, win[:, kc, :], HT[kc][b].v, start=(kc == 0), stop=(kc == 7))
                        K.copy("act" if b % 2 else "dve", praw[:, c0:c0 + 512], pb)
                    ch = xi * 8 + h
                    for (s0, s1) in [(0, 256), (256, 512), (512, 2560)]:
                        K.ts("dve", cv[:, s0:s1], praw[:, s0:s1], cw[:, 24 + ch:25 + ch], ALU.mult)
                        K.stt("dve", cv[:, s0 + 1:s1], praw[:, s0:s1 - 1], cw[:, ch:ch + 1], cv[:, s0 + 1:s1], ALU.mult, ALU.add)
                        K.stt("dve", cv[:, s0:s1 - 1], praw[:, s0 + 1:s1], cw[:, 48 + ch:49 + ch], cv[:, s0:s1 - 1], ALU.mult, ALU.add)
                    if xi == 2:
                        sv_top = R.top
                        R.top = top_h
                        vT = R.alloc((128, T), BF16, "vT")
                        R.top = sv_top
                        K.act(vT.v, cv.v, AF.Silu)
                    else:
                        K.act(cv.v, cv.v, AF.Silu)
                    if xi < 2:
                        dstT = qT if xi == 0 else kT
                        for b, (c0, n, cond) in enumerate(TB):
                            fm_norm(dstT[:, c0:c0 + 512], cv[:, c0:c0 + 512], 128, 512, gq[:, xi:xi + 1], 128, scr, None, eps_v=eps2)
                for (srcT, dstc) in ((kT, k_c), (vT, v_c)):
                    for b in range(5):
                        pbf = psb(2 + b % 2)
                        for c in range(8):
                            h0, pr = 64 * (c // 4), c % 4
                            K.mm(pbf[h0:h0 + 64, pr * 128:(pr + 1) * 128], srcT[:, b * 512 + c * 64:b * 512 + (c + 1) * 64], identb.v)
                        K.copy("act" if b % 2 else "dve", dstc[:, b, :, :], pbf[:, 0:512].re("p (c d) -> p c d", d=128))
                R.top = top_h
                scrs = [R.alloc((64, 8, 64), F32, "gs%d" % q) for q in range(3)]
                st_ = [R.alloc((128, 4, 64), F32, "gt%d" % q) for q in range(10)]
                kgn = R.alloc((128, 4, 128), BF16, "kgn")
                erow = R.alloc((128, 512), BF16, "erow")
                vnew2 = [R.alloc((128, 128), BF16, "vnew%d" % q) for q in range(2)]
                S2 = [S, R.alloc((128, 128), F32, "S_b")]
                Sb2 = [Sb, R.alloc((128, 128), BF16, "Sb_b")]
                for q in range(5):
                    K.memset("pool", oaccs[q].v, 0.0)
                psets = []
                for q in range(4):
                    psets.append({"kdec": R.alloc((128, 4, 128), BF16, "kdec%d" % q),
                                  "qdT": R.alloc((128, 8, 64), BF16, "qdT%d" % q),
                                  "QKt": R.alloc((128, 4, 64), BF16, "QKt%d" % q),
                                  "Tt": R.alloc((128, 4, 64), BF16, "Tt%d" % q),
                                  "WTn": R.alloc((128, 8, 64), BF16, "WTn%d" % q)})
                f2 = lambda v_: v_.re("p c i -> p (c i)")
                I64b = View(ident, I64.ap.unsqueeze(1).broadcast_to([64, 8, 64]))
                Idupb = View(Idup, Idup.ap.unsqueeze(1).broadcast_to([128, 4, 64]))
                HP = [(64 * (c // 4), c % 4) for c in range(8)]
                CORD = [0, 4, 1, 5, 2, 6, 3, 7]

                def intra(d, b, ps_):
                    dh = d * 8 + h
                    cb0 = b * 8
                    kdec, qdT, QKt, Tt, WTn = ps_["kdec"], ps_["qdT"], ps_["QKt"], ps_["Tt"], ps_["WTn"]
                    gc_b = gcum[:, cb0:cb0 + 8, dh]
                    Dg, RBA, RBQ = scrs
                    EA, EQ, At, Xa, M0_, M1_, PA0, PA1, PT0, PT1 = st_
                    K.tt("pool", Dg.v, I64b, gc_b.un(2).bc([64, 8, 64]), ALU.mult)
                    K.tt("dve", RBA.v, View(msk, NEGS[d].ap.unsqueeze(1).broadcast_to([64, 8, 64])),
                         sA[:, cb0:cb0 + 8, dh].un(2).bc([64, 8, 64]), ALU.add)
                    K.tt("pool", RBQ.v, View(msk, NEGI[d].ap.unsqueeze(1).broadcast_to([64, 8, 64])),
                         gc_b.un(2).bc([64, 8, 64]), ALU.subtract)
                    for hf in range(2):
                        r0 = 64 * hf
                        cc = cb0 + 4 * hf
                        K.tt("pool", kgn[r0:r0 + 64], k_c[r0:r0 + 64, b], eg2[r0:r0 + 64, cc:cc + 4, dh].un(2).bc([64, 4, 128]), ALU.mult)
                        K.tt("pool", kdec[r0:r0 + 64], k_c[r0:r0 + 64, b], ekd2[r0:r0 + 64, cc:cc + 4, dh].un(2).bc([64, 4, 128]), ALU.mult)
                    yield
                    pA, pQ, pE, pK = psb(0), psb(1), psb(2), psb(3)
                    for hf in range(2):
                        r0 = 64 * hf
                        K.mm(pA[r0:r0 + 64, 0:256], onesf[0:64, 0:64], f2(Dg[:, 4 * hf:4 * hf + 4, :]), start=True, stop=False)
                        K.mm(pA[r0:r0 + 64, 0:256], I64, f2(RBA[:, 4 * hf:4 * hf + 4, :]), start=False, stop=True)
                    for hf in range(2):
                        r0 = 64 * hf
                        K.mm(pQ[r0:r0 + 64, 0:256], onesf[0:64, 0:64], f2(Dg[:, 4 * hf:4 * hf + 4, :]), start=True, stop=False)
                        K.mm(pQ[r0:r0 + 64, 0:256], I64, f2(RBQ[:, 4 * hf:4 * hf + 4, :]), start=False, stop=True)
                    K.mm(pE, onesf[0:64, 0:128], f2(Dg.v))
                    for c in CORD:
                        r0, pr = HP[c]
                        cs = slice(b * 512 + c * 64, b * 512 + (c + 1) * 64)
                        K.mm(pK[r0:r0 + 64, pr * 64:(pr + 1) * 64], kT[:, cs], kT[:, cs])
                    pK2 = psb(4)
                    for c in CORD:
                        r0, pr = HP[c]
                        cs = slice(b * 512 + c * 64, b * 512 + (c + 1) * 64)
                        K.mm(pK2[r0:r0 + 64, pr * 64:(pr + 1) * 64], kT[:, cs], qT[:, cs])
                    K.act(f2(EA.v), pA[:, 0:256], AF.Exp)
                    K.act(f2(EQ.v), pQ[:, 0:256], AF.Exp)
                    K.act(erow.v, pE, AF.Exp)
                    yield
                    K.tt("dve", f2(At.v), pK[:, 0:256], f2(EA.v), ALU.mult)
                    K.tt("dve", f2(QKt.v), pK2[:, 0:256], f2(EQ.v), ALU.mult)
                    K.tt("pool", qdT.v.re("p c i -> p (c i)"), qT[:, b * 512:(b + 1) * 512], erow.v, ALU.mult)
                    pX = (psb(0), psb(1))
                    for c in CORD:
                        r0, pr = HP[c]
                        K.mm(pX[r0 // 64][r0:r0 + 64, pr * 64:(pr + 1) * 64], At[r0:r0 + 64, pr, :], ident[r0:r0 + 64, r0:r0 + 64])
                    K.copy("act", f2(Xa[0:64]), pX[0][0:64, 0:256])
                    K.copy("act", f2(Xa[64:128]), pX[1][64:128, 0:256])
                    M = M0_
                    K.tt("dve", M.v, Idupb, At.v, ALU.subtract)
                    yield
                    Pa, Pt = Xa, At
                    for lvl in range(1, 6):
                        p1, p2, p3 = (psb(1), psb(2)), (psb(3), psb(4)), (psb(0), psb(5))
                        for c in CORD:
                            r0, pr = HP[c]
                            K.mm(p1[r0 // 64][r0:r0 + 64, pr * 64:(pr + 1) * 64], Pt[r0:r0 + 64, pr, :], Pa[r0:r0 + 64, pr, :])
                        if lvl < 5:
                            for c in CORD:
                                r0, pr = HP[c]
                                K.mm(p2[r0 // 64][r0:r0 + 64, pr * 64:(pr + 1) * 64], Pa[r0:r0 + 64, pr, :], Pt[r0:r0 + 64, pr, :])
                        Pa2 = PA0 if lvl % 2 == 1 else PA1
                        for hf in range(2):
                            K.copy("act", f2(Pa2[64 * hf:64 * hf + 64]), p1[hf][64 * hf:64 * hf + 64, 0:256])
                        if lvl < 5:
                            Pt2 = PT0 if lvl % 2 == 1 else PT1
                            for hf in range(2):
                                K.copy("dve", f2(Pt2[64 * hf:64 * hf + 64]), p2[hf][64 * hf:64 * hf + 64, 0:256])
                        yield
                        for c in CORD:
                            r0, pr = HP[c]
                            K.mm(p3[r0 // 64][r0:r0 + 64, pr * 64:(pr + 1) * 64], Pa2[r0:r0 + 64, pr, :], M[r0:r0 + 64, pr, :])
                        if lvl < 5:
                            Mn = M1_ if M is M0_ else M0_
                            for hf in range(2):
                                K.tt("dve", f2(Mn[64 * hf:64 * hf + 64]), p3[hf][64 * hf:64 * hf + 64, 0:256], f2(M[64 * hf:64 * hf + 64]), ALU.add)
                            M = Mn
                        else:
                            for hf in range(2):
                                K.tt("dve", f2(Tt[64 * hf:64 * hf + 64]), p3[hf][64 * hf:64 * hf + 64, 0:256], f2(M[64 * hf:64 * hf + 64]), ALU.add)
                        Pa = Pa2
                        if lvl < 5:
                            Pt = Pt2
                        yield
                    pW = (psb(1), psb(2))
                    for c in CORD:
                        r0, pr = HP[c]
                        K.mm(pW[r0 // 64][:, pr * 64:(pr + 1) * 64], kgn[r0:r0 + 64, pr, :], Tt[r0:r0 + 64, pr, :])
                    for hf in range(2):
                        K.copy("act", WTn[:, 4 * hf:4 * hf + 4, :].re("p c i -> p (c i)"), pW[hf][:, 0:256])
                    yield

                def scan(d, b, ps_):
                    dh = d * 8 + h
                    cb0 = b * 8
                    kdec, qdT, QKt, Tt, WTn = ps_["kdec"], ps_["qdT"], ps_["QKt"], ps_["Tt"], ps_["WTn"]
                    bank = psb(6) if d == 0 else psb(7)
                    S, Sb, vnew = S2[d], Sb2[d], vnew2[d]
                    corder = list(range(8)) if d == 0 else list(range(7, -1, -1))
                    for c in corder:
                        cg = cb0 + c
                        r0, pr = HP[c]
                        seq = 0 if cg < 4 else (1 if cg < 8 else 2)
                        first = (cg == SEQ[seq][0]) if d == 0 else (cg == SEQ[seq][1] - 1)
                        last = (cg == SEQ[seq][1] - 1) if d == 0 else (cg == SEQ[seq][0])
                        if first:
                            if seq < 2:
                                K.memset("dve", S.v, 0.0)
                            else:
                                K.dma(S.v, (I["state_f"] if d == 0 else I["state_b"])[j, h])
                            K.copy("act", Sb.v, S.v)
                        pv = bank
                        pOc = bank[:, 256:320]
                        K.mm(pv[r0:r0 + 64, 0:128], Tt[r0:r0 + 64, pr, :], v_c[r0:r0 + 64, b, pr, :], start=True, stop=False)
                        K.mm(pv[r0:r0 + 64, 0:128], WTn[:, c, :], Sb.v, start=False, stop=True)
                        K.ts("dve", vnew[r0:r0 + 64, :], pv[r0:r0 + 64, 0:128], beta2[r0:r0 + 64, cg, dh:dh + 1], ALU.mult)
                        K.mm(pOc, Sb.v, qdT[:, c, :], start=True, stop=False)
                        K.mm(pOc, vnew[r0:r0 + 64, :], QKt[r0:r0 + 64, pr, :], start=False, stop=True)
                        K.mm(pv[:, 128:256], kdec[r0:r0 + 64, pr, :], vnew[r0:r0 + 64, :])
                        K.stt("dve", S.v, S.v, egt[:, cg, dh:dh + 1], pv[:, 128:256], ALU.mult, ALU.add)
                        K.tt("dve", oaccs[b][:, c * 64:(c + 1) * 64], pOc, oaccs[b][:, c * 64:(c + 1) * 64], ALU.add)
                        if last and seq < 2:
                            K.dma((O["o_sf"] if d == 0 else O["o_sb"])[seq, j, h], S.v)
                        if not last:
                            K.copy("act", Sb.v, S.v)
                        yield
                    yield

                fwd_u = [(0, bb) for bb in [0, 1, 2, 3, 4]]
                bwd_u = [(1, bb) for bb in [0, 4, 3, 2, 1]]
                units = []
                for q in range(5):
                    units += [fwd_u[q], bwd_u[q]]

                def chain(*gs):
                    for g_ in gs:
                        yield from g_

                def drive(gens):
                    gens = list(gens)
                    while gens:
                        for g_ in list(gens):
                            try:
                                next(g_)
                            except StopIteration:
                                gens.remove(g_)
                drive([chain(intra(units[0][0], units[0][1], psets[0]), intra(units[1][0], units[1][1], psets[1]))])
                for r in range(5):
                    gens = [scan(units[2 * r][0], units[2 * r][1], psets[(2 * r) % 4]),
                            scan(units[2 * r + 1][0], units[2 * r + 1][1], psets[(2 * r + 1) % 4])]
                    if r < 4:
                        gens.append(chain(intra(units[2 * r + 2][0], units[2 * r + 2][1], psets[(2 * r + 2) % 4]),
                                          intra(units[2 * r + 3][0], units[2 * r + 3][1], psets[(2 * r + 3) % 4])))
                    drive(gens)
                R.top = top_h
                scr = norm_scr(light=True)
                wz = R.alloc((128, 8, 128), BF16, "wz")
                woh = R.alloc((128, D), BF16, "gwoh")
                og = R.alloc((128, T), BF16, "og")
                zs = R.alloc((128, 512), F32, "zs")
                on = R.alloc((128, 512), F32, "on")
                K.dma(wz.v, I["gdn_w_in"][j].re("(k p) n -> p k n", p=128)[:, :, 3072 + h * 128:3072 + (h + 1) * 128], eng="pool")
                K.dma(woh.v, I["gdn_w_out"][j][h * 128:(h + 1) * 128, :], eng="pool")
                for b, (c0, n, cond) in enumerate(TB):
                    pb = psb(b % 2)
                    for kc in range(8):
                        K.mm(pb, wz[:, kc, :], HT[kc][b].v, start=(kc == 0), stop=(kc == 7))
                    K.act(zs.v, pb, AF.Silu)
                    fm_norm(on.v, oaccs[b].v, 128, 512, cw[:, 72:73], 128, scr, None)
                    K.tt("dve", og[:, c0:c0 + 512], on.v, zs.v, ALU.mult)
                out_accum(i, [og[:, c0:c0 + 512] for (c0, n, cond) in TB], woh.v)

        if depth:
            load_gains(0)
            run(adaln(0))
        cnt = {0: 0, 1: 0, 2: 0}
        for i, kind in enumerate(kinds):
            j = cnt[kind]
            cnt[kind] += 1
            R.reset()
            make_coef(i)
            mod_norm(i, 0)
            if kind == 2:
                gqa_layer(i, j)
            elif kind == 1:
                mla_layer(i, j)
            elif kind == 0:
                gdn_layer(i, j)
            R.reset()
            mod_norm(i, 1)
            bg = None
            if i + 1 < depth:
                def bgf(i=i):
                    yield from adaln(i + 1)
                bg = bgf()
            if do_mlp:
                mlp(i, bg)
            elif bg is not None:
                run(bg)
            if i + 1 < depth:
                load_gains(i + 1)

        R.reset()
        ost = [R.alloc((128, D), F32, "xo%d" % q) for q in range(2)]
        for t in range(int(os.environ.get("KNT_OUT", str(NT_DBG)))):
            b, off = divmod(t * 128, 512)
            o_ = ost[(t + int(os.environ.get("KSL", "0"))) % 2]
            for g in range(2):
                pb = psb((t * 2 + g + int(os.environ.get("KOFF", "0"))) % int(os.environ.get("KNB", "4")))
                for q in range(4):
                    kc = g * 4 + q
                    K.transpose(pb[:, q * 128:(q + 1) * 128], XT[kc][b][:, off:off + 128], ident.v)
                K.copy("dve" if g == 0 else "act", o_[:, g * 512:(g + 1) * 512], pb)
            dst = O["yp"][t * 128:(t + 1) * 128, :] if t < 4 else O["ys"][(t - 4) * 128:(t - 3) * 128, :]
            K.dma(dst, o_.v)
        K.emit()
    return nc


def make_in_maps(inp, kinds=KINDS_FULL):
    f = lambda a: np.ascontiguousarray(np.asarray(a, dtype=np.float32))
    depth = max(len(kinds), 1)
    n_a = sum(1 for k in kinds if k == 0)
    n_b = sum(1 for k in kinds if k == 1)
    n_c = sum(1 for k in kinds if k == 2)
    shared = {
        "norm_mix": f(inp["norm_mix"])[:depth].reshape(depth, 8, 128),
        "norm_mlp": f(inp["norm_mlp"])[:depth].reshape(depth, 8, 128),
        "w_mod": f(inp["w_mod"])[:depth],
        "b_mod": f(inp["b_mod"])[:depth].reshape(depth, 48, 128),
        "w_mlp_in": f(inp["w_mlp_in"])[:depth],
        "w_mlp_out": f(inp["w_mlp_out"])[:depth],
        "ident": np.eye(128, dtype=np.float32),
        "ones": np.ones((128, 128), np.float32),
    }
    if n_c:
        cos, sin, pT = rope_tables(32, 2048)
        shared.update({
            "gqa_w_in": f(inp["gqa_w_in"])[:n_c], "gqa_q_norm": f(inp["gqa_q_norm"])[:n_c],
            "gqa_k_norm": f(inp["gqa_k_norm"])[:n_c], "gqa_w_out": f(inp["gqa_w_out"])[:n_c],
            "rope_c_cos": cos, "rope_c_sin": sin, "rope_c_p": pT})
    if n_b:
        cos, sin, pT = rope_tables(16, 2048)
        shared.update({
            "mla_w_down": f(inp["mla_w_down"])[:n_b],
            "mla_q_lat_norm": f(inp["mla_q_lat_norm"])[:n_b].reshape(n_b, 3, 128),
            "mla_kv_lat_norm": f(inp["mla_kv_lat_norm"])[:n_b].reshape(n_b, 2, 128),
            "mla_w_uq": f(inp["mla_w_uq"])[:n_b], "mla_w_ukv": f(inp["mla_w_ukv"])[:n_b],
            "mla_qn_nope": f(inp["mla_qn_nope"])[:n_b], "mla_qn_rope": f(inp["mla_qn_rope"])[:n_b],
            "mla_kn_nope": f(inp["mla_kn_nope"])[:n_b], "mla_kn_rope": f(inp["mla_kn_rope"])[:n_b],
            "mla_w_out": f(inp["mla_w_out"])[:n_b],
            "rope_b_cos": cos, "rope_b_sin": sin, "rope_b_p": pT})
    if n_a:
        shared.update(gdn_host_inputs(inp, n_a))
    maps = []
    xp = f(inp["x_prompt"])
    xs = f(inp["x_sample"])
    for c in range(NCORES):
        m = dict(shared)
        m["xp"] = xp[2 * c:2 * c + 2].reshape(512, D)
        m["xs"] = xs[c]
        m["cond"] = np.stack([f(inp["c_ctx"]).reshape(8, 128), f(inp["c"])[c].reshape(8, 128)], 0)
        if n_c:
            m["cache_gqa_k"] = f(inp["cache_gqa_k"])[c, :n_c].reshape(n_c, 512, 256)
            m["cache_gqa_v"] = f(inp["cache_gqa_v"])[c, :n_c].reshape(n_c, 512, 256)
        if n_b:
            m["cache_mla_ckv"] = f(inp["cache_mla_ckv"])[c, :n_b]
            m["cache_mla_krope"] = f(inp["cache_mla_krope"])[c, :n_b]
        if n_a:
            m["state_f"] = f(inp["state_gdn_fwd"])[c, :n_a]
            m["state_b"] = f(inp["state_gdn_bwd"])[c, :n_a]
        maps.append(m)
    return maps


def gdn_host_inputs(inp, n_a):
    f = lambda a: np.ascontiguousarray(np.asarray(a, dtype=np.float32))
    w = f(inp["gdn_w_in"])[:n_a]
    idx = np.arange(64)
    NEG = -30000.0
    m = np.zeros((8, 128, 128), np.float32)
    jj, ii = np.meshgrid(idx, idx, indexing="ij")
    m[0, :64, :64] = (jj <= ii)
    m[1, :64, :64] = (jj >= ii)
    m[2, :64, :64] = np.where(ii > jj, 0.0, NEG)
    m[3, :64, :64] = np.where(ii >= jj, 0.0, NEG)
    m[4, :64, :64] = np.where(ii < jj, 0.0, NEG)
    m[5, :64, :64] = np.where(ii <= jj, 0.0, NEG)
    return {
        "gdn_w_in": np.ascontiguousarray(w[:, :, :4096]),
        "gdn_w_gb": np.ascontiguousarray(w[:, :, 4096:4128]),
        "gdn_conv": f(inp["gdn_conv"])[:n_a].reshape(n_a, 72, 128),
        "gdn_a_log": f(inp["gdn_a_log"])[:n_a].reshape(n_a, 16),
        "gdn_dt_bias": f(inp["gdn_dt_bias"])[:n_a].reshape(n_a, 16),
        "gdn_out_norm": f(inp["gdn_out_norm"])[:n_a],
        "gdn_w_out": f(inp["gdn_w_out"])[:n_a],
        "gdn_masks": m,
    }


def assemble(results, kinds=KINDS_FULL):
    n_a = sum(1 for k in kinds if k == 0)
    n_b = sum(1 for k in kinds if k == 1)
    n_c = sum(1 for k in kinds if k == 2)
    yp = np.concatenate([r["yp"].reshape(2, 256, D) for r in results], 0)
    ys = np.stack([r["ys"] for r in results], 0)
    sf = np.concatenate([r["o_sf"][:, :n_a] for r in results], 0)
    sbw = np.concatenate([r["o_sb"][:, :n_a] for r in results], 0)
    ckv = np.concatenate([r["o_ckv"][:, :n_b] for r in results], 0)
    kr = np.concatenate([r["o_kr"][:, :n_b] for r in results], 0)
    gk = np.concatenate([r["o_gk"][:, :n_c].reshape(2, n_c, 256, 2, 128) for r in results], 0)
    gv = np.concatenate([r["o_gv"][:, :n_c].reshape(2, n_c, 256, 2, 128) for r in results], 0)
    return (yp, ys, sf, sbw, ckv, kr, gk, gv)


_NC_CACHE = {}


def kernel(**inputs):
    kinds = KINDS_FULL
    if kinds not in _NC_CACHE:
        _NC_CACHE[kinds] = build(kinds)
    nc = _NC_CACHE[kinds]
    maps = make_in_maps(inputs, kinds)
    res = run_bass_kernel_spmd(nc, maps, core_ids=list(range(NCORES)))
    return assemble(res.results, kinds)
```

```python
import contextlib
import math
import numpy as np
import concourse.bass as bass
import concourse.mybir as mybir
from concourse.bass_utils import run_bass_kernel_spmd

F32 = mybir.dt.float32
BF16 = mybir.dt.bfloat16
AF = mybir.ActivationFunctionType
ALU = mybir.AluOpType
AX = mybir.AxisListType

D = 1024
NKC = 8
T = 2560
NCORES = 8
EPS = 1e-6
TB = [(0, 512, 0), (512, 512, 1), (1024, 512, 1), (1536, 512, 1), (2048, 512, 1)]
KINDS_FULL = (0, 1, 2, 0)


class Buf:
    __slots__ = ("ap", "name", "last_write", "reads", "excl")

    def __init__(self, ap, name="", excl=False):
        self.ap = ap
        self.name = name
        self.last_write = None
        self.reads = []
        self.excl = excl

    def __getitem__(self, idx):
        return View(self, self.ap[idx])

    @property
    def v(self):
        return View(self, self.ap)


class View:
    __slots__ = ("buf", "ap")

    def __init__(self, buf, ap):
        self.buf = buf
        self.ap = ap

    def __getitem__(self, idx):
        return View(self.buf, self.ap[idx])

    def re(self, pat, **kw):
        return View(self.buf, self.ap.rearrange(pat, **kw))

    def bc(self, shape):
        return View(self.buf, self.ap.broadcast_to(shape))

    def un(self, axis):
        return View(self.buf, self.ap.unsqueeze(axis))


class Op:
    __slots__ = ("eng", "fn", "deps", "flag", "ticket", "dma", "dsem", "dval", "dprev", "idx")

    def __init__(self, eng, fn):
        self.eng = eng
        self.fn = fn
        self.deps = []
        self.flag = False
        self.ticket = 0
        self.dma = False
        self.dsem = None
        self.dval = 0
        self.dprev = None
        self.idx = 0


ENGS = ("pe", "act", "dve", "pool", "sp")
import os
DUMP = os.environ.get("KDUMP", "") == "1"


def compress(ops):
    best = {}
    for o in ops:
        key = ("d", o.dsem) if o.dma else ("e", o.eng)
        b = best.get(key)
        if b is None or o.idx > b.idx:
            best[key] = o
    return list(best.values())


class Sched:
    def __init__(self, nc, n_dma_sems=32):
        self.nc = nc
        self.ops = {e: [] for e in ENGS}
        self.n_dma_sems = n_dma_sems
        self.dma_count = 0
        self.dma_cnt_q = {}
        self.dma_last = [None] * n_dma_sems
        self.dma_vals = [0] * n_dma_sems
        self.nops = 0

    def op(self, eng, fn, reads=(), writes=(), dma=False):
        o = Op(eng, fn)
        o.idx = self.nops
        self.nops += 1
        deps = {}
        for r in reads:
            b = r.buf if isinstance(r, View) else r
            if b.last_write is not None:
                deps[id(b.last_write)] = (b.last_write, True)
            if b.excl:
                for rd in b.reads:
                    if rd.eng != eng and id(rd) not in deps:
                        deps[id(rd)] = (rd, False)
        for w in writes:
            b = w.buf if isinstance(w, View) else w
            if b.last_write is not None and id(b.last_write) not in deps:
                deps[id(b.last_write)] = (b.last_write, False)
            for rd in b.reads:
                if id(rd) not in deps:
                    deps[id(rd)] = (rd, False)
        for d, strong in deps.values():
            if d is o:
                continue
            if d.eng == eng and not d.dma and not dma:
                if eng == "pe" or not strong:
                    continue
            o.deps.append(d)
            d.flag = True
        for w in writes:
            b = w.buf if isinstance(w, View) else w
            b.last_write = o
            b.reads = []
        for r in reads:
            b = r.buf if isinstance(r, View) else r
            if b.last_write is not o:
                b.reads.append(o)
                if len(b.reads) > 24:
                    b.reads = compress(b.reads)
        if dma:
            o.dma = True
            half = self.n_dma_sems // 2
            cnt = self.dma_cnt_q.get(eng, 0)
            self.dma_cnt_q[eng] = cnt + 1
            s = (cnt % half) + (half if eng == "pool" else 0)
            self.dma_count += 1
            o.dsem = s
            self.dma_vals[s] += 16
            o.dval = self.dma_vals[s]
            o.dprev = self.dma_last[s]
            self.dma_last[s] = o
        self.ops[eng].append(o)
        return o

    def mm(self, out, lhsT, rhs, start=True, stop=True):
        return self.op("pe", lambda e: e.matmul(out.ap, lhsT.ap, rhs.ap, start=start, stop=stop),
                       reads=[lhsT, rhs] + ([] if start else [out]), writes=[out])

    def transpose(self, out, in_, ident):
        return self.op("pe", lambda e: e.transpose(out.ap, in_.ap, ident.ap), reads=[in_, ident], writes=[out])

    def act(self, out, in_, func, bias=None, scale=None, accum=None):
        kw = {}
        reads = [in_]
        writes = [out]
        if bias is not None:
            if isinstance(bias, View):
                kw["bias"] = bias.ap
                reads.append(bias)
            else:
                kw["bias"] = bias
        if scale is not None:
            if isinstance(scale, View):
                kw["scale"] = scale.ap
                reads.append(scale)
            else:
                kw["scale"] = scale
        if accum is not None:
            kw["accum_out"] = accum.ap
            writes.append(accum)
        return self.op("act", lambda e: e.activation(out.ap, in_.ap, func, **kw), reads=reads, writes=writes)

    def tt(self, eng, out, a, b, op):
        return self.op(eng, lambda e: e.tensor_tensor(out.ap, a.ap, b.ap, op), reads=[a, b], writes=[out])

    def ts(self, eng, out, a, s1, op0, s2=None, op1=None):
        reads = [a]
        s1a = s1.ap if isinstance(s1, View) else s1
        s2a = s2.ap if isinstance(s2, View) else s2
        if isinstance(s1, View):
            reads.append(s1)
        if isinstance(s2, View):
            reads.append(s2)
        kw = {}
        if op1 is not None:
            kw["op1"] = op1
        return self.op(eng, lambda e: e.tensor_scalar(out.ap, a.ap, s1a, s2a, op0, **kw), reads=reads, writes=[out])

    def stt(self, eng, out, a, s, b, op0, op1):
        reads = [a, b]
        sa = s.ap if isinstance(s, View) else s
        if isinstance(s, View):
            reads.append(s)
        return self.op(eng, lambda e: e.scalar_tensor_tensor(out.ap, a.ap, sa, b.ap, op0, op1),
                       reads=reads, writes=[out])

    def copy(self, eng, out, in_):
        if eng == "act":
            return self.op(eng, lambda e: e.copy(out.ap, in_.ap), reads=[in_], writes=[out])
        return self.op(eng, lambda e: e.tensor_copy(out.ap, in_.ap), reads=[in_], writes=[out])

    def recip(self, out, in_):
        return self.op("dve", lambda e: e.reciprocal(out.ap, in_.ap), reads=[in_], writes=[out])

    def reduce(self, eng, out, in_, op=None):
        return self.op(eng, lambda e: e.tensor_reduce(out.ap, in_.ap, AX.X, op or ALU.add), reads=[in_], writes=[out])

    def memset(self, eng, out, val):
        return self.op(eng, lambda e: e.memset(out.ap, val), writes=[out])

    def dma(self, out, in_, eng="sp"):
        return self.op(eng, lambda e: e.dma_start(out=out.ap, in_=in_.ap), reads=[in_], writes=[out], dma=True)

    def emit(self):
        nc = self.nc
        for e in ENGS:
            c = 0
            for o in self.ops[e]:
                if o.dma:
                    continue
                if o.flag:
                    c += 1
                    o.ticket = c
        with contextlib.ExitStack() as st:
            esems = {e: st.enter_context(nc.semaphore("s_" + e)) for e in ENGS}
            dsems = [st.enter_context(nc.semaphore("d%d" % i)) for i in range(self.n_dma_sems)]
            block = st.enter_context(nc.Block())
            sched = self

            def make(ename):
                def body(eng):
                    waited = {}
                    for o in sched.ops[ename]:
                        ws = []
                        for d in o.deps:
                            if d.dma:
                                ws.append((("d", d.dsem), dsems[d.dsem], d.dval))
                            else:
                                ws.append((("e", d.eng), esems[d.eng], d.ticket))
                        if o.dma and o.dprev is not None:
                            d = o.dprev
                            ws.append((("d", d.dsem), dsems[d.dsem], d.dval))
                        for key, sem, val in ws:
                            if waited.get(key, 0) >= val:
                                continue
                            waited[key] = val
                            eng.wait_ge(sem, val)
                            if DUMP:
                                print("   ", ename, "wait", key, val)
                        ins = o.fn(eng)
                        if DUMP:
                            print(ename, o.idx, "dma" if o.dma else "", ("inc d%d->%d" % (o.dsem, o.dval)) if o.dma else ("inc e->%d" % o.ticket if o.flag else ""), str(ins)[:150])
                        if o.dma:
                            ins.then_inc(dsems[o.dsem], 16)
                        elif o.flag:
                            ins.then_inc(esems[ename], 1)
                    if ename == "sp":
                        for s in range(sched.n_dma_sems):
                            if sched.dma_vals[s] > 0 and waited.get(("d", s), 0) < sched.dma_vals[s]:
                                eng.wait_ge(dsems[s], sched.dma_vals[s])
                return body

            block.tensor(make("pe"))
            block.scalar(make("act"))
            block.vector(make("dve"))
            block.gpsimd(make("pool"))
            block.sync(make("sp"))


class Region:
    def __init__(self, ap_f32, nbytes):
        self.ap = ap_f32
        self.nbytes = nbytes
        self.live = []
        self.top = 0

    def reset(self):
        self.top = 0

    def alloc(self, shape, dt, name=""):
        esz = 2 if dt == BF16 else 4
        free = int(np.prod(shape[1:]))
        nb = (free * esz + 31) // 32 * 32
        s = self.top
        e = s + nb
        assert e <= self.nbytes, (name, e, self.nbytes)
        self.top = e
        ap = self.ap[0:shape[0], s // 4:e // 4]
        if dt == BF16:
            ap = ap.bitcast(BF16)
        ap = ap[:, 0:free]
        if len(shape) == 3:
            ap = ap.rearrange("p (a b) -> p a b", a=shape[1])
        elif len(shape) == 4:
            ap = ap.rearrange("p (a b c) -> p a b c", a=shape[1], b=shape[2])
        b = Buf(ap, name)
        inherit = []
        keep = []
        for (ps, pe, pb) in self.live:
            if ps < e and s < pe:
                inherit.extend(pb.reads)
                if pb.last_write is not None:
                    inherit.append(pb.last_write)
                if ps >= s and pe <= e:
                    continue
            keep.append((ps, pe, pb))
        keep.append((s, e, b))
        self.live = keep
        b.reads = compress(inherit)
        return b


def rope_tables(half, n_tok_rows):
    rows = 2048 // 64
    row = np.repeat(np.arange(rows, dtype=np.float32), 64)
    col = np.tile(np.arange(64, dtype=np.float32), rows)
    freqs = (10000.0 ** (-np.arange(half, dtype=np.float32) / half)).astype(np.float32)
    ang_r = row[None, :] * freqs[:, None]
    ang_c = col[None, :] * freqs[:, None]
    cos = np.concatenate([np.cos(ang_r), np.cos(ang_r), np.cos(ang_c), np.cos(ang_c)], 0).astype(np.float32)
    sin = np.concatenate([np.sin(ang_r), np.sin(ang_r), np.sin(ang_c), np.sin(ang_c)], 0).astype(np.float32)
    n = 4 * half
    P = np.zeros((n, n), np.float32)
    for blk in range(2):
        o = blk * 2 * half
        for d in range(half):
            P[o + d, o + d + half] = -1.0
            P[o + d + half, o + d] = 1.0
    return cos, sin, np.ascontiguousarray(P.T)


def build(kinds=KINDS_FULL, do_mlp=True):
    nc = bass.Bass("TRN2", target_bir_lowering=False)
    n_a = sum(1 for k in kinds if k == 0)
    n_b = sum(1 for k in kinds if k == 1)
    n_c = sum(1 for k in kinds if k == 2)
    depth = len(kinds)

    def din(name, shape):
        return Buf(nc.dram_tensor(name, list(shape), F32, kind="ExternalInput").ap(), name)

    def dout(name, shape):
        return Buf(nc.dram_tensor(name, list(shape), F32, kind="ExternalOutput").ap(), name)

    I = {}
    I["xp"] = din("xp", (512, D))
    I["xs"] = din("xs", (2048, D))
    I["cond"] = din("cond", (2, 8, 128))
    dd = max(depth, 1)
    I["norm_mix"] = din("norm_mix", (dd, 8, 128))
    I["norm_mlp"] = din("norm_mlp", (dd, 8, 128))
    I["w_mod"] = din("w_mod", (dd, D, 6 * D))
    I["b_mod"] = din("b_mod", (dd, 48, 128))
    I["w_mlp_in"] = din("w_mlp_in", (dd, D, 4 * D))
    I["w_mlp_out"] = din("w_mlp_out", (dd, 4 * D, D))
    I["ident"] = din("ident", (128, 128))
    I["ones"] = din("ones", (128, 128))
    if n_c:
        I["gqa_w_in"] = din("gqa_w_in", (n_c, D, 1536))
        I["gqa_q_norm"] = din("gqa_q_norm", (n_c, 128))
        I["gqa_k_norm"] = din("gqa_k_norm", (n_c, 128))
        I["gqa_w_out"] = din("gqa_w_out", (n_c, D, D))
        I["cache_gqa_k"] = din("cache_gqa_k", (n_c, 512, 256))
        I["cache_gqa_v"] = din("cache_gqa_v", (n_c, 512, 256))
        I["rope_c_cos"] = din("rope_c_cos", (128, 2048))
        I["rope_c_sin"] = din("rope_c_sin", (128, 2048))
        I["rope_c_p"] = din("rope_c_p", (128, 128))
    if n_b:
        I["mla_w_down"] = din("mla_w_down", (n_b, D, 704))
        I["mla_q_lat_norm"] = din("mla_q_lat_norm", (n_b, 3, 128))
        I["mla_kv_lat_norm"] = din("mla_kv_lat_norm", (n_b, 2, 128))
        I["mla_w_uq"] = din("mla_w_uq", (n_b, 384, 1536))
        I["mla_w_ukv"] = din("mla_w_ukv", (n_b, 256, 2048))
        I["mla_qn_nope"] = din("mla_qn_nope", (n_b, 128))
        I["mla_qn_rope"] = din("mla_qn_rope", (n_b, 64))
        I["mla_kn_nope"] = din("mla_kn_nope", (n_b, 128))
        I["mla_kn_rope"] = din("mla_kn_rope", (n_b, 64))
        I["mla_w_out"] = din("mla_w_out", (n_b, D, D))
        I["cache_mla_ckv"] = din("cache_mla_ckv", (n_b, 512, 256))
        I["cache_mla_krope"] = din("cache_mla_krope", (n_b, 512, 64))
        I["rope_b_cos"] = din("rope_b_cos", (64, 2048))
        I["rope_b_sin"] = din("rope_b_sin", (64, 2048))
        I["rope_b_p"] = din("rope_b_p", (64, 64))
    if n_a:
        I["gdn_w_in"] = din("gdn_w_in", (n_a, D, 4096))
        I["gdn_w_gb"] = din("gdn_w_gb", (n_a, D, 32))
        I["gdn_conv"] = din("gdn_conv", (n_a, 72, 128))
        I["gdn_a_log"] = din("gdn_a_log", (n_a, 16))
        I["gdn_dt_bias"] = din("gdn_dt_bias", (n_a, 16))
        I["gdn_out_norm"] = din("gdn_out_norm", (n_a, 128))
        I["gdn_w_out"] = din("gdn_w_out", (n_a, D, D))
        I["state_f"] = din("state_f", (n_a, 8, 128, 128))
        I["state_b"] = din("state_b", (n_a, 8, 128, 128))
        I["gdn_masks"] = din("gdn_masks", (8, 128, 128))

    O = {}
    O["yp"] = dout("yp", (512, D))
    O["ys"] = dout("ys", (2048, D))
    O["o_sf"] = dout("o_sf", (2, max(n_a, 1), 8, 128, 128))
    O["o_sb"] = dout("o_sb", (2, max(n_a, 1), 8, 128, 128))
    O["o_ckv"] = dout("o_ckv", (2, max(n_b, 1), 256, 256))
    O["o_kr"] = dout("o_kr", (2, max(n_b, 1), 256, 64))
    O["o_gk"] = dout("o_gk", (2, max(n_c, 1), 256, 256))
    O["o_gv"] = dout("o_gv", (2, max(n_c, 1), 256, 256))

    st = contextlib.ExitStack()
    with st:
        K = Sched(nc)

        def sb(name, shape, dt):
            return Buf(st.enter_context(nc.sbuf_tensor("sb_" + name, list(shape), dt)).ap(), name)

        XT = [[sb("x%d_%d" % (kc, b), (128, 512), F32) for b in range(5)] for kc in range(NKC)]
        HT = [[sb("h%d_%d" % (kc, b), (128, 512), BF16) for b in range(5)] for kc in range(NKC)]
        ident = sb("ident", (128, 128), F32)
        identb = sb("identb", (128, 128), BF16)
        onesf = sb("onesf", (128, 128), F32)
        onesb = sb("onesb", (128, 128), BF16)
        condT = sb("condT", (128, 2, 8), F32)
        modT = [sb("modT%d" % i, (128, 48, 2), F32) for i in range(2)]
        coefA = sb("coefA", (128, 2, 8, 2), F32)
        gnorm = sb("gnorm", (128, 2, 8), F32)
        epsv = sb("epsv", (128, 1), F32)
        WR_BYTES = 85 * 1024
        wr_t = sb("wr", (128, WR_BYTES // 4), F32)
        R = Region(wr_t.ap, WR_BYTES)
        PS = [Buf(st.enter_context(nc.psum_tensor("ps%d" % i, [128, 512], F32)).ap(), "ps%d" % i, excl=True) for i in range(8)]

        def psb(i, dt=F32):
            if dt == BF16:
                return View(PS[i], PS[i].ap.bitcast(BF16))
            return PS[i].v

        K.dma(ident.v, I["ident"].v)
        K.dma(onesf.v, I["ones"].v)
        K.dma(identb.v, I["ident"].v, eng="pool")
        K.dma(onesb.v, I["ones"].v, eng="pool")
        K.memset("dve", epsv.v, EPS)

        R.reset()
        xst = [R.alloc((128, D), F32, "xst%d" % i) for i in range(2)]
        import os
        NT_DBG = int(os.environ.get("KNT", "20"))
        for t in range(NT_DBG):
            src = I["xp"][t * 128:(t + 1) * 128, :] if t < 4 else I["xs"][(t - 4) * 128:(t - 3) * 128, :]
            s_ = xst[(t + int(os.environ.get("KSL", "0"))) % 2]
            if os.environ.get("KV", "") == "a":
                src = I["xp"][0:128, :]
            if os.environ.get("KV", "") == "b":
                s_ = xst[0]
            K.dma(s_.v, src, eng=os.environ.get("KQ", "sp"))
            if os.environ.get("KV", "") == "c" and t == 1:
                continue
            b, off = divmod(t * 128, 512)
            for g in range(2):
                pb = psb((t * 2 + g + int(os.environ.get("KOFF", "0"))) % int(os.environ.get("KNB", "4")))
                for q in range(4):
                    kc = g * 4 + q
                    K.transpose(pb[:, q * 128:(q + 1) * 128], s_[:, kc * 128:(kc + 1) * 128], ident.v)
                for q in range(4):
                    kc = g * 4 + q
                    eng = "dve" if q % 2 == 0 else "act"
                    if os.environ.get("KV", "") == "d":
                        eng = "dve"
                    if os.environ.get("KV", "") == "e":
                        eng = "act"
                    K.copy(eng, XT[kc][b][:, off:off + 128], pb[:, q * 128:(q + 1) * 128])

        import os
        if os.environ.get("KDBG", "") != "1":
            cst = R.alloc((16, 128), F32, "cst")
            K.dma(cst.v, I["cond"].v.re("c k p -> (c k) p"))
            K.transpose(psb(4)[:, 0:16], cst.v, ident[0:16, 0:16])
            K.act(condT.v.re("p c k -> p (c k)"), psb(4)[:, 0:16], AF.Silu)

        def adaln(i):
            mt = modT[i % 2]
            bst = R.alloc((48, 128), F32, "bst")
            bT = R.alloc((128, 48), F32, "bT")
            K.dma(bst.v, I["b_mod"][i])
            K.transpose(psb(4)[:, 0:48], bst.v, ident[0:48, 0:48])
            K.copy("dve", bT.v, psb(4)[:, 0:48])
            wst = [R.alloc((128, 8, 256), F32, "wmod%d" % q) for q in range(2)]
            mrow = [R.alloc((2, 256), F32, "mrow%d" % q) for q in range(2)]
            for cb in range(24):
                w_ = wst[cb % 2]
                K.dma(w_.v, I["w_mod"][i].re("(k p) n -> p k n", p=128)[:, :, cb * 256:(cb + 1) * 256])
                pr_ = psb(4)
                for kc in range(8):
                    K.mm(pr_[0:2, 0:256], condT[:, :, kc], w_[:, kc, :], start=(kc == 0), stop=(kc == 7))
                m_ = mrow[cb % 2]
                K.copy("act", m_.v, pr_[0:2, 0:256])
                pb = psb(5)
                for fl in range(2):
                    f = cb * 2 + fl
                    K.transpose(pb[:, f * 2:f * 2 + 2], m_[:, fl * 128:(fl + 1) * 128], ident[0:2, 0:2])
                if cb % 2 == 1:
                    yield
            K.tt("dve", mt.v, psb(5)[:, 0:96].re("p (f c) -> p f c", c=2), bT.v.un(2).bc([128, 48, 2]), ALU.add)
            yield

        def run(gen):
            for _ in gen:
                pass

        def load_gains(i):
            gst = R.alloc((16, 128), F32, "gst")
            K.dma(gst[0:8, :], I["norm_mix"][i])
            K.dma(gst[8:16, :], I["norm_mlp"][i])
            K.transpose(psb(4)[:, 0:16], gst.v, ident[0:16, 0:16])
            K.copy("dve", gnorm.v.re("p w k -> p (w k)"), psb(4)[:, 0:16])

        def make_coef(i):
            mt = modT[i % 2]
            for w in range(2):
                sc = mt[:, (3 * w + 1) * 8:(3 * w + 2) * 8, :]
                K.ts("dve", coefA[:, w], sc, 1.0, ALU.add)
                K.tt("dve", coefA[:, w], coefA[:, w], gnorm[:, w].un(2).bc([128, 8, 2]), ALU.mult)

        def mod_norm(i, w):
            mt = modT[i % 2]
            sq = [R.alloc((128, 512), BF16, "sq%d" % q) for q in range(2)]
            rs = [R.alloc((128, 512), F32, "rs%d" % q) for q in range(2)]
            tmp = [R.alloc((128, 512), F32, "ntmp%d" % q) for q in range(2)]
            for b, (c0, n, cond) in enumerate(TB):
                pb = psb(b % 2)
                for kc in range(8):
                    s_ = sq[kc % 2]
                    if kc % 2 == 0:
                        K.act(s_.v, XT[kc][b].v, AF.Square)
                    else:
                        K.tt("pool", s_.v, XT[kc][b].v, XT[kc][b].v, ALU.mult)
                    K.mm(pb, onesb.v, s_.v, start=(kc == 0), stop=(kc == 7))
                r_ = rs[b % 2]
                K.act(r_.v, pb, AF.Ln, bias=epsv.v, scale=1.0 / D)
                K.act(r_.v, r_.v, AF.Exp, scale=-0.5)
                for kc in range(8):
                    t_ = tmp[kc % 2]
                    K.stt("dve", t_.v, XT[kc][b].v, coefA[:, w, kc, cond:cond + 1], r_.v, ALU.mult, ALU.mult)
                    K.act(HT[kc][b].v, t_.v, AF.Identity, bias=mt[:, 3 * w * 8 + kc, cond:cond + 1])

        def mlp(i, bg):
            mt = modT[i % 2]
            win = [R.alloc((128, 8, 512), BF16, "win%d" % q) for q in range(2)]
            wout = [R.alloc((128, 4, D), BF16, "wout%d" % q) for q in range(2)]
            uu = [R.alloc((128, 4, 512), BF16, "uu%d" % q) for q in range(2)]
            rr = [R.alloc((128, 512), BF16, "rr%d" % q) for q in range(2)]
            iters = [(fb, b) for fb in range(8) for b in range(5)]
            wts = {}

            def u_phase(n):
                fb, b = iters[n]
                if b == 0:
                    wi = win[fb % 2]
                    wo = wout[fb % 2]
                    K.dma(wi.v, I["w_mlp_in"][i].re("(k p) n -> p k n", p=128)[:, :, fb * 512:(fb + 1) * 512], eng="pool")
                    K.dma(wo.v, I["w_mlp_out"][i][fb * 512:(fb + 1) * 512, :].re("(f p) n -> p f n", p=128), eng="pool")
                wi = win[fb % 2]
                u_ = uu[n % 2]
                for fc in range(4):
                    pb = psb(fc % 2)
                    for kc in range(8):
                        K.mm(pb, wi[:, kc, fc * 128:(fc + 1) * 128], HT[kc][b].v, start=(kc == 0), stop=(kc == 7))
                    r_ = rr[fc % 2]
                    K.act(r_.v, pb, AF.Relu)
                    K.tt("pool", u_[:, fc, :], r_.v, r_.v, ALU.mult)

            def y_phase(n):
                fb, b = iters[n]
                cond = TB[b][2]
                wo = wout[fb % 2]
                u_ = uu[n % 2]
                for oc in range(8):
                    pb = psb(2 + oc % 2)
                    for fc in range(4):
                        K.mm(pb, wo[:, fc, oc * 128:(oc + 1) * 128], u_[:, fc, :], start=(fc == 0), stop=(fc == 3))
                    K.stt("dve", XT[oc][b].v, pb, mt[:, 40 + oc, cond:cond + 1], XT[oc][b].v, ALU.mult, ALU.add)
                if bg is not None:
                    next(bg, None)
            for n in range(len(iters) + 1):
                if n < len(iters):
                    u_phase(n)
                if n >= 1:
                    y_phase(n - 1)
            if bg is not None:
                run(bg)

        def out_accum(i, oT, wo_h, extra=None):
            mt = modT[i % 2]
            for b, (c0, n, cond) in enumerate(TB):
                for oc in range(8):
                    pb = psb(6 + oc % 2)
                    if extra is None:
                        K.mm(pb, wo_h[:, oc * 128:(oc + 1) * 128], oT[b])
                    else:
                        K.mm(pb, extra[1][:, oc * 128:(oc + 1) * 128], extra[0][b], start=True, stop=False)
                        K.mm(pb, wo_h[:, oc * 128:(oc + 1) * 128], oT[b], start=False, stop=True)
                    K.stt("dve", XT[oc][b].v, pb, mt[:, 16 + oc, cond:cond + 1], XT[oc][b].v, ALU.mult, ALU.add)

        def fm_norm(dst, src_ps, npart, ncols, gain, nfeat, scr, rope=None, ones_v=None, eps_v=None):
            P = npart
            K.act(scr["sq"][0:P, 0:ncols], src_ps[0:P, 0:ncols], AF.Square)
            pd = scr["pd"]
            K.mm(pd[0:P, 0:ncols], onesb[0:P, 0:P], scr["sq"][0:P, 0:ncols])
            K.act(scr["rs"][0:P, 0:ncols], pd[0:P, 0:ncols], AF.Ln, bias=(eps_v or epsv)[0:P, :], scale=1.0 / nfeat)
            K.act(scr["rs"][0:P, 0:ncols], scr["rs"][0:P, 0:ncols], AF.Exp, scale=-0.5)
            if rope is None:
                K.stt("dve", dst, src_ps[0:P, 0:ncols], gain, scr["rs"][0:P, 0:ncols], ALU.mult, ALU.mult)
            else:
                cos, sin, permT = rope
                nb = scr["nb"]
                K.stt("dve", nb[0:P, 0:ncols], src_ps[0:P, 0:ncols], gain, scr["rs"][0:P, 0:ncols], ALU.mult, ALU.mult)
                K.mm(pd[0:P, 0:ncols], permT, nb[0:P, 0:ncols])
                K.tt("pool", scr["t1"][0:P, 0:ncols], nb[0:P, 0:ncols], cos, ALU.mult)
                K.tt("dve", scr["t2"][0:P, 0:ncols], pd[0:P, 0:ncols], sin, ALU.mult)
                K.tt("pool", dst, scr["t1"][0:P, 0:ncols], scr["t2"][0:P, 0:ncols], ALU.add)

        def norm_scr(light=False):
            if light:
                return {"sq": R.alloc((128, 512), BF16, "nsq"), "rs": R.alloc((128, 512), F32, "nrs"), "pd": psb(5)}
            return {"sq": R.alloc((128, 512), BF16, "nsq"), "rs": R.alloc((128, 512), F32, "nrs"),
                    "t1": R.alloc((128, 512), F32, "nt1"), "t2": R.alloc((128, 512), F32, "nt2"),
                    "nb": R.alloc((128, 512), BF16, "nnb"), "pd": psb(5)}

        QB = [(0, 256, [0, 1]), (256, 256, [2, 3])] + [(512 + 512 * q, 512, list(range(4, 24))) for q in range(4)]

        def attn_head(kparts, vtile, qparts, oT, scale, pbufs):
            steps = []
            for qi, (q0, nq, kts) in enumerate(QB):
                for ji, kt in enumerate(kts):
                    steps.append((qi, q0, nq, ji, kt, len(kts)))
            pts = {}

            def issue_s(n):
                qi, q0, nq, ji, kt, nk = steps[n]
                pss = psb(n % 3)
                for pi, (kT, P) in enumerate(kparts):
                    K.mm(pss[:, 0:nq], kT[0:P, kt * 128:(kt + 1) * 128], qparts[pi][0:P, q0:q0 + nq],
                         start=(pi == 0), stop=(pi == len(kparts) - 1))
                pt = pbufs[n % len(pbufs)]
                K.act(pt[:, 0:nq], pss[:, 0:nq], AF.Exp, scale=scale)
                pts[n] = pt

            def issue_pv(n):
                qi, q0, nq, ji, kt, nk = steps[n]
                po, pdn = (psb(3), psb(4)) if qi % 2 == 0 else (psb(5), psb(6))
                pt = pts.pop(n)
                K.mm(po[:, 0:nq], vtile(kt), pt[:, 0:nq], start=(ji == 0), stop=(ji == nk - 1))
                K.mm(pdn[:, 0:nq], onesb.v, pt[:, 0:nq], start=(ji == 0), stop=(ji == nk - 1))
                if ji == nk - 1:
                    rd = pbufs_rd[qi % 2]
                    K.act(rd[:, 0:nq], pdn[:, 0:nq], AF.Ln)
                    K.act(rd[:, 0:nq], rd[:, 0:nq], AF.Exp, scale=-1.0)
                    K.tt("dve", oT[:, q0:q0 + nq], po[:, 0:nq], rd[:, 0:nq], ALU.mult)
            SK = 2
            for n in range(len(steps) + SK):
                if n < len(steps):
                    issue_s(n)
                if n - SK >= 0:
                    issue_pv(n - SK)

        pbufs_rd = [None, None]

        def gqa_layer(i, j):
            R.reset()
            kT = R.alloc((128, 2, 3072), BF16, "kT")
            V = R.alloc((128, 24, 256), BF16, "V")
            cos = R.alloc((128, 2048), BF16, "cos")
            sin = R.alloc((128, 2048), BF16, "sin")
            permT = R.alloc((128, 128), BF16, "permT")
            kg = R.alloc((128, 2), F32, "kg")
            kgb = R.alloc((128, 128), F32, "kgb")
            K.dma(cos.v, I["rope_c_cos"].v, eng="pool")
            K.dma(sin.v, I["rope_c_sin"].v, eng="pool")
            K.dma(permT.v, I["rope_c_p"].v, eng="pool")
            gst = R.alloc((2, 128), F32, "gst2")
            K.dma(gst[0:1, :], I["gqa_q_norm"][j:j + 1, :])
            K.dma(gst[1:2, :], I["gqa_k_norm"][j:j + 1, :])
            K.transpose(psb(4)[:, 0:2], gst.v, ident[0:2, 0:2])
            K.copy("dve", kg.v, psb(4)[:, 0:2])
            K.dma(kgb.v, View(I["gqa_k_norm"], I["gqa_k_norm"].ap[j].partition_broadcast(128)))
            top0 = R.top
            wkv = R.alloc((128, 8, 512), BF16, "wkv")
            K.dma(wkv.v, I["gqa_w_in"][j].re("(k p) n -> p k n", p=128)[:, :, 1024:1536], eng="pool")
            scr = norm_scr()
            for kvh in range(2):
                for b, (c0, n, cond) in enumerate(TB):
                    pb = psb(b % 2)
                    for kc in range(8):
                        K.mm(pb, wkv[:, kc, kvh * 128:(kvh + 1) * 128], HT[kc][b].v, start=(kc == 0), stop=(kc == 7))
                    rope = None if b == 0 else (cos[:, c0 - 512:c0], sin[:, c0 - 512:c0], permT.v)
                    fm_norm(kT[:, kvh, c0:c0 + 512], pb, 128, 512, kg[:, 1:2], 128, scr, rope)
            vst = R.alloc((128, 256), F32, "vst")
            kst = R.alloc((128, 256), F32, "kst")
            ssq = R.alloc((128, 2), F32, "ssq")
            ksq = R.alloc((128, 256), F32, "ksq")
            for t in range(20):
                b, off = divmod(t * 128, 512)
                pb = psb(2 + t % 2)
                ncol = 512 if t < 4 else 256
                for kc in range(8):
                    rhs = wkv[:, kc, 0:512] if t < 4 else wkv[:, kc, 256:512]
                    K.mm(pb[:, 0:ncol], HT[kc][b][:, off:off + 128], rhs, start=(kc == 0), stop=(kc == 7))
                if t < 4:
                    K.copy("act", V[:, t, :], pb[:, 256:512])
                    K.copy("dve", vst.v, pb[:, 256:512])
                    K.dma(O["o_gv"][t // 2, j, (t % 2) * 128:(t % 2 + 1) * 128, :], vst.v)
                    K.act(ksq.v, pb[:, 0:256], AF.Square)
                    K.reduce("dve", ssq.v, ksq.v.re("p (h d) -> p h d", h=2))
                    K.act(ssq.v, ssq.v, AF.Ln, bias=epsv.v, scale=1.0 / 128)
                    K.act(ssq.v, ssq.v, AF.Exp, scale=-0.5)
                    K.tt("dve", kst.v.re("p (h d) -> p h d", h=2), pb[:, 0:256].re("p (h d) -> p h d", h=2),
                         ssq.v.un(2).bc([128, 2, 128]), ALU.mult)
                    K.tt("pool", kst.v.re("p (h d) -> p h d", h=2), kst.v.re("p (h d) -> p h d", h=2),
                         kgb.v.un(1).bc([128, 2, 128]), ALU.mult)
                    K.dma(O["o_gk"][t // 2, j, (t % 2) * 128:(t % 2 + 1) * 128, :], kst.v)
                else:
                    K.copy("act", V[:, t, :], pb[:, 0:256])
            K.dma(V[:, 20:24, :], I["cache_gqa_v"][j].re("(t p) n -> p t n", p=128), eng="pool")
            cst_ = R.alloc((128, 4, 256), F32, "cst_")
            K.dma(cst_.v, I["cache_gqa_k"][j].re("(t p) n -> p t n", p=128))
            for t in range(4):
                for kvh in range(2):
                    pb = psb(t % 2)
                    K.transpose(pb[:, 0:128], cst_[:, t, kvh * 128:(kvh + 1) * 128], ident.v)
                    K.copy("act", kT[:, kvh, 2560 + t * 128:2560 + (t + 1) * 128], pb[:, 0:128])
            R.top = top0
            scr = norm_scr()
            wq = [R.alloc((128, 8, 128), BF16, "wq%d" % q) for q in range(2)]
            woh = [R.alloc((128, D), BF16, "woh%d" % q) for q in range(2)]
            qT = [R.alloc((128, T), BF16, "qT%d" % q) for q in range(2)]
            oT = [R.alloc((128, T), BF16, "oT%d" % q) for q in range(2)]
            pb_ = [R.alloc((128, 512), BF16, "pt%d" % q) for q in range(4)]
            pbufs_rd[0] = R.alloc((128, 512), F32, "rd0")
            pbufs_rd[1] = R.alloc((128, 512), F32, "rd1")
            for h in range(8):
                kvh = h // 4
                w_ = wq[h % 2]
                K.dma(w_.v, I["gqa_w_in"][j].re("(k p) n -> p k n", p=128)[:, :, h * 128:(h + 1) * 128], eng="pool")
                K.dma(woh[h % 2].v, I["gqa_w_out"][j][h * 128:(h + 1) * 128, :], eng="pool")
                q_ = qT[h % 2]
                for b, (c0, n, cond) in enumerate(TB):
                    pb = psb(6 + b % 2)
                    for kc in range(8):
                        K.mm(pb, w_[:, kc, :], HT[kc][b].v, start=(kc == 0), stop=(kc == 7))
                    rope = None if b == 0 else (cos[:, c0 - 512:c0], sin[:, c0 - 512:c0], permT.v)
                    fm_norm(q_[:, c0:c0 + 512], pb, 128, 512, kg[:, 0:1], 128, scr, rope)
                o_ = oT[h % 2]
                attn_head([(kT[:, kvh, :], 128)], lambda kt, kvh=kvh: V[:, kt, kvh * 128:(kvh + 1) * 128],
                          [q_.v], o_.v, 128 ** -0.5, pb_)
                if h % 2 == 1:
                    out_accum(i, [o_[:, c0:c0 + 512] for (c0, n, cond) in TB], woh[1].v,
                              extra=([oT[0][:, c0:c0 + 512] for (c0, n, cond) in TB], woh[0].v))

        def mla_layer(i, j):
            R.reset()
            cqn = R.alloc((128, 3, T), BF16, "cqn")
            ckvn = R.alloc((128, 2, 3072), BF16, "ckvn")
            krT = R.alloc((64, 3072), BF16, "krT")
            cos = R.alloc((64, 2048), BF16, "cosb")
            sin = R.alloc((64, 2048), BF16, "sinb")
            permT = R.alloc((64, 64), BF16, "permTb")
            gl = R.alloc((128, 9), F32, "gl")
            K.dma(cos.v, I["rope_b_cos"].v, eng="pool")
            K.dma(sin.v, I["rope_b_sin"].v, eng="pool")
            K.dma(permT.v, I["rope_b_p"].v, eng="pool")
            gst = R.alloc((9, 128), F32, "gst9")
            K.memset("dve", gst.v, 0.0)
            K.dma(gst[0:3, :], I["mla_q_lat_norm"][j])
            K.dma(gst[3:5, :], I["mla_kv_lat_norm"][j])
            K.dma(gst[5:6, :], I["mla_qn_nope"][j:j + 1, :])
            K.dma(gst[6:7, :], I["mla_kn_nope"][j:j + 1, :])
            K.dma(gst[7:8, 0:64], I["mla_qn_rope"][j:j + 1, :])
            K.dma(gst[8:9, 0:64], I["mla_kn_rope"][j:j + 1, :])
            K.transpose(psb(4)[:, 0:9], gst.v, ident[0:9, 0:9])
            K.copy("dve", gl.v, psb(4)[:, 0:9])
            top0 = R.top
            wd = R.alloc((128, 8, 704), BF16, "wd")
            K.dma(wd.v, I["mla_w_down"][j].re("(k p) n -> p k n", p=128), eng="pool")
            scr = norm_scr()
            sqs = [R.alloc((128, 512), BF16, "msq%d" % q) for q in range(3)]
            rs = R.alloc((128, 512), F32, "mrs")
            f32o = R.alloc((128, 3, 512), F32, "f32o")
            groups = [(0, 3, 0, 384), (384, 2, 3, 256)]
            for b, (c0, n, cond) in enumerate(TB):
                for (col0, nch, gcol, nfeat) in groups:
                    pbs = [psb(q) for q in range(nch)]
                    for c in range(nch):
                        for kc in range(8):
                            K.mm(pbs[c], wd[:, kc, col0 + c * 128:col0 + (c + 1) * 128], HT[kc][b].v,
                                 start=(kc == 0), stop=(kc == 7))
                    pd = psb(5)
                    for c in range(nch):
                        K.act(sqs[c].v, pbs[c], AF.Square)
                        K.mm(pd, onesb.v, sqs[c].v, start=(c == 0), stop=(c == nch - 1))
                    K.act(rs.v, pd, AF.Ln, bias=epsv.v, scale=1.0 / nfeat)
                    K.act(rs.v, rs.v, AF.Exp, scale=-0.5)
                    for c in range(nch):
                        if nch == 3:
                            dst = cqn[:, c, c0:c0 + 512]
                        else:
                            dst = ckvn[:, c, c0:c0 + 512]
                        K.stt("dve", dst, pbs[c], gl[:, gcol + c:gcol + c + 1], rs.v, ALU.mult, ALU.mult)
                        if nch == 2 and b == 0:
                            K.stt("dve", f32o[:, c, :], pbs[c], gl[:, gcol + c:gcol + c + 1], rs.v, ALU.mult, ALU.mult)
                pb = psb(3)
                for kc in range(8):
                    K.mm(pb[0:64, :], wd[:, kc, 640:704], HT[kc][b].v, start=(kc == 0), stop=(kc == 7))
                rope = None if b == 0 else (cos[:, c0 - 512:c0], sin[:, c0 - 512:c0], permT.v)
                fm_norm(krT[:, c0:c0 + 512], pb, 64, 512, gl[0:64, 8:9], 64, scr, rope)
                if b == 0:
                    fm_norm(f32o[0:64, 2, :], pb, 64, 512, gl[0:64, 8:9], 64, scr, None)
            ost = R.alloc((128, 320), F32, "ost")
            for t in range(4):
                pb = psb(t % 2)
                for c in range(2):
                    K.transpose(pb[:, c * 128:(c + 1) * 128], f32o[:, c, t * 128:(t + 1) * 128], ident.v)
                K.transpose(pb[:, 256:320], f32o[0:64, 2, t * 128:(t + 1) * 128], ident[0:64, 0:64])
                K.copy("dve", ost.v, pb[:, 0:320])
                K.dma(O["o_ckv"][t // 2, j, (t % 2) * 128:(t % 2 + 1) * 128, :], ost[:, 0:256])
                K.dma(O["o_kr"][t // 2, j, (t % 2) * 128:(t % 2 + 1) * 128, :], ost[:, 256:320])
            cst_ = R.alloc((128, 4, 320), F32, "mcst")
            K.dma(cst_[:, :, 0:256], I["cache_mla_ckv"][j].re("(t p) n -> p t n", p=128))
            K.dma(cst_[:, :, 256:320], I["cache_mla_krope"][j].re("(t p) n -> p t n", p=128))
            for t in range(4):
                pb = psb(t % 2)
                for c in range(2):
                    K.transpose(pb[:, c * 128:(c + 1) * 128], cst_[:, t, c * 128:(c + 1) * 128], ident.v)
                K.transpose(pb[0:64, 256:384], cst_[:, t, 256:320], ident.v)
                for c in range(2):
                    K.copy("act", ckvn[:, c, 2560 + t * 128:2560 + (t + 1) * 128], pb[:, c * 128:(c + 1) * 128])
                K.copy("dve", krT[:, 2560 + t * 128:2560 + (t + 1) * 128], pb[0:64, 256:384])
            R.top = top0
            scr = norm_scr()
            wuq = [R.alloc((128, 3, 192), BF16, "wuq%d" % q) for q in range(1)]
            wukv = [R.alloc((128, 2, 256), BF16, "wukv%d" % q) for q in range(1)]
            woh = [R.alloc((128, D), BF16, "mwoh%d" % q) for q in range(1)]
            qn = [R.alloc((128, T), BF16, "qn%d" % q) for q in range(1)]
            qr = [R.alloc((64, T), BF16, "qr%d" % q) for q in range(1)]
            kn = [R.alloc((128, 3072), BF16, "kn%d" % q) for q in range(1)]
            Vh = [R.alloc((128, 24, 128), BF16, "Vh%d" % q) for q in range(1)]
            oT = qn
            pb_ = [R.alloc((128, 512), BF16, "mpt%d" % q) for q in range(3)]
            pbufs_rd[0] = R.alloc((128, 512), F32, "mrd0")
            pbufs_rd[1] = pbufs_rd[0]
            for h in range(8):
                p_ = 0
                K.dma(wuq[p_].v, I["mla_w_uq"][j].re("(k p) n -> p k n", p=128)[:, :, h * 192:(h + 1) * 192], eng="pool")
                K.dma(wukv[p_].v, I["mla_w_ukv"][j].re("(k p) n -> p k n", p=128)[:, :, h * 256:(h + 1) * 256], eng="pool")
                K.dma(woh[p_].v, I["mla_w_out"][j][h * 128:(h + 1) * 128, :], eng="pool")
                for cb in range(6):
                    c0 = cb * 512
                    pb = psb(6 + cb % 2)
                    for c in range(2):
                        K.mm(pb, wukv[p_][:, c, 0:128], ckvn[:, c, c0:c0 + 512], start=(c == 0), stop=(c == 1))
                    fm_norm(kn[p_][:, c0:c0 + 512], pb, 128, 512, gl[:, 6:7], 128, scr, None)
                for g in range(6):
                    pb = psb(6 + g % 2)
                    for q in range(4):
                        t = g * 4 + q
                        for c in range(2):
                            K.mm(pb[:, q * 128:(q + 1) * 128], ckvn[:, c, t * 128:(t + 1) * 128], wukv[p_][:, c, 128:256],
                                 start=(c == 0), stop=(c == 1))
                    K.copy("act", Vh[p_][:, g * 4:(g + 1) * 4, :], pb.re("p (q d) -> p q d", q=4))
                for b, (c0, n, cond) in enumerate(TB):
                    pb = psb(6 + b % 2)
                    for c in range(3):
                        K.mm(pb, wuq[p_][:, c, 0:128], cqn[:, c, c0:c0 + 512], start=(c == 0), stop=(c == 2))
                    fm_norm(qn[p_][:, c0:c0 + 512], pb, 128, 512, gl[:, 5:6], 128, scr, None)
                    pb2 = psb(7 - b % 2)
                    for c in range(3):
                        K.mm(pb2[0:64, :], wuq[p_][:, c, 128:192], cqn[:, c, c0:c0 + 512], start=(c == 0), stop=(c == 2))
                    rope = None if b == 0 else (cos[:, c0 - 512:c0], sin[:, c0 - 512:c0], permT.v)
                    fm_norm(qr[p_][:, c0:c0 + 512], pb2, 64, 512, gl[0:64, 7:8], 64, scr, rope)
                attn_head([(kn[p_].v, 128), (krT.v, 64)], lambda kt, p_=p_: Vh[p_][:, kt, :],
                          [qn[p_].v, qr[p_].v], oT[p_].v, 192 ** -0.5, pb_)
                out_accum(i, [oT[p_][:, c0:c0 + 512] for (c0, n, cond) in TB], woh[p_].v)


        def gdn_layer(i, j):
            R.reset()
            NEG = -30000.0
            msk = R.alloc((64, 6, 64), F32, "msk")
            K.dma(msk.v, I["gdn_masks"][0:6, 0:64, 0:64].re("m p n -> p m n"))
            U = [msk[:, 0, :], msk[:, 1, :]]
            NEGS = [msk[:, 2, :], msk[:, 4, :]]
            NEGI = [msk[:, 3, :], msk[:, 5, :]]
            I64 = ident[0:64, 0:64]
            eps2 = R.alloc((128, 1), F32, "eps2")
            K.memset("dve", eps2.v, EPS / 128.0)
            gq = R.alloc((128, 3), F32, "gq")
            K.memset("dve", gq[:, 0:1], 1.0 / 128.0)
            K.memset("dve", gq[:, 1:2], 128.0 ** -0.5)
            cst_ = R.alloc((73, 128), F32, "gcst")
            K.dma(cst_[0:72, :], I["gdn_conv"][j])
            K.dma(cst_[72:73, :], I["gdn_out_norm"][j:j + 1, :])
            cw = R.alloc((128, 73), F32, "cw")
            K.transpose(psb(4)[:, 0:73], cst_.v, ident[0:73, 0:73])
            K.copy("dve", cw.v, psb(4)[:, 0:73])
            alb = R.alloc((64, 16), F32, "alb")
            dtb = R.alloc((64, 16), F32, "dtb")
            K.dma(alb.v, View(I["gdn_a_log"], I["gdn_a_log"].ap[j].partition_broadcast(64)))
            K.dma(dtb.v, View(I["gdn_dt_bias"], I["gdn_dt_bias"].ap[j].partition_broadcast(64)))
            K.act(alb.v, alb.v, AF.Exp)
            K.ts("dve", alb.v, alb.v, -1.0, ALU.mult)
            gcum = R.alloc((64, 40, 16), F32, "gcum")
            sA = R.alloc((64, 40, 16), F32, "sA")
            eg2 = R.alloc((128, 40, 16), F32, "eg2")
            ekd2 = R.alloc((128, 40, 16), F32, "ekd2")
            beta2 = R.alloc((128, 40, 16), F32, "beta2")
            egt = R.alloc((128, 40, 16), F32, "egt")
            Idd = R.alloc((64, 128), F32, "Idd")
            Idup = R.alloc((128, 64), F32, "Idup")
            K.copy("dve", Idd[:, 0:64], ident[0:64, 0:64])
            K.copy("dve", Idd[:, 64:128], ident[0:64, 0:64])
            K.copy("dve", Idup[0:64, :], ident[0:64, 0:64])
            K.copy("dve", Idup[64:128, :], ident[64:128, 64:128])
            top_g = R.top
            eg = R.alloc((64, 40, 16), F32, "eg")
            ekd = R.alloc((64, 40, 16), F32, "ekd")
            beta = R.alloc((64, 40, 16), F32, "beta")
            wgb = R.alloc((128, 8, 32), BF16, "wgb")
            K.dma(wgb.v, I["gdn_w_gb"][j].re("(k p) n -> p k n", p=128), eng="pool")
            G = R.alloc((64, 40, 32), F32, "G")
            g_ = R.alloc((64, 40, 16), F32, "g_")
            t1 = R.alloc((64, 40, 16), F32, "gt1")
            t2 = R.alloc((64, 40, 16), F32, "gt2")
            for grp in range(3):
                c_lo, c_hi = grp * 16, min(40, grp * 16 + 16)
                pb = psb(grp % 2)
                for c in range(c_lo, c_hi):
                    b, off = divmod(c * 64, 512)
                    for kc in range(8):
                        K.mm(pb[0:64, (c - c_lo) * 32:(c - c_lo + 1) * 32], HT[kc][b][:, off:off + 64], wgb[:, kc, :],
                             start=(kc == 0), stop=(kc == 7))
                K.copy("dve", G[:, c_lo:c_hi, :], pb[0:64, 0:(c_hi - c_lo) * 32].re("p (c n) -> p c n", n=32))

            def softplus(dst, x):
                K.act(t1.v, x, AF.Abs)
                K.act(t1.v, t1.v, AF.Exp, scale=-1.0)
                K.act(t1.v, t1.v, AF.Ln, bias=1.0)
                K.ts("dve", t2.v, x, 0.0, ALU.max)
                K.tt("dve", dst, t1.v, t2.v, ALU.add)
            K.tt("dve", g_.v, G[:, :, 0:16], dtb.v.un(1).bc([64, 40, 16]), ALU.add)
            softplus(g_.v, g_.v)
            K.tt("dve", g_.v, g_.v, alb.v.un(1).bc([64, 40, 16]), ALU.mult)
            K.ts("dve", sA.v, G[:, :, 16:32], -1.0, ALU.mult)
            softplus(sA.v, sA.v)
            K.ts("dve", sA.v, sA.v, -1.0, ALU.mult)
            K.act(beta.v, sA.v, AF.Exp)
            for d in range(2):
                pb = psb(2 + d)
                K.mm(pb[0:64, 0:320].re("p (c h) -> p c h", h=8), U[d], g_[:, :, d * 8:(d + 1) * 8])
                K.copy("dve", gcum[:, :, d * 8:(d + 1) * 8], pb[0:64, 0:320].re("p (c h) -> p c h", h=8))
            pbt = [psb(4), psb(5)]
            K.mm(pbt[0][:, 0:320], onesf[0:64, 0:128], g_[:, 0:20, :])
            K.mm(pbt[1][:, 0:320], onesf[0:64, 0:128], g_[:, 20:40, :])
            for q in range(2):
                K.act(egt[:, q * 20:(q + 1) * 20, :], pbt[q][:, 0:320].re("p (c h) -> p c h", h=16), AF.Exp)
                K.tt("dve", ekd[:, q * 20:(q + 1) * 20, :], pbt[q][0:64, 0:320].re("p (c h) -> p c h", h=16),
                     gcum[:, q * 20:(q + 1) * 20, :], ALU.subtract)
            K.act(ekd.v, ekd.v, AF.Exp)
            K.act(eg.v, gcum.v, AF.Exp)
            K.ts("dve", eg.v, eg.v, -1.0, ALU.mult)
            K.tt("dve", sA.v, sA.v, gcum.v, ALU.subtract)
            for (src_, dst_) in ((eg, eg2), (ekd, ekd2), (beta, beta2)):
                for q in range(2):
                    pbq = psb(q)
                    K.mm(pbq[:, 0:320], Idd.v, src_[:, q * 20:(q + 1) * 20, :])
                    K.copy("act" if q else "dve", dst_[:, q * 20:(q + 1) * 20, :], pbq[:, 0:320].re("p (c h) -> p c h", h=16))
            R.top = top_g
            qT = R.alloc((128, T), BF16, "gqT")
            kT = R.alloc((128, T), BF16, "gkT")
            k_c = R.alloc((128, 5, 4, 128), BF16, "k_c")
            v_c = R.alloc((128, 5, 4, 128), BF16, "v_c")
            oaccs = [R.alloc((128, 512), BF16, "oacc%d" % q) for q in range(5)]
            og_keep = R.alloc((128, T), BF16, "og_keep")
            S = R.alloc((128, 128), F32, "S")
            Sb = R.alloc((128, 128), BF16, "Sb")
            top_h = R.top
            SEQ = [(0, 4, 0), (4, 8, 1), (8, 40, 2)]
            for h in range(8):
                R.top = top_h
                praw = R.alloc((128, T), F32, "praw")
                cv = R.alloc((128, T), F32, "cv")
                win = R.alloc((128, 8, 128), BF16, "gwin")
                scr = norm_scr(light=True)
                vT = None
                for xi in range(3):
                    K.dma(win.v, I["gdn_w_in"][j].re("(k p) n -> p k n", p=128)[:, :, xi * 1024 + h * 128:xi * 1024 + (h + 1) * 128],
                          eng="pool")
                    for b, (c0, n, cond) in enumerate(TB):
                        pb = psb(b % 2)
                        for kc in range(8):
                            K.mm(pb, win[:, kc, :], HT[kc][b].v, start=(kc == 0), stop=(kc == 7))
                        K.copy("act" if b % 2 else "dve", praw[:, c0:c0 + 512], pb)
                    ch = xi * 8 + h
                    for (s0, s1) in [(0, 256), (256, 512), (512, 2560)]:
                        K.ts("dve", cv[:, s0:s1], praw[:, s0:s1], cw[:, 24 + ch:25 + ch], ALU.mult)
                        K.stt("dve", cv[:, s0 + 1:s1], praw[:, s0:s1 - 1], cw[:, ch:ch + 1], cv[:, s0 + 1:s1], ALU.mult, ALU.add)
                        K.stt("dve", cv[:, s0:s1 - 1], praw[:, s0 + 1:s1], cw[:, 48 + ch:49 + ch], cv[:, s0:s1 - 1], ALU.mult, ALU.add)
                    K.act(cv.v, cv.v, AF.Silu)
                    if xi < 2:
                        dstT = qT if xi == 0 else kT
                        for b, (c0, n, cond) in enumerate(TB):
                            fm_norm(dstT[:, c0:c0 + 512], cv[:, c0:c0 + 512], 128, 512, gq[:, xi:xi + 1], 128, scr, None, eps_v=eps2)
                    else:
                        sv_top = R.top
                        R.top = top_h
                        vT = R.alloc((128, T), BF16, "vT")
                        R.top = sv_top
                        K.copy("pool", vT.v, cv.v)
                for (srcT, dstc) in ((kT, k_c), (vT, v_c)):
                    for b in range(5):
                        pbf = psb(2 + b % 2)
                        for c in range(8):
                            h0, pr = 64 * (c // 4), c % 4
                            K.mm(pbf[h0:h0 + 64, pr * 128:(pr + 1) * 128], srcT[:, b * 512 + c * 64:b * 512 + (c + 1) * 64], identb.v)
                        K.copy("act" if b % 2 else "dve", dstc[:, b, :, :], pbf[:, 0:512].re("p (c d) -> p c d", d=128))
                R.top = top_h
                scrs = [R.alloc((64, 8, 64), F32, "gs%d" % q) for q in range(3)]
                st_ = [R.alloc((128, 4, 64), F32, "gt%d" % q) for q in range(10)]
                kgn = R.alloc((128, 4, 128), BF16, "kgn")
                erow = R.alloc((128, 512), BF16, "erow")
                vnew2 = [R.alloc((128, 128), BF16, "vnew%d" % q) for q in range(2)]
                S2 = [S, R.alloc((128, 128), F32, "S_b")]
                Sb2 = [Sb, R.alloc((128, 128), BF16, "Sb_b")]
                for q in range(5):
                    K.memset("pool", oaccs[q].v, 0.0)
                psets = []
                for q in range(4):
                    psets.append({"kdec": R.alloc((128, 4, 128), BF16, "kdec%d" % q),
                                  "qdT": R.alloc((128, 8, 64), BF16, "qdT%d" % q),
                                  "QKt": R.alloc((128, 4, 64), BF16, "QKt%d" % q),
                                  "Tt": R.alloc((128, 4, 64), BF16, "Tt%d" % q),
                                  "WTn": R.alloc((128, 8, 64), BF16, "WTn%d" % q)})
                f2 = lambda v_: v_.re("p c i -> p (c i)")
                I64b = View(ident, I64.ap.unsqueeze(1).broadcast_to([64, 8, 64]))
                Idupb = View(Idup, Idup.ap.unsqueeze(1).broadcast_to([128, 4, 64]))
                HP = [(64 * (c // 4), c % 4) for c in range(8)]
                CORD = [0, 4, 1, 5, 2, 6, 3, 7]

                def intra(d, b, ps_):
                    dh = d * 8 + h
                    cb0 = b * 8
                    kdec, qdT, QKt, Tt, WTn = ps_["kdec"], ps_["qdT"], ps_["QKt"], ps_["Tt"], ps_["WTn"]
                    gc_b = gcum[:, cb0:cb0 + 8, dh]
                    Dg, RBA, RBQ = scrs
                    EA, EQ, At, Xa, M0_, M1_, PA0, PA1, PT0, PT1 = st_
                    K.tt("pool", Dg.v, I64b, gc_b.un(2).bc([64, 8, 64]), ALU.mult)
                    K.tt("dve", RBA.v, View(msk, NEGS[d].ap.unsqueeze(1).broadcast_to([64, 8, 64])),
                         sA[:, cb0:cb0 + 8, dh].un(2).bc([64, 8, 64]), ALU.add)
                    K.tt("pool", RBQ.v, View(msk, NEGI[d].ap.unsqueeze(1).broadcast_to([64, 8, 64])),
                         gc_b.un(2).bc([64, 8, 64]), ALU.subtract)
                    for hf in range(2):
                        r0 = 64 * hf
                        cc = cb0 + 4 * hf
                        K.tt("pool", kgn[r0:r0 + 64], k_c[r0:r0 + 64, b], eg2[r0:r0 + 64, cc:cc + 4, dh].un(2).bc([64, 4, 128]), ALU.mult)
                        K.tt("pool", kdec[r0:r0 + 64], k_c[r0:r0 + 64, b], ekd2[r0:r0 + 64, cc:cc + 4, dh].un(2).bc([64, 4, 128]), ALU.mult)
                    yield
                    pA, pQ, pE, pK = psb(0), psb(1), psb(2), psb(3)
                    for hf in range(2):
                        r0 = 64 * hf
                        K.mm(pA[r0:r0 + 64, 0:256], onesf[0:64, 0:64], f2(Dg[:, 4 * hf:4 * hf + 4, :]), start=True, stop=False)
                        K.mm(pA[r0:r0 + 64, 0:256], I64, f2(RBA[:, 4 * hf:4 * hf + 4, :]), start=False, stop=True)
                    for hf in range(2):
                        r0 = 64 * hf
                        K.mm(pQ[r0:r0 + 64, 0:256], onesf[0:64, 0:64], f2(Dg[:, 4 * hf:4 * hf + 4, :]), start=True, stop=False)
                        K.mm(pQ[r0:r0 + 64, 0:256], I64, f2(RBQ[:, 4 * hf:4 * hf + 4, :]), start=False, stop=True)
                    K.mm(pE, onesf[0:64, 0:128], f2(Dg.v))
                    for c in CORD:
                        r0, pr = HP[c]
                        cs = slice(b * 512 + c * 64, b * 512 + (c + 1) * 64)
                        K.mm(pK[r0:r0 + 64, pr * 64:(pr + 1) * 64], kT[:, cs], kT[:, cs])
                    pK2 = psb(4)
                    for c in CORD:
                        r0, pr = HP[c]
                        cs = slice(b * 512 + c * 64, b * 512 + (c + 1) * 64)
                        K.mm(pK2[r0:r0 + 64, pr * 64:(pr + 1) * 64], kT[:, cs], qT[:, cs])
                    K.act(f2(EA.v), pA[:, 0:256], AF.Exp)
                    K.act(f2(EQ.v), pQ[:, 0:256], AF.Exp)
                    K.act(erow.v, pE, AF.Exp)
                    yield
                    K.tt("dve", f2(At.v), pK[:, 0:256], f2(EA.v), ALU.mult)
                    K.tt("dve", f2(QKt.v), pK2[:, 0:256], f2(EQ.v), ALU.mult)
                    K.tt("pool", qdT.v.re("p c i -> p (c i)"), qT[:, b * 512:(b + 1) * 512], erow.v, ALU.mult)
                    pX = (psb(0), psb(1))
                    for c in CORD:
                        r0, pr = HP[c]
                        K.mm(pX[r0 // 64][r0:r0 + 64, pr * 64:(pr + 1) * 64], At[r0:r0 + 64, pr, :], ident[r0:r0 + 64, r0:r0 + 64])
                    K.copy("act", f2(Xa[0:64]), pX[0][0:64, 0:256])
                    K.copy("act", f2(Xa[64:128]), pX[1][64:128, 0:256])
                    M = M0_
                    K.tt("dve", M.v, Idupb, At.v, ALU.subtract)
                    yield
                    Pa, Pt = Xa, At
                    for lvl in range(1, 6):
                        p1, p2, p3 = (psb(1), psb(2)), (psb(3), psb(4)), (psb(0), psb(5))
                        for c in CORD:
                            r0, pr = HP[c]
                            K.mm(p1[r0 // 64][r0:r0 + 64, pr * 64:(pr + 1) * 64], Pt[r0:r0 + 64, pr, :], Pa[r0:r0 + 64, pr, :])
                        if lvl < 5:
                            for c in CORD:
                                r0, pr = HP[c]
                                K.mm(p2[r0 // 64][r0:r0 + 64, pr * 64:(pr + 1) * 64], Pa[r0:r0 + 64, pr, :], Pt[r0:r0 + 64, pr, :])
                        Pa2 = PA0 if lvl % 2 == 1 else PA1
                        for hf in range(2):
                            K.copy("act", f2(Pa2[64 * hf:64 * hf + 64]), p1[hf][64 * hf:64 * hf + 64, 0:256])
                        if lvl < 5:
                            Pt2 = PT0 if lvl % 2 == 1 else PT1
                            for hf in range(2):
                                K.copy("dve", f2(Pt2[64 * hf:64 * hf + 64]), p2[hf][64 * hf:64 * hf + 64, 0:256])
                        yield
                        for c in CORD:
                            r0, pr = HP[c]
                            K.mm(p3[r0 // 64][r0:r0 + 64, pr * 64:(pr + 1) * 64], Pa2[r0:r0 + 64, pr, :], M[r0:r0 + 64, pr, :])
                        if lvl < 5:
                            Mn = M1_ if M is M0_ else M0_
                            for hf in range(2):
                                K.tt("dve", f2(Mn[64 * hf:64 * hf + 64]), p3[hf][64 * hf:64 * hf + 64, 0:256], f2(M[64 * hf:64 * hf + 64]), ALU.add)
                            M = Mn
                        else:
                            for hf in range(2):
                                K.tt("dve", f2(Tt[64 * hf:64 * hf + 64]), p3[hf][64 * hf:64 * hf + 64, 0:256], f2(M[64 * hf:64 * hf + 64]), ALU.add)
                        Pa = Pa2
                        if lvl < 5:
                            Pt = Pt2
                        yield
                    pW = (psb(1), psb(2))
                    for c in CORD:
                        r0, pr = HP[c]
                        K.mm(pW[r0 // 64][:, pr * 64:(pr + 1) * 64], kgn[r0:r0 + 64, pr, :], Tt[r0:r0 + 64, pr, :])
                    for hf in range(2):
                        K.copy("act", WTn[:, 4 * hf:4 * hf + 4, :].re("p c i -> p (c i)"), pW[hf][:, 0:256])
                    yield

                def scan(d, b, ps_):
                    dh = d * 8 + h
                    cb0 = b * 8
                    kdec, qdT, QKt, Tt, WTn = ps_["kdec"], ps_["qdT"], ps_["QKt"], ps_["Tt"], ps_["WTn"]
                    bank = psb(6) if d == 0 else psb(7)
                    S, Sb, vnew = S2[d], Sb2[d], vnew2[d]
                    corder = list(range(8)) if d == 0 else list(range(7, -1, -1))
                    for c in corder:
                        cg = cb0 + c
                        r0, pr = HP[c]
                        seq = 0 if cg < 4 else (1 if cg < 8 else 2)
                        first = (cg == SEQ[seq][0]) if d == 0 else (cg == SEQ[seq][1] - 1)
                        last = (cg == SEQ[seq][1] - 1) if d == 0 else (cg == SEQ[seq][0])
                        if first:
                            if seq < 2:
                                K.memset("dve", S.v, 0.0)
                            else:
                                K.dma(S.v, (I["state_f"] if d == 0 else I["state_b"])[j, h])
                            K.copy("act", Sb.v, S.v)
                        pv = bank
                        pOc = bank[:, 256:320]
                        K.mm(pv[r0:r0 + 64, 0:128], Tt[r0:r0 + 64, pr, :], v_c[r0:r0 + 64, b, pr, :], start=True, stop=False)
                        K.mm(pv[r0:r0 + 64, 0:128], WTn[:, c, :], Sb.v, start=False, stop=True)
                        K.ts("dve", vnew[r0:r0 + 64, :], pv[r0:r0 + 64, 0:128], beta2[r0:r0 + 64, cg, dh:dh + 1], ALU.mult)
                        K.mm(pOc, Sb.v, qdT[:, c, :], start=True, stop=False)
                        K.mm(pOc, vnew[r0:r0 + 64, :], QKt[r0:r0 + 64, pr, :], start=False, stop=True)
                        K.mm(pv[:, 128:256], kdec[r0:r0 + 64, pr, :], vnew[r0:r0 + 64, :])
                        K.stt("dve", S.v, S.v, egt[:, cg, dh:dh + 1], pv[:, 128:256], ALU.mult, ALU.add)
                        K.tt("dve", oaccs[b][:, c * 64:(c + 1) * 64], pOc, oaccs[b][:, c * 64:(c + 1) * 64], ALU.add)
                        if last and seq < 2:
                            K.dma((O["o_sf"] if d == 0 else O["o_sb"])[seq, j, h], S.v)
                        if not last:
                            K.copy("act", Sb.v, S.v)
                        yield
                    yield

                fwd_u = [(0, bb) for bb in [0, 1, 2, 3, 4]]
                bwd_u = [(1, bb) for bb in [0, 4, 3, 2, 1]]
                units = []
                for q in range(5):
                    units += [fwd_u[q], bwd_u[q]]

                def chain(*gs):
                    for g_ in gs:
                        yield from g_

                def drive(gens):
                    gens = list(gens)
                    while gens:
                        for g_ in list(gens):
                            try:
                                next(g_)
                            except StopIteration:
                                gens.remove(g_)
                drive([chain(intra(units[0][0], units[0][1], psets[0]), intra(units[1][0], units[1][1], psets[1]))])
                for r in range(5):
                    gens = [scan(units[2 * r][0], units[2 * r][1], psets[(2 * r) % 4]),
                            scan(units[2 * r + 1][0], units[2 * r + 1][1], psets[(2 * r + 1) % 4])]
                    if r < 4:
                        gens.append(chain(intra(units[2 * r + 2][0], units[2 * r + 2][1], psets[(2 * r + 2) % 4]),
                                          intra(units[2 * r + 3][0], units[2 * r + 3][1], psets[(2 * r + 3) % 4])))
                    drive(gens)
                R.top = top_h
                scr = norm_scr(light=True)
                wz = R.alloc((128, 8, 128), BF16, "wz")
                woh = R.alloc((128, D), BF16, "gwoh")
                wohp = R.alloc((128, D), BF16, "gwohp")
                og = og_keep if h % 2 == 0 else R.alloc((128, T), BF16, "og")
                zs = R.alloc((128, 512), F32, "zs")
                on = R.alloc((128, 512), F32, "on")
                K.dma(wz.v, I["gdn_w_in"][j].re("(k p) n -> p k n", p=128)[:, :, 3072 + h * 128:3072 + (h + 1) * 128], eng="pool")
                if h % 2 == 1:
                    K.dma(woh.v, I["gdn_w_out"][j][h * 128:(h + 1) * 128, :], eng="pool")
                    K.dma(wohp.v, I["gdn_w_out"][j][(h - 1) * 128:h * 128, :], eng="pool")
                for b, (c0, n, cond) in enumerate(TB):
                    pb = psb(b % 2)
                    for kc in range(8):
                        K.mm(pb, wz[:, kc, :], HT[kc][b].v, start=(kc == 0), stop=(kc == 7))
                    K.act(zs.v, pb, AF.Silu)
                    fm_norm(on.v, oaccs[b].v, 128, 512, cw[:, 72:73], 128, scr, None)
                    K.tt("dve", og[:, c0:c0 + 512], on.v, zs.v, ALU.mult)
                if h % 2 == 1:
                    out_accum(i, [og[:, c0:c0 + 512] for (c0, n, cond) in TB], woh.v,
                              extra=([og_keep[:, c0:c0 + 512] for (c0, n, cond) in TB], wohp.v))

        if depth:
            load_gains(0)
            run(adaln(0))
        cnt = {0: 0, 1: 0, 2: 0}
        for i, kind in enumerate(kinds):
            j = cnt[kind]
            cnt[kind] += 1
            R.reset()
            make_coef(i)
            mod_norm(i, 0)
            if kind == 2:
                gqa_layer(i, j)
            elif kind == 1:
                mla_layer(i, j)
            elif kind == 0:
                gdn_layer(i, j)
            R.reset()
            mod_norm(i, 1)
            bg = None
            if i + 1 < depth:
                def bgf(i=i):
                    yield from adaln(i + 1)
                bg = bgf()
            if do_mlp:
                mlp(i, bg)
            elif bg is not None:
                run(bg)
            if i + 1 < depth:
                load_gains(i + 1)

        R.reset()
        ost = [R.alloc((128, D), F32, "xo%d" % q) for q in range(2)]
        for t in range(int(os.environ.get("KNT_OUT", str(NT_DBG)))):
            b, off = divmod(t * 128, 512)
            o_ = ost[(t + int(os.environ.get("KSL", "0"))) % 2]
            for g in range(2):
                pb = psb((t * 2 + g + int(os.environ.get("KOFF", "0"))) % int(os.environ.get("KNB", "4")))
                for q in range(4):
                    kc = g * 4 + q
                    K.transpose(pb[:, q * 128:(q + 1) * 128], XT[kc][b][:, off:off + 128], ident.v)
                K.copy("dve" if g == 0 else "act", o_[:, g * 512:(g + 1) * 512], pb)
            dst = O["yp"][t * 128:(t + 1) * 128, :] if t < 4 else O["ys"][(t - 4) * 128:(t - 3) * 128, :]
            K.dma(dst, o_.v)
        K.emit()
    return nc


def make_in_maps(inp, kinds=KINDS_FULL):
    f = lambda a: np.ascontiguousarray(np.asarray(a, dtype=np.float32))
    depth = max(len(kinds), 1)
    n_a = sum(1 for k in kinds if k == 0)
    n_b = sum(1 for k in kinds if k == 1)
    n_c = sum(1 for k in kinds if k == 2)
    shared = {
        "norm_mix": f(inp["norm_mix"])[:depth].reshape(depth, 8, 128),
        "norm_mlp": f(inp["norm_mlp"])[:depth].reshape(depth, 8, 128),
        "w_mod": f(inp["w_mod"])[:depth],
        "b_mod": f(inp["b_mod"])[:depth].reshape(depth, 48, 128),
        "w_mlp_in": f(inp["w_mlp_in"])[:depth],
        "w_mlp_out": f(inp["w_mlp_out"])[:depth],
        "ident": np.eye(128, dtype=np.float32),
        "ones": np.ones((128, 128), np.float32),
    }
    if n_c:
        cos, sin, pT = rope_tables(32, 2048)
        shared.update({
            "gqa_w_in": f(inp["gqa_w_in"])[:n_c], "gqa_q_norm": f(inp["gqa_q_norm"])[:n_c],
            "gqa_k_norm": f(inp["gqa_k_norm"])[:n_c], "gqa_w_out": f(inp["gqa_w_out"])[:n_c],
            "rope_c_cos": cos, "rope_c_sin": sin, "rope_c_p": pT})
    if n_b:
        cos, sin, pT = rope_tables(16, 2048)
        shared.update({
            "mla_w_down": f(inp["mla_w_down"])[:n_b],
            "mla_q_lat_norm": f(inp["mla_q_lat_norm"])[:n_b].reshape(n_b, 3, 128),
            "mla_kv_lat_norm": f(inp["mla_kv_lat_norm"])[:n_b].reshape(n_b, 2, 128),
            "mla_w_uq": f(inp["mla_w_uq"])[:n_b], "mla_w_ukv": f(inp["mla_w_ukv"])[:n_b],
            "mla_qn_nope": f(inp["mla_qn_nope"])[:n_b], "mla_qn_rope": f(inp["mla_qn_rope"])[:n_b],
            "mla_kn_nope": f(inp["mla_kn_nope"])[:n_b], "mla_kn_rope": f(inp["mla_kn_rope"])[:n_b],
            "mla_w_out": f(inp["mla_w_out"])[:n_b],
            "rope_b_cos": cos, "rope_b_sin": sin, "rope_b_p": pT})
    if n_a:
        shared.update(gdn_host_inputs(inp, n_a))
    maps = []
    xp = f(inp["x_prompt"])
    xs = f(inp["x_sample"])
    for c in range(NCORES):
        m = dict(shared)
        m["xp"] = xp[2 * c:2 * c + 2].reshape(512, D)
        m["xs"] = xs[c]
        m["cond"] = np.stack([f(inp["c_ctx"]).reshape(8, 128), f(inp["c"])[c].reshape(8, 128)], 0)
        if n_c:
            m["cache_gqa_k"] = f(inp["cache_gqa_k"])[c, :n_c].reshape(n_c, 512, 256)
            m["cache_gqa_v"] = f(inp["cache_gqa_v"])[c, :n_c].reshape(n_c, 512, 256)
        if n_b:
            m["cache_mla_ckv"] = f(inp["cache_mla_ckv"])[c, :n_b]
            m["cache_mla_krope"] = f(inp["cache_mla_krope"])[c, :n_b]
        if n_a:
            m["state_f"] = f(inp["state_gdn_fwd"])[c, :n_a]
            m["state_b"] = f(inp["state_gdn_bwd"])[c, :n_a]
        maps.append(m)
    return maps


def gdn_host_inputs(inp, n_a):
    f = lambda a: np.ascontiguousarray(np.asarray(a, dtype=np.float32))
    w = f(inp["gdn_w_in"])[:n_a]
    idx = np.arange(64)
    NEG = -30000.0
    m = np.zeros((8, 128, 128), np.float32)
    jj, ii = np.meshgrid(idx, idx, indexing="ij")
    m[0, :64, :64] = (jj <= ii)
    m[1, :64, :64] = (jj >= ii)
    m[2, :64, :64] = np.where(ii > jj, 0.0, NEG)
    m[3, :64, :64] = np.where(ii >= jj, 0.0, NEG)
    m[4, :64, :64] = np.where(ii < jj, 0.0, NEG)
    m[5, :64, :64] = np.where(ii <= jj, 0.0, NEG)
    return {
        "gdn_w_in": np.ascontiguousarray(w[:, :, :4096]),
        "gdn_w_gb": np.ascontiguousarray(w[:, :, 4096:4128]),
        "gdn_conv": f(inp["gdn_conv"])[:n_a].reshape(n_a, 72, 128),
        "gdn_a_log": f(inp["gdn_a_log"])[:n_a].reshape(n_a, 16),
        "gdn_dt_bias": f(inp["gdn_dt_bias"])[:n_a].reshape(n_a, 16),
        "gdn_out_norm": f(inp["gdn_out_norm"])[:n_a],
        "gdn_w_out": f(inp["gdn_w_out"])[:n_a],
        "gdn_masks": m,
    }


def assemble(results, kinds=KINDS_FULL):
    n_a = sum(1 for k in kinds if k == 0)
    n_b = sum(1 for k in kinds if k == 1)
    n_c = sum(1 for k in kinds if k == 2)
    yp = np.concatenate([r["yp"].reshape(2, 256, D) for r in results], 0)
    ys = np.stack([r["ys"] for r in results], 0)
    sf = np.concatenate([r["o_sf"][:, :n_a] for r in results], 0)
    sbw = np.concatenate([r["o_sb"][:, :n_a] for r in results], 0)
    ckv = np.concatenate([r["o_ckv"][:, :n_b] for r in results], 0)
    kr = np.concatenate([r["o_kr"][:, :n_b] for r in results], 0)
    gk = np.concatenate([r["o_gk"][:, :n_c].reshape(2, n_c, 256, 2, 128) for r in results], 0)
    gv = np.concatenate([r["o_gv"][:, :n_c].reshape(2, n_c, 256, 2, 128) for r in results], 0)
    return (yp, ys, sf, sbw, ckv, kr, gk, gv)


_NC_CACHE = {}


def kernel(**inputs):
    kinds = KINDS_FULL
    if kinds not in _NC_CACHE:
        _NC_CACHE[kinds] = build(kinds)
    nc = _NC_CACHE[kinds]
    maps = make_in_maps(inputs, kinds)
    res = run_bass_kernel_spmd(nc, maps, core_ids=list(range(NCORES)))
    return assemble(res.results, kinds)
```
